# Optimizing a Trainium2 kernel written in Bass

```python
import jax
import jax.numpy as jnp
from jax import lax
import numpy as np

D_MODEL = 4096
BATCH = 4
SEQ = 2048
DEPTH = 1
DEC_BATCH = 128
DEC_SEQ = 8
PAST_LEN = 16384
PAGE_SIZE = 128

GLA_WIDTH = D_MODEL // 2
GLA_HEADS = 4
GLA_DK = GLA_WIDTH // 2 // GLA_HEADS
GLA_DV = GLA_WIDTH // GLA_HEADS
GLA_QK = GLA_HEADS * GLA_DK
GLA_GATE_RANK = 16
GLA_TAU = 16.0
GLA_CHUNK = 64
RWKV_WIDTH = D_MODEL // 2
RWKV_HEAD = 64
RWKV_HEADS = RWKV_WIDTH // RWKV_HEAD
RWKV_W_LORA = 96
RWKV_A_LORA = 96
RWKV_G_LORA = 256
RWKV_SIZES = (RWKV_WIDTH, RWKV_W_LORA, RWKV_WIDTH, RWKV_WIDTH, RWKV_A_LORA, RWKV_G_LORA)
RWKV_COLS = sum(RWKV_SIZES)
RWKV_SPLITS = tuple(int(s) for s in np.cumsum(RWKV_SIZES)[:-1])
GN_EPS = 64e-5
IN_SIZES = (GLA_QK, GLA_QK, GLA_WIDTH, GLA_GATE_RANK, GLA_WIDTH, RWKV_COLS, D_MODEL, D_MODEL)
IN_TOTAL = sum(IN_SIZES)
IN_SPLITS = tuple(int(s) for s in np.cumsum(IN_SIZES)[:-1])
D_FF = 11008
CONV_W = 3
P_DIM = 256
EPS = 1e-6

kernel_name = 'hybrid_gla_rwkv7_decoder_step'


def rmsnorm(x, gain):
    xf = x.astype(jnp.float32)
    y = xf * lax.rsqrt(jnp.mean(xf * xf, axis=-1, keepdims=True) + EPS)
    return (y * gain.astype(jnp.float32)).astype(x.dtype)


def gla_chunked(q, k, v, log_a, s0):
    B, L, H, _ = q.shape
    c = min(GLA_CHUNK, L)
    n = -(-L // c)
    pad = n * c - L

    def blocks(t):
        t = jnp.pad(t, ((0, 0), (0, pad), (0, 0), (0, 0)))
        return t.reshape(B, n, c, H, t.shape[-1]).transpose(1, 0, 3, 2, 4)

    causal = jnp.tril(jnp.ones((c, c), dtype=bool))

    def step(S, blk):
        qb, kb, vb, ab = blk
        b = jnp.cumsum(ab, axis=2)
        b_end = b[:, :, -1:, :]
        qd = qb * jnp.exp(b)
        kd = kb * jnp.exp(-b)
        att = jnp.where(causal, jnp.einsum('bhtd,bhsd->bhts', qd, kd), 0.0)
        o = jnp.einsum('bhts,bhsv->bhtv', att, vb) + jnp.einsum('bhtd,bhdv->bhtv', qd, S)
        S = S * jnp.exp(b_end[:, :, 0, :, None]) + jnp.einsum('bhsd,bhsv->bhdv', kb * jnp.exp(b_end - b), vb)
        return S, o

    S, o = lax.scan(step, s0.astype(jnp.float32), (blocks(q), blocks(k), blocks(v), blocks(log_a)))
    o = o.transpose(1, 0, 3, 2, 4).reshape(B, n * c, H, -1)[:, :L]
    return o, S


def rwkv7_scan(r, w, k, v, kk, a, s0):
    def step(S, inp):
        r_t, w_t, k_t, v_t, kk_t, a_t = inp
        sa = jnp.einsum('bhvk,bhk->bhv', S, -kk_t)
        S = (S * w_t[:, :, None, :] + sa[..., None] * (kk_t * a_t)[:, :, None, :]
             + v_t[..., None] * k_t[:, :, None, :])
        return S, jnp.einsum('bhvk,bhk->bhv', S, r_t)

    xs = tuple(jnp.swapaxes(t, 0, 1) for t in (r, w, k, v, kk, a))
    S, o = lax.scan(step, s0.astype(jnp.float32), xs)
    return jnp.swapaxes(o, 0, 1), S


def layer(x, p, s_gla, s_rwkv, s_shift, s_conv,
          w_in, w_alpha2, b_alpha, gla_norm, w_branch_a,
          mu_shift, w0, w_decay2, a0, w_iclr2, w_gate2, k_k, k_a, r_k, ln_x_w, ln_x_b, w_branch_b,
          w_out, g_pre_mix, g_post_mix, g_pre_ffn, g_post_ffn,
          w_up, conv_w, conv_b, w_down, g_pe, w_pe_gate, w_pe):
    B, L, _ = x.shape
    f32 = jnp.float32
    h = rmsnorm(x, g_pre_mix)
    proj = h @ w_in
    q, k, v, za, zg, zr, gate_a, gate_b = jnp.split(proj, IN_SPLITS, axis=-1)

    q = q.reshape(B, L, GLA_HEADS, GLA_DK).astype(f32) * (GLA_DK ** -0.5)
    k = k.reshape(B, L, GLA_HEADS, GLA_DK).astype(f32)
    v = v.reshape(B, L, GLA_HEADS, GLA_DV).astype(f32)
    log_a = (jax.nn.log_sigmoid((za @ w_alpha2 + b_alpha).astype(f32)) / GLA_TAU).reshape(B, L, GLA_HEADS, GLA_DK)
    o_a, s_gla_new = gla_chunked(q, k, v, log_a, s_gla)
    o_a = o_a * lax.rsqrt(jnp.mean(o_a * o_a, axis=-1, keepdims=True) + EPS) * gla_norm.astype(f32)
    o_a = o_a.reshape(B, L, GLA_WIDTH).astype(x.dtype) * jax.nn.silu(zg)
    y_a = o_a @ w_branch_a

    prev = jnp.concatenate([s_shift[:, None, :].astype(zr.dtype), zr[:, :-1]], axis=1)
    zs = zr + (prev - zr) * mu_shift
    new_shift = zr[:, -1]
    r, xw, kr, vr, xa, xg = jnp.split(zs, RWKV_SPLITS, axis=-1)
    w_log = -jax.nn.softplus(-(w0 + jnp.tanh(xw) @ w_decay2).astype(f32)) - 0.5
    decay = jnp.exp(-jnp.exp(w_log))
    a = jax.nn.sigmoid((a0 + xa @ w_iclr2).astype(f32))
    g = jax.nn.sigmoid(xg) @ w_gate2
    kr = kr.astype(f32)

    def heads(t):
        return t.reshape(B, L, RWKV_HEADS, RWKV_HEAD)

    kk = heads(kr * k_k.astype(f32))
    kk = kk / jnp.maximum(jnp.sqrt(jnp.sum(kk * kk, axis=-1, keepdims=True)), 1e-12)
    kr = kr * (1.0 + (a - 1.0) * k_a.astype(f32))
    rh, kh, vh = heads(r.astype(f32)), heads(kr), heads(vr.astype(f32))
    o_b, s_rwkv_new = rwkv7_scan(rh, heads(decay), kh, vh, kk, heads(a), s_rwkv)
    mu = jnp.mean(o_b, axis=-1, keepdims=True)
    var = jnp.mean(jnp.square(o_b - mu), axis=-1, keepdims=True)
    o_b = ((o_b - mu) * lax.rsqrt(var + GN_EPS)).reshape(B, L, RWKV_WIDTH) * ln_x_w + ln_x_b
    bonus = jnp.sum(rh * kh * r_k.astype(f32), axis=-1, keepdims=True) * vh
    o_b = o_b + bonus.reshape(B, L, RWKV_WIDTH)
    y_b = (o_b.astype(x.dtype) * g) @ w_branch_b

    mixed = jax.nn.sigmoid(gate_a) * y_a + jax.nn.sigmoid(gate_b) * y_b
    x = x + rmsnorm(mixed @ w_out, g_post_mix)

    hf = rmsnorm(x, g_pre_ffn)
    up_g, up_v = jnp.split(hf @ w_up, 2, axis=-1)
    gpad = jnp.concatenate([s_conv.astype(up_g.dtype), up_g], axis=1)
    conv = conv_b + gpad[:, 0:L] * conv_w[0]
    for j in range(1, CONV_W):
        conv = conv + gpad[:, j:j + L] * conv_w[j]
    new_conv = gpad[:, L:]
    f = (jax.nn.gelu(conv) * up_v) @ w_down
    x = x + rmsnorm(f, g_post_ffn)

    x = x + jax.nn.sigmoid(rmsnorm(x, g_pe) @ w_pe_gate) * (p.astype(x.dtype) @ w_pe)
    return x, s_gla_new, s_rwkv_new, new_shift, new_conv


def setup_inputs(seed: int = 0) -> dict:
    key = jax.random.key(seed)
    ks = iter(jax.random.split(key, 48))
    f32 = jnp.float32

    def nrm(shape, scale):
        return jax.random.normal(next(ks), shape, f32) * scale

    def unif(shape, lo, hi):
        return jax.random.uniform(next(ks), shape, f32, lo, hi)

    def gain(n):
        return 1.0 + nrm((DEPTH, n), 0.02)

    return {
        'x_prompt': nrm((BATCH, SEQ, D_MODEL), 1.0),
        'x_sample': nrm((DEC_BATCH, DEC_SEQ, D_MODEL), 1.0),
        'state_gla': nrm((DEPTH, DEC_BATCH, GLA_HEADS, GLA_DK, GLA_DV), 0.5),
        'state_rwkv': nrm((DEPTH, DEC_BATCH, RWKV_HEADS, RWKV_HEAD, RWKV_HEAD), 0.3),
        'state_shift': nrm((DEPTH, DEC_BATCH, RWKV_COLS), 1.0),
        'state_ffn_conv': nrm((DEPTH, DEC_BATCH, CONV_W - 1, D_FF), 1.0),
        'p_prompt': nrm((DEPTH, BATCH, SEQ, P_DIM), 1.0),
        'p_sample': nrm((DEPTH, DEC_BATCH, DEC_SEQ, P_DIM), 1.0),
        'w_in': nrm((DEPTH, D_MODEL, IN_TOTAL), D_MODEL ** -0.5),
        'w_alpha2': nrm((DEPTH, GLA_GATE_RANK, GLA_QK), GLA_GATE_RANK ** -0.5),
        'b_alpha': unif((DEPTH, GLA_QK), -1.0, 4.0),
        'gla_norm': gain(GLA_DV),
        'w_branch_a': nrm((DEPTH, GLA_WIDTH, D_MODEL), GLA_WIDTH ** -0.5),
        'mu_shift': unif((DEPTH, RWKV_COLS), 0.0, 1.0),
        'w0': unif((DEPTH, RWKV_WIDTH), -4.0, 1.0),
        'w_decay2': nrm((DEPTH, RWKV_W_LORA, RWKV_WIDTH), 0.1),
        'a0': nrm((DEPTH, RWKV_WIDTH), 0.1),
        'w_iclr2': nrm((DEPTH, RWKV_A_LORA, RWKV_WIDTH), RWKV_A_LORA ** -0.5),
        'w_gate2': nrm((DEPTH, RWKV_G_LORA, RWKV_WIDTH), RWKV_G_LORA ** -0.5),
        'k_k': 0.85 + nrm((DEPTH, RWKV_WIDTH), 0.02),
        'k_a': gain(RWKV_WIDTH),
        'r_k': nrm((DEPTH, RWKV_HEADS, RWKV_HEAD), 0.1),
        'ln_x_w': gain(RWKV_WIDTH),
        'ln_x_b': nrm((DEPTH, RWKV_WIDTH), 0.02),
        'w_branch_b': nrm((DEPTH, RWKV_WIDTH, D_MODEL), RWKV_WIDTH ** -0.5),
        'w_out': nrm((DEPTH, D_MODEL, D_MODEL), D_MODEL ** -0.5),
        'g_pre_mix': gain(D_MODEL),
        'g_post_mix': gain(D_MODEL),
        'g_pre_ffn': gain(D_MODEL),
        'g_post_ffn': gain(D_MODEL),
        'w_up': nrm((DEPTH, D_MODEL, 2 * D_FF), D_MODEL ** -0.5),
        'conv_w': nrm((DEPTH, CONV_W, D_FF), CONV_W ** -0.5),
        'conv_b': nrm((DEPTH, D_FF), 0.02),
        'w_down': nrm((DEPTH, D_FF, D_MODEL), D_FF ** -0.5),
        'g_pe': gain(D_MODEL),
        'w_pe_gate': nrm((DEPTH, D_MODEL, D_MODEL), D_MODEL ** -0.5),
        'w_pe': nrm((DEPTH, P_DIM, D_MODEL), P_DIM ** -0.5),
    }


def reference(x_prompt, x_sample, state_gla, state_rwkv, state_shift, state_ffn_conv, p_prompt, p_sample,
              w_in, w_alpha2, b_alpha, gla_norm, w_branch_a,
              mu_shift, w0, w_decay2, a0, w_iclr2, w_gate2, k_k, k_a, r_k, ln_x_w, ln_x_b, w_branch_b,
              w_out, g_pre_mix, g_post_mix, g_pre_ffn, g_post_ffn,
              w_up, conv_w, conv_b, w_down, g_pe, w_pe_gate, w_pe):
    params = (w_in, w_alpha2, b_alpha, gla_norm, w_branch_a,
              mu_shift, w0, w_decay2, a0, w_iclr2, w_gate2, k_k, k_a, r_k, ln_x_w, ln_x_b, w_branch_b,
              w_out, g_pre_mix, g_post_mix, g_pre_ffn, g_post_ffn,
              w_up, conv_w, conv_b, w_down, g_pe, w_pe_gate, w_pe)
    yp, ys = x_prompt, x_sample
    gla_p, rwkv_p, shift_p, conv_p = [], [], [], []
    gla_s, rwkv_s, shift_s, conv_s = [], [], [], []
    for i in range(DEPTH):
        lp = tuple(t[i] for t in params)
        z_gla = jnp.zeros((BATCH, GLA_HEADS, GLA_DK, GLA_DV), jnp.float32)
        z_rwkv = jnp.zeros((BATCH, RWKV_HEADS, RWKV_HEAD, RWKV_HEAD), jnp.float32)
        z_shift = jnp.zeros((BATCH, RWKV_COLS), x_prompt.dtype)
        z_conv = jnp.zeros((BATCH, CONV_W - 1, D_FF), x_prompt.dtype)
        yp, sg, sr, ss, sc = layer(yp, p_prompt[i], z_gla, z_rwkv, z_shift, z_conv, *lp)
        gla_p.append(sg); rwkv_p.append(sr); shift_p.append(ss); conv_p.append(sc)
        ys, sg, sr, ss, sc = layer(ys, p_sample[i], state_gla[i], state_rwkv[i], state_shift[i],
                                   state_ffn_conv[i], *lp)
        gla_s.append(sg); rwkv_s.append(sr); shift_s.append(ss); conv_s.append(sc)
    return (yp, ys,
            jnp.stack(gla_p), jnp.stack(rwkv_p), jnp.stack(shift_p), jnp.stack(conv_p),
            jnp.stack(gla_s), jnp.stack(rwkv_s), jnp.stack(shift_s), jnp.stack(conv_s))
```

```python
import contextlib
import numpy as np
import concourse.bass as bass
import concourse.mybir as mybir
from concourse.bass_utils import run_bass_kernel_spmd

F32 = mybir.dt.float32
BF16 = mybir.dt.bfloat16
ALU = mybir.AluOpType
AF = mybir.ActivationFunctionType
AX = mybir.AxisListType

D = 4096
DFF = 11008
PD = 256
GH, GDK, GDV = 4, 256, 512
RH, RN = 32, 64
EPS = 1e-6
GN_EPS = 64e-5
IN_TOTAL = 20944
PIECES = dict(q=(0, 1024), k=(1024, 1024), v=(2048, 2048), za=(4096, 16), zg=(4112, 2048),
              r=(6160, 2048), xw=(8208, 96), kr=(8304, 2048), vr=(10352, 2048), xa=(12400, 96),
              xg=(12496, 256), ga=(12752, 4096), gb=(16848, 4096))
ZR0 = 6160

COMPUTE = ("pe", "act", "dve", "pool")
import os as _os
RS_CUT = int(_os.environ["RS_CUT"]) if "RS_CUT" in _os.environ else None
RS_SUB = _os.environ.get("RS_SUB", "")


class Op:
    __slots__ = ("eng", "fn", "deps", "key", "sig", "cnt")

    def __init__(self, eng, fn, deps, key):
        self.eng, self.fn, self.deps, self.key = eng, fn, deps, key
        self.sig = False
        self.cnt = 0


class Prog:
    def __init__(self):
        self.ops = []
        self.lastw = {}
        self.rd_eng = {}
        self.rd_dma = {}

    def add(self, eng, fn, reads=(), writes=(), key=None):
        i = len(self.ops)
        deps = set()
        for r in reads:
            w = self.lastw.get(r)
            if w is not None:
                deps.add(w)
        for r in writes:
            w = self.lastw.get(r)
            if w is not None:
                deps.add(w)
            d = self.rd_eng.get(r)
            if d:
                deps.update(d.values())
            l = self.rd_dma.get(r)
            if l:
                deps.update(l)
        for r in reads:
            if key is not None:
                self.rd_dma.setdefault(r, []).append(i)
            else:
                self.rd_eng.setdefault(r, {})[eng] = i
        for r in writes:
            self.lastw[r] = i
            self.rd_eng[r] = {}
            self.rd_dma[r] = []
        deps.discard(i)
        self.ops.append(Op(eng, fn, deps, key))
        return i

    def pe(self, fn, reads=(), writes=()):
        return self.add("pe", fn, reads, writes)

    def act(self, fn, reads=(), writes=()):
        return self.add("act", fn, reads, writes)

    def dve(self, fn, reads=(), writes=()):
        return self.add("dve", fn, reads, writes)

    def pool(self, fn, reads=(), writes=()):
        return self.add("pool", fn, reads, writes)

    def dma(self, fn, key, reads=(), writes=(), q="sp"):
        base = (list(writes) + list(reads))[0]
        return self.add(q, fn, reads, writes, key=str(key).split("_")[0] + "_" + base)

    def emit(self, nc, stack, kb=None):
        ops = self.ops
        for o in ops:
            for d in o.deps:
                y = ops[d]
                if y.key is None and (y.eng != o.eng or o.key is not None or o.eng != "pe"):
                    y.sig = True
        ecnt = {e: 0 for e in COMPUTE}
        kcnt = {}
        kq = {}
        for o in ops:
            if o.key is not None:
                kcnt[o.key] = kcnt.get(o.key, 0) + 16
                o.cnt = kcnt[o.key]
                assert kq.setdefault(o.key, o.eng) == o.eng, ("dma key used from two queues", o.key)
            elif o.sig:
                ecnt[o.eng] += 1
                o.cnt = ecnt[o.eng]
        if not kb.esem:
            for e_ in COMPUTE:
                kb.esem[e_] = kb.gst.enter_context(nc.semaphore("s_" + e_))
                kb.ebase[e_] = 0
        esem = kb.esem
        eb = dict(kb.ebase)
        for o in ops:
            if o.key is None and o.sig:
                o.cnt += eb[o.eng]
        for e_ in COMPUTE:
            kb.ebase[e_] += ecnt[e_]
        ksem = {}
        kbase = {}
        nidx = {"sw": 0, "hw": 0}
        for k in kcnt:
            kind = "sw" if kq[k] == "pool" else "hw"
            sems, bases = kb.ksems[kind], kb.kbases[kind]
            i_ = nidx[kind]
            nidx[kind] += 1
            while len(sems) <= i_:
                sems.append(kb.gst.enter_context(nc.semaphore("%s%d" % (kind, len(sems)))))
                bases.append(0)
            ksem[k] = sems[i_]
            kbase[k] = bases[i_]
            bases[i_] += kcnt[k]
        for o in ops:
            if o.key is not None:
                o.cnt += kbase[o.key]
        for k in kcnt:
            kcnt[k] += kbase[k]
        engs = {}
        for o in ops:
            engs.setdefault(o.eng, []).append(o)
        block = stack.enter_context(nc.Block())

        def run(eng_name, e):
            waited = {}
            for o in engs.get(eng_name, ()):
                need = {}
                for d in o.deps:
                    y = ops[d]
                    if y.key is not None:
                        s = ("k", y.key)
                    elif y.eng != o.eng or o.key is not None or o.eng != "pe":
                        s = ("e", y.eng)
                    else:
                        continue
                    if y.cnt > need.get(s, 0):
                        need[s] = y.cnt
                for s, v in need.items():
                    if v > waited.get(s, 0):
                        waited[s] = v
                        e.wait_ge(ksem[s[1]] if s[0] == "k" else esem[s[1]], v)
                ins = o.fn(e)
                if o.key is not None:
                    ins.then_inc(ksem[o.key], 16)
                elif o.sig:
                    ins.then_inc(esem[o.eng], 1)
            if eng_name == "sp":
                for k, v in kcnt.items():
                    if v > waited.get(("k", k), 0):
                        e.wait_ge(ksem[k], v)

        @block.sync
        def _(e):
            run("sp", e)

        @block.tensor
        def _(e):
            run("pe", e)

        @block.scalar
        def _(e):
            run("act", e)

        @block.vector
        def _(e):
            run("dve", e)

        @block.gpsimd
        def _(e):
            run("pool", e)


class Stage:
    def __init__(self, kb, name):
        self.kb, self.nc, self.name = kb, kb.nc, name
        self.st = contextlib.ExitStack()
        self.P = Prog()
        self.nps = 0
        self.rr = 0
        self.uid = 0

    def __enter__(self):
        self.st.__enter__()
        return self

    def __exit__(self, *a):
        if a[0] is None:
            self.P.emit(self.nc, self.st, self.kb)
        return self.st.__exit__(*a)

    def sb(self, name, shape, dt):
        return self.st.enter_context(self.nc.sbuf_tensor(self.name + "_" + name, list(shape), dt))

    def psum_banks(self, n=8):
        self.banks = [self.st.enter_context(self.nc.psum_tensor("%s_pb%d" % (self.name, i), [128, 512], F32))
                      for i in range(n)]
        self.nbanks = n

    def bank(self, pool=None):
        if pool is None:
            i = self.rr % self.nbanks
        else:
            i = pool[self.rr % len(pool)]
        self.rr += 1
        return self.banks[i], "pb%d" % i

    def key(self, base):
        return self.name + "_" + base

    def evac_eng(self):
        self.uid += 1
        return "act" if self.uid % 2 else "dve"


def tok_blocks(n, maxb=512):
    out = []
    a = 0
    while a < n:
        b = min(maxb, n - a)
        out.append((a, b))
        a += b
    return out


class Cfg:
    def __init__(self, nct=16, nown=8):
        self.NCT = nct
        self.NOWN = nown
        self.TT = (nct + 1) * 128
        self.CTX = nct * 128
        self.OWN0 = (nct - nown) * 128
        self.HALO0 = self.OWN0 - 128
        self.TOW = self.TT - self.OWN0
        self.TPM = self.TT - self.HALO0
        self.ZW = 1 + self.CTX + 16 * 9


def pcol_layout():
    lay = {}
    o = 0
    for name, n in [("g_pre_mix", 32), ("g_post_mix", 32), ("g_pre_ffn", 32), ("g_post_ffn", 32), ("g_pe", 32),
                    ("b_alpha", 8), ("gla_norm", 4), ("mu_r", 16), ("mu_xw", 1), ("mu_kr", 16), ("mu_vr", 16),
                    ("mu_xa", 1), ("mu_xg", 2), ("w0", 16), ("a0", 16), ("k_k", 16), ("k_a", 16), ("r_k", 16),
                    ("ln_x_w", 16), ("ln_x_b", 16), ("conv_w0", 86), ("conv_w1", 86), ("conv_w2", 86),
                    ("conv_b", 86), ("flag", 1)]:
        lay[name] = (o, n)
        o += n
    return lay, o


PCL, NPC = pcol_layout()


def host_pcols(inp, flag):
    t = np.zeros((128, NPC), np.float32)

    def put(name, vec):
        o, n = PCL[name]
        v = np.zeros(n * 128, np.float32)
        v[:vec.size] = vec.reshape(-1)
        t[:, o:o + n] = v.reshape(n, 128).T

    for nm in ["g_pre_mix", "g_post_mix", "g_pre_ffn", "g_post_ffn", "g_pe", "b_alpha", "gla_norm", "w0", "a0",
               "k_k", "k_a", "r_k", "ln_x_w", "ln_x_b", "conv_b"]:
        put(nm, np.asarray(inp[nm][0]))
    mu = np.asarray(inp["mu_shift"][0])
    put("mu_r", mu[0:2048])
    put("mu_xw", mu[2048:2144])
    put("mu_kr", mu[2144:4192])
    put("mu_vr", mu[4192:6240])
    put("mu_xa", mu[6240:6336])
    put("mu_xg", mu[6336:6592])
    cw = np.asarray(inp["conv_w"][0])
    for j in range(3):
        put("conv_w%d" % j, cw[j])
    o, n = PCL["flag"]
    t[:, o] = flag
    return t


def host_consts():
    c = {}
    p = np.arange(128)
    c["ident"] = np.eye(128, dtype=np.float32)
    c["ones"] = np.ones((128, 128), np.float32)
    c["blk64"] = (p[:, None] // 64 == p[None, :] // 64).astype(np.float32)
    for nm, L in (("p", 128), ("s", 8)):
        same = (p[:, None] // L == p[None, :] // L)
        c["incT_" + nm] = (same & (p[None, :] >= p[:, None])).astype(np.float32)
        c["strT_" + nm] = (same & (p[None, :] > p[:, None])).astype(np.float32)
        c["str_" + nm] = (same & (p[:, None] > p[None, :])).astype(np.float32)
    c["seg16"] = (p[:, None] // 8 == np.arange(16)[None, :]).astype(np.float32)
    names = ["ident", "ones", "blk64", "incT_p", "strT_p", "str_p", "incT_s", "strT_s", "str_s"]
    tab = np.concatenate([c[n] for n in names] + [np.pad(c["seg16"], ((0, 0), (0, 112)))], axis=1)
    return tab.astype(np.float32), names + ["seg16"]


CONST_NAMES = ["ident", "ones", "blk64", "incT_p", "strT_p", "str_p", "incT_s", "strT_s", "str_s", "seg16"]


class KB:
    def __init__(self, cfg, debug=False, stages=None):
        self.cfg = cfg
        self.debug = debug
        self.stages = stages
        self.nc = bass.Bass("TRN2", target_bir_lowering=False)
        self.gst = contextlib.ExitStack()
        self.dr = {}
        self.esem = {}
        self.ebase = {}
        self.ksems = {"sw": [], "hw": []}
        self.kbases = {"sw": [], "hw": []}

    def inp(self, name, shape, dt=F32):
        self.dr[name] = self.nc.dram_tensor(name, list(shape), dt, kind="ExternalInput").ap()
        return self.dr[name]

    def out(self, name, shape, dt=F32):
        self.dr[name] = self.nc.dram_tensor(name, list(shape), dt, kind="ExternalOutput").ap()
        return self.dr[name]

    def scr(self, name, shape, dt=F32):
        kind = "ExternalOutput" if self.debug else "Internal"
        self.dr[name] = self.nc.dram_tensor(name, list(shape), dt, kind=kind).ap()
        return self.dr[name]

    def want(self, s):
        return self.stages is None or s in self.stages

    def declare(self):
        c = self.cfg
        TT = c.TT
        i = self.inp
        i("xT", [D, TT])
        i("pT", [PD, c.TOW])
        i("pcols", [128, NPC])
        i("consts", [128, 128 * 10])
        i("segcol", [128, 16 * 128])
        i("sgla", [16, GH, GDK, GDV])
        i("srwkv", [16, 128, 16, 128])
        i("sshiftT", [6592, 16])
        i("sconvT", [DFF, 16, 2])
        need = {"w_in": "win", "w_branch_a": "bra", "w_branch_b": "brb", "w_out": "wout", "w_up": "up",
                "w_down": "down", "w_pe_gate": "pe"}
        for nm, shp in (("w_in", [D, IN_TOTAL]), ("w_alpha2", [16, 1024]), ("w_branch_a", [2048, D]),
                        ("w_decay2", [96, 2048]), ("w_iclr2", [96, 2048]), ("w_gate2", [256, 2048]),
                        ("w_branch_b", [2048, D]), ("w_out", [D, D]), ("w_up", [D, 2 * DFF]), ("w_down", [DFF, D]),
                        ("w_pe_gate", [D, D]), ("w_pe", [PD, D])):
            if nm in need and not self.want(need[nm]):
                continue
            i(nm, shp)
        o = self.out
        o("yT", [D, c.TOW])
        o("glap", [GH, GDK, GDV])
        o("glas", [16, GH, GDK, GDV])
        o("rwkvp", [128, 16, 128])
        o("rwkvs", [16, 128, 16, 128])
        o("shiftTp", [6592, 1])
        o("shiftTs", [6592, 16])
        o("convTp", [DFF, 2])
        o("convTs", [DFF, 16, 2])
        s = self.scr
        s("hT", [D, TT], BF16)
        for nm in ("q", "k", "zg", "ga", "gb"):
            s(nm + "T", [PIECES[nm][1], TT])
        s("zaT", [16, TT])
        for nm in ("r", "xw", "kr", "vr", "xa", "xg"):
            s(nm + "T", [PIECES[nm][1], c.ZW])
        s("vtm", [TT, 2048], BF16)
        s("oaT", [2048, TT], BF16)
        s("obT", [2048, TT], BF16)
        s("arbk", [128, 16, 4, TT], BF16)
        s("vTb", [128, 16, TT], BF16)
        s("gcT", [128, 16, c.NCT + 16])
        s("bonT", [2048, TT])
        s("gT", [2048, TT])
        s("mixT", [D, TT], BF16)
        s("yoT", [D, TT])
        s("x1T", [D, TT])
        s("hfT", [D, TT], BF16)
        s("ugT", [DFF, TT])
        s("uvT", [DFF, TT])
        s("actT", [DFF, TT], BF16)
        s("fT", [D, TT])
        s("x2T", [D, TT])
        s("hpT", [D, TT], BF16)
        s("ppT", [D, TT])
        s("pTb", [PD, TT], BF16)

    def consts(self):
        nc = self.nc
        g = self.gst
        self.pc = g.enter_context(nc.sbuf_tensor("pc", [128, NPC], F32))
        self.pcd = g.enter_context(nc.sbuf_tensor("pcd", [128, 64], F32))
        self.c32 = g.enter_context(nc.sbuf_tensor("c32", [128, 10, 128], F32))
        self.cbf = g.enter_context(nc.sbuf_tensor("cbf", [128, 10, 128], BF16))
        self.c32x = g.enter_context(nc.sbuf_tensor("c32x", [128, 16, 128], BF16))
        with Stage(self, "c") as S:
            P = S.P
            P.dma(lambda e: e.dma_start(out=self.pc[:], in_=self.dr["pcols"]), S.key("a"), writes=["pc"])
            P.dma(lambda e: e.dma_start(out=self.c32[:], in_=self.dr["consts"].rearrange("p (n c) -> p n c", c=128)),
                  S.key("b"), writes=["c32"])
            P.dve(lambda e: e.tensor_copy(out=self.cbf[:], in_=self.c32[:]), reads=["c32"], writes=["cbf"])
            P.dma(lambda e: e.dma_start(out=self.c32x[:], in_=self.dr["segcol"].rearrange("p (n c) -> p n c", c=128)),
                  S.key("d"), writes=["c32x"], q="pool")
            pcd, pc = self.pcd, self.pc
            o, n = PCL["b_alpha"]
            P.dve(lambda e: e.tensor_scalar(out=pcd[:, 0:8], in0=pc[:, o:o + 8], scalar1=-1.0, scalar2=None,
                                            op0=ALU.mult), reads=["pc"], writes=["pcd"])
            P.pool(lambda e: e.memset(pcd[:, 60:61], -0.5), writes=["pcd"])
            om, _ = PCL["mu_r"]
            P.dve(lambda e: e.tensor_scalar(out=pcd[:, 8:60], in0=pc[:, om:om + 52], scalar1=-1.0, scalar2=1.0,
                                            op0=ALU.mult, op1=ALU.add), reads=["pc"], writes=["pcd"])

    def cst(self, name, bf=False):
        i = CONST_NAMES.index(name)
        return (self.cbf if bf else self.c32)[:, i, :]

    def pcol(self, name, j=0, n=1):
        o, _ = PCL[name]
        return self.pc[:, o + j:o + j + n]

    def fm_linear(self, S, W, KC, kparts, chunks, aT, aname, nt, evac, wblk=512, wq="pool", krange=None):
        P, nc = S.P, self.nc
        tb = tok_blocks(nt)
        astep = max(1, KC // 4) if kparts == 128 else KC
        k0, k1 = (0, KC) if krange is None else krange
        nk = k1 - k0
        blocks = []
        cur = []
        for (c0, cw) in chunks:
            if cur and (cur[-1][0] + cur[-1][1] != c0 or (c0 + cw - cur[0][0]) > wblk):
                blocks.append(cur)
                cur = []
            cur.append((c0, cw))
        if cur:
            blocks.append(cur)
        ck = ("wb", nk, wblk)
        if not hasattr(S, "cache"):
            S.cache = {}
        if ck not in S.cache:
            S.cache[ck] = [S.sb("w%d_%d_%d" % (i, nk, wblk), [128, nk, wblk], BF16) for i in range(2)]
            S.wbi = getattr(S, "wbi", 0)
        wb = S.cache[ck]
        wstep = max(1, nk // 4) if kparts == 128 else nk

        def load(bi):
            blk = blocks[bi]
            b0 = blk[0][0]
            bw = blk[-1][0] + blk[-1][1] - b0
            t = wb[bi % 2]
            kp = kparts
            if kp == 128:
                src = W[k0 * 128:k1 * 128, b0:b0 + bw].rearrange("(k p) c -> p k c", p=128)
                step = max(1, nk // 4)
                for ka in range(0, nk, step):
                    kb_ = min(nk, ka + step)
                    P.dma(lambda e, t=t, src=src, ka=ka, kb_=kb_, bw=bw: e.dma_start(
                        out=t[:, ka:kb_, 0:bw], in_=src[:, ka:kb_, :]),
                        S.key("w%d" % (bi % 2)), writes=["w%d.%d" % (bi % 2, ka // step)], q=wq)
            else:
                P.dma(lambda e, t=t, b0=b0, bw=bw, kp=kp: e.dma_start(out=t[0:kp, 0, 0:bw], in_=W[0:kp, b0:b0 + bw]),
                      S.key("w%d" % (bi % 2)), writes=["w%d.0" % (bi % 2)], q=wq)

        load(0)
        ci = 0
        for bi, blk in enumerate(blocks):
            if bi + 1 < len(blocks):
                load(bi + 1)
            t = wb[bi % 2]
            b0 = blk[0][0]
            for (c0, cw) in blk:
                pss = []
                for (a0, n) in tb:
                    pt, pr = S.bank()
                    pss.append((pt, pr, a0, n))
                for k in range(nk):
                    for (pt, pr, a0, n) in pss:
                        P.pe(lambda e, pt=pt, t=t, k=k, c0=c0, cw=cw, b0=b0, a0=a0, n=n: e.matmul(
                            pt[0:cw, 0:n], lhsT=t[0:kparts, k, c0 - b0:c0 - b0 + cw],
                            rhs=aT[0:kparts, k0 + k, a0:a0 + n], start=(k == 0), stop=(k == nk - 1)),
                            reads=["w%d.%d" % (bi % 2, k // wstep), aname + ".%d" % ((k0 + k) // astep)], writes=[pr])
                evac(ci, c0, cw, [(pt[0:cw, 0:n], pr, a0, n) for (pt, pr, a0, n) in pss])
                ci += 1

    def load_aT(self, S, name, src, KC, t0, nt, kparts=128):
        t = S.sb(name, [128, KC, nt], BF16)
        if kparts == 128:
            v = src[:, t0:t0 + nt].rearrange("(k p) t -> p k t", p=128)
            step = max(1, KC // 4)
            for ka in range(0, KC, step):
                kb_ = min(KC, ka + step)
                S.P.dma(lambda e, ka=ka, kb_=kb_: e.dma_start(out=t[:, ka:kb_, :], in_=v[:, ka:kb_, :]),
                        S.key(name), writes=[name + ".%d" % (ka // step)])
        else:
            S.P.dma(lambda e: e.dma_start(out=t[0:kparts, 0, :], in_=src[0:kparts, t0:t0 + nt]),
                    S.key(name), writes=[name + ".0"])
        return t

    def copy_ps(self, S, out, ps, pr, wres, func=None, scale=1.0):
        eng = S.evac_eng()
        if func is not None or eng == "act":
            S.P.act(lambda e: e.activation(out=out, in_=ps, func=(func or AF.Copy), scale=scale),
                    reads=[pr], writes=[wres])
        else:
            S.P.dve(lambda e: e.tensor_copy(out=out, in_=ps), reads=[pr], writes=[wres])

    def stt_mm(self, P, on_pool, out, in0, scal, in1, reads, writes, tmp=None):
        if not on_pool:
            P.dve(lambda e: e.scalar_tensor_tensor(out=out, in0=in0, scalar=scal, in1=in1, op0=ALU.mult,
                                                   op1=ALU.mult), reads=reads, writes=writes)
        else:
            P.pool(lambda e: e.tensor_scalar(out=tmp, in0=in0, scalar1=scal, scalar2=None, op0=ALU.mult),
                   reads=reads, writes=["_pooltmp"])
            P.pool(lambda e: e.tensor_tensor(out=out, in0=tmp, in1=in1, op=ALU.mult),
                   reads=list(reads) + ["_pooltmp"], writes=writes)

    def norm_stage(self, name, src, t0, nt, gain, out_bf, res=None, out_x=None, gain2=None, NB=256):
        with Stage(self, name) as S:
            P = S.P
            S.psum_banks(2)
            s32 = S.sb("s32", [128, 32, NB], F32)
            sq = S.sb("sq", [128, 32, NB], F32)
            hb = S.sb("hb", [128, 32, NB], BF16)
            rs = S.sb("rs", [128, NB], F32)
            tmpc = S.sb("tmpc", [128, NB], F32)
            r32 = S.sb("r32", [128, 32, NB], F32) if res is not None else None
            ones = self.cst("ones")

            def stats(x, xname, nb):
                P.act(lambda e, x=x, nb=nb: e.activation(out=sq[:, :, 0:nb], in_=x, func=AF.Square),
                      reads=[xname], writes=["sq"])
                pt, pr = S.bank()
                for c in range(32):
                    P.pe(lambda e, c=c, pt=pt, nb=nb: e.matmul(pt[:, 0:nb], lhsT=ones, rhs=sq[:, c, 0:nb],
                                                                 start=(c == 0), stop=(c == 31)),
                         reads=["sq"], writes=[pr])
                P.act(lambda e, pt=pt, nb=nb: e.activation(out=rs[:, 0:nb], in_=pt[:, 0:nb], func=AF.Sqrt,
                                                           scale=1.0 / D, bias=EPS), reads=[pr], writes=["rs"])
                P.dve(lambda e, nb=nb: e.reciprocal(out=rs[:, 0:nb], in_=rs[:, 0:nb]), reads=["rs"], writes=["rs"])

            for (a0, nb) in tok_blocks(nt, NB):
                sv = src[:, t0 + a0:t0 + a0 + nb].rearrange("(c p) t -> p c t", p=128)
                for h in range(2):
                    P.dma(lambda e, h=h, sv=sv, nb=nb: e.dma_start(out=s32[:, 16 * h:16 * h + 16, 0:nb],
                                                                   in_=sv[:, 16 * h:16 * h + 16, :]),
                          S.key("ls"), writes=["s32"])
                if res is not None:
                    rv = res[:, t0 + a0:t0 + a0 + nb].rearrange("(c p) t -> p c t", p=128)
                    for h in range(2):
                        P.dma(lambda e, h=h, rv=rv, nb=nb: e.dma_start(out=r32[:, 16 * h:16 * h + 16, 0:nb],
                                                                       in_=rv[:, 16 * h:16 * h + 16, :]),
                              S.key("lr"), writes=["r32"])
                stats(s32[:, :, 0:nb], "s32", nb)
                if res is None:
                    for c in range(32):
                        self.stt_mm(P, c % 3 == 0, hb[:, c, 0:nb], s32[:, c, 0:nb], self.pcol(gain, c), rs[:, 0:nb],
                                    ["s32", "rs"], ["hb"], tmp=tmpc[:, 0:nb])
                else:
                    for c in range(32):
                        self.stt_mm(P, c % 3 == 0, s32[:, c, 0:nb], s32[:, c, 0:nb], self.pcol(gain, c), rs[:, 0:nb],
                                    ["s32", "rs"], ["s32"], tmp=tmpc[:, 0:nb])
                    P.dve(lambda e, nb=nb: e.tensor_tensor(out=r32[:, :, 0:nb], in0=r32[:, :, 0:nb],
                                                           in1=s32[:, :, 0:nb], op=ALU.add),
                          reads=["r32", "s32"], writes=["r32"])
                    ov = out_x[:, t0 + a0:t0 + a0 + nb].rearrange("(c p) t -> p c t", p=128)
                    P.dma(lambda e, ov=ov, nb=nb: e.dma_start(out=ov, in_=r32[:, :, 0:nb]), S.key("sx"),
                          reads=["r32"])
                    stats(r32[:, :, 0:nb], "r32", nb)
                    for c in range(32):
                        self.stt_mm(P, c % 3 == 0, hb[:, c, 0:nb], r32[:, c, 0:nb], self.pcol(gain2, c), rs[:, 0:nb],
                                    ["r32", "rs"], ["hb"], tmp=tmpc[:, 0:nb])
                hv = out_bf[:, t0 + a0:t0 + a0 + nb].rearrange("(c p) t -> p c t", p=128)
                P.dma(lambda e, hv=hv, nb=nb: e.dma_start(out=hv, in_=hb[:, :, 0:nb]), S.key("sh"), reads=["hb"])

    def win_stage(self, name, t0, nt, pieces):
        c = self.cfg
        with Stage(self, name) as S:
            P = S.P
            S.psum_banks(8)
            hT = self.load_aT(S, "hT", self.dr["hT"], 32, t0, nt)
            stg = [S.sb("stg%d" % i, [128, nt], F32) for i in range(3)]
            lastc = [S.sb("lastc%d" % i, [128, 16], F32) for i in range(3)]
            zt = S.sb("zt", [128, 16], F32)
            P.pool(lambda e: e.memset(zt[:], 0.0), writes=["zt"])
            cnt = [0]
            Win = self.dr["w_in"]
            n_ctx = max(0, min(c.CTX, t0 + nt) - t0)
            has_smp = (t0 + nt) > c.CTX
            for pn in pieces:
                if pn == "v":
                    continue
                p0, pw = PIECES[pn]
                chunks = [(p0 + a, min(128, pw - a)) for a in range(0, pw, 128)]
                dst = self.dr[pn + "T"]
                padded = pn in ("r", "xw", "kr", "vr", "xa", "xg")
                func = AF.Sigmoid if pn in ("ga", "gb") else None

                def evac(ci, c0, cw, pss, p0=p0, dst=dst, padded=padded, func=func, pn=pn):
                    si = cnt[0] % 3
                    cnt[0] += 1
                    sg = stg[si]
                    sn = "stg%d" % si
                    for (ps, pr, a0, n) in pss:
                        self.copy_ps(S, sg[0:cw, a0:a0 + n], ps, pr, sn, func=func)
                    r0 = c0 - p0
                    if not padded:
                        P.dma(lambda e: e.dma_start(out=dst[r0:r0 + cw, t0:t0 + nt], in_=sg[0:cw, :]),
                              S.key("st"), reads=[sn])
                    else:
                        if t0 == 0:
                            P.dma(lambda e: e.dma_start(out=dst[r0:r0 + cw, 0:1], in_=zt[0:cw, 0:1],
                                                        allow_slow_non_contiguous=True), S.key("st"),
                                  reads=["zt"])
                        if n_ctx > 0:
                            P.dma(lambda e: e.dma_start(out=dst[r0:r0 + cw, 1 + t0:1 + t0 + n_ctx],
                                                        in_=sg[0:cw, 0:n_ctx]), S.key("st"), reads=[sn])
                            if t0 + n_ctx == c.CTX:
                                zr = c0 - ZR0
                                P.dma(lambda e: e.dma_start(out=self.dr["shiftTp"][zr:zr + cw, 0:1],
                                                            in_=sg[0:cw, n_ctx - 1:n_ctx]), S.key("st"), reads=[sn])
                        if has_smp:
                            dv = dst[r0:r0 + cw, 1 + c.CTX:1 + c.CTX + 144].rearrange("p (s j) -> p s j", j=9)
                            sv = sg[0:cw, nt - 128:nt].rearrange("p (s j) -> p s j", j=8)
                            P.dma(lambda e: e.dma_start(out=dv[:, :, 1:9], in_=sv), S.key("st"), reads=[sn])
                            P.dma(lambda e: e.dma_start(out=dv[:, :, 0], in_=zt[0:cw, :],
                                                        allow_slow_non_contiguous=True), S.key("st"), reads=["zt"])
                            zr = c0 - ZR0
                            lt = lastc[si]
                            P.pool(lambda e: e.tensor_copy(out=lt[0:cw, :], in_=sv[:, :, 7]), reads=[sn],
                                   writes=["lastc%d" % si])
                            P.dma(lambda e: e.dma_start(out=self.dr["shiftTs"][zr:zr + cw, :], in_=lt[0:cw, :]),
                                  S.key("st"), reads=["lastc%d" % si])

                self.fm_linear(S, Win, 32, 128, chunks, hT, "hT", nt, evac)
            if "v" in pieces:
                p0, pw = PIECES["v"]
                wv = S.cache[("wb", 32, 512)]
                vst = [S.sb("vst%d" % i, [128, 512], BF16) for i in range(2)]
                for bi in range(4):
                    t = wv[bi % 2]
                    src = Win[:, p0 + bi * 512:p0 + (bi + 1) * 512].rearrange("(k p) c -> p k c", p=128)
                    for ka in range(0, 32, 8):
                        P.dma(lambda e, t=t, src=src, ka=ka: e.dma_start(out=t[:, ka:ka + 8, :],
                                                                       in_=src[:, ka:ka + 8, :]),
                              S.key("w%d" % (bi % 2)), writes=["w%d.%d" % (bi % 2, ka // 8)], q="pool")
                    for ti in range(nt // 128):
                        pt, pr = S.bank()
                        for k in range(32):
                            P.pe(lambda e, pt=pt, t=t, k=k, ti=ti: e.matmul(
                                pt[:, :], lhsT=hT[:, k, ti * 128:(ti + 1) * 128], rhs=t[:, k, :],
                                start=(k == 0), stop=(k == 31)), reads=["w%d.%d" % (bi % 2, k // 8), "hT.%d" % (k // 8)],
                                writes=[pr])
                        vi = (bi * (nt // 128) + ti) % 2
                        self.copy_ps(S, vst[vi][:, :], pt[:, :], pr, "vst%d" % vi)
                        P.dma(lambda e, vi=vi, ti=ti, bi=bi: e.dma_start(
                            out=self.dr["vtm"][t0 + ti * 128:t0 + (ti + 1) * 128, bi * 512:(bi + 1) * 512],
                            in_=vst[vi][:, :]), S.key("sv"), reads=["vst%d" % vi])

    def build(self):
        c = self.cfg
        self.declare()
        with self.gst:
            self.consts()
            if self.want("norm1"):
                self.norm_stage("n1", self.dr["xT"], 0, c.TT, "g_pre_mix", self.dr["hT"])
            if self.want("win"):
                allp = list(PIECES.keys())
                g0 = c.HALO0
                if g0 > 0:
                    self.win_stage("wa", 0, g0, allp)
                self.win_stage("wb", g0, c.TT - g0, allp)
            self.build_rest()
        return self.nc

    def gla_stage(self):
        c = self.cfg
        dr = self.dr
        with Stage(self, "gla") as S:
            P = S.P
            S.psum_banks(7)
            qk = S.sb("qk", [128, 16, 128], F32)
            zab = S.sb("zab", [16, 128], BF16)
            wal = S.sb("wal", [16, 1024], BF16)
            vb = S.sb("vb", [128, 2048], BF16)
            sp = S.sb("sp", [128, 8, 128], F32)
            cs = S.sb("cs", [128, 8, 128], F32)
            cs2 = S.sb("cs2", [128, 8, 128], F32)
            ex = S.sb("ex", [128, 8, 128], F32)
            onesf = S.sb("onesf", [128, 128], F32)
            zer = S.sb("zer", [128, 512], BF16)
            P.pool(lambda e: e.memset(zer[:], 0.0), writes=["zer"])
            qd = S.sb("qd", [128, 8, 128], BF16)
            kd = S.sb("kd", [128, 8, 128], BF16)
            kdp = S.sb("kdp", [128, 8, 128], BF16)
            kdt = S.sb("kdt", [128, 8, 128], BF16)
            att = S.sb("att", [128, 4, 128], BF16)
            S32 = S.sb("S32", [128, 8, 512], F32)
            Sbf = S.sb("Sbf", [128, 8, 512], BF16)
            sdec = S.sb("sdec", [128, 8, 16], F32)
            cend = S.sb("cend", [128, 8, 16], F32)
            o32 = S.sb("o32", [128, 16, 128], F32)
            sq = S.sb("sq", [128, 16, 128], F32)
            rstd = S.sb("rstd", [128, 4, 128], F32)
            zg = S.sb("zg", [128, 16, 128], F32)
            sg = S.sb("sg", [128, 16, 128], F32)
            oab = S.sb("oab", [128, 16, 128], BF16)
            Ss32 = [S.sb("Ss32_%d" % i, [128, 8, 512], F32) for i in range(2)]
            Ssbf = [S.sb("Ssbf_%d" % i, [128, 8, 512], BF16) for i in range(2)]
            kdm = S.sb("kdm", [128, 8, 16, 128], BF16)
            ptr = S.st.enter_context(self.nc.psum_tensor("gla_ptr", [128, 8, 128], BF16))
            rn = lambda t: {id(sp): "sp", id(cs): "cs", id(cs2): "cs2"}[id(t)]
            identb = self.cst("ident", True)
            ones32 = self.cst("ones")
            P.dma(lambda e: e.dma_start(out=wal[:], in_=dr["w_alpha2"]), S.key("c"), writes=["wal"], q="pool")
            P.pool(lambda e: e.memset(onesf[:], 1.0), writes=["onesf"])
            P.pool(lambda e: e.memset(S32[:], 0.0), writes=["S32"])
            P.pool(lambda e: e.memset(Sbf[:], 0.0), writes=["Sbf"])
            for tile in range(c.NCT + 1):
                smp = tile == c.NCT
                t0 = tile * 128
                need_out = t0 >= c.HALO0
                P.dma(lambda e, t0=t0: e.dma_start(out=qk[:, 0:8, :], in_=dr["qT"][:, t0:t0 + 128].rearrange(
                    "(c p) t -> p c t", p=128)), S.key("l1"), writes=["qk"])
                P.dma(lambda e, t0=t0: e.dma_start(out=qk[:, 8:16, :], in_=dr["kT"][:, t0:t0 + 128].rearrange(
                    "(c p) t -> p c t", p=128)), S.key("l1"), writes=["qk"])
                P.dma(lambda e, t0=t0: e.dma_start(out=zab[:], in_=dr["zaT"][:, t0:t0 + 128]), S.key("l2"),
                      writes=["zab"], q="pool")
                P.dma(lambda e, t0=t0: e.dma_start(out=vb[:], in_=dr["vtm"][t0:t0 + 128, :]), S.key("l3"),
                      writes=["vb"])
                for half in range(2):
                    pt, pr = S.bank([0, 1])
                    for cc in range(4):
                        ch = half * 4 + cc
                        P.pe(lambda e, pt=pt, cc=cc, ch=ch: e.matmul(pt[:, cc * 128:(cc + 1) * 128],
                                                                   lhsT=wal[:, ch * 128:(ch + 1) * 128], rhs=zab[:, :],
                                                                   start=True, stop=True),
                             reads=["wal", "zab"], writes=[pr])
                    for cc in range(4):
                        ch = half * 4 + cc
                        P.act(lambda e, pt=pt, cc=cc, ch=ch: e.activation(
                            out=sp[:, ch, :], in_=pt[:, cc * 128:(cc + 1) * 128], func=AF.Exp, scale=-1.0,
                            bias=self.pcd[:, ch:ch + 1]), reads=[pr, "pcd"], writes=["sp"])
                P.act(lambda e: e.activation(out=sp[:], in_=sp[:], func=AF.Ln, bias=1.0), reads=["sp"], writes=["sp"])
                if not smp:
                    for ch in range(8):
                        P.dve(lambda e, ch=ch: e.tensor_tensor_scan(out=cs[:, ch, :], data0=onesf[:, :],
                                                                    data1=sp[:, ch, :], initial=0.0,
                                                                    op0=ALU.mult, op1=ALU.add),
                              reads=["sp", "onesf"], writes=["cs"])
                    csf = cs
                    P.dve(lambda e: e.tensor_copy(out=cend[:, :, 0:1], in_=cs[:, :, 127:128]), reads=["cs"],
                          writes=["cend"])
                    nseg = 1
                else:
                    v = lambda t: t[:].rearrange("p c (s j) -> p (c s) j", j=8)
                    src, dst = sp, cs
                    for d in (1, 2, 4):
                        P.dve(lambda e, src=src, dst=dst, d=d: e.tensor_tensor(
                            out=v(dst)[:, :, d:8], in0=v(src)[:, :, d:8], in1=v(src)[:, :, 0:8 - d], op=ALU.add),
                            reads=[rn(src)], writes=[rn(dst)])
                        P.pool(lambda e, src=src, dst=dst, d=d: e.tensor_copy(out=v(dst)[:, :, 0:d],
                                                                              in_=v(src)[:, :, 0:d]),
                               reads=[rn(src)], writes=[rn(dst)])
                        src, dst = dst, (cs2 if dst is cs else cs)
                    csf = src
                    P.dve(lambda e, csf=csf: e.tensor_copy(
                        out=cend[:, :, :], in_=csf[:].rearrange("p c (s j) -> p c s j", j=8)[:, :, :, 7]),
                        reads=[rn(csf)], writes=["cend"])
                    nseg = 16
                cn = rn(csf)
                P.act(lambda e, csf=csf: e.activation(out=ex[:], in_=csf[:], func=AF.Exp, scale=-1.0 / 16),
                      reads=[cn], writes=["ex"])
                P.dve(lambda e: e.scalar_tensor_tensor(out=qd[:], in0=qk[:, 0:8, :], scalar=float(GDK) ** -0.5,
                                                       in1=ex[:], op0=ALU.mult, op1=ALU.mult),
                      reads=["qk", "ex"], writes=["qd"])
                P.act(lambda e, csf=csf: e.activation(out=ex[:], in_=csf[:], func=AF.Exp, scale=1.0 / 16),
                      reads=[cn], writes=["ex"])
                P.dve(lambda e: e.tensor_tensor(out=kd[:], in0=qk[:, 8:16, :], in1=ex[:], op=ALU.mult),
                      reads=["qk", "ex"], writes=["kd"])
                L = 128 // nseg
                P.dve(lambda e, csf=csf, nseg=nseg, L=L: e.tensor_tensor(
                    out=ex[:].rearrange("p c (s j) -> p c s j", j=L),
                    in0=csf[:].rearrange("p c (s j) -> p c s j", j=L),
                    in1=cend[:, :, 0:nseg].unsqueeze(3).to_broadcast([128, 8, nseg, L]), op=ALU.subtract),
                    reads=[cn, "cend"], writes=["ex"])
                P.act(lambda e: e.activation(out=ex[:], in_=ex[:], func=AF.Exp, scale=1.0 / 16), reads=["ex"],
                      writes=["ex"])
                P.dve(lambda e: e.tensor_tensor(out=kdp[:], in0=qk[:, 8:16, :], in1=ex[:], op=ALU.mult),
                      reads=["qk", "ex"], writes=["kdp"])
                P.act(lambda e, nseg=nseg: e.activation(out=sdec[:, :, 0:nseg], in_=cend[:, :, 0:nseg], func=AF.Exp,
                                                        scale=-1.0 / 16), reads=["cend"], writes=["sdec"])
                for ch in range(8):
                    P.pe(lambda e, ch=ch: e.transpose(ptr[:, ch, :], kdp[:, ch, :], identb), reads=["kdp"],
                         writes=["ptr"])
                P.act(lambda e: e.activation(out=kdt[:], in_=ptr[:], func=AF.Copy), reads=["ptr"], writes=["kdt"])
                pa, par = S.bank([2])
                for h in range(4):
                    for cc in range(2):
                        P.pe(lambda e, h=h, cc=cc, pa=pa: e.matmul(pa[:, h * 128:(h + 1) * 128],
                                                                 lhsT=kd[:, 2 * h + cc, :], rhs=qd[:, 2 * h + cc, :],
                                                                 start=(cc == 0), stop=(cc == 1)),
                             reads=["kd", "qd"], writes=[par])
                mk = self.cst("incT_s" if smp else "incT_p")
                P.dve(lambda e, pa=pa, mk=mk: e.tensor_tensor(
                    out=att[:], in0=pa[:].rearrange("p (h t) -> p h t", t=128),
                    in1=mk.unsqueeze(1).to_broadcast([128, 4, 128]), op=ALU.mult),
                    reads=[par], writes=["att"])
                pos = [S.bank([3 + h_]) for h_ in range(4)]
                for h in range(4):
                    po, por = pos[h]
                    P.pe(lambda e, po=po: e.matmul(po[:, :], lhsT=zer[:, 0:128], rhs=zer[:, :], start=True,
                                                   stop=False), reads=["zer"], writes=[por])
                    for j in range(4):
                        P.pe(lambda e, h=h, j=j, po=po: e.matmul(
                            po[:, j * 128:(j + 1) * 128], lhsT=vb[:, h * 512 + j * 128:h * 512 + (j + 1) * 128],
                            rhs=att[:, h, :], start=False, stop=False),
                            reads=["vb", "att"], writes=[por])
                        if not smp:
                            for cc in range(2):
                                P.pe(lambda e, h=h, j=j, cc=cc, po=po: e.matmul(
                                    po[:, j * 128:(j + 1) * 128], lhsT=Sbf[:, 2 * h + cc, j * 128:(j + 1) * 128],
                                    rhs=qd[:, 2 * h + cc, :], start=False, stop=(cc == 1 and j == 3)),
                                    reads=["Sbf", "qd"], writes=[por])
                if not smp:
                    for h in range(4):
                        for cc in range(2):
                            i = 2 * h + cc
                            pu, pur = S.bank([0, 1])
                            P.pe(lambda e, i=i, h=h, pu=pu: e.matmul(pu[:, :], lhsT=kdt[:, i, :],
                                                                     rhs=vb[:, h * 512:(h + 1) * 512], start=True,
                                                                     stop=True), reads=["kdt", "vb"], writes=[pur])
                            P.dve(lambda e, i=i, pu=pu: e.scalar_tensor_tensor(
                                out=S32[:, i, :], in0=S32[:, i, :], scalar=sdec[:, i, 0:1], in1=pu[:, :],
                                op0=ALU.mult, op1=ALU.add), reads=[pur, "S32", "sdec"], writes=["S32"])
                    P.act(lambda e: e.activation(out=Sbf[:], in_=S32[:], func=AF.Copy), reads=["S32"], writes=["Sbf"])
                    if tile == c.NCT - 1:
                        P.dma(lambda e: e.dma_start(out=dr["glap"].rearrange("h (c p) v -> p h c v", p=128),
                                                    in_=S32[:].rearrange("p (h c) v -> p h c v", c=2)),
                              S.key("so"), reads=["S32"])
                else:
                    seg16 = self.cst("seg16", True)
                    for i8 in range(8):
                        P.dve(lambda e, i8=i8: e.tensor_tensor(
                            out=kdm[:, i8, :, :], in0=kdt[:, i8, :].unsqueeze(1).to_broadcast([128, 16, 128]),
                            in1=seg16[:, 0:16].unsqueeze(2).to_broadcast([128, 16, 128]), op=ALU.mult),
                            reads=["kdt"], writes=["kdm"])
                    for si in range(16):
                        b = si % 2
                        s32, sbf = Ss32[b], Ssbf[b]
                        n32, nbf = "Ss32_%d" % b, "Ssbf_%d" % b
                        P.dma(lambda e, si=si, s32=s32: e.dma_start(
                            out=s32[:].rearrange("p (h c) v -> p h c v", c=2),
                            in_=dr["sgla"][si].rearrange("h (c p) v -> p h c v", p=128)), S.key("ls%d" % b),
                            writes=[n32])
                        P.act(lambda e, s32=s32, sbf=sbf: e.activation(out=sbf[:], in_=s32[:], func=AF.Copy),
                              reads=[n32], writes=[nbf])
                        for h in range(4):
                            po, por = pos[h]
                            for j in range(4):
                                for cc in range(2):
                                    P.pe(lambda e, h=h, j=j, cc=cc, po=po, sbf=sbf, si=si: e.matmul(
                                        po[:, j * 128 + si * 8:j * 128 + si * 8 + 8],
                                        lhsT=sbf[:, 2 * h + cc, j * 128:(j + 1) * 128],
                                        rhs=qd[:, 2 * h + cc, si * 8:si * 8 + 8], start=False,
                                        stop=(cc == 1 and j == 3 and si == 15)),
                                        reads=[nbf, "qd"], writes=[por])
                        for h in range(4):
                            for cc in range(2):
                                i = 2 * h + cc
                                pu, pur = S.bank([0, 1])
                                P.pe(lambda e, i=i, h=h, pu=pu, si=si: e.matmul(
                                    pu[:, :], lhsT=kdm[:, i, si, :], rhs=vb[:, h * 512:(h + 1) * 512], start=True,
                                    stop=True), reads=["kdm", "vb"], writes=[pur])
                                P.dve(lambda e, i=i, pu=pu, s32=s32, si=si: e.scalar_tensor_tensor(
                                    out=s32[:, i, :], in0=s32[:, i, :], scalar=sdec[:, i, si:si + 1], in1=pu[:, :],
                                    op0=ALU.mult, op1=ALU.add), reads=[pur, n32, "sdec"], writes=[n32])
                        P.dma(lambda e, si=si, s32=s32: e.dma_start(
                            out=dr["glas"][si].rearrange("h (c p) v -> p h c v", p=128),
                            in_=s32[:].rearrange("p (h c) v -> p h c v", c=2)), S.key("ss%d" % b), reads=[n32])
                if not need_out:
                    continue
                for h in range(4):
                    po, por = pos[h]
                    P.act(lambda e, h=h, po=po: e.activation(out=o32[:, 4 * h:4 * h + 4, :],
                                                             in_=po[:].rearrange("p (j t) -> p j t", t=128),
                                                             func=AF.Copy), reads=[por], writes=["o32"])
                    P.act(lambda e, h=h, po=po: e.activation(out=sq[:, 4 * h:4 * h + 4, :],
                                                             in_=po[:].rearrange("p (j t) -> p j t", t=128),
                                                             func=AF.Square), reads=[por], writes=["sq"])
                pst, pstr = S.bank([2])
                for h in range(4):
                    for j in range(4):
                        P.pe(lambda e, h=h, j=j, pst=pst: e.matmul(pst[:, h * 128:(h + 1) * 128], lhsT=ones32,
                                                                   rhs=sq[:, 4 * h + j, :], start=(j == 0),
                                                                   stop=(j == 3)), reads=["sq"], writes=[pstr])
                P.act(lambda e, pst=pst: e.activation(out=rstd[:], in_=pst[:].rearrange("p (h t) -> p h t", t=128),
                                                      func=AF.Sqrt, scale=1.0 / GDV, bias=EPS), reads=[pstr],
                      writes=["rstd"])
                P.dve(lambda e: e.reciprocal(out=rstd[:], in_=rstd[:]), reads=["rstd"], writes=["rstd"])
                P.dma(lambda e, t0=t0: e.dma_start(out=zg[:], in_=dr["zgT"][:, t0:t0 + 128].rearrange(
                    "(c p) t -> p c t", p=128)), S.key("l4"), writes=["zg"])
                P.act(lambda e: e.activation(out=sg[:], in_=zg[:], func=AF.Sigmoid), reads=["zg"], writes=["sg"])
                P.pool(lambda e: e.tensor_tensor(out=sg[:], in0=sg[:], in1=zg[:], op=ALU.mult), reads=["sg", "zg"],
                       writes=["sg"])
                P.pool(lambda e: e.tensor_tensor(
                    out=sg[:].rearrange("p (h j) t -> p h j t", j=4),
                    in0=sg[:].rearrange("p (h j) t -> p h j t", j=4),
                    in1=rstd[:].unsqueeze(2).to_broadcast([128, 4, 4, 128]), op=ALU.mult),
                    reads=["sg", "rstd"], writes=["sg"])
                for j in range(4):
                    P.dve(lambda e, j=j: e.scalar_tensor_tensor(
                        out=oab[:].rearrange("p (h j) t -> p h j t", j=4)[:, :, j, :],
                        in0=o32[:].rearrange("p (h j) t -> p h j t", j=4)[:, :, j, :],
                        scalar=self.pcol("gla_norm", j),
                        in1=sg[:].rearrange("p (h j) t -> p h j t", j=4)[:, :, j, :],
                        op0=ALU.mult, op1=ALU.mult), reads=["o32", "sg"], writes=["oab"])
                P.dma(lambda e, t0=t0: e.dma_start(out=dr["oaT"][:, t0:t0 + 128].rearrange("(c p) t -> p c t", p=128),
                                                   in_=oab[:]), S.key("so2"), reads=["oab"])

    def rfront_stage(self):
        c = self.cfg
        dr = self.dr
        with Stage(self, "rf") as S:
            P = S.P
            S.psum_banks(6)
            NB = 512
            W1 = NB + 16
            wdec = S.sb("wdec", [96, 2048], BF16)
            wicl = S.sb("wicl", [96, 2048], BF16)
            wgat = S.sb("wgat", [128, 2, 2048], BF16)
            P.dma(lambda e: e.dma_start(out=wdec[:], in_=dr["w_decay2"]), S.key("c1"), writes=["wdec"], q="pool")
            P.dma(lambda e: e.dma_start(out=wicl[:], in_=dr["w_iclr2"]), S.key("c1"), writes=["wicl"], q="pool")
            P.dma(lambda e: e.dma_start(out=wgat[:], in_=dr["w_gate2"].rearrange("(k p) c -> p k c", p=128)),
                  S.key("c1"), writes=["wgat"], q="pool")
            raw = {n: S.sb("raw_" + n, [128, W1], F32) for n in ("r", "k", "v")}
            rawx = {n: S.sb("rawx_" + n, [128, W1], F32) for n in ("xw", "xa", "xg0", "xg1")}
            sst = S.sb("sst", [128, 16], F32)
            tmp = S.sb("tmp", [128, NB], F32)
            xsh = S.sb("xsh", [128, NB], F32)
            thb = S.sb("thb", [96, NB], BF16)
            xab = S.sb("xab", [96, NB], BF16)
            sgx = S.sb("sgx", [128, 2, NB], BF16)
            rs_ = S.sb("rs_", [128, NB], F32)
            ks_ = S.sb("ks_", [128, NB], F32)
            vs_ = S.sb("vs_", [128, NB], F32)
            pl = S.sb("pl", [128, NB], F32)
            a_ = S.sb("a_", [128, NB], F32)
            g_ = S.sb("g_", [128, NB], F32)
            kk = S.sb("kk", [128, NB], F32)
            t1 = S.sb("t1", [128, NB], F32)
            t2 = S.sb("t2", [128, NB], F32)
            k2 = S.sb("k2", [128, NB], F32)
            cp = S.sb("cp", [128, NB], F32)
            ex1 = S.sb("ex1", [128, NB], F32)
            ex2 = S.sb("ex2", [128, NB], F32)
            bon = S.sb("bon", [128, NB], F32)
            arbk = S.sb("arbk", [128, 4, NB], BF16)
            vbf = S.sb("vbf", [128, NB], BF16)
            gc = S.sb("gc", [128, 16], F32)
            rmask = S.sb("rmask", [128, 2, NB], F32)
            P.pool(lambda e: e.memset(rmask[:], 1.0), writes=["rmask"])
            P.pool(lambda e: e.memset(rmask[:, 0, :].rearrange("p (s j) -> p s j", j=128)[:, :, 0:1], 0.0),
                   writes=["rmask"])
            P.pool(lambda e: e.memset(rmask[:, 1, :].rearrange("p (s j) -> p s j", j=8)[:, :, 0:1], 0.0),
                   writes=["rmask"])
            blk64 = self.cst("blk64")
            pcd = self.pcd

            def blocks():
                for a0 in range(0, c.CTX, NB):
                    yield (a0, min(NB, c.CTX - a0), False)
                yield (c.CTX, 128, True)

            def load_raw(t, tn, src, rows, r0, a0, n, smp, state_rows):
                if not smp:
                    P.dma(lambda e: e.dma_start(out=t[0:rows, 0:n + 1], in_=src[r0:r0 + rows, a0:a0 + n + 1]),
                          S.key("lr"), writes=[tn])
                else:
                    P.dma(lambda e: e.dma_start(out=t[0:rows, 0:144],
                                                in_=src[r0:r0 + rows, 1 + c.CTX:1 + c.CTX + 144]),
                          S.key("lr"), writes=[tn])
                    P.dma(lambda e: e.dma_start(out=sst[0:rows, :], in_=dr["sshiftT"][state_rows:state_rows + rows, :]),
                          S.key("lr2"), writes=["sst"])
                    P.pool(lambda e: e.tensor_copy(
                        out=t[0:rows, 0:144].rearrange("p (s j) -> p s j", j=9)[:, :, 0], in_=sst[0:rows, :]),
                        reads=["sst", tn], writes=[tn])

            def shift(out, t, tn, rows, n, smp, mu, omm, wname):
                if not smp:
                    cur, prev = t[0:rows, 1:n + 1], t[0:rows, 0:n]
                    o, tm = out, tmp[0:rows, 0:n]
                else:
                    v9 = t[0:rows, 0:144].rearrange("p (s j) -> p s j", j=9)
                    cur, prev = v9[:, :, 1:9], v9[:, :, 0:8]
                    o = out.rearrange("p (s j) -> p s j", j=8)
                    tm = tmp[0:rows, 0:128].rearrange("p (s j) -> p s j", j=8)
                P.pool(lambda e: e.tensor_scalar(out=tm, in0=prev, scalar1=mu, scalar2=None, op0=ALU.mult),
                       reads=[tn], writes=["tmp"])
                P.dve(lambda e: e.scalar_tensor_tensor(out=o, in0=cur, scalar=omm, in1=tm, op0=ALU.mult, op1=ALU.add),
                      reads=[tn, "tmp"], writes=[wname])

            ZR = {"r": 0, "xw": 2048, "kr": 2144, "vr": 4192, "xa": 6240, "xg": 6336}
            chunk_i = 0
            for (a0, n, smp) in blocks():
                mi = 1 if smp else 0
                load_raw(rawx["xw"], "rawx_xw", dr["xwT"], 96, 0, a0, n, smp, ZR["xw"])
                shift(xsh[0:96, 0:n], rawx["xw"], "rawx_xw", 96, n, smp, self.pcol("mu_xw")[0:96], pcd[0:96, 24:25], "xsh")
                P.act(lambda e, n=n: e.activation(out=thb[:, 0:n], in_=xsh[0:96, 0:n], func=AF.Tanh), reads=["xsh"],
                      writes=["thb"])
                load_raw(rawx["xa"], "rawx_xa", dr["xaT"], 96, 0, a0, n, smp, ZR["xa"])
                shift(xsh[0:96, 0:n], rawx["xa"], "rawx_xa", 96, n, smp, self.pcol("mu_xa")[0:96], pcd[0:96, 57:58], "xsh")
                P.act(lambda e, n=n: e.activation(out=xab[:, 0:n], in_=xsh[0:96, 0:n], func=AF.Copy), reads=["xsh"],
                      writes=["xab"])
                for j in range(2):
                    nm = "xg%d" % j
                    load_raw(rawx[nm], "rawx_" + nm, dr["xgT"], 128, j * 128, a0, n, smp, ZR["xg"] + j * 128)
                    shift(xsh[:, 0:n], rawx[nm], "rawx_" + nm, 128, n, smp, self.pcol("mu_xg", j), pcd[:, 58 + j:59 + j],
                          "xsh")
                    P.act(lambda e, n=n, j=j: e.activation(out=sgx[:, j, 0:n], in_=xsh[:, 0:n], func=AF.Sigmoid),
                          reads=["xsh"], writes=["sgx"])
                for p in range(16):
                    pc0 = p * 128
                    load_raw(raw["r"], "raw_r", dr["rT"], 128, pc0, a0, n, smp, ZR["r"] + pc0)
                    load_raw(raw["k"], "raw_k", dr["krT"], 128, pc0, a0, n, smp, ZR["kr"] + pc0)
                    load_raw(raw["v"], "raw_v", dr["vrT"], 128, pc0, a0, n, smp, ZR["vr"] + pc0)
                    shift(rs_[:, 0:n], raw["r"], "raw_r", 128, n, smp, self.pcol("mu_r", p), pcd[:, 8 + p:9 + p], "rs_")
                    shift(ks_[:, 0:n], raw["k"], "raw_k", 128, n, smp, self.pcol("mu_kr", p), pcd[:, 25 + p:26 + p], "ks_")
                    shift(vs_[:, 0:n], raw["v"], "raw_v", 128, n, smp, self.pcol("mu_vr", p), pcd[:, 41 + p:42 + p], "vs_")
                    pw, pwr = S.bank()
                    P.pe(lambda e, pw=pw, p=p, n=n: e.matmul(pw[:, 0:n], lhsT=wdec[:, p * 128:(p + 1) * 128],
                                                             rhs=thb[:, 0:n], start=True, stop=True),
                         reads=["wdec", "thb"], writes=[pwr])
                    P.dve(lambda e, pw=pw, p=p, n=n: e.tensor_scalar(out=t1[:, 0:n], in0=pw[:, 0:n],
                                                                     scalar1=self.pcol("w0", p), scalar2=-1.0,
                                                                     op0=ALU.add, op1=ALU.mult),
                          reads=[pwr], writes=["t1"])
                    P.act(lambda e, n=n: e.activation(out=t1[:, 0:n], in_=t1[:, 0:n], func=AF.Exp), reads=["t1"],
                          writes=["t1"])
                    P.act(lambda e, n=n: e.activation(out=t1[:, 0:n], in_=t1[:, 0:n], func=AF.Ln, bias=1.0),
                          reads=["t1"], writes=["t1"])
                    P.act(lambda e, n=n: e.activation(out=pl[:, 0:n], in_=t1[:, 0:n], func=AF.Exp, scale=-1.0,
                                                      bias=self.kneg05()), reads=["t1"], writes=["pl"])
                    pa, par = S.bank()
                    P.pe(lambda e, pa=pa, p=p, n=n: e.matmul(pa[:, 0:n], lhsT=wicl[:, p * 128:(p + 1) * 128],
                                                             rhs=xab[:, 0:n], start=True, stop=True),
                         reads=["wicl", "xab"], writes=[par])
                    P.act(lambda e, pa=pa, p=p, n=n: e.activation(out=a_[:, 0:n], in_=pa[:, 0:n], func=AF.Sigmoid,
                                                                  bias=self.pcol("a0", p)), reads=[par], writes=["a_"])
                    pg, pgr = S.bank()
                    for kc in range(2):
                        P.pe(lambda e, pg=pg, p=p, n=n, kc=kc: e.matmul(
                            pg[:, 0:n], lhsT=wgat[:, kc, p * 128:(p + 1) * 128], rhs=sgx[:, kc, 0:n],
                            start=(kc == 0), stop=(kc == 1)), reads=["wgat", "sgx"], writes=[pgr])
                    P.act(lambda e, pg=pg, n=n: e.activation(out=g_[:, 0:n], in_=pg[:, 0:n], func=AF.Copy),
                          reads=[pgr], writes=["g_"])
                    P.dma(lambda e, pc0=pc0, a0=a0, n=n: e.dma_start(out=dr["gT"][pc0:pc0 + 128, a0:a0 + n],
                                                                   in_=g_[:, 0:n]), S.key("sg"), reads=["g_"])
                    P.pool(lambda e, p=p, n=n: e.tensor_scalar(out=kk[:, 0:n], in0=ks_[:, 0:n],
                                                               scalar1=self.pcol("k_k", p), scalar2=None,
                                                               op0=ALU.mult), reads=["ks_"], writes=["kk"])
                    P.act(lambda e, n=n: e.activation(out=t2[:, 0:n], in_=kk[:, 0:n], func=AF.Square), reads=["kk"],
                          writes=["t2"])
                    pq, pqr = S.bank()
                    P.pe(lambda e, pq=pq, n=n: e.matmul(pq[:, 0:n], lhsT=blk64, rhs=t2[:, 0:n], start=True,
                                                         stop=True), reads=["t2"], writes=[pqr])
                    P.dve(lambda e, pq=pq, n=n: e.tensor_scalar(out=t2[:, 0:n], in0=pq[:, 0:n], scalar1=1e-24,
                                                                scalar2=None, op0=ALU.max), reads=[pqr],
                          writes=["t2"])
                    P.act(lambda e, n=n: e.activation(out=t2[:, 0:n], in_=t2[:, 0:n], func=AF.Sqrt), reads=["t2"],
                          writes=["t2"])
                    P.dve(lambda e, n=n: e.reciprocal(out=t2[:, 0:n], in_=t2[:, 0:n]), reads=["t2"], writes=["t2"])
                    P.dve(lambda e, n=n: e.tensor_tensor(out=kk[:, 0:n], in0=kk[:, 0:n], in1=t2[:, 0:n],
                                                         op=ALU.mult), reads=["kk", "t2"], writes=["kk"])
                    P.dve(lambda e, p=p, n=n: e.tensor_scalar(out=t2[:, 0:n], in0=a_[:, 0:n],
                                                              scalar1=self.pcol("k_a", p), scalar2=self.pcol("k_a", p),
                                                              op0=ALU.mult, op1=ALU.subtract), reads=["a_"],
                          writes=["t2"])
                    P.dve(lambda e, n=n: e.scalar_tensor_tensor(out=k2[:, 0:n], in0=t2[:, 0:n], scalar=1.0,
                                                                in1=ks_[:, 0:n], op0=ALU.add, op1=ALU.mult),
                          reads=["t2", "ks_"], writes=["k2"])
                    P.dve(lambda e, p=p, n=n: e.scalar_tensor_tensor(out=t2[:, 0:n], in0=rs_[:, 0:n],
                                                                     scalar=self.pcol("r_k", p), in1=k2[:, 0:n],
                                                                     op0=ALU.mult, op1=ALU.mult),
                          reads=["rs_", "k2"], writes=["t2"])
                    pb_, pbr = S.bank()
                    P.pe(lambda e, pb_=pb_, n=n: e.matmul(pb_[:, 0:n], lhsT=blk64, rhs=t2[:, 0:n], start=True,
                                                           stop=True), reads=["t2"], writes=[pbr])
                    P.dve(lambda e, pb_=pb_, n=n: e.tensor_tensor(out=bon[:, 0:n], in0=pb_[:, 0:n], in1=vs_[:, 0:n],
                                                                  op=ALU.mult), reads=[pbr, "vs_"], writes=["bon"])
                    P.dma(lambda e, pc0=pc0, a0=a0, n=n: e.dma_start(out=dr["bonT"][pc0:pc0 + 128, a0:a0 + n],
                                                                   in_=bon[:, 0:n]), S.key("sb"), reads=["bon"])
                    P.dve(lambda e, n=n, mi=mi: e.tensor_tensor_scan(out=cp[:, 0:n], data0=rmask[:, mi, 0:n],
                                                                     data1=pl[:, 0:n], initial=0.0, op0=ALU.mult,
                                                                     op1=ALU.add), reads=["pl", "rmask"], writes=["cp"])
                    P.act(lambda e, n=n: e.activation(out=ex1[:, 0:n], in_=cp[:, 0:n], func=AF.Exp, scale=-1.0),
                          reads=["cp"], writes=["ex1"])
                    P.dve(lambda e, n=n: e.tensor_tensor(out=arbk[:, 1, 0:n], in0=rs_[:, 0:n], in1=ex1[:, 0:n],
                                                         op=ALU.mult), reads=["rs_", "ex1"], writes=["arbk"])
                    L = 8 if smp else 128
                    ns = n // L
                    P.pool(lambda e, n=n, L=L, ns=ns: e.tensor_copy(
                        out=gc[:, 0:ns], in_=ex1[:, 0:n].rearrange("p (s j) -> p s j", j=L)[:, :, L - 1]),
                        reads=["ex1"], writes=["gc"])
                    gc0 = (c.NCT if smp else a0 // 128)
                    P.dma(lambda e, p=p, gc0=gc0, ns=ns: e.dma_start(out=dr["gcT"][:, p, gc0:gc0 + ns],
                                                                   in_=gc[:, 0:ns]), S.key("sc"), reads=["gc"])
                    P.pool(lambda e, n=n: e.tensor_tensor(out=t2[:, 0:n], in0=pl[:, 0:n], in1=cp[:, 0:n],
                                                          op=ALU.subtract), reads=["pl", "cp", ], writes=["t2"])
                    P.act(lambda e, n=n: e.activation(out=ex2[:, 0:n], in_=t2[:, 0:n], func=AF.Exp), reads=["t2"],
                          writes=["ex2"])
                    P.dve(lambda e, n=n: e.scalar_tensor_tensor(out=arbk[:, 0, 0:n], in0=kk[:, 0:n], scalar=-1.0,
                                                                in1=ex2[:, 0:n], op0=ALU.mult, op1=ALU.mult),
                          reads=["kk", "ex2"], writes=["arbk"])
                    P.act(lambda e, n=n: e.activation(out=ex2[:, 0:n], in_=cp[:, 0:n], func=AF.Exp), reads=["cp"],
                          writes=["ex2"])
                    P.pool(lambda e, n=n: e.tensor_tensor(out=t2[:, 0:n], in0=kk[:, 0:n], in1=a_[:, 0:n],
                                                          op=ALU.mult), reads=["kk", "a_"], writes=["t2"])
                    P.dve(lambda e, n=n: e.tensor_tensor(out=arbk[:, 2, 0:n], in0=t2[:, 0:n], in1=ex2[:, 0:n],
                                                         op=ALU.mult), reads=["t2", "ex2"], writes=["arbk"])
                    P.dve(lambda e, n=n: e.tensor_tensor(out=arbk[:, 3, 0:n], in0=k2[:, 0:n], in1=ex2[:, 0:n],
                                                         op=ALU.mult), reads=["k2", "ex2"], writes=["arbk"])
                    P.act(lambda e, n=n: e.activation(out=vbf[:, 0:n], in_=vs_[:, 0:n], func=AF.Copy), reads=["vs_"],
                          writes=["vbf"])
                    P.dma(lambda e, p=p, a0=a0, n=n: e.dma_start(out=dr["arbk"][:, p, :, a0:a0 + n],
                                                               in_=arbk[:, :, 0:n]), S.key("sa"), reads=["arbk"])
                    P.dma(lambda e, p=p, a0=a0, n=n: e.dma_start(out=dr["vTb"][:, p, a0:a0 + n], in_=vbf[:, 0:n]),
                          S.key("sv"), reads=["vbf"])

    def rscan_stage(self, smp):
        c = self.cfg
        dr = self.dr
        NP = 4 if smp else 8
        NH = 2 * NP
        nseg = 16 if smp else 1
        L = 128 // nseg
        sfx = "s" if smp else "p"
        with Stage(self, "rs" + sfx) as S:
            P = S.P
            S.psum_banks(7)
            ptr = S.st.enter_context(self.nc.psum_tensor("rs%s_ptr" % sfx, [128, 8, 128], BF16))
            ARBK = S.sb("ARBK", [128, NP, 4, 128], BF16)
            AR = ARBK[:, :, 0:2, :]
            BK = ARBK[:, :, 2:4, :]
            vT = S.sb("vT", [128, NP, 128], BF16)
            tmB = S.sb("tmB", [128, NP, 128], BF16)
            tmK = S.sb("tmK", [128, NP, 128], BF16)
            tmV = S.sb("tmV", [128, NP, 128], BF16)
            Vpad = S.sb("Vpad", [128, 2, NP, 128], BF16)
            Upad = S.sb("Upad", [128, 2, NP, 128], BF16)
            mats = S.sb("mats", [128, NH, 4, 128], BF16)
            Nb = [S.sb("N%d" % i, [128, NH, 128], BF16) for i in range(2)]
            NTb = [S.sb("NT%d" % i, [128, NH, 128], BF16) for i in range(2)]
            MTb = [S.sb("MT%d" % i, [128, NH, 128], BF16) for i in range(2)]
            maskM = S.sb("maskM", [128, 4, 128], F32)
            T32 = S.sb("T32", [128, nseg, NP, 128], F32)
            Tbf = S.sb("Tbf", [128, nseg, NP, 128], BF16)
            LV = S.sb("LV", [128, NP, 128], F32)
            Xb = S.sb("Xb", [128, NP, 128], BF16)
            Ub = S.sb("Ub", [128, NP, 128], BF16)
            gcs = S.sb("gcs", [128, NP, c.NCT + 16], F32)
            tmpT = S.sb("tmpT", [128, 4, 128], F32)
            o32 = S.sb("o32", [128, NP, 128], F32)
            cen = S.sb("cen", [128, NP, 128], F32)
            sqv = S.sb("sqv", [128, NP, 128], F32)
            rsd = S.sb("rsd", [128, NP, 128], F32)
            bon = S.sb("bon", [128, NP, 128], F32)
            gg = S.sb("gg", [128, NP, 128], F32)
            obb = S.sb("obb", [128, NP, 128], BF16)
            if smp:
                Am = S.sb("Am", [128, NP, 16, 128], BF16)
                Bm = S.sb("Bm", [128, NP, 16, 128], BF16)
                Km = S.sb("Km", [128, NP, 16, 128], BF16)
            identb = self.cst("ident", True)
            blk64 = self.cst("blk64")
            blk64b = self.cst("blk64", True)
            strT = self.cst("strT_" + sfx)
            incT = self.cst("incT_" + sfx)
            strN = self.cst("str_" + sfx)
            for i_, m_ in enumerate((strT, incT, strT, incT)):
                P.pool(lambda e, i_=i_, m_=m_: e.tensor_copy(out=maskM[:, i_, :], in_=m_), writes=["maskM"])
            zer = S.sb("zer", [128, 512], BF16)
            P.pool(lambda e: e.memset(zer[:], 0.0), writes=["zer"])
            P.pool(lambda e: e.memset(Vpad[:], 0.0), writes=["Vpad"])
            P.pool(lambda e: e.memset(Upad[:], 0.0), writes=["Upad"])
            RP = [0, 1, 2]
            OB = [3, 4, 5, 6]
            levels = []
            pw_ = 2
            while pw_ < L:
                levels.append(pw_)
                pw_ *= 2
            work = ([(t, p0) for p0 in range(0, 16, NP) for t in range(c.NCT)] if not smp
                    else [(c.NCT, p0) for p0 in range(0, 16, NP)])
            for (tile, p0) in work:
                t0 = tile * 128
                need_out = t0 >= c.HALO0
                if not smp and tile == 0:
                    P.pool(lambda e: e.memset(T32[:], 0.0), writes=["T32"])
                    P.pool(lambda e: e.memset(Tbf[:], 0.0), writes=["Tbf"])
                P.dma(lambda e, t0=t0, p0=p0: e.dma_start(
                    out=ARBK[:].rearrange("p q f t -> p (q f) t"),
                    in_=dr["arbk"][:, p0:p0 + NP, :, t0:t0 + 128].rearrange("p q f t -> p (q f) t")),
                    S.key("l1"), writes=["AR", "BK"])
                P.dma(lambda e, t0=t0, p0=p0: e.dma_start(out=vT[:], in_=dr["vTb"][:, p0:p0 + NP, t0:t0 + 128]),
                      S.key("l2"), writes=["vT"])
                gq0 = c.NCT if smp else tile
                if smp or tile == 0:
                    P.dma(lambda e, p0=p0: e.dma_start(out=gcs[:], in_=dr["gcT"][:, p0:p0 + NP, :]),
                          S.key("l3"), writes=["gcs"])
                if smp:
                    for sg_ in range(16):
                        P.dma(lambda e, sg_=sg_, p0=p0: e.dma_start(out=T32[:, sg_, :, :],
                                                                   in_=dr["srwkv"][sg_][:, p0:p0 + NP, :]),
                              S.key("l4"), writes=["T32"])
                    P.act(lambda e: e.activation(out=Tbf[:], in_=T32[:], func=AF.Copy), reads=["T32"], writes=["Tbf"])
                for (src, fi, dst, dn) in ((BK, 0, tmB, "tmB"), (BK, 1, tmK, "tmK"), (vT, None, tmV, "tmV")):
                    if RS_CUT is not None and RS_CUT <= -2:
                        continue
                    for g0 in range(0, NP, 8):
                        gn = min(8, NP - g0)
                        for q in range(gn):
                            in_ = src[:, g0 + q, fi, :] if fi is not None else src[:, g0 + q, :]
                            P.pe(lambda e, q=q, in_=in_: e.transpose(ptr[:, q, :], in_, identb),
                                 reads=["BK" if fi is not None else "vT"], writes=["ptr"])
                        P.act(lambda e, dst=dst, g0=g0, gn=gn: e.activation(out=dst[:, g0:g0 + gn, :],
                                                                            in_=ptr[:, 0:gn, :], func=AF.Copy),
                              reads=["ptr"], writes=[dn])
                        if dst is tmV:
                            P.pool(lambda e, g0=g0, gn=gn: e.tensor_copy(out=Vpad[:, 0, g0:g0 + gn, 0:64],
                                                                        in_=tmV[:, g0:g0 + gn, 0:64]), reads=["tmV"],
                                   writes=["Vpad"])
                            P.pool(lambda e, g0=g0, gn=gn: e.tensor_copy(out=Vpad[:, 1, g0:g0 + gn, 64:128],
                                                                        in_=tmV[:, g0:g0 + gn, 64:128]), reads=["tmV"],
                                   writes=["Vpad"])
                if RS_CUT is not None and RS_CUT < 1:
                    continue
                for hh in range(NH):
                    if "m" in RS_SUB:
                        break
                    q, h = hh // 2, hh % 2
                    pm, pmr = S.bank(RP)
                    rr_ = slice(64 * h, 64 * h + 64)
                    P.pe(lambda e, pm=pm, q=q, rr_=rr_: e.matmul(
                        pm[:, 0:256], lhsT=BK[rr_, q, 0, :], rhs=AR[rr_, q, :, :].rearrange("p f t -> p (f t)"),
                        start=True, stop=True), reads=["BK", "AR"], writes=[pmr])
                    P.pe(lambda e, pm=pm, q=q, rr_=rr_: e.matmul(
                        pm[:, 256:512], lhsT=BK[rr_, q, 1, :], rhs=AR[rr_, q, :, :].rearrange("p f t -> p (f t)"),
                        start=True, stop=True), reads=["BK", "AR"], writes=[pmr])
                    if "e" in RS_SUB:
                        continue
                    P.dve(lambda e, pm=pm, hh=hh: e.tensor_tensor(
                        out=mats[:, hh, :, :], in0=pm[:].rearrange("p (f t) -> p f t", t=128), in1=maskM[:],
                        op=ALU.mult), reads=[pmr, "maskM"], writes=["mats"])
                for q0 in range(0, NP, 4):
                    for h in range(2):
                        pl_, plr = S.bank(RP)
                        rr_ = slice(64 * h, 64 * h + 64)
                        for j in range(4):
                            q = q0 + j
                            P.pe(lambda e, pl_=pl_, j=j, q=q, rr_=rr_: e.matmul(
                                pl_[:, j * 128:(j + 1) * 128], lhsT=AR[rr_, q, 0, :], rhs=BK[rr_, q, 0, :], start=True,
                                stop=True), reads=["AR", "BK"], writes=[plr])
                        P.dve(lambda e, pl_=pl_, q0=q0, h=h: e.tensor_tensor(
                            out=Nb[0][:].rearrange("p (q h) t -> p q h t", h=2)[:, q0:q0 + 4, h, :],
                            in0=pl_[:].rearrange("p (j t) -> p j t", t=128),
                            in1=strN.unsqueeze(1).to_broadcast([128, 4, 128]), op=ALU.mult), reads=[plr],
                            writes=["N0"])
                if RS_CUT is not None and RS_CUT < 2:
                    continue
                P.pool(lambda e: e.tensor_copy(out=NTb[0][:], in_=mats[:, :, 0, :]), reads=["mats"], writes=["NT0"])
                P.pool(lambda e: e.tensor_tensor(out=MTb[0][:], in0=mats[:, :, 0, :],
                                                 in1=identb.unsqueeze(1).to_broadcast([128, NH, 128]), op=ALU.add),
                       reads=["mats"], writes=["MT0"])
                cur = 0
                for li, lv in enumerate(levels):
                    last = li == len(levels) - 1
                    nxt = 1 - cur
                    for g0 in range(0, NH, 4):
                        pn, pnr = S.bank(RP)
                        for j in range(4):
                            P.pe(lambda e, pn=pn, j=j, g0=g0, cur=cur: e.matmul(
                                pn[:, j * 128:(j + 1) * 128], lhsT=NTb[cur][:, g0 + j, :], rhs=Nb[cur][:, g0 + j, :],
                                start=True, stop=True), reads=["N%d" % cur, "NT%d" % cur], writes=[pnr])
                        P.act(lambda e, pn=pn, g0=g0, nxt=nxt: e.activation(
                            out=Nb[nxt][:, g0:g0 + 4, :], in_=pn[:].rearrange("p (j t) -> p j t", t=128),
                            func=AF.Copy), reads=[pnr], writes=["N%d" % nxt])
                        if not last:
                            pt_, ptr_ = S.bank(RP)
                            for j in range(4):
                                P.pe(lambda e, pt_=pt_, j=j, g0=g0, cur=cur: e.matmul(
                                    pt_[:, j * 128:(j + 1) * 128], lhsT=Nb[cur][:, g0 + j, :],
                                    rhs=NTb[cur][:, g0 + j, :], start=True, stop=True),
                                    reads=["N%d" % cur, "NT%d" % cur], writes=[ptr_])
                            P.act(lambda e, pt_=pt_, g0=g0, nxt=nxt: e.activation(
                                out=NTb[nxt][:, g0:g0 + 4, :], in_=pt_[:].rearrange("p (j t) -> p j t", t=128),
                                func=AF.Copy), reads=[ptr_], writes=["NT%d" % nxt])
                        pp, ppr = S.bank(RP)
                        for j in range(4):
                            P.pe(lambda e, pp=pp, j=j, g0=g0, cur=cur, nxt=nxt: e.matmul(
                                pp[:, j * 128:(j + 1) * 128], lhsT=Nb[nxt][:, g0 + j, :], rhs=MTb[cur][:, g0 + j, :],
                                start=True, stop=False), reads=["N%d" % nxt, "MT%d" % cur], writes=[ppr])
                            P.pe(lambda e, pp=pp, j=j, g0=g0, cur=cur: e.matmul(
                                pp[:, j * 128:(j + 1) * 128], lhsT=identb, rhs=MTb[cur][:, g0 + j, :],
                                start=False, stop=True), reads=["MT%d" % cur], writes=[ppr])
                        P.act(lambda e, pp=pp, g0=g0, nxt=nxt: e.activation(
                            out=MTb[nxt][:, g0:g0 + 4, :], in_=pp[:].rearrange("p (j t) -> p j t", t=128),
                            func=AF.Copy), reads=[ppr], writes=["MT%d" % nxt])
                    cur = nxt
                MT = MTb[cur]
                mtn = "MT%d" % cur
                if RS_CUT is not None and RS_CUT < 3:
                    continue
                if smp:
                    segcol = self.c32x
                    seg16 = self.cst("seg16", True)
                    for q in range(NP):
                        P.dve(lambda e, q=q: e.tensor_tensor(
                            out=Am[:, q, :, :], in0=AR[:, q, 0, :].unsqueeze(1).to_broadcast([128, 16, 128]),
                            in1=segcol[:], op=ALU.mult), reads=["AR"], writes=["Am"])
                        P.pool(lambda e, q=q: e.tensor_tensor(
                            out=Bm[:, q, :, :], in0=tmB[:, q, :].unsqueeze(1).to_broadcast([128, 16, 128]),
                            in1=seg16[:, 0:16].unsqueeze(2).to_broadcast([128, 16, 128]), op=ALU.mult),
                            reads=["tmB"], writes=["Bm"])
                        P.pool(lambda e, q=q: e.tensor_tensor(
                            out=Km[:, q, :, :], in0=tmK[:, q, :].unsqueeze(1).to_broadcast([128, 16, 128]),
                            in1=seg16[:, 0:16].unsqueeze(2).to_broadcast([128, 16, 128]), op=ALU.mult),
                            reads=["tmK"], writes=["Km"])
                if RS_CUT is not None and RS_CUT < 4:
                    continue
                if need_out:
                    for q in range(NP):
                        po, por = S.banks[OB[q // 4]], "pb%d" % OB[q // 4]
                        osl = slice((q % 4) * 128, (q % 4) * 128 + 128)
                        if q % 4 == 0:
                            P.pe(lambda e, po=po: e.matmul(po[:, :], lhsT=zer[:, 0:128], rhs=zer[:, :], start=True,
                                                           stop=False), reads=["zer"], writes=[por])
                        for h in range(2):
                            P.pe(lambda e, po=po, osl=osl, q=q, h=h: e.matmul(
                                po[:, osl], lhsT=Vpad[:, h, q, :], rhs=mats[:, 2 * q + h, 3, :], start=False,
                                stop=False), reads=["Vpad", "mats"], writes=[por])
                        for sg_ in range(nseg):
                            P.pe(lambda e, po=po, q=q, sg_=sg_: e.matmul(
                                po[:, (q % 4) * 128 + sg_ * L:(q % 4) * 128 + (sg_ + 1) * L], lhsT=Tbf[:, sg_, q, :],
                                rhs=AR[:, q, 1, sg_ * L:(sg_ + 1) * L], start=False, stop=False),
                                reads=["Tbf", "AR"], writes=[por])
                if RS_CUT is not None and RS_CUT < 5:
                    continue
                for g0 in range(0, NP, 4):
                    pv, pvr = S.bank(RP)
                    for j in range(4):
                        q = g0 + j
                        for h in range(2):
                            P.pe(lambda e, pv=pv, j=j, q=q, h=h: e.matmul(
                                pv[:, j * 128 + 64 * h:j * 128 + 64 * h + 64], lhsT=mats[:, 2 * q + h, 2, :],
                                rhs=tmV[:, q, 64 * h:64 * h + 64], start=True, stop=True),
                                reads=["mats", "tmV"], writes=[pvr])
                    P.act(lambda e, pv=pv, g0=g0: e.activation(out=LV[:, g0:g0 + 4, :],
                                                               in_=pv[:].rearrange("p (j t) -> p j t", t=128),
                                                               func=AF.Copy), reads=[pvr], writes=["LV"])
                if RS_CUT is not None and RS_CUT < 6:
                    continue
                for g0 in range(0, NP, 4):
                    px, pxr = S.bank(RP)
                    for j in range(4):
                        q = g0 + j
                        for sg_ in range(nseg):
                            lh = Am[:, q, sg_, :] if smp else AR[:, q, 0, :]
                            P.pe(lambda e, px=px, j=j, q=q, sg_=sg_, lh=lh: e.matmul(
                                px[:, j * 128:(j + 1) * 128], lhsT=lh, rhs=Tbf[:, sg_, q, :], start=(sg_ == 0),
                                stop=(sg_ == nseg - 1)), reads=["Am" if smp else "AR", "Tbf"], writes=[pxr])
                    P.dve(lambda e, px=px, g0=g0: e.tensor_tensor(
                        out=Xb[:, g0:g0 + 4, :], in0=px[:].rearrange("p (j t) -> p j t", t=128),
                        in1=LV[:, g0:g0 + 4, :], op=ALU.add), reads=[pxr, "LV"], writes=["Xb"])
                if RS_CUT is not None and RS_CUT < 7:
                    continue
                for g0 in range(0, NP, 4):
                    pu, pur = S.bank(RP)
                    for j in range(4):
                        q = g0 + j
                        for h in range(2):
                            P.pe(lambda e, pu=pu, j=j, q=q, h=h: e.matmul(
                                pu[:, j * 128 + 64 * h:j * 128 + 64 * h + 64], lhsT=MT[:, 2 * q + h, :],
                                rhs=Xb[:, q, 64 * h:64 * h + 64], start=True, stop=True), reads=[mtn, "Xb"],
                                writes=[pur])
                    puv = pu[:].rearrange("p (j t) -> p j t", t=128)
                    P.act(lambda e, puv=puv, g0=g0: e.activation(out=Ub[:, g0:g0 + 4, :], in_=puv, func=AF.Copy),
                          reads=[pur], writes=["Ub"])
                    if need_out:
                        P.pool(lambda e, g0=g0: e.tensor_copy(out=Upad[:, 0, g0:g0 + 4, 0:64],
                                                              in_=Ub[:, g0:g0 + 4, 0:64]), reads=["Ub"],
                               writes=["Upad"])
                        P.pool(lambda e, g0=g0: e.tensor_copy(out=Upad[:, 1, g0:g0 + 4, 64:128],
                                                              in_=Ub[:, g0:g0 + 4, 64:128]), reads=["Ub"],
                               writes=["Upad"])
                if RS_CUT is not None and RS_CUT < 8:
                    continue
                if need_out:
                    for q in range(NP):
                        po, por = S.banks[OB[q // 4]], "pb%d" % OB[q // 4]
                        osl = slice((q % 4) * 128, (q % 4) * 128 + 128)
                        for h in range(2):
                            P.pe(lambda e, po=po, osl=osl, q=q, h=h: e.matmul(
                                po[:, osl], lhsT=Upad[:, h, q, :], rhs=mats[:, 2 * q + h, 1, :], start=False,
                                stop=(h == 1 and q % 4 == 3)), reads=["Upad", "mats"], writes=[por])
                if RS_CUT is not None and RS_CUT < 9:
                    continue
                for sg_ in range(nseg):
                    for g0 in range(0, NP, 4):
                        pt2, pt2r = S.bank(RP)
                        for j in range(4):
                            q = g0 + j
                            lb = Bm[:, q, sg_, :] if smp else tmB[:, q, :]
                            lk = Km[:, q, sg_, :] if smp else tmK[:, q, :]
                            P.pe(lambda e, pt2=pt2, j=j, q=q, lb=lb: e.matmul(
                                pt2[:, j * 128:(j + 1) * 128], lhsT=lb, rhs=Ub[:, q, :], start=True, stop=False),
                                reads=["Bm" if smp else "tmB", "Ub"], writes=[pt2r])
                            P.pe(lambda e, pt2=pt2, j=j, q=q, lk=lk: e.matmul(
                                pt2[:, j * 128:(j + 1) * 128], lhsT=lk, rhs=tmV[:, q, :], start=False, stop=True),
                                reads=["Km" if smp else "tmK", "tmV"], writes=[pt2r])
                        P.dve(lambda e, pt2=pt2: e.tensor_tensor(
                            out=tmpT[:], in0=pt2[:].rearrange("p (j t) -> p j t", t=128),
                            in1=blk64.unsqueeze(1).to_broadcast([128, 4, 128]), op=ALU.mult), reads=[pt2r],
                            writes=["tmpT"])
                        P.dve(lambda e, sg_=sg_, g0=g0: e.tensor_tensor(out=tmpT[:], in0=tmpT[:],
                                                                       in1=T32[:, sg_, g0:g0 + 4, :], op=ALU.add),
                              reads=["tmpT", "T32"], writes=["tmpT"])
                        P.dve(lambda e, sg_=sg_, g0=g0, gq0=gq0: e.tensor_tensor(
                            out=T32[:, sg_, g0:g0 + 4, :], in0=tmpT[:],
                            in1=gcs[:, g0:g0 + 4, gq0 + sg_:gq0 + sg_ + 1].to_broadcast([128, 4, 128]), op=ALU.mult),
                            reads=["tmpT", "gcs"], writes=["T32"])
                if not smp:
                    P.act(lambda e: e.activation(out=Tbf[:], in_=T32[:], func=AF.Copy), reads=["T32"], writes=["Tbf"])
                    if tile == c.NCT - 1:
                        P.dma(lambda e, p0=p0: e.dma_start(out=dr["rwkvp"][:, p0:p0 + NP, :], in_=T32[:, 0, :, :]),
                              S.key("so"), reads=["T32"])
                else:
                    for sg_ in range(16):
                        P.dma(lambda e, sg_=sg_, p0=p0: e.dma_start(out=dr["rwkvs"][sg_][:, p0:p0 + NP, :],
                                                                   in_=T32[:, sg_, :, :]), S.key("so"),
                              reads=["T32"])
                if not need_out:
                    continue
                if RS_CUT is not None and RS_CUT < 11:
                    continue
                P.dma(lambda e, t0=t0, p0=p0: e.dma_start(
                    out=bon[:], in_=dr["bonT"][p0 * 128:(p0 + NP) * 128, t0:t0 + 128].rearrange("(q p) t -> p q t",
                                                                                             p=128)),
                    S.key("l5"), writes=["bon"])
                P.dma(lambda e, t0=t0, p0=p0: e.dma_start(
                    out=gg[:], in_=dr["gT"][p0 * 128:(p0 + NP) * 128, t0:t0 + 128].rearrange("(q p) t -> p q t",
                                                                                           p=128)),
                    S.key("l6"), writes=["gg"])
                for g0 in range(0, NP, 4):
                    po, por = S.banks[OB[g0 // 4]], "pb%d" % OB[g0 // 4]
                    P.act(lambda e, po=po, g0=g0: e.activation(out=o32[:, g0:g0 + 4, :],
                                                               in_=po[:].rearrange("p (j t) -> p j t", t=128),
                                                               func=AF.Copy), reads=[por], writes=["o32"])
                    pm1, pm1r = S.bank(RP)
                    P.pe(lambda e, pm1=pm1, g0=g0: e.matmul(pm1[:, :], lhsT=blk64,
                                                            rhs=o32[:, g0:g0 + 4, :].rearrange("p j t -> p (j t)"),
                                                            start=True, stop=True), reads=["o32"], writes=[pm1r])
                    P.dve(lambda e, pm1=pm1, g0=g0: e.scalar_tensor_tensor(
                        out=cen[:, g0:g0 + 4, :], in0=pm1[:].rearrange("p (j t) -> p j t", t=128), scalar=-1.0 / 64,
                        in1=o32[:, g0:g0 + 4, :], op0=ALU.mult, op1=ALU.add), reads=[pm1r, "o32"], writes=["cen"])
                    P.act(lambda e, g0=g0: e.activation(out=sqv[:, g0:g0 + 4, :], in_=cen[:, g0:g0 + 4, :],
                                                        func=AF.Square), reads=["cen"], writes=["sqv"])
                    pm2, pm2r = S.bank(RP)
                    P.pe(lambda e, pm2=pm2, g0=g0: e.matmul(pm2[:, :], lhsT=blk64,
                                                            rhs=sqv[:, g0:g0 + 4, :].rearrange("p j t -> p (j t)"),
                                                            start=True, stop=True), reads=["sqv"], writes=[pm2r])
                    P.act(lambda e, pm2=pm2, g0=g0: e.activation(
                        out=rsd[:, g0:g0 + 4, :], in_=pm2[:].rearrange("p (j t) -> p j t", t=128), func=AF.Sqrt,
                        scale=1.0 / 64, bias=GN_EPS), reads=[pm2r], writes=["rsd"])
                    P.dve(lambda e, g0=g0: e.reciprocal(out=rsd[:, g0:g0 + 4, :], in_=rsd[:, g0:g0 + 4, :]),
                          reads=["rsd"], writes=["rsd"])
                    P.dve(lambda e, g0=g0: e.tensor_tensor(out=cen[:, g0:g0 + 4, :], in0=cen[:, g0:g0 + 4, :],
                                                           in1=rsd[:, g0:g0 + 4, :], op=ALU.mult),
                          reads=["cen", "rsd"], writes=["cen"])
                    for j in range(4):
                        q = g0 + j
                        P.pool(lambda e, q=q, p0=p0: e.tensor_scalar(
                            out=cen[:, q, :], in0=cen[:, q, :], scalar1=self.pcol("ln_x_w", p0 + q),
                            scalar2=self.pcol("ln_x_b", p0 + q), op0=ALU.mult, op1=ALU.add), reads=["cen"],
                            writes=["cen"])
                    P.dve(lambda e, g0=g0: e.tensor_tensor(out=cen[:, g0:g0 + 4, :], in0=cen[:, g0:g0 + 4, :],
                                                           in1=bon[:, g0:g0 + 4, :], op=ALU.add),
                          reads=["cen", "bon"], writes=["cen"])
                    P.dve(lambda e, g0=g0: e.tensor_tensor(out=obb[:, g0:g0 + 4, :], in0=cen[:, g0:g0 + 4, :],
                                                           in1=gg[:, g0:g0 + 4, :], op=ALU.mult),
                          reads=["cen", "gg"], writes=["obb"])
                P.dma(lambda e, t0=t0, p0=p0: e.dma_start(
                    out=dr["obT"][p0 * 128:(p0 + NP) * 128, t0:t0 + 128].rearrange("(q p) t -> p q t", p=128),
                    in_=obb[:]), S.key("so2"), reads=["obb"])

    def lin_stage(self, name, W, KC, kparts, a_src, t0, nt, chunks, epi, a_cast=False, extra=None, wblk=512):
        with Stage(self, name) as S:
            S.psum_banks(8)
            if a_cast:
                aT = S.sb("aT", [128, KC, nt], BF16)
                S.P.dma(lambda e: e.dma_start(out=aT[:], in_=a_src[:, t0:t0 + nt].rearrange("(k p) t -> p k t", p=128)),
                        S.key("aT"), writes=["aT.0"], q="pool")
            else:
                aT = self.load_aT(S, "aT", a_src, KC, t0, nt, kparts)
            S.stg = [S.sb("stg%d" % i, [128, nt], F32) for i in range(3)]
            S.side = {}
            S.cnt = 0
            if extra:
                extra(S)

            def evac(ci, c0, cw, pss):
                si = S.cnt % 3
                S.cnt += 1
                epi(S, ci, c0, cw, pss, S.stg[si], "stg%d" % si)

            self.fm_linear(S, W, KC, kparts, chunks, aT, "aT", nt, evac, wblk=wblk)

    def side_load(self, S, tag, src, r0, cw, t0, nt, dt=F32):
        if tag not in S.side:
            S.side[tag] = [[S.sb("sd%s%d" % (tag, i), [128, nt], dt) for i in range(2)], 0]
        bufs, k = S.side[tag]
        S.side[tag][1] = k + 1
        t = bufs[k % 2]
        rn = "sd%s%d" % (tag, k % 2)
        S.P.dma(lambda e: e.dma_start(out=t[0:cw, :], in_=src[r0:r0 + cw, t0:t0 + nt]), S.key(rn), writes=[rn])
        return t, rn

    def post_stages(self):
        c = self.cfg
        dr = self.dr
        T0, NT = c.HALO0, c.TPM
        ch32 = [(j * 128, 128) for j in range(32)]

        def epi_a(S, ci, c0, cw, pss, stg, sn):
            ga, gn = self.side_load(S, "ga", dr["gaT"], c0, cw, T0, NT)
            for (ps, pr, a0, n) in pss:
                S.P.dve(lambda e, ps=ps, a0=a0, n=n: e.tensor_tensor(out=stg[0:cw, a0:a0 + n], in0=ps,
                                                                     in1=ga[0:cw, a0:a0 + n], op=ALU.mult),
                        reads=[pr, gn], writes=[sn])
            S.P.dma(lambda e: e.dma_start(out=dr["yoT"][c0:c0 + cw, T0:T0 + NT], in_=stg[0:cw, :]), S.key("st"),
                    reads=[sn])

        if self.want("bra"):
            self.lin_stage("bra", dr["w_branch_a"], 16, 128, dr["oaT"], T0, NT, ch32, epi_a)

        def epi_b(S, ci, c0, cw, pss, stg, sn):
            gb, gn = self.side_load(S, "gb", dr["gbT"], c0, cw, T0, NT)
            ma, mn = self.side_load(S, "ma", dr["yoT"], c0, cw, T0, NT)
            mb, mbn = S.mixb[S.cnt % 2], "mixb%d" % (S.cnt % 2)
            for (ps, pr, a0, n) in pss:
                S.P.dve(lambda e, ps=ps, a0=a0, n=n: e.tensor_tensor(out=stg[0:cw, a0:a0 + n], in0=ps,
                                                                     in1=gb[0:cw, a0:a0 + n], op=ALU.mult),
                        reads=[pr, gn], writes=[sn])
            S.P.pool(lambda e: e.tensor_tensor(out=mb[0:cw, :], in0=stg[0:cw, :], in1=ma[0:cw, :], op=ALU.add),
                     reads=[sn, mn], writes=[mbn])
            S.P.dma(lambda e: e.dma_start(out=dr["mixT"][c0:c0 + cw, T0:T0 + NT], in_=mb[0:cw, :]), S.key("st"),
                    reads=[mbn])

        def extra_b(S):
            S.mixb = [S.sb("mixb%d" % i, [128, NT], BF16) for i in range(2)]

        if self.want("brb"):
            self.lin_stage("brb", dr["w_branch_b"], 16, 128, dr["obT"], T0, NT, ch32, epi_b, extra=extra_b)

        def epi_store(dst, tofs, ntk):
            def epi(S, ci, c0, cw, pss, stg, sn):
                for (ps, pr, a0, n) in pss:
                    self.copy_ps(S, stg[0:cw, a0:a0 + n], ps, pr, sn)
                S.P.dma(lambda e: e.dma_start(out=dst[c0:c0 + cw, tofs:tofs + ntk], in_=stg[0:cw, :]),
                        S.key("st"), reads=[sn])
            return epi

        if self.want("wout"):
            self.lin_stage("wo", dr["w_out"], 32, 128, dr["mixT"], T0, NT, ch32, epi_store(dr["yoT"], T0, NT))
        if self.want("norm2"):
            self.norm_stage("n2", dr["yoT"], T0, NT, "g_post_mix", dr["hfT"], res=dr["xT"], out_x=dr["x1T"],
                            gain2="g_pre_ffn")

        nfc = DFF // 128

        def epi_up(S, ci, c0, cw, pss, stg, sn):
            isg = c0 < DFF
            dst = dr["ugT"] if isg else dr["uvT"]
            r0 = c0 if isg else c0 - DFF
            for (ps, pr, a0, n) in pss:
                self.copy_ps(S, stg[0:cw, a0:a0 + n], ps, pr, sn)
            S.P.dma(lambda e: e.dma_start(out=dst[r0:r0 + cw, T0:T0 + NT], in_=stg[0:cw, :]), S.key("st"), reads=[sn])
            if isg:
                nctx = c.CTX - T0
                S.P.dma(lambda e: e.dma_start(out=dr["convTp"][r0:r0 + cw, :], in_=stg[0:cw, nctx - 2:nctx]),
                        S.key("st"), reads=[sn])
                lc, ln = S.lc[S.cnt % 2], "lc%d" % (S.cnt % 2)
                S.P.pool(lambda e: e.tensor_copy(
                    out=lc[0:cw, :, :], in_=stg[0:cw, NT - 128:NT].rearrange("p (s j) -> p s j", j=8)[:, :, 6:8]),
                    reads=[sn], writes=[ln])
                S.P.dma(lambda e: e.dma_start(out=dr["convTs"][r0:r0 + cw, :, :], in_=lc[0:cw, :, :]), S.key("st"),
                        reads=[ln])

        def extra_up(S):
            S.lc = [S.sb("lc%d" % i, [128, 16, 2], F32) for i in range(2)]

        if self.want("up"):
            chunks = [(j * 128, 128) for j in range(2 * nfc)]
            self.lin_stage("up", dr["w_up"], 32, 128, dr["hfT"], T0, NT, chunks, epi_up, extra=extra_up)
        if self.want("ffact"):
            self.ffact_stage()

        O0, NO = c.OWN0, c.TOW
        KH = nfc // 2

        def epi_d2(S, ci, c0, cw, pss, stg, sn):
            pa, pn = self.side_load(S, "pa", dr["fT"], c0, cw, O0, NO)
            for (ps, pr, a0, n) in pss:
                S.P.dve(lambda e, ps=ps, a0=a0, n=n: e.tensor_tensor(out=stg[0:cw, a0:a0 + n], in0=ps,
                                                                     in1=pa[0:cw, a0:a0 + n], op=ALU.add),
                        reads=[pr, pn], writes=[sn])
            S.P.dma(lambda e: e.dma_start(out=dr["x2T"][c0:c0 + cw, O0:O0 + NO], in_=stg[0:cw, :]), S.key("st"),
                    reads=[sn])

        if self.want("down"):
            self.lin_stage("d1", dr["w_down"][0:KH * 128, :], KH, 128, dr["actT"][0:KH * 128, :], O0, NO, ch32,
                           epi_store(dr["fT"], O0, NO), wblk=256)
            self.lin_stage("d2", dr["w_down"][KH * 128:, :], nfc - KH, 128, dr["actT"][KH * 128:, :], O0, NO, ch32,
                           epi_d2, wblk=256)
        if self.want("norm3"):
            self.norm_stage("n3", dr["x2T"], O0, NO, "g_post_ffn", dr["hpT"], res=dr["x1T"], out_x=dr["fT"],
                            gain2="g_pe")

        if self.want("pe"):
            self.lin_stage("pp", dr["w_pe"], 2, 128, dr["pT"], 0, NO, ch32, epi_store(dr["ppT"], O0, NO), a_cast=True)

            def epi_g(S, ci, c0, cw, pss, stg, sn):
                pp, ppn = self.side_load(S, "pp", dr["ppT"], c0, cw, O0, NO)
                x2, x2n = self.side_load(S, "x2", dr["fT"], c0, cw, O0, NO)
                for (ps, pr, a0, n) in pss:
                    S.P.act(lambda e, ps=ps, a0=a0, n=n: e.activation(out=stg[0:cw, a0:a0 + n], in_=ps,
                                                                      func=AF.Sigmoid), reads=[pr], writes=[sn])
                S.P.dve(lambda e: e.tensor_tensor(out=stg[0:cw, :], in0=stg[0:cw, :], in1=pp[0:cw, :], op=ALU.mult),
                        reads=[sn, ppn], writes=[sn])
                S.P.pool(lambda e: e.tensor_tensor(out=stg[0:cw, :], in0=stg[0:cw, :], in1=x2[0:cw, :], op=ALU.add),
                         reads=[sn, x2n], writes=[sn])
                S.P.dma(lambda e: e.dma_start(out=dr["yT"][c0:c0 + cw, :], in_=stg[0:cw, :]), S.key("st"), reads=[sn])

            self.lin_stage("pg", dr["w_pe_gate"], 32, 128, dr["hpT"], O0, NO, ch32, epi_g)

    def ffact_stage(self):
        c = self.cfg
        dr = self.dr
        NOP = c.CTX - c.OWN0
        with Stage(self, "fa") as S:
            P = S.P
            gp = [S.sb("gp%d" % i, [128, NOP + 2], F32) for i in range(2)]
            gs = [S.sb("gs%d" % i, [128, 16, 10], F32) for i in range(2)]
            st = [S.sb("st%d" % i, [128, 16, 2], F32) for i in range(2)]
            vv = [S.sb("vv%d" % i, [128, c.TOW], F32) for i in range(2)]
            cv = S.sb("cv", [128, c.TOW], F32)
            uu = S.sb("uu", [128, c.TOW], F32)
            ab = [S.sb("ab%d" % i, [128, c.TOW], BF16) for i in range(2)]
            for j in range(DFF // 128):
                b = j % 2
                r0 = j * 128
                g_, gn = gp[b], "gp%d" % b
                s_, sn_ = gs[b], "gs%d" % b
                t_, tn = st[b], "st%d" % b
                v_, vn = vv[b], "vv%d" % b
                a_, an = ab[b], "ab%d" % b
                P.dma(lambda e, g_=g_, r0=r0: e.dma_start(out=g_[:, :], in_=dr["ugT"][r0:r0 + 128, c.OWN0 - 2:c.CTX]),
                      S.key("l"), writes=[gn])
                P.dma(lambda e, s_=s_, r0=r0: e.dma_start(
                    out=s_[:, :, 2:10], in_=dr["ugT"][r0:r0 + 128, c.CTX:c.TT].rearrange("p (s j) -> p s j", j=8)),
                    S.key("l"), writes=[sn_])
                P.dma(lambda e, t_=t_, r0=r0: e.dma_start(out=t_[:], in_=dr["sconvT"][r0:r0 + 128, :, :]),
                      S.key("l"), writes=[tn])
                P.dma(lambda e, v_=v_, r0=r0: e.dma_start(out=v_[:, :], in_=dr["uvT"][r0:r0 + 128, c.OWN0:c.TT]),
                      S.key("l"), writes=[vn])
                P.pool(lambda e, s_=s_, t_=t_: e.tensor_copy(out=s_[:, :, 0:2], in_=t_[:]), reads=[tn, sn_],
                       writes=[sn_])
                P.pool(lambda e, g_=g_: e.tensor_scalar(out=g_[:, 0:2], in0=g_[:, 0:2], scalar1=self.pcol("flag"),
                                                        scalar2=None, op0=ALU.mult), reads=[gn], writes=[gn])
                w = [self.pcol("conv_w%d" % k, j) for k in range(3)]
                bcol = self.pcol("conv_b", j)
                for (cvv, src, rn, nn) in ((cv[:, 0:NOP], lambda k, g_=g_: g_[:, k:k + NOP], gn, None),
                                           (cv[:, NOP:].rearrange("p (s j) -> p s j", j=8),
                                            lambda k, s_=s_: s_[:, :, k:k + 8], sn_, None)):
                    P.dve(lambda e, cvv=cvv, src=src, w=w, bcol=bcol: e.tensor_scalar(out=cvv, in0=src(2), scalar1=w[2], scalar2=bcol,
                                                                      op0=ALU.mult, op1=ALU.add), reads=[rn],
                          writes=["cv"])
                    P.dve(lambda e, cvv=cvv, src=src, w=w: e.scalar_tensor_tensor(out=cvv, in0=src(1), scalar=w[1], in1=cvv,
                                                                             op0=ALU.mult, op1=ALU.add),
                          reads=[rn, "cv"], writes=["cv"])
                    P.dve(lambda e, cvv=cvv, src=src, w=w: e.scalar_tensor_tensor(out=cvv, in0=src(0), scalar=w[0], in1=cvv,
                                                                             op0=ALU.mult, op1=ALU.add),
                          reads=[rn, "cv"], writes=["cv"])
                P.pool(lambda e: e.tensor_tensor(out=uu[:], in0=cv[:], in1=cv[:], op=ALU.mult), reads=["cv"],
                       writes=["uu"])
                P.pool(lambda e: e.tensor_scalar(out=uu[:], in0=uu[:], scalar1=0.044715, scalar2=1.0, op0=ALU.mult,
                                                 op1=ALU.add), reads=["uu"], writes=["uu"])
                P.pool(lambda e: e.tensor_tensor(out=uu[:], in0=uu[:], in1=cv[:], op=ALU.mult), reads=["uu", "cv"],
                       writes=["uu"])
                P.act(lambda e: e.activation(out=uu[:], in_=uu[:], func=AF.Sigmoid, scale=1.5957691216057308),
                      reads=["uu"], writes=["uu"])
                P.dve(lambda e: e.tensor_tensor(out=uu[:], in0=uu[:], in1=cv[:], op=ALU.mult), reads=["uu", "cv"],
                      writes=["uu"])
                P.dve(lambda e, a_=a_, v_=v_: e.tensor_tensor(out=a_[:], in0=uu[:], in1=v_[:], op=ALU.mult),
                      reads=["uu", vn], writes=[an])
                P.dma(lambda e, a_=a_, r0=r0: e.dma_start(out=dr["actT"][r0:r0 + 128, c.OWN0:c.TT], in_=a_[:]),
                      S.key("s"), reads=[an])

    def kneg05(self):
        return self.pcd[:, 60:61]

    def build_rest(self):
        if self.want("gla"):
            self.gla_stage()
        if self.want("rfront"):
            self.rfront_stage()
        if self.want("rscan"):
            self.rscan_stage(False)
            self.rscan_stage(True)
        self.post_stages()

WNAMES = ["w_in", "w_alpha2", "w_branch_a", "w_decay2", "w_iclr2", "w_gate2", "w_branch_b", "w_out", "w_up",
          "w_down", "w_pe_gate", "w_pe"]


def host_core_inputs(cfg, inp, x_ctx, p_ctx, xs, ps, sgla, srwkv, sshift, sconv, flag, shared=None):
    d = {}
    d["xT"] = np.ascontiguousarray(np.concatenate([x_ctx, xs.reshape(128, D)], axis=0).T)
    d["pT"] = np.ascontiguousarray(np.concatenate([p_ctx[cfg.OWN0:], ps.reshape(128, PD)], axis=0).T)
    d["pcols"] = host_pcols(inp, flag)
    if shared is None:
        shared = {}
    if "consts" not in shared:
        shared["consts"] = host_consts()[0]
        t_ = np.arange(128)
        shared["segcol"] = np.ascontiguousarray(np.broadcast_to(
            (t_[None, :] // 8 == np.arange(16)[:, None]).astype(np.float32).reshape(1, 16 * 128), (128, 16 * 128)))
        for n in WNAMES:
            shared[n] = np.ascontiguousarray(np.asarray(inp[n][0], np.float32))
    d.update(shared)
    d["sgla"] = np.ascontiguousarray(sgla, np.float32)
    S = np.asarray(srwkv, np.float32).reshape(16, 16, 2, 64, 64)
    T = np.zeros((16, 2, 64, 16, 2, 64), np.float32)
    for hl in range(2):
        T[:, hl, :, :, hl, :] = S[:, :, hl].transpose(0, 3, 1, 2)
    d["srwkv"] = T.reshape(16, 128, 16, 128)
    d["sshiftT"] = np.ascontiguousarray(np.asarray(sshift, np.float32).T)
    d["sconvT"] = np.ascontiguousarray(np.asarray(sconv, np.float32).transpose(2, 0, 1))
    return d


def _unpad_rwkv(T):
    lead = T.shape[:-3]
    T = T.reshape(lead + (2, 64, 16, 2, 64))
    out = np.zeros(lead + (16, 2, 64, 64), np.float32)
    for hl in range(2):
        blk = T[..., hl, :, :, hl, :]
        out[..., :, hl, :, :] = np.moveaxis(blk, -3, -1)
    return out.reshape(lead + (32, 64, 64))


def host_outputs_core(cfg, r):
    o = {}
    yT = np.asarray(r["yT"])
    nown = cfg.CTX - cfg.OWN0
    o["y_own"] = yT[:, :nown].T
    o["y_smp"] = yT[:, nown:].T.reshape(16, 8, D)
    o["gla_p"] = np.asarray(r["glap"])
    o["gla_s"] = np.asarray(r["glas"])
    o["rwkv_p"] = _unpad_rwkv(np.asarray(r["rwkvp"]))
    o["rwkv_s"] = _unpad_rwkv(np.asarray(r["rwkvs"]))
    o["shift_p"] = np.asarray(r["shiftTp"])[:, 0]
    o["shift_s"] = np.asarray(r["shiftTs"]).T
    o["conv_p"] = np.asarray(r["convTp"]).T
    o["conv_s"] = np.asarray(r["convTs"]).transpose(1, 2, 0)
    return o


_NC_CACHE = {}


def kernel(**inputs):
    cfg = Cfg(16, 8)
    inp = {k: np.asarray(v) for k, v in inputs.items()}
    xp, xs = inp["x_prompt"], inp["x_sample"]
    pp, psm = inp["p_prompt"][0], inp["p_sample"][0]
    shared = {}
    in_maps = []
    for c in range(8):
        seq, half = c // 2, c % 2
        if half == 1:
            x_ctx, p_ctx = xp[seq], pp[seq]
        else:
            x_ctx = np.concatenate([np.zeros((1024, D), np.float32), xp[seq, :1024]], axis=0)
            p_ctx = np.concatenate([np.zeros((1024, PD), np.float32), pp[seq, :1024]], axis=0)
        sl = slice(16 * c, 16 * c + 16)
        in_maps.append(host_core_inputs(cfg, inp, x_ctx, p_ctx, xs[sl], psm[sl], inp["state_gla"][0, sl],
                                        inp["state_rwkv"][0, sl], inp["state_shift"][0, sl],
                                        inp["state_ffn_conv"][0, sl], flag=float(half), shared=shared))
    if "nc" not in _NC_CACHE:
        _NC_CACHE["nc"] = KB(cfg).build()
    res = run_bass_kernel_spmd(_NC_CACHE["nc"], in_maps, core_ids=list(range(8)))
    y_p = np.zeros((4, 2048, D), np.float32)
    y_s = np.zeros((128, 8, D), np.float32)
    gla_p = np.zeros((1, 4, GH, GDK, GDV), np.float32)
    rwkv_p = np.zeros((1, 4, RH, RN, RN), np.float32)
    shift_p = np.zeros((1, 4, 6592), np.float32)
    conv_p = np.zeros((1, 4, 2, DFF), np.float32)
    gla_s = np.zeros((1, 128, GH, GDK, GDV), np.float32)
    rwkv_s = np.zeros((1, 128, RH, RN, RN), np.float32)
    shift_s = np.zeros((1, 128, 6592), np.float32)
    conv_s = np.zeros((1, 128, 2, DFF), np.float32)
    for c in range(8):
        seq, half = c // 2, c % 2
        o = host_outputs_core(cfg, res.results[c])
        sl = slice(16 * c, 16 * c + 16)
        y_p[seq, half * 1024:(half + 1) * 1024] = o["y_own"]
        y_s[sl] = o["y_smp"]
        gla_s[0, sl] = o["gla_s"]
        rwkv_s[0, sl] = o["rwkv_s"]
        shift_s[0, sl] = o["shift_s"]
        conv_s[0, sl] = o["conv_s"]
        if half == 1:
            gla_p[0, seq] = o["gla_p"]
            rwkv_p[0, seq] = o["rwkv_p"]
            shift_p[0, seq] = o["shift_p"]
            conv_p[0, seq] = o["conv_p"]
    return (y_p, y_s, gla_p, rwkv_p, shift_p, conv_p, gla_s, rwkv_s, shift_s, conv_s)
```

```python
import contextlib
import numpy as np
import concourse.bass as bass
import concourse.mybir as mybir
from concourse.bass_utils import run_bass_kernel_spmd

F32 = mybir.dt.float32
BF16 = mybir.dt.bfloat16
ALU = mybir.AluOpType
AF = mybir.ActivationFunctionType
AX = mybir.AxisListType

D = 4096
DFF = 11008
PD = 256
GH, GDK, GDV = 4, 256, 512
RH, RN = 32, 64
EPS = 1e-6
GN_EPS = 64e-5
IN_TOTAL = 20944
PIECES = dict(q=(0, 1024), k=(1024, 1024), v=(2048, 2048), za=(4096, 16), zg=(4112, 2048),
              r=(6160, 2048), xw=(8208, 96), kr=(8304, 2048), vr=(10352, 2048), xa=(12400, 96),
              xg=(12496, 256), ga=(12752, 4096), gb=(16848, 4096))
ZR0 = 6160

COMPUTE = ("pe", "act", "dve", "pool")
import os as _os
RS_CUT = int(_os.environ["RS_CUT"]) if "RS_CUT" in _os.environ else None
RS_SUB = _os.environ.get("RS_SUB", "")


class Op:
    __slots__ = ("eng", "fn", "deps", "key", "sig", "cnt")

    def __init__(self, eng, fn, deps, key):
        self.eng, self.fn, self.deps, self.key = eng, fn, deps, key
        self.sig = False
        self.cnt = 0


class Prog:
    def __init__(self):
        self.ops = []
        self.lastw = {}
        self.rd_eng = {}
        self.rd_dma = {}

    def add(self, eng, fn, reads=(), writes=(), key=None):
        i = len(self.ops)
        deps = set()
        for r in reads:
            w = self.lastw.get(r)
            if w is not None:
                deps.add(w)
        for r in writes:
            w = self.lastw.get(r)
            if w is not None:
                deps.add(w)
            d = self.rd_eng.get(r)
            if d:
                deps.update(d.values())
            l = self.rd_dma.get(r)
            if l:
                deps.update(l)
        for r in reads:
            if key is not None:
                self.rd_dma.setdefault(r, []).append(i)
            else:
                self.rd_eng.setdefault(r, {})[eng] = i
        for r in writes:
            self.lastw[r] = i
            self.rd_eng[r] = {}
            self.rd_dma[r] = []
        deps.discard(i)
        self.ops.append(Op(eng, fn, deps, key))
        return i

    def pe(self, fn, reads=(), writes=()):
        return self.add("pe", fn, reads, writes)

    def act(self, fn, reads=(), writes=()):
        return self.add("act", fn, reads, writes)

    def dve(self, fn, reads=(), writes=()):
        return self.add("dve", fn, reads, writes)

    def pool(self, fn, reads=(), writes=()):
        return self.add("pool", fn, reads, writes)

    def dma(self, fn, key, reads=(), writes=(), q="sp"):
        base = (list(writes) + list(reads))[0]
        return self.add(q, fn, reads, writes, key=str(key).split("_")[0] + "_" + base)

    def emit(self, nc, stack, kb=None):
        ops = self.ops
        for o in ops:
            for d in o.deps:
                y = ops[d]
                if y.key is None and (y.eng != o.eng or o.key is not None or o.eng != "pe"):
                    y.sig = True
        ecnt = {e: 0 for e in COMPUTE}
        kcnt = {}
        kq = {}
        for o in ops:
            if o.key is not None:
                kcnt[o.key] = kcnt.get(o.key, 0) + 16
                o.cnt = kcnt[o.key]
                assert kq.setdefault(o.key, o.eng) == o.eng, ("dma key used from two queues", o.key)
            elif o.sig:
                ecnt[o.eng] += 1
                o.cnt = ecnt[o.eng]
        if not kb.esem:
            for e_ in COMPUTE:
                kb.esem[e_] = kb.gst.enter_context(nc.semaphore("s_" + e_))
                kb.ebase[e_] = 0
        esem = kb.esem
        eb = dict(kb.ebase)
        for o in ops:
            if o.key is None and o.sig:
                o.cnt += eb[o.eng]
        for e_ in COMPUTE:
            kb.ebase[e_] += ecnt[e_]
        ksem = {}
        kbase = {}
        nidx = {"sw": 0, "hw": 0}
        for k in kcnt:
            kind = "sw" if kq[k] == "pool" else "hw"
            sems, bases = kb.ksems[kind], kb.kbases[kind]
            i_ = nidx[kind]
            nidx[kind] += 1
            while len(sems) <= i_:
                sems.append(kb.gst.enter_context(nc.semaphore("%s%d" % (kind, len(sems)))))
                bases.append(0)
            ksem[k] = sems[i_]
            kbase[k] = bases[i_]
            bases[i_] += kcnt[k]
        for o in ops:
            if o.key is not None:
                o.cnt += kbase[o.key]
        for k in kcnt:
            kcnt[k] += kbase[k]
        engs = {}
        for o in ops:
            engs.setdefault(o.eng, []).append(o)
        block = stack.enter_context(nc.Block())

        def run(eng_name, e):
            waited = {}
            for o in engs.get(eng_name, ()):
                need = {}
                for d in o.deps:
                    y = ops[d]
                    if y.key is not None:
                        s = ("k", y.key)
                    elif y.eng != o.eng or o.key is not None or o.eng != "pe":
                        s = ("e", y.eng)
                    else:
                        continue
                    if y.cnt > need.get(s, 0):
                        need[s] = y.cnt
                for s, v in need.items():
                    if v > waited.get(s, 0):
                        waited[s] = v
                        e.wait_ge(ksem[s[1]] if s[0] == "k" else esem[s[1]], v)
                ins = o.fn(e)
                if o.key is not None:
                    ins.then_inc(ksem[o.key], 16)
                elif o.sig:
                    ins.then_inc(esem[o.eng], 1)
            if eng_name == "sp":
                for k, v in kcnt.items():
                    if v > waited.get(("k", k), 0):
                        e.wait_ge(ksem[k], v)

        @block.sync
        def _(e):
            run("sp", e)

        @block.tensor
        def _(e):
            run("pe", e)

        @block.scalar
        def _(e):
            run("act", e)

        @block.vector
        def _(e):
            run("dve", e)

        @block.gpsimd
        def _(e):
            run("pool", e)


class Stage:
    def __init__(self, kb, name):
        self.kb, self.nc, self.name = kb, kb.nc, name
        self.st = contextlib.ExitStack()
        self.P = Prog()
        self.nps = 0
        self.rr = 0
        self.uid = 0

    def __enter__(self):
        self.st.__enter__()
        return self

    def __exit__(self, *a):
        if a[0] is None:
            self.P.emit(self.nc, self.st, self.kb)
        return self.st.__exit__(*a)

    def sb(self, name, shape, dt):
        return self.st.enter_context(self.nc.sbuf_tensor(self.name + "_" + name, list(shape), dt))

    def psum_banks(self, n=8):
        self.banks = [self.st.enter_context(self.nc.psum_tensor("%s_pb%d" % (self.name, i), [128, 512], F32))
                      for i in range(n)]
        self.nbanks = n

    def bank(self, pool=None):
        if pool is None:
            i = self.rr % self.nbanks
        else:
            i = pool[self.rr % len(pool)]
        self.rr += 1
        return self.banks[i], "pb%d" % i

    def key(self, base):
        return self.name + "_" + base

    def evac_eng(self):
        self.uid += 1
        return "act" if self.uid % 2 else "dve"


def tok_blocks(n, maxb=512):
    out = []
    a = 0
    while a < n:
        b = min(maxb, n - a)
        out.append((a, b))
        a += b
    return out


class Cfg:
    def __init__(self, nct=16, nown=8):
        self.NCT = nct
        self.NOWN = nown
        self.TT = (nct + 1) * 128
        self.CTX = nct * 128
        self.OWN0 = (nct - nown) * 128
        self.HALO0 = self.OWN0 - 128
        self.TOW = self.TT - self.OWN0
        self.TPM = self.TT - self.HALO0
        self.ZW = 1 + self.CTX + 16 * 9


def pcol_layout():
    lay = {}
    o = 0
    for name, n in [("g_pre_mix", 32), ("g_post_mix", 32), ("g_pre_ffn", 32), ("g_post_ffn", 32), ("g_pe", 32),
                    ("b_alpha", 8), ("gla_norm", 4), ("mu_r", 16), ("mu_xw", 1), ("mu_kr", 16), ("mu_vr", 16),
                    ("mu_xa", 1), ("mu_xg", 2), ("w0", 16), ("a0", 16), ("k_k", 16), ("k_a", 16), ("r_k", 16),
                    ("ln_x_w", 16), ("ln_x_b", 16), ("conv_w0", 86), ("conv_w1", 86), ("conv_w2", 86),
                    ("conv_b", 86), ("flag", 1)]:
        lay[name] = (o, n)
        o += n
    return lay, o


PCL, NPC = pcol_layout()


def host_pcols(inp, flag):
    t = np.zeros((128, NPC), np.float32)

    def put(name, vec):
        o, n = PCL[name]
        v = np.zeros(n * 128, np.float32)
        v[:vec.size] = vec.reshape(-1)
        t[:, o:o + n] = v.reshape(n, 128).T

    for nm in ["g_pre_mix", "g_post_mix", "g_pre_ffn", "g_post_ffn", "g_pe", "b_alpha", "gla_norm", "w0", "a0",
               "k_k", "k_a", "r_k", "ln_x_w", "ln_x_b", "conv_b"]:
        put(nm, np.asarray(inp[nm][0]))
    mu = np.asarray(inp["mu_shift"][0])
    put("mu_r", mu[0:2048])
    put("mu_xw", mu[2048:2144])
    put("mu_kr", mu[2144:4192])
    put("mu_vr", mu[4192:6240])
    put("mu_xa", mu[6240:6336])
    put("mu_xg", mu[6336:6592])
    cw = np.asarray(inp["conv_w"][0])
    for j in range(3):
        put("conv_w%d" % j, cw[j])
    o, n = PCL["flag"]
    t[:, o] = flag
    return t


def host_consts():
    c = {}
    p = np.arange(128)
    c["ident"] = np.eye(128, dtype=np.float32)
    c["ones"] = np.ones((128, 128), np.float32)
    c["blk64"] = (p[:, None] // 64 == p[None, :] // 64).astype(np.float32)
    for nm, L in (("p", 128), ("s", 8)):
        same = (p[:, None] // L == p[None, :] // L)
        c["incT_" + nm] = (same & (p[None, :] >= p[:, None])).astype(np.float32)
        c["strT_" + nm] = (same & (p[None, :] > p[:, None])).astype(np.float32)
        c["str_" + nm] = (same & (p[:, None] > p[None, :])).astype(np.float32)
    c["seg16"] = (p[:, None] // 8 == np.arange(16)[None, :]).astype(np.float32)
    names = ["ident", "ones", "blk64", "incT_p", "strT_p", "str_p", "incT_s", "strT_s", "str_s"]
    tab = np.concatenate([c[n] for n in names] + [np.pad(c["seg16"], ((0, 0), (0, 112)))], axis=1)
    return tab.astype(np.float32), names + ["seg16"]


CONST_NAMES = ["ident", "ones", "blk64", "incT_p", "strT_p", "str_p", "incT_s", "strT_s", "str_s", "seg16"]


class KB:
    def __init__(self, cfg, debug=False, stages=None):
        self.cfg = cfg
        self.debug = debug
        self.stages = stages
        self.nc = bass.Bass("TRN2", target_bir_lowering=False)
        self.gst = contextlib.ExitStack()
        self.dr = {}
        self.esem = {}
        self.ebase = {}
        self.ksems = {"sw": [], "hw": []}
        self.kbases = {"sw": [], "hw": []}

    def inp(self, name, shape, dt=F32):
        self.dr[name] = self.nc.dram_tensor(name, list(shape), dt, kind="ExternalInput").ap()
        return self.dr[name]

    def out(self, name, shape, dt=F32):
        self.dr[name] = self.nc.dram_tensor(name, list(shape), dt, kind="ExternalOutput").ap()
        return self.dr[name]

    def scr(self, name, shape, dt=F32):
        kind = "ExternalOutput" if self.debug else "Internal"
        self.dr[name] = self.nc.dram_tensor(name, list(shape), dt, kind=kind).ap()
        return self.dr[name]

    def want(self, s):
        return self.stages is None or s in self.stages

    def declare(self):
        c = self.cfg
        TT = c.TT
        i = self.inp
        i("xT", [D, TT])
        i("pT", [PD, c.TOW])
        i("pcols", [128, NPC])
        i("consts", [128, 128 * 10])
        i("segcol", [128, 16 * 128])
        i("sgla", [16, GH, GDK, GDV])
        i("srwkv", [16, 128, 16, 128])
        i("sshiftT", [6592, 16])
        i("sconvT", [DFF, 16, 2])
        need = {"w_in": "win", "w_branch_a": "bra", "w_branch_b": "brb", "w_out": "wout", "w_up": "up",
                "w_down": "down", "w_pe_gate": "pe"}
        for nm, shp in (("w_in", [D, IN_TOTAL]), ("w_alpha2", [16, 1024]), ("w_branch_a", [2048, D]),
                        ("w_decay2", [96, 2048]), ("w_iclr2", [96, 2048]), ("w_gate2", [256, 2048]),
                        ("w_branch_b", [2048, D]), ("w_out", [D, D]), ("w_up", [D, 2 * DFF]), ("w_down", [DFF, D]),
                        ("w_pe_gate", [D, D]), ("w_pe", [PD, D])):
            if nm in need and not self.want(need[nm]):
                continue
            i(nm, shp)
        o = self.out
        o("yT", [D, c.TOW])
        o("glap", [GH, GDK, GDV])
        o("glas", [16, GH, GDK, GDV])
        o("rwkvp", [128, 16, 128])
        o("rwkvs", [16, 128, 16, 128])
        o("shiftTp", [6592, 1])
        o("shiftTs", [6592, 16])
        o("convTp", [DFF, 2])
        o("convTs", [DFF, 16, 2])
        s = self.scr
        s("hT", [D, TT], BF16)
        for nm in ("q", "k", "zg", "ga", "gb"):
            s(nm + "T", [PIECES[nm][1], TT])
        s("zaT", [16, TT])
        for nm in ("r", "xw", "kr", "vr", "xa", "xg"):
            s(nm + "T", [PIECES[nm][1], c.ZW])
        s("vtm", [TT, 2048], BF16)
        s("oaT", [2048, TT], BF16)
        s("obT", [2048, TT], BF16)
        s("arbk", [128, 16, 4, TT], BF16)
        s("vTb", [128, 16, TT], BF16)
        s("gcT", [128, 16, c.NCT + 16])
        s("bonT", [2048, TT])
        s("gT", [2048, TT])
        s("mixT", [D, TT], BF16)
        s("yoT", [D, TT])
        s("x1T", [D, TT])
        s("hfT", [D, TT], BF16)
        s("ugT", [DFF, TT])
        s("uvT", [DFF, TT])
        s("actT", [DFF, TT], BF16)
        s("fT", [D, TT])
        s("x2T", [D, TT])
        s("hpT", [D, TT], BF16)
        s("ppT", [D, TT])
        s("pTb", [PD, TT], BF16)

    def consts(self):
        nc = self.nc
        g = self.gst
        self.pc = g.enter_context(nc.sbuf_tensor("pc", [128, NPC], F32))
        self.pcd = g.enter_context(nc.sbuf_tensor("pcd", [128, 64], F32))
        self.c32 = g.enter_context(nc.sbuf_tensor("c32", [128, 10, 128], F32))
        self.cbf = g.enter_context(nc.sbuf_tensor("cbf", [128, 10, 128], BF16))
        self.c32x = g.enter_context(nc.sbuf_tensor("c32x", [128, 16, 128], BF16))
        with Stage(self, "c") as S:
            P = S.P
            P.dma(lambda e: e.dma_start(out=self.pc[:], in_=self.dr["pcols"]), S.key("a"), writes=["pc"])
            P.dma(lambda e: e.dma_start(out=self.c32[:], in_=self.dr["consts"].rearrange("p (n c) -> p n c", c=128)),
                  S.key("b"), writes=["c32"])
            P.dve(lambda e: e.tensor_copy(out=self.cbf[:], in_=self.c32[:]), reads=["c32"], writes=["cbf"])
            P.dma(lambda e: e.dma_start(out=self.c32x[:], in_=self.dr["segcol"].rearrange("p (n c) -> p n c", c=128)),
                  S.key("d"), writes=["c32x"], q="pool")
            pcd, pc = self.pcd, self.pc
            o, n = PCL["b_alpha"]
            P.dve(lambda e: e.tensor_scalar(out=pcd[:, 0:8], in0=pc[:, o:o + 8], scalar1=-1.0, scalar2=None,
                                            op0=ALU.mult), reads=["pc"], writes=["pcd"])
            P.pool(lambda e: e.memset(pcd[:, 60:61], -0.5), writes=["pcd"])
            om, _ = PCL["mu_r"]
            P.dve(lambda e: e.tensor_scalar(out=pcd[:, 8:60], in0=pc[:, om:om + 52], scalar1=-1.0, scalar2=1.0,
                                            op0=ALU.mult, op1=ALU.add), reads=["pc"], writes=["pcd"])

    def cst(self, name, bf=False):
        i = CONST_NAMES.index(name)
        return (self.cbf if bf else self.c32)[:, i, :]

    def pcol(self, name, j=0, n=1):
        o, _ = PCL[name]
        return self.pc[:, o + j:o + j + n]

    def fm_linear(self, S, W, KC, kparts, chunks, aT, aname, nt, evac, wblk=512, wq="pool", krange=None):
        P, nc = S.P, self.nc
        tb = tok_blocks(nt)
        astep = max(1, KC // 4) if kparts == 128 else KC
        k0, k1 = (0, KC) if krange is None else krange
        nk = k1 - k0
        blocks = []
        cur = []
        for (c0, cw) in chunks:
            if cur and (cur[-1][0] + cur[-1][1] != c0 or (c0 + cw - cur[0][0]) > wblk):
                blocks.append(cur)
                cur = []
            cur.append((c0, cw))
        if cur:
            blocks.append(cur)
        ck = ("wb", nk, wblk)
        if not hasattr(S, "cache"):
            S.cache = {}
        if ck not in S.cache:
            S.cache[ck] = [S.sb("w%d_%d_%d" % (i, nk, wblk), [128, nk, wblk], BF16) for i in range(2)]
            S.wbi = getattr(S, "wbi", 0)
        wb = S.cache[ck]
        wstep = max(1, nk // 4) if kparts == 128 else nk

        def load(bi):
            blk = blocks[bi]
            b0 = blk[0][0]
            bw = blk[-1][0] + blk[-1][1] - b0
            t = wb[bi % 2]
            kp = kparts
            if kp == 128:
                src = W[k0 * 128:k1 * 128, b0:b0 + bw].rearrange("(k p) c -> p k c", p=128)
                step = max(1, nk // 4)
                for ka in range(0, nk, step):
                    kb_ = min(nk, ka + step)
                    P.dma(lambda e, t=t, src=src, ka=ka, kb_=kb_, bw=bw: e.dma_start(
                        out=t[:, ka:kb_, 0:bw], in_=src[:, ka:kb_, :]),
                        S.key("w%d" % (bi % 2)), writes=["w%d.%d" % (bi % 2, ka // step)], q=wq)
            else:
                P.dma(lambda e, t=t, b0=b0, bw=bw, kp=kp: e.dma_start(out=t[0:kp, 0, 0:bw], in_=W[0:kp, b0:b0 + bw]),
                      S.key("w%d" % (bi % 2)), writes=["w%d.0" % (bi % 2)], q=wq)

        load(0)
        ci = 0
        for bi, blk in enumerate(blocks):
            if bi + 1 < len(blocks):
                load(bi + 1)
            t = wb[bi % 2]
            b0 = blk[0][0]
            for (c0, cw) in blk:
                pss = []
                for (a0, n) in tb:
                    pt, pr = S.bank()
                    pss.append((pt, pr, a0, n))
                for k in range(nk):
                    for (pt, pr, a0, n) in pss:
                        P.pe(lambda e, pt=pt, t=t, k=k, c0=c0, cw=cw, b0=b0, a0=a0, n=n: e.matmul(
                            pt[0:cw, 0:n], lhsT=t[0:kparts, k, c0 - b0:c0 - b0 + cw],
                            rhs=aT[0:kparts, k0 + k, a0:a0 + n], start=(k == 0), stop=(k == nk - 1)),
                            reads=["w%d.%d" % (bi % 2, k // wstep), aname + ".%d" % ((k0 + k) // astep)], writes=[pr])
                evac(ci, c0, cw, [(pt[0:cw, 0:n], pr, a0, n) for (pt, pr, a0, n) in pss])
                ci += 1

    def load_aT(self, S, name, src, KC, t0, nt, kparts=128):
        t = S.sb(name, [128, KC, nt], BF16)
        if kparts == 128:
            v = src[:, t0:t0 + nt].rearrange("(k p) t -> p k t", p=128)
            step = max(1, KC // 4)
            for ka in range(0, KC, step):
                kb_ = min(KC, ka + step)
                S.P.dma(lambda e, ka=ka, kb_=kb_: e.dma_start(out=t[:, ka:kb_, :], in_=v[:, ka:kb_, :]),
                        S.key(name), writes=[name + ".%d" % (ka // step)])
        else:
            S.P.dma(lambda e: e.dma_start(out=t[0:kparts, 0, :], in_=src[0:kparts, t0:t0 + nt]),
                    S.key(name), writes=[name + ".0"])
        return t

    def copy_ps(self, S, out, ps, pr, wres, func=None, scale=1.0):
        eng = S.evac_eng()
        if func is not None or eng == "act":
            S.P.act(lambda e: e.activation(out=out, in_=ps, func=(func or AF.Copy), scale=scale),
                    reads=[pr], writes=[wres])
        else:
            S.P.dve(lambda e: e.tensor_copy(out=out, in_=ps), reads=[pr], writes=[wres])

    def stt_mm(self, P, on_pool, out, in0, scal, in1, reads, writes, tmp=None):
        if not on_pool:
            P.dve(lambda e: e.scalar_tensor_tensor(out=out, in0=in0, scalar=scal, in1=in1, op0=ALU.mult,
                                                   op1=ALU.mult), reads=reads, writes=writes)
        else:
            P.pool(lambda e: e.tensor_scalar(out=tmp, in0=in0, scalar1=scal, scalar2=None, op0=ALU.mult),
                   reads=reads, writes=["_pooltmp"])
            P.pool(lambda e: e.tensor_tensor(out=out, in0=tmp, in1=in1, op=ALU.mult),
                   reads=list(reads) + ["_pooltmp"], writes=writes)

    def norm_stage(self, name, src, t0, nt, gain, out_bf, res=None, out_x=None, gain2=None, NB=256):
        with Stage(self, name) as S:
            P = S.P
            S.psum_banks(2)
            s32 = S.sb("s32", [128, 32, NB], F32)
            sq = S.sb("sq", [128, 32, NB], F32)
            hb = S.sb("hb", [128, 32, NB], BF16)
            rs = S.sb("rs", [128, NB], F32)
            tmpc = S.sb("tmpc", [128, NB], F32)
            r32 = S.sb("r32", [128, 32, NB], F32) if res is not None else None
            ones = self.cst("ones")

            def stats(x, xname, nb):
                P.act(lambda e, x=x, nb=nb: e.activation(out=sq[:, :, 0:nb], in_=x, func=AF.Square),
                      reads=[xname], writes=["sq"])
                pt, pr = S.bank()
                for c in range(32):
                    P.pe(lambda e, c=c, pt=pt, nb=nb: e.matmul(pt[:, 0:nb], lhsT=ones, rhs=sq[:, c, 0:nb],
                                                                 start=(c == 0), stop=(c == 31)),
                         reads=["sq"], writes=[pr])
                P.act(lambda e, pt=pt, nb=nb: e.activation(out=rs[:, 0:nb], in_=pt[:, 0:nb], func=AF.Sqrt,
                                                           scale=1.0 / D, bias=EPS), reads=[pr], writes=["rs"])
                P.dve(lambda e, nb=nb: e.reciprocal(out=rs[:, 0:nb], in_=rs[:, 0:nb]), reads=["rs"], writes=["rs"])

            for (a0, nb) in tok_blocks(nt, NB):
                sv = src[:, t0 + a0:t0 + a0 + nb].rearrange("(c p) t -> p c t", p=128)
                for h in range(2):
                    P.dma(lambda e, h=h, sv=sv, nb=nb: e.dma_start(out=s32[:, 16 * h:16 * h + 16, 0:nb],
                                                                   in_=sv[:, 16 * h:16 * h + 16, :]),
                          S.key("ls"), writes=["s32"])
                if res is not None:
                    rv = res[:, t0 + a0:t0 + a0 + nb].rearrange("(c p) t -> p c t", p=128)
                    for h in range(2):
                        P.dma(lambda e, h=h, rv=rv, nb=nb: e.dma_start(out=r32[:, 16 * h:16 * h + 16, 0:nb],
                                                                       in_=rv[:, 16 * h:16 * h + 16, :]),
                              S.key("lr"), writes=["r32"])
                stats(s32[:, :, 0:nb], "s32", nb)
                if res is None:
                    for c in range(32):
                        self.stt_mm(P, c % 3 == 0, hb[:, c, 0:nb], s32[:, c, 0:nb], self.pcol(gain, c), rs[:, 0:nb],
                                    ["s32", "rs"], ["hb"], tmp=tmpc[:, 0:nb])
                else:
                    for c in range(32):
                        self.stt_mm(P, c % 3 == 0, s32[:, c, 0:nb], s32[:, c, 0:nb], self.pcol(gain, c), rs[:, 0:nb],
                                    ["s32", "rs"], ["s32"], tmp=tmpc[:, 0:nb])
                    P.dve(lambda e, nb=nb: e.tensor_tensor(out=r32[:, :, 0:nb], in0=r32[:, :, 0:nb],
                                                           in1=s32[:, :, 0:nb], op=ALU.add),
                          reads=["r32", "s32"], writes=["r32"])
                    ov = out_x[:, t0 + a0:t0 + a0 + nb].rearrange("(c p) t -> p c t", p=128)
                    P.dma(lambda e, ov=ov, nb=nb: e.dma_start(out=ov, in_=r32[:, :, 0:nb]), S.key("sx"),
                          reads=["r32"])
                    stats(r32[:, :, 0:nb], "r32", nb)
                    for c in range(32):
                        self.stt_mm(P, c % 3 == 0, hb[:, c, 0:nb], r32[:, c, 0:nb], self.pcol(gain2, c), rs[:, 0:nb],
                                    ["r32", "rs"], ["hb"], tmp=tmpc[:, 0:nb])
                hv = out_bf[:, t0 + a0:t0 + a0 + nb].rearrange("(c p) t -> p c t", p=128)
                P.dma(lambda e, hv=hv, nb=nb: e.dma_start(out=hv, in_=hb[:, :, 0:nb]), S.key("sh"), reads=["hb"])

    def win_stage(self, name, t0, nt, pieces):
        c = self.cfg
        with Stage(self, name) as S:
            P = S.P
            S.psum_banks(8)
            hT = self.load_aT(S, "hT", self.dr["hT"], 32, t0, nt)
            stg = [S.sb("stg%d" % i, [128, nt], F32) for i in range(3)]
            lastc = [S.sb("lastc%d" % i, [128, 16], F32) for i in range(3)]
            zt = S.sb("zt", [128, 16], F32)
            P.pool(lambda e: e.memset(zt[:], 0.0), writes=["zt"])
            cnt = [0]
            Win = self.dr["w_in"]
            n_ctx = max(0, min(c.CTX, t0 + nt) - t0)
            has_smp = (t0 + nt) > c.CTX
            for pn in pieces:
                if pn == "v":
                    continue
                p0, pw = PIECES[pn]
                chunks = [(p0 + a, min(128, pw - a)) for a in range(0, pw, 128)]
                dst = self.dr[pn + "T"]
                padded = pn in ("r", "xw", "kr", "vr", "xa", "xg")
                func = AF.Sigmoid if pn in ("ga", "gb") else None

                def evac(ci, c0, cw, pss, p0=p0, dst=dst, padded=padded, func=func, pn=pn):
                    si = cnt[0] % 3
                    cnt[0] += 1
                    sg = stg[si]
                    sn = "stg%d" % si
                    for (ps, pr, a0, n) in pss:
                        self.copy_ps(S, sg[0:cw, a0:a0 + n], ps, pr, sn, func=func)
                    r0 = c0 - p0
                    if not padded:
                        P.dma(lambda e: e.dma_start(out=dst[r0:r0 + cw, t0:t0 + nt], in_=sg[0:cw, :]),
                              S.key("st"), reads=[sn])
                    else:
                        if t0 == 0:
                            P.dma(lambda e: e.dma_start(out=dst[r0:r0 + cw, 0:1], in_=zt[0:cw, 0:1],
                                                        allow_slow_non_contiguous=True), S.key("st"),
                                  reads=["zt"])
                        if n_ctx > 0:
                            P.dma(lambda e: e.dma_start(out=dst[r0:r0 + cw, 1 + t0:1 + t0 + n_ctx],
                                                        in_=sg[0:cw, 0:n_ctx]), S.key("st"), reads=[sn])
                            if t0 + n_ctx == c.CTX:
                                zr = c0 - ZR0
                                P.dma(lambda e: e.dma_start(out=self.dr["shiftTp"][zr:zr + cw, 0:1],
                                                            in_=sg[0:cw, n_ctx - 1:n_ctx]), S.key("st"), reads=[sn])
                        if has_smp:
                            dv = dst[r0:r0 + cw, 1 + c.CTX:1 + c.CTX + 144].rearrange("p (s j) -> p s j", j=9)
                            sv = sg[0:cw, nt - 128:nt].rearrange("p (s j) -> p s j", j=8)
                            P.dma(lambda e: e.dma_start(out=dv[:, :, 1:9], in_=sv), S.key("st"), reads=[sn])
                            P.dma(lambda e: e.dma_start(out=dv[:, :, 0], in_=zt[0:cw, :],
                                                        allow_slow_non_contiguous=True), S.key("st"), reads=["zt"])
                            zr = c0 - ZR0
                            lt = lastc[si]
                            P.pool(lambda e: e.tensor_copy(out=lt[0:cw, :], in_=sv[:, :, 7]), reads=[sn],
                                   writes=["lastc%d" % si])
                            P.dma(lambda e: e.dma_start(out=self.dr["shiftTs"][zr:zr + cw, :], in_=lt[0:cw, :]),
                                  S.key("st"), reads=["lastc%d" % si])

                self.fm_linear(S, Win, 32, 128, chunks, hT, "hT", nt, evac)
            if "v" in pieces:
                p0, pw = PIECES["v"]
                wv = S.cache[("wb", 32, 512)]
                vst = [S.sb("vst%d" % i, [128, 512], BF16) for i in range(2)]
                for bi in range(4):
                    t = wv[bi % 2]
                    src = Win[:, p0 + bi * 512:p0 + (bi + 1) * 512].rearrange("(k p) c -> p k c", p=128)
                    for ka in range(0, 32, 8):
                        P.dma(lambda e, t=t, src=src, ka=ka: e.dma_start(out=t[:, ka:ka + 8, :],
                                                                       in_=src[:, ka:ka + 8, :]),
                              S.key("w%d" % (bi % 2)), writes=["w%d.%d" % (bi % 2, ka // 8)], q="pool")
                    for ti in range(nt // 128):
                        pt, pr = S.bank()
                        for k in range(32):
                            P.pe(lambda e, pt=pt, t=t, k=k, ti=ti: e.matmul(
                                pt[:, :], lhsT=hT[:, k, ti * 128:(ti + 1) * 128], rhs=t[:, k, :],
                                start=(k == 0), stop=(k == 31)), reads=["w%d.%d" % (bi % 2, k // 8), "hT.%d" % (k // 8)],
                                writes=[pr])
                        vi = (bi * (nt // 128) + ti) % 2
                        self.copy_ps(S, vst[vi][:, :], pt[:, :], pr, "vst%d" % vi)
                        P.dma(lambda e, vi=vi, ti=ti, bi=bi: e.dma_start(
                            out=self.dr["vtm"][t0 + ti * 128:t0 + (ti + 1) * 128, bi * 512:(bi + 1) * 512],
                            in_=vst[vi][:, :]), S.key("sv"), reads=["vst%d" % vi])

    def build(self):
        c = self.cfg
        self.declare()
        with self.gst:
            self.consts()
            if self.want("norm1"):
                self.norm_stage("n1", self.dr["xT"], 0, c.TT, "g_pre_mix", self.dr["hT"])
            if self.want("win"):
                allp = list(PIECES.keys())
                g0 = c.HALO0
                if g0 > 0:
                    self.win_stage("wa", 0, g0, [p_ for p_ in allp if p_ not in ("zg", "ga", "gb")])
                self.win_stage("wb", g0, c.TT - g0, allp)
            self.build_rest()
        return self.nc

    def gla_stage(self):
        c = self.cfg
        dr = self.dr
        with Stage(self, "gla") as S:
            P = S.P
            S.psum_banks(7)
            qk = S.sb("qk", [128, 16, 128], F32)
            zab = S.sb("zab", [16, 128], BF16)
            wal = S.sb("wal", [16, 1024], BF16)
            vb = S.sb("vb", [128, 2048], BF16)
            sp = S.sb("sp", [128, 8, 128], F32)
            cs = S.sb("cs", [128, 8, 128], F32)
            cs2 = S.sb("cs2", [128, 8, 128], F32)
            ex = S.sb("ex", [128, 8, 128], F32)
            onesf = S.sb("onesf", [128, 128], F32)
            zer = S.sb("zer", [128, 512], BF16)
            P.pool(lambda e: e.memset(zer[:], 0.0), writes=["zer"])
            qd = S.sb("qd", [128, 8, 128], BF16)
            kd = S.sb("kd", [128, 8, 128], BF16)
            kdp = S.sb("kdp", [128, 8, 128], BF16)
            kdt = S.sb("kdt", [128, 8, 128], BF16)
            att = S.sb("att", [128, 4, 128], BF16)
            S32 = S.sb("S32", [128, 8, 512], F32)
            Sbf = S.sb("Sbf", [128, 8, 512], BF16)
            sdec = S.sb("sdec", [128, 8, 16], F32)
            cend = S.sb("cend", [128, 8, 16], F32)
            o32 = S.sb("o32", [128, 16, 128], F32)
            sq = S.sb("sq", [128, 16, 128], F32)
            rstd = S.sb("rstd", [128, 4, 128], F32)
            zg = S.sb("zg", [128, 16, 128], F32)
            sg = S.sb("sg", [128, 16, 128], F32)
            oab = S.sb("oab", [128, 16, 128], BF16)
            Ss32 = [S.sb("Ss32_%d" % i, [128, 8, 512], F32) for i in range(2)]
            Ssbf = [S.sb("Ssbf_%d" % i, [128, 8, 512], BF16) for i in range(2)]
            kdm = S.sb("kdm", [128, 8, 16, 128], BF16)
            ptr = S.st.enter_context(self.nc.psum_tensor("gla_ptr", [128, 8, 128], BF16))
            rn = lambda t: {id(sp): "sp", id(cs): "cs", id(cs2): "cs2"}[id(t)]
            identb = self.cst("ident", True)
            ones32 = self.cst("ones")
            P.dma(lambda e: e.dma_start(out=wal[:], in_=dr["w_alpha2"]), S.key("c"), writes=["wal"], q="pool")
            P.pool(lambda e: e.memset(onesf[:], 1.0), writes=["onesf"])
            P.pool(lambda e: e.memset(S32[:], 0.0), writes=["S32"])
            P.pool(lambda e: e.memset(Sbf[:], 0.0), writes=["Sbf"])
            for tile in range(c.NCT + 1):
                smp = tile == c.NCT
                t0 = tile * 128
                need_out = t0 >= c.HALO0
                P.dma(lambda e, t0=t0: e.dma_start(out=qk[:, 0:8, :], in_=dr["qT"][:, t0:t0 + 128].rearrange(
                    "(c p) t -> p c t", p=128)), S.key("l1"), writes=["qk"])
                P.dma(lambda e, t0=t0: e.dma_start(out=qk[:, 8:16, :], in_=dr["kT"][:, t0:t0 + 128].rearrange(
                    "(c p) t -> p c t", p=128)), S.key("l1"), writes=["qk"])
                P.dma(lambda e, t0=t0: e.dma_start(out=zab[:], in_=dr["zaT"][:, t0:t0 + 128]), S.key("l2"),
                      writes=["zab"], q="pool")
                P.dma(lambda e, t0=t0: e.dma_start(out=vb[:], in_=dr["vtm"][t0:t0 + 128, :]), S.key("l3"),
                      writes=["vb"])
                for half in range(2):
                    pt, pr = S.bank([0, 1])
                    for cc in range(4):
                        ch = half * 4 + cc
                        P.pe(lambda e, pt=pt, cc=cc, ch=ch: e.matmul(pt[:, cc * 128:(cc + 1) * 128],
                                                                   lhsT=wal[:, ch * 128:(ch + 1) * 128], rhs=zab[:, :],
                                                                   start=True, stop=True),
                             reads=["wal", "zab"], writes=[pr])
                    for cc in range(4):
                        ch = half * 4 + cc
                        P.act(lambda e, pt=pt, cc=cc, ch=ch: e.activation(
                            out=sp[:, ch, :], in_=pt[:, cc * 128:(cc + 1) * 128], func=AF.Exp, scale=-1.0,
                            bias=self.pcd[:, ch:ch + 1]), reads=[pr, "pcd"], writes=["sp"])
                P.act(lambda e: e.activation(out=sp[:], in_=sp[:], func=AF.Ln, bias=1.0), reads=["sp"], writes=["sp"])
                if not smp:
                    for ch in range(8):
                        P.dve(lambda e, ch=ch: e.tensor_tensor_scan(out=cs[:, ch, :], data0=onesf[:, :],
                                                                    data1=sp[:, ch, :], initial=0.0,
                                                                    op0=ALU.mult, op1=ALU.add),
                              reads=["sp", "onesf"], writes=["cs"])
                    csf = cs
                    P.dve(lambda e: e.tensor_copy(out=cend[:, :, 0:1], in_=cs[:, :, 127:128]), reads=["cs"],
                          writes=["cend"])
                    nseg = 1
                else:
                    v = lambda t: t[:].rearrange("p c (s j) -> p (c s) j", j=8)
                    src, dst = sp, cs
                    for d in (1, 2, 4):
                        P.dve(lambda e, src=src, dst=dst, d=d: e.tensor_tensor(
                            out=v(dst)[:, :, d:8], in0=v(src)[:, :, d:8], in1=v(src)[:, :, 0:8 - d], op=ALU.add),
                            reads=[rn(src)], writes=[rn(dst)])
                        P.pool(lambda e, src=src, dst=dst, d=d: e.tensor_copy(out=v(dst)[:, :, 0:d],
                                                                              in_=v(src)[:, :, 0:d]),
                               reads=[rn(src)], writes=[rn(dst)])
                        src, dst = dst, (cs2 if dst is cs else cs)
                    csf = src
                    P.dve(lambda e, csf=csf: e.tensor_copy(
                        out=cend[:, :, :], in_=csf[:].rearrange("p c (s j) -> p c s j", j=8)[:, :, :, 7]),
                        reads=[rn(csf)], writes=["cend"])
                    nseg = 16
                cn = rn(csf)
                P.act(lambda e, csf=csf: e.activation(out=ex[:], in_=csf[:], func=AF.Exp, scale=-1.0 / 16),
                      reads=[cn], writes=["ex"])
                P.dve(lambda e: e.scalar_tensor_tensor(out=qd[:], in0=qk[:, 0:8, :], scalar=float(GDK) ** -0.5,
                                                       in1=ex[:], op0=ALU.mult, op1=ALU.mult),
                      reads=["qk", "ex"], writes=["qd"])
                P.act(lambda e, csf=csf: e.activation(out=ex[:], in_=csf[:], func=AF.Exp, scale=1.0 / 16),
                      reads=[cn], writes=["ex"])
                P.dve(lambda e: e.tensor_tensor(out=kd[:], in0=qk[:, 8:16, :], in1=ex[:], op=ALU.mult),
                      reads=["qk", "ex"], writes=["kd"])
                L = 128 // nseg
                P.dve(lambda e, csf=csf, nseg=nseg, L=L: e.tensor_tensor(
                    out=ex[:].rearrange("p c (s j) -> p c s j", j=L),
                    in0=csf[:].rearrange("p c (s j) -> p c s j", j=L),
                    in1=cend[:, :, 0:nseg].unsqueeze(3).to_broadcast([128, 8, nseg, L]), op=ALU.subtract),
                    reads=[cn, "cend"], writes=["ex"])
                P.act(lambda e: e.activation(out=ex[:], in_=ex[:], func=AF.Exp, scale=1.0 / 16), reads=["ex"],
                      writes=["ex"])
                P.dve(lambda e: e.tensor_tensor(out=kdp[:], in0=qk[:, 8:16, :], in1=ex[:], op=ALU.mult),
                      reads=["qk", "ex"], writes=["kdp"])
                P.act(lambda e, nseg=nseg: e.activation(out=sdec[:, :, 0:nseg], in_=cend[:, :, 0:nseg], func=AF.Exp,
                                                        scale=-1.0 / 16), reads=["cend"], writes=["sdec"])
                for ch in range(8):
                    P.pe(lambda e, ch=ch: e.transpose(ptr[:, ch, :], kdp[:, ch, :], identb), reads=["kdp"],
                         writes=["ptr"])
                P.act(lambda e: e.activation(out=kdt[:], in_=ptr[:], func=AF.Copy), reads=["ptr"], writes=["kdt"])
                pa, par = S.bank([2])
                for h in range(4):
                    for cc in range(2):
                        P.pe(lambda e, h=h, cc=cc, pa=pa: e.matmul(pa[:, h * 128:(h + 1) * 128],
                                                                 lhsT=kd[:, 2 * h + cc, :], rhs=qd[:, 2 * h + cc, :],
                                                                 start=(cc == 0), stop=(cc == 1)),
                             reads=["kd", "qd"], writes=[par])
                mk = self.cst("incT_s" if smp else "incT_p")
                P.dve(lambda e, pa=pa, mk=mk: e.tensor_tensor(
                    out=att[:], in0=pa[:].rearrange("p (h t) -> p h t", t=128),
                    in1=mk.unsqueeze(1).to_broadcast([128, 4, 128]), op=ALU.mult),
                    reads=[par], writes=["att"])
                pos = [S.bank([3 + h_]) for h_ in range(4)]
                for h in range(4):
                    po, por = pos[h]
                    P.pe(lambda e, po=po: e.matmul(po[:, :], lhsT=zer[:, 0:128], rhs=zer[:, :], start=True,
                                                   stop=False), reads=["zer"], writes=[por])
                    for j in range(4):
                        P.pe(lambda e, h=h, j=j, po=po: e.matmul(
                            po[:, j * 128:(j + 1) * 128], lhsT=vb[:, h * 512 + j * 128:h * 512 + (j + 1) * 128],
                            rhs=att[:, h, :], start=False, stop=False),
                            reads=["vb", "att"], writes=[por])
                        if not smp:
                            for cc in range(2):
                                P.pe(lambda e, h=h, j=j, cc=cc, po=po: e.matmul(
                                    po[:, j * 128:(j + 1) * 128], lhsT=Sbf[:, 2 * h + cc, j * 128:(j + 1) * 128],
                                    rhs=qd[:, 2 * h + cc, :], start=False, stop=(cc == 1 and j == 3)),
                                    reads=["Sbf", "qd"], writes=[por])
                if not smp:
                    for h in range(4):
                        for cc in range(2):
                            i = 2 * h + cc
                            pu, pur = S.bank([0, 1])
                            P.pe(lambda e, i=i, h=h, pu=pu: e.matmul(pu[:, :], lhsT=kdt[:, i, :],
                                                                     rhs=vb[:, h * 512:(h + 1) * 512], start=True,
                                                                     stop=True), reads=["kdt", "vb"], writes=[pur])
                            P.dve(lambda e, i=i, pu=pu: e.scalar_tensor_tensor(
                                out=S32[:, i, :], in0=S32[:, i, :], scalar=sdec[:, i, 0:1], in1=pu[:, :],
                                op0=ALU.mult, op1=ALU.add), reads=[pur, "S32", "sdec"], writes=["S32"])
                    P.act(lambda e: e.activation(out=Sbf[:], in_=S32[:], func=AF.Copy), reads=["S32"], writes=["Sbf"])
                    if tile == c.NCT - 1:
                        P.dma(lambda e: e.dma_start(out=dr["glap"].rearrange("h (c p) v -> p h c v", p=128),
                                                    in_=S32[:].rearrange("p (h c) v -> p h c v", c=2)),
                              S.key("so"), reads=["S32"])
                else:
                    seg16 = self.cst("seg16", True)
                    for i8 in range(8):
                        P.dve(lambda e, i8=i8: e.tensor_tensor(
                            out=kdm[:, i8, :, :], in0=kdt[:, i8, :].unsqueeze(1).to_broadcast([128, 16, 128]),
                            in1=seg16[:, 0:16].unsqueeze(2).to_broadcast([128, 16, 128]), op=ALU.mult),
                            reads=["kdt"], writes=["kdm"])
                    for si in range(16):
                        b = si % 2
                        s32, sbf = Ss32[b], Ssbf[b]
                        n32, nbf = "Ss32_%d" % b, "Ssbf_%d" % b
                        P.dma(lambda e, si=si, s32=s32: e.dma_start(
                            out=s32[:].rearrange("p (h c) v -> p h c v", c=2),
                            in_=dr["sgla"][si].rearrange("h (c p) v -> p h c v", p=128)), S.key("ls%d" % b),
                            writes=[n32])
                        P.act(lambda e, s32=s32, sbf=sbf: e.activation(out=sbf[:], in_=s32[:], func=AF.Copy),
                              reads=[n32], writes=[nbf])
                        for h in range(4):
                            po, por = pos[h]
                            for j in range(4):
                                for cc in range(2):
                                    P.pe(lambda e, h=h, j=j, cc=cc, po=po, sbf=sbf, si=si: e.matmul(
                                        po[:, j * 128 + si * 8:j * 128 + si * 8 + 8],
                                        lhsT=sbf[:, 2 * h + cc, j * 128:(j + 1) * 128],
                                        rhs=qd[:, 2 * h + cc, si * 8:si * 8 + 8], start=False,
                                        stop=(cc == 1 and j == 3 and si == 15)),
                                        reads=[nbf, "qd"], writes=[por])
                        for h in range(4):
                            for cc in range(2):
                                i = 2 * h + cc
                                pu, pur = S.bank([0, 1])
                                P.pe(lambda e, i=i, h=h, pu=pu, si=si: e.matmul(
                                    pu[:, :], lhsT=kdm[:, i, si, :], rhs=vb[:, h * 512:(h + 1) * 512], start=True,
                                    stop=True), reads=["kdm", "vb"], writes=[pur])
                                P.dve(lambda e, i=i, pu=pu, s32=s32, si=si: e.scalar_tensor_tensor(
                                    out=s32[:, i, :], in0=s32[:, i, :], scalar=sdec[:, i, si:si + 1], in1=pu[:, :],
                                    op0=ALU.mult, op1=ALU.add), reads=[pur, n32, "sdec"], writes=[n32])
                        P.dma(lambda e, si=si, s32=s32: e.dma_start(
                            out=dr["glas"][si].rearrange("h (c p) v -> p h c v", p=128),
                            in_=s32[:].rearrange("p (h c) v -> p h c v", c=2)), S.key("ss%d" % b), reads=[n32])
                if not need_out:
                    continue
                for h in range(4):
                    po, por = pos[h]
                    P.act(lambda e, h=h, po=po: e.activation(out=o32[:, 4 * h:4 * h + 4, :],
                                                             in_=po[:].rearrange("p (j t) -> p j t", t=128),
                                                             func=AF.Copy), reads=[por], writes=["o32"])
                    P.act(lambda e, h=h, po=po: e.activation(out=sq[:, 4 * h:4 * h + 4, :],
                                                             in_=po[:].rearrange("p (j t) -> p j t", t=128),
                                                             func=AF.Square), reads=[por], writes=["sq"])
                pst, pstr = S.bank([2])
                for h in range(4):
                    for j in range(4):
                        P.pe(lambda e, h=h, j=j, pst=pst: e.matmul(pst[:, h * 128:(h + 1) * 128], lhsT=ones32,
                                                                   rhs=sq[:, 4 * h + j, :], start=(j == 0),
                                                                   stop=(j == 3)), reads=["sq"], writes=[pstr])
                P.act(lambda e, pst=pst: e.activation(out=rstd[:], in_=pst[:].rearrange("p (h t) -> p h t", t=128),
                                                      func=AF.Sqrt, scale=1.0 / GDV, bias=EPS), reads=[pstr],
                      writes=["rstd"])
                P.dve(lambda e: e.reciprocal(out=rstd[:], in_=rstd[:]), reads=["rstd"], writes=["rstd"])
                P.dma(lambda e, t0=t0: e.dma_start(out=zg[:], in_=dr["zgT"][:, t0:t0 + 128].rearrange(
                    "(c p) t -> p c t", p=128)), S.key("l4"), writes=["zg"])
                P.act(lambda e: e.activation(out=sg[:], in_=zg[:], func=AF.Sigmoid), reads=["zg"], writes=["sg"])
                P.pool(lambda e: e.tensor_tensor(out=sg[:], in0=sg[:], in1=zg[:], op=ALU.mult), reads=["sg", "zg"],
                       writes=["sg"])
                P.pool(lambda e: e.tensor_tensor(
                    out=sg[:].rearrange("p (h j) t -> p h j t", j=4),
                    in0=sg[:].rearrange("p (h j) t -> p h j t", j=4),
                    in1=rstd[:].unsqueeze(2).to_broadcast([128, 4, 4, 128]), op=ALU.mult),
                    reads=["sg", "rstd"], writes=["sg"])
                for j in range(4):
                    P.dve(lambda e, j=j: e.scalar_tensor_tensor(
                        out=oab[:].rearrange("p (h j) t -> p h j t", j=4)[:, :, j, :],
                        in0=o32[:].rearrange("p (h j) t -> p h j t", j=4)[:, :, j, :],
                        scalar=self.pcol("gla_norm", j),
                        in1=sg[:].rearrange("p (h j) t -> p h j t", j=4)[:, :, j, :],
                        op0=ALU.mult, op1=ALU.mult), reads=["o32", "sg"], writes=["oab"])
                P.dma(lambda e, t0=t0: e.dma_start(out=dr["oaT"][:, t0:t0 + 128].rearrange("(c p) t -> p c t", p=128),
                                                   in_=oab[:]), S.key("so2"), reads=["oab"])

    def rfront_stage(self):
        c = self.cfg
        dr = self.dr
        with Stage(self, "rf") as S:
            P = S.P
            S.psum_banks(6)
            NB = 512
            W1 = NB + 16
            wdec = S.sb("wdec", [96, 2048], BF16)
            wicl = S.sb("wicl", [96, 2048], BF16)
            wgat = S.sb("wgat", [128, 2, 2048], BF16)
            P.dma(lambda e: e.dma_start(out=wdec[:], in_=dr["w_decay2"]), S.key("c1"), writes=["wdec"], q="pool")
            P.dma(lambda e: e.dma_start(out=wicl[:], in_=dr["w_iclr2"]), S.key("c1"), writes=["wicl"], q="pool")
            P.dma(lambda e: e.dma_start(out=wgat[:], in_=dr["w_gate2"].rearrange("(k p) c -> p k c", p=128)),
                  S.key("c1"), writes=["wgat"], q="pool")
            raw = {n: S.sb("raw_" + n, [128, W1], F32) for n in ("r", "k", "v")}
            rawx = {n: S.sb("rawx_" + n, [128, W1], F32) for n in ("xw", "xa", "xg0", "xg1")}
            sst = S.sb("sst", [128, 16], F32)
            tmp = S.sb("tmp", [128, NB], F32)
            xsh = S.sb("xsh", [128, NB], F32)
            thb = S.sb("thb", [96, NB], BF16)
            xab = S.sb("xab", [96, NB], BF16)
            sgx = S.sb("sgx", [128, 2, NB], BF16)
            rs_ = S.sb("rs_", [128, NB], F32)
            ks_ = S.sb("ks_", [128, NB], F32)
            vs_ = S.sb("vs_", [128, NB], F32)
            pl = S.sb("pl", [128, NB], F32)
            a_ = S.sb("a_", [128, NB], F32)
            g_ = S.sb("g_", [128, NB], F32)
            kk = S.sb("kk", [128, NB], F32)
            t1 = S.sb("t1", [128, NB], F32)
            t2 = S.sb("t2", [128, NB], F32)
            k2 = S.sb("k2", [128, NB], F32)
            cp = S.sb("cp", [128, NB], F32)
            ex1 = S.sb("ex1", [128, NB], F32)
            ex2 = S.sb("ex2", [128, NB], F32)
            bon = S.sb("bon", [128, NB], F32)
            arbk = S.sb("arbk", [128, 4, NB], BF16)
            vbf = S.sb("vbf", [128, NB], BF16)
            gc = S.sb("gc", [128, 16], F32)
            rmask = S.sb("rmask", [128, 2, NB], F32)
            P.pool(lambda e: e.memset(rmask[:], 1.0), writes=["rmask"])
            P.pool(lambda e: e.memset(rmask[:, 0, :].rearrange("p (s j) -> p s j", j=128)[:, :, 0:1], 0.0),
                   writes=["rmask"])
            P.pool(lambda e: e.memset(rmask[:, 1, :].rearrange("p (s j) -> p s j", j=8)[:, :, 0:1], 0.0),
                   writes=["rmask"])
            blk64 = self.cst("blk64")
            pcd = self.pcd
            set1 = dict(raw={n_: S.sb("rawB_" + n_, [128, W1], F32) for n_ in ("r", "k", "v")},
                        t=[S.sb("sstB", [128, 16], F32)] + [S.sb(nm_ + "B", [128, NB], F32) for nm_ in
                           ("tmp", "rs_", "ks_", "vs_", "pl", "a_", "g_", "kk", "t1", "t2", "k2", "cp", "ex1", "ex2",
                            "bon")] + [S.sb("arbkB", [128, 4, NB], BF16), S.sb("vbfB", [128, NB], BF16),
                                       S.sb("gcB", [128, 16], F32)])
            set0 = dict(raw=raw, t=[sst, tmp, rs_, ks_, vs_, pl, a_, g_, kk, t1, t2, k2, cp, ex1, ex2, bon, arbk, vbf,
                                    gc])
            BUFS = [set0, set1]

            def blocks():
                for a0 in range(0, c.CTX, NB):
                    yield (a0, min(NB, c.CTX - a0), False)
                yield (c.CTX, 128, True)

            def load_raw(PP, sst, t, tn, src, rows, r0, a0, n, smp, state_rows):
                if not smp:
                    PP.dma(lambda e: e.dma_start(out=t[0:rows, 0:n + 1], in_=src[r0:r0 + rows, a0:a0 + n + 1]),
                          S.key("lr"), writes=[tn])
                else:
                    PP.dma(lambda e: e.dma_start(out=t[0:rows, 0:144],
                                                in_=src[r0:r0 + rows, 1 + c.CTX:1 + c.CTX + 144]),
                          S.key("lr"), writes=[tn])
                    PP.dma(lambda e: e.dma_start(out=sst[0:rows, :], in_=dr["sshiftT"][state_rows:state_rows + rows, :]),
                          S.key("lr2"), writes=["sst"])
                    PP.pool(lambda e: e.tensor_copy(
                        out=t[0:rows, 0:144].rearrange("p (s j) -> p s j", j=9)[:, :, 0], in_=sst[0:rows, :]),
                        reads=["sst", tn], writes=[tn])

            def shift(PP, tmp, out, t, tn, rows, n, smp, mu, omm, wname):
                if not smp:
                    cur, prev = t[0:rows, 1:n + 1], t[0:rows, 0:n]
                    o, tm = out, tmp[0:rows, 0:n]
                else:
                    v9 = t[0:rows, 0:144].rearrange("p (s j) -> p s j", j=9)
                    cur, prev = v9[:, :, 1:9], v9[:, :, 0:8]
                    o = out.rearrange("p (s j) -> p s j", j=8)
                    tm = tmp[0:rows, 0:128].rearrange("p (s j) -> p s j", j=8)
                PP.pool(lambda e: e.tensor_scalar(out=tm, in0=prev, scalar1=mu, scalar2=None, op0=ALU.mult),
                       reads=[tn], writes=["tmp"])
                PP.dve(lambda e: e.scalar_tensor_tensor(out=o, in0=cur, scalar=omm, in1=tm, op0=ALU.mult, op1=ALU.add),
                      reads=[tn, "tmp"], writes=[wname])

            DBL = ['raw_r', 'raw_k', 'raw_v', 'sst', 'tmp', 'rs_', 'ks_', 'vs_', 'pl', 'a_', 'g_', 'kk', 't1', 't2', 'k2', 'cp', 'ex1', 'ex2', 'bon', 'arbk', 'vbf', 'gc']

            class SP:
                def __init__(self, P0, sfx):
                    self.P0, self.sfx = P0, sfx

                def ren(self, names):
                    return [(x + self.sfx) if x in DBL else x for x in names]

                def dve(self, fn, reads=(), writes=()):
                    return self.P0.dve(fn, self.ren(reads), self.ren(writes))

                def act(self, fn, reads=(), writes=()):
                    return self.P0.act(fn, self.ren(reads), self.ren(writes))

                def pool(self, fn, reads=(), writes=()):
                    return self.P0.pool(fn, self.ren(reads), self.ren(writes))

                def pe(self, fn, reads=(), writes=()):
                    return self.P0.pe(fn, self.ren(reads), self.ren(writes))

                def dma(self, fn, key, reads=(), writes=(), q="sp"):
                    return self.P0.dma(fn, key, self.ren(reads), self.ren(writes), q=q)

            P0 = P
            PS0 = SP(P0, "_0")

            def pair(p, a0, n, smp, mi):
                P = SP(P0, "_%d" % (p % 2))
                X = BUFS[p % 2]
                raw = X["raw"]
                sst, tmp, rs_, ks_, vs_, pl, a_, g_, kk, t1, t2, k2, cp, ex1, ex2, bon, arbk, vbf, gc = X["t"]
                pc0 = p * 128
                load_raw(P, sst, raw["r"], "raw_r", dr["rT"], 128, pc0, a0, n, smp, ZR["r"] + pc0)
                load_raw(P, sst, raw["k"], "raw_k", dr["krT"], 128, pc0, a0, n, smp, ZR["kr"] + pc0)
                load_raw(P, sst, raw["v"], "raw_v", dr["vrT"], 128, pc0, a0, n, smp, ZR["vr"] + pc0)
                shift(P, tmp, rs_[:, 0:n], raw["r"], "raw_r", 128, n, smp, self.pcol("mu_r", p), pcd[:, 8 + p:9 + p], "rs_")
                shift(P, tmp, ks_[:, 0:n], raw["k"], "raw_k", 128, n, smp, self.pcol("mu_kr", p), pcd[:, 25 + p:26 + p], "ks_")
                shift(P, tmp, vs_[:, 0:n], raw["v"], "raw_v", 128, n, smp, self.pcol("mu_vr", p), pcd[:, 41 + p:42 + p], "vs_")
                pw, pwr = S.bank()
                P.pe(lambda e, pw=pw, p=p, n=n: e.matmul(pw[:, 0:n], lhsT=wdec[:, p * 128:(p + 1) * 128],
                                                         rhs=thb[:, 0:n], start=True, stop=True),
                     reads=["wdec", "thb"], writes=[pwr])
                P.dve(lambda e, pw=pw, p=p, n=n: e.tensor_scalar(out=t1[:, 0:n], in0=pw[:, 0:n],
                                                                 scalar1=self.pcol("w0", p), scalar2=-1.0,
                                                                 op0=ALU.add, op1=ALU.mult),
                      reads=[pwr], writes=["t1"])
                P.act(lambda e, n=n: e.activation(out=t1[:, 0:n], in_=t1[:, 0:n], func=AF.Exp), reads=["t1"],
                      writes=["t1"])
                P.act(lambda e, n=n: e.activation(out=t1[:, 0:n], in_=t1[:, 0:n], func=AF.Ln, bias=1.0),
                      reads=["t1"], writes=["t1"])
                P.act(lambda e, n=n: e.activation(out=pl[:, 0:n], in_=t1[:, 0:n], func=AF.Exp, scale=-1.0,
                                                  bias=self.kneg05()), reads=["t1"], writes=["pl"])
                pa, par = S.bank()
                P.pe(lambda e, pa=pa, p=p, n=n: e.matmul(pa[:, 0:n], lhsT=wicl[:, p * 128:(p + 1) * 128],
                                                         rhs=xab[:, 0:n], start=True, stop=True),
                     reads=["wicl", "xab"], writes=[par])
                P.act(lambda e, pa=pa, p=p, n=n: e.activation(out=a_[:, 0:n], in_=pa[:, 0:n], func=AF.Sigmoid,
                                                              bias=self.pcol("a0", p)), reads=[par], writes=["a_"])
                pg, pgr = S.bank()
                for kc in range(2):
                    P.pe(lambda e, pg=pg, p=p, n=n, kc=kc: e.matmul(
                        pg[:, 0:n], lhsT=wgat[:, kc, p * 128:(p + 1) * 128], rhs=sgx[:, kc, 0:n],
                        start=(kc == 0), stop=(kc == 1)), reads=["wgat", "sgx"], writes=[pgr])
                P.act(lambda e, pg=pg, n=n: e.activation(out=g_[:, 0:n], in_=pg[:, 0:n], func=AF.Copy),
                      reads=[pgr], writes=["g_"])
                P.dma(lambda e, pc0=pc0, a0=a0, n=n: e.dma_start(out=dr["gT"][pc0:pc0 + 128, a0:a0 + n],
                                                               in_=g_[:, 0:n]), S.key("sg"), reads=["g_"])
                P.pool(lambda e, p=p, n=n: e.tensor_scalar(out=kk[:, 0:n], in0=ks_[:, 0:n],
                                                           scalar1=self.pcol("k_k", p), scalar2=None,
                                                           op0=ALU.mult), reads=["ks_"], writes=["kk"])
                P.act(lambda e, n=n: e.activation(out=t2[:, 0:n], in_=kk[:, 0:n], func=AF.Square), reads=["kk"],
                      writes=["t2"])
                pq, pqr = S.bank()
                P.pe(lambda e, pq=pq, n=n: e.matmul(pq[:, 0:n], lhsT=blk64, rhs=t2[:, 0:n], start=True,
                                                     stop=True), reads=["t2"], writes=[pqr])
                P.dve(lambda e, pq=pq, n=n: e.tensor_scalar(out=t2[:, 0:n], in0=pq[:, 0:n], scalar1=1e-24,
                                                            scalar2=None, op0=ALU.max), reads=[pqr],
                      writes=["t2"])
                P.act(lambda e, n=n: e.activation(out=t2[:, 0:n], in_=t2[:, 0:n], func=AF.Sqrt), reads=["t2"],
                      writes=["t2"])
                P.dve(lambda e, n=n: e.reciprocal(out=t2[:, 0:n], in_=t2[:, 0:n]), reads=["t2"], writes=["t2"])
                P.dve(lambda e, n=n: e.tensor_tensor(out=kk[:, 0:n], in0=kk[:, 0:n], in1=t2[:, 0:n],
                                                     op=ALU.mult), reads=["kk", "t2"], writes=["kk"])
                P.dve(lambda e, p=p, n=n: e.tensor_scalar(out=t2[:, 0:n], in0=a_[:, 0:n],
                                                          scalar1=self.pcol("k_a", p), scalar2=self.pcol("k_a", p),
                                                          op0=ALU.mult, op1=ALU.subtract), reads=["a_"],
                      writes=["t2"])
                P.dve(lambda e, n=n: e.scalar_tensor_tensor(out=k2[:, 0:n], in0=t2[:, 0:n], scalar=1.0,
                                                            in1=ks_[:, 0:n], op0=ALU.add, op1=ALU.mult),
                      reads=["t2", "ks_"], writes=["k2"])
                P.dve(lambda e, p=p, n=n: e.scalar_tensor_tensor(out=t2[:, 0:n], in0=rs_[:, 0:n],
                                                                 scalar=self.pcol("r_k", p), in1=k2[:, 0:n],
                                                                 op0=ALU.mult, op1=ALU.mult),
                      reads=["rs_", "k2"], writes=["t2"])
                pb_, pbr = S.bank()
                P.pe(lambda e, pb_=pb_, n=n: e.matmul(pb_[:, 0:n], lhsT=blk64, rhs=t2[:, 0:n], start=True,
                                                       stop=True), reads=["t2"], writes=[pbr])
                P.dve(lambda e, pb_=pb_, n=n: e.tensor_tensor(out=bon[:, 0:n], in0=pb_[:, 0:n], in1=vs_[:, 0:n],
                                                              op=ALU.mult), reads=[pbr, "vs_"], writes=["bon"])
                P.dma(lambda e, pc0=pc0, a0=a0, n=n: e.dma_start(out=dr["bonT"][pc0:pc0 + 128, a0:a0 + n],
                                                               in_=bon[:, 0:n]), S.key("sb"), reads=["bon"])
                P.dve(lambda e, n=n, mi=mi: e.tensor_tensor_scan(out=cp[:, 0:n], data0=rmask[:, mi, 0:n],
                                                                 data1=pl[:, 0:n], initial=0.0, op0=ALU.mult,
                                                                 op1=ALU.add), reads=["pl", "rmask"], writes=["cp"])
                P.act(lambda e, n=n: e.activation(out=ex1[:, 0:n], in_=cp[:, 0:n], func=AF.Exp, scale=-1.0),
                      reads=["cp"], writes=["ex1"])
                P.dve(lambda e, n=n: e.tensor_tensor(out=arbk[:, 1, 0:n], in0=rs_[:, 0:n], in1=ex1[:, 0:n],
                                                     op=ALU.mult), reads=["rs_", "ex1"], writes=["arbk"])
                L = 8 if smp else 128
                ns = n // L
                P.pool(lambda e, n=n, L=L, ns=ns: e.tensor_copy(
                    out=gc[:, 0:ns], in_=ex1[:, 0:n].rearrange("p (s j) -> p s j", j=L)[:, :, L - 1]),
                    reads=["ex1"], writes=["gc"])
                gc0 = (c.NCT if smp else a0 // 128)
                P.dma(lambda e, p=p, gc0=gc0, ns=ns: e.dma_start(out=dr["gcT"][:, p, gc0:gc0 + ns],
                                                               in_=gc[:, 0:ns]), S.key("sc"), reads=["gc"])
                P.pool(lambda e, n=n: e.tensor_tensor(out=t2[:, 0:n], in0=pl[:, 0:n], in1=cp[:, 0:n],
                                                      op=ALU.subtract), reads=["pl", "cp", ], writes=["t2"])
                P.act(lambda e, n=n: e.activation(out=ex2[:, 0:n], in_=t2[:, 0:n], func=AF.Exp), reads=["t2"],
                      writes=["ex2"])
                P.dve(lambda e, n=n: e.scalar_tensor_tensor(out=arbk[:, 0, 0:n], in0=kk[:, 0:n], scalar=-1.0,
                                                            in1=ex2[:, 0:n], op0=ALU.mult, op1=ALU.mult),
                      reads=["kk", "ex2"], writes=["arbk"])
                P.act(lambda e, n=n: e.activation(out=ex2[:, 0:n], in_=cp[:, 0:n], func=AF.Exp), reads=["cp"],
                      writes=["ex2"])
                P.pool(lambda e, n=n: e.tensor_tensor(out=t2[:, 0:n], in0=kk[:, 0:n], in1=a_[:, 0:n],
                                                      op=ALU.mult), reads=["kk", "a_"], writes=["t2"])
                P.dve(lambda e, n=n: e.tensor_tensor(out=arbk[:, 2, 0:n], in0=t2[:, 0:n], in1=ex2[:, 0:n],
                                                     op=ALU.mult), reads=["t2", "ex2"], writes=["arbk"])
                P.dve(lambda e, n=n: e.tensor_tensor(out=arbk[:, 3, 0:n], in0=k2[:, 0:n], in1=ex2[:, 0:n],
                                                     op=ALU.mult), reads=["k2", "ex2"], writes=["arbk"])
                P.act(lambda e, n=n: e.activation(out=vbf[:, 0:n], in_=vs_[:, 0:n], func=AF.Copy), reads=["vs_"],
                      writes=["vbf"])
                P.dma(lambda e, p=p, a0=a0, n=n: e.dma_start(out=dr["arbk"][:, p, :, a0:a0 + n],
                                                           in_=arbk[:, :, 0:n]), S.key("sa"), reads=["arbk"])
                P.dma(lambda e, p=p, a0=a0, n=n: e.dma_start(out=dr["vTb"][:, p, a0:a0 + n], in_=vbf[:, 0:n]),
                      S.key("sv"), reads=["vbf"])

            ZR = {"r": 0, "xw": 2048, "kr": 2144, "vr": 4192, "xa": 6240, "xg": 6336}
            chunk_i = 0
            for (a0, n, smp) in blocks():
                mi = 1 if smp else 0
                load_raw(PS0, sst, rawx["xw"], "rawx_xw", dr["xwT"], 96, 0, a0, n, smp, ZR["xw"])
                shift(PS0, tmp, xsh[0:96, 0:n], rawx["xw"], "rawx_xw", 96, n, smp, self.pcol("mu_xw")[0:96], pcd[0:96, 24:25], "xsh")
                P.act(lambda e, n=n: e.activation(out=thb[:, 0:n], in_=xsh[0:96, 0:n], func=AF.Tanh), reads=["xsh"],
                      writes=["thb"])
                load_raw(PS0, sst, rawx["xa"], "rawx_xa", dr["xaT"], 96, 0, a0, n, smp, ZR["xa"])
                shift(PS0, tmp, xsh[0:96, 0:n], rawx["xa"], "rawx_xa", 96, n, smp, self.pcol("mu_xa")[0:96], pcd[0:96, 57:58], "xsh")
                P.act(lambda e, n=n: e.activation(out=xab[:, 0:n], in_=xsh[0:96, 0:n], func=AF.Copy), reads=["xsh"],
                      writes=["xab"])
                for j in range(2):
                    nm = "xg%d" % j
                    load_raw(PS0, sst, rawx[nm], "rawx_" + nm, dr["xgT"], 128, j * 128, a0, n, smp, ZR["xg"] + j * 128)
                    shift(PS0, tmp, xsh[:, 0:n], rawx[nm], "rawx_" + nm, 128, n, smp, self.pcol("mu_xg", j), pcd[:, 58 + j:59 + j],
                          "xsh")
                    P.act(lambda e, n=n, j=j: e.activation(out=sgx[:, j, 0:n], in_=xsh[:, 0:n], func=AF.Sigmoid),
                          reads=["xsh"], writes=["sgx"])
                for p in range(16):
                    pair(p, a0, n, smp, mi)

    def rscan_stage(self, smp):
        c = self.cfg
        dr = self.dr
        NP = 4 if smp else 8
        NH = 2 * NP
        nseg = 16 if smp else 1
        L = 128 // nseg
        sfx = "s" if smp else "p"
        with Stage(self, "rs" + sfx) as S:
            P = S.P
            S.psum_banks(7)
            ptr = S.st.enter_context(self.nc.psum_tensor("rs%s_ptr" % sfx, [128, 8, 128], BF16))
            ARBK = S.sb("ARBK", [128, NP, 4, 128], BF16)
            AR = ARBK[:, :, 0:2, :]
            BK = ARBK[:, :, 2:4, :]
            vT = S.sb("vT", [128, NP, 128], BF16)
            tmB = S.sb("tmB", [128, NP, 128], BF16)
            tmK = S.sb("tmK", [128, NP, 128], BF16)
            tmV = S.sb("tmV", [128, NP, 128], BF16)
            Vpad = S.sb("Vpad", [128, 2, NP, 128], BF16)
            Upad = S.sb("Upad", [128, 2, NP, 128], BF16)
            mats = S.sb("mats", [128, NH, 4, 128], BF16)
            Nb = [S.sb("N%d" % i, [128, NH, 128], BF16) for i in range(2)]
            NTb = [S.sb("NT%d" % i, [128, NH, 128], BF16) for i in range(2)]
            MTb = [S.sb("MT%d" % i, [128, NH, 128], BF16) for i in range(2)]
            maskM = S.sb("maskM", [128, 4, 128], F32)
            T32 = S.sb("T32", [128, nseg, NP, 128], F32)
            Tbf = S.sb("Tbf", [128, nseg, NP, 128], BF16)
            LV = S.sb("LV", [128, NP, 128], F32)
            Xb = S.sb("Xb", [128, NP, 128], BF16)
            Ub = S.sb("Ub", [128, NP, 128], BF16)
            gcs = S.sb("gcs", [128, NP, c.NCT + 16], F32)
            tmpT = S.sb("tmpT", [128, 4, 128], F32)
            o32 = S.sb("o32", [128, NP, 128], F32)
            cen = S.sb("cen", [128, NP, 128], F32)
            sqv = S.sb("sqv", [128, NP, 128], F32)
            rsd = S.sb("rsd", [128, NP, 128], F32)
            bon = S.sb("bon", [128, NP, 128], F32)
            gg = S.sb("gg", [128, NP, 128], F32)
            obb = S.sb("obb", [128, NP, 128], BF16)
            if smp:
                Am = S.sb("Am", [128, NP, 16, 128], BF16)
                Bm = S.sb("Bm", [128, NP, 16, 128], BF16)
                Km = S.sb("Km", [128, NP, 16, 128], BF16)
            identb = self.cst("ident", True)
            blk64 = self.cst("blk64")
            blk64b = self.cst("blk64", True)
            strT = self.cst("strT_" + sfx)
            incT = self.cst("incT_" + sfx)
            strN = self.cst("str_" + sfx)
            for i_, m_ in enumerate((strT, incT, strT, incT)):
                P.pool(lambda e, i_=i_, m_=m_: e.tensor_copy(out=maskM[:, i_, :], in_=m_), writes=["maskM"])
            zer = S.sb("zer", [128, 512], BF16)
            P.pool(lambda e: e.memset(zer[:], 0.0), writes=["zer"])
            P.pool(lambda e: e.memset(Vpad[:], 0.0), writes=["Vpad"])
            P.pool(lambda e: e.memset(Upad[:], 0.0), writes=["Upad"])
            RP = [0, 1, 2]
            OB = [3, 4, 5, 6]
            levels = []
            pw_ = 2
            while pw_ < L:
                levels.append(pw_)
                pw_ *= 2
            work = ([(t, p0) for p0 in range(0, 16, NP) for t in range(c.NCT)] if not smp
                    else [(c.NCT, p0) for p0 in range(0, 16, NP)])
            for (tile, p0) in work:
                t0 = tile * 128
                need_out = t0 >= c.HALO0
                if not smp and tile == 0:
                    P.pool(lambda e: e.memset(T32[:], 0.0), writes=["T32"])
                    P.pool(lambda e: e.memset(Tbf[:], 0.0), writes=["Tbf"])
                P.dma(lambda e, t0=t0, p0=p0: e.dma_start(
                    out=ARBK[:].rearrange("p q f t -> p (q f) t"),
                    in_=dr["arbk"][:, p0:p0 + NP, :, t0:t0 + 128].rearrange("p q f t -> p (q f) t")),
                    S.key("l1"), writes=["AR", "BK"])
                P.dma(lambda e, t0=t0, p0=p0: e.dma_start(out=vT[:], in_=dr["vTb"][:, p0:p0 + NP, t0:t0 + 128]),
                      S.key("l2"), writes=["vT"])
                gq0 = c.NCT if smp else tile
                if smp or tile == 0:
                    P.dma(lambda e, p0=p0: e.dma_start(out=gcs[:], in_=dr["gcT"][:, p0:p0 + NP, :]),
                          S.key("l3"), writes=["gcs"])
                if smp:
                    for sg_ in range(16):
                        P.dma(lambda e, sg_=sg_, p0=p0: e.dma_start(out=T32[:, sg_, :, :],
                                                                   in_=dr["srwkv"][sg_][:, p0:p0 + NP, :]),
                              S.key("l4"), writes=["T32"])
                    P.act(lambda e: e.activation(out=Tbf[:], in_=T32[:], func=AF.Copy), reads=["T32"], writes=["Tbf"])
                for (src, fi, dst, dn) in ((BK, 0, tmB, "tmB"), (BK, 1, tmK, "tmK"), (vT, None, tmV, "tmV")):
                    if RS_CUT is not None and RS_CUT <= -2:
                        continue
                    for g0 in range(0, NP, 8):
                        gn = min(8, NP - g0)
                        for q in range(gn):
                            in_ = src[:, g0 + q, fi, :] if fi is not None else src[:, g0 + q, :]
                            P.pe(lambda e, q=q, in_=in_: e.transpose(ptr[:, q, :], in_, identb),
                                 reads=["BK" if fi is not None else "vT"], writes=["ptr"])
                        P.act(lambda e, dst=dst, g0=g0, gn=gn: e.activation(out=dst[:, g0:g0 + gn, :],
                                                                            in_=ptr[:, 0:gn, :], func=AF.Copy),
                              reads=["ptr"], writes=[dn])
                        if dst is tmV:
                            P.pool(lambda e, g0=g0, gn=gn: e.tensor_copy(out=Vpad[:, 0, g0:g0 + gn, 0:64],
                                                                        in_=tmV[:, g0:g0 + gn, 0:64]), reads=["tmV"],
                                   writes=["Vpad"])
                            P.pool(lambda e, g0=g0, gn=gn: e.tensor_copy(out=Vpad[:, 1, g0:g0 + gn, 64:128],
                                                                        in_=tmV[:, g0:g0 + gn, 64:128]), reads=["tmV"],
                                   writes=["Vpad"])
                if RS_CUT is not None and RS_CUT < 1:
                    continue
                for hh in range(NH):
                    if "m" in RS_SUB:
                        break
                    q, h = hh // 2, hh % 2
                    pm, pmr = S.bank(RP)
                    rr_ = slice(64 * h, 64 * h + 64)
                    P.pe(lambda e, pm=pm, q=q, rr_=rr_: e.matmul(
                        pm[:, 0:256], lhsT=BK[rr_, q, 0, :], rhs=AR[rr_, q, :, :].rearrange("p f t -> p (f t)"),
                        start=True, stop=True), reads=["BK", "AR"], writes=[pmr])
                    P.pe(lambda e, pm=pm, q=q, rr_=rr_: e.matmul(
                        pm[:, 256:512], lhsT=BK[rr_, q, 1, :], rhs=AR[rr_, q, :, :].rearrange("p f t -> p (f t)"),
                        start=True, stop=True), reads=["BK", "AR"], writes=[pmr])
                    if "e" in RS_SUB:
                        continue
                    P.dve(lambda e, pm=pm, hh=hh: e.tensor_tensor(
                        out=mats[:, hh, :, :], in0=pm[:].rearrange("p (f t) -> p f t", t=128), in1=maskM[:],
                        op=ALU.mult), reads=[pmr, "maskM"], writes=["mats"])
                for q0 in range(0, NP, 4):
                    for h in range(2):
                        pl_, plr = S.bank(RP)
                        rr_ = slice(64 * h, 64 * h + 64)
                        for j in range(4):
                            q = q0 + j
                            P.pe(lambda e, pl_=pl_, j=j, q=q, rr_=rr_: e.matmul(
                                pl_[:, j * 128:(j + 1) * 128], lhsT=AR[rr_, q, 0, :], rhs=BK[rr_, q, 0, :], start=True,
                                stop=True), reads=["AR", "BK"], writes=[plr])
                        P.dve(lambda e, pl_=pl_, q0=q0, h=h: e.tensor_tensor(
                            out=Nb[0][:].rearrange("p (q h) t -> p q h t", h=2)[:, q0:q0 + 4, h, :],
                            in0=pl_[:].rearrange("p (j t) -> p j t", t=128),
                            in1=strN.unsqueeze(1).to_broadcast([128, 4, 128]), op=ALU.mult), reads=[plr],
                            writes=["N0"])
                if RS_CUT is not None and RS_CUT < 2:
                    continue
                P.pool(lambda e: e.tensor_copy(out=NTb[0][:], in_=mats[:, :, 0, :]), reads=["mats"], writes=["NT0"])
                P.pool(lambda e: e.tensor_tensor(out=MTb[0][:], in0=mats[:, :, 0, :],
                                                 in1=identb.unsqueeze(1).to_broadcast([128, NH, 128]), op=ALU.add),
                       reads=["mats"], writes=["MT0"])
                cur = 0
                for li, lv in enumerate(levels):
                    last = li == len(levels) - 1
                    nxt = 1 - cur
                    for g0 in range(0, NH, 4):
                        pn, pnr = S.bank(RP)
                        for j in range(4):
                            P.pe(lambda e, pn=pn, j=j, g0=g0, cur=cur: e.matmul(
                                pn[:, j * 128:(j + 1) * 128], lhsT=NTb[cur][:, g0 + j, :], rhs=Nb[cur][:, g0 + j, :],
                                start=True, stop=True), reads=["N%d" % cur, "NT%d" % cur], writes=[pnr])
                        P.act(lambda e, pn=pn, g0=g0, nxt=nxt: e.activation(
                            out=Nb[nxt][:, g0:g0 + 4, :], in_=pn[:].rearrange("p (j t) -> p j t", t=128),
                            func=AF.Copy), reads=[pnr], writes=["N%d" % nxt])
                        if not last:
                            pt_, ptr_ = S.bank(RP)
                            for j in range(4):
                                P.pe(lambda e, pt_=pt_, j=j, g0=g0, cur=cur: e.matmul(
                                    pt_[:, j * 128:(j + 1) * 128], lhsT=Nb[cur][:, g0 + j, :],
                                    rhs=NTb[cur][:, g0 + j, :], start=True, stop=True),
                                    reads=["N%d" % cur, "NT%d" % cur], writes=[ptr_])
                            P.act(lambda e, pt_=pt_, g0=g0, nxt=nxt: e.activation(
                                out=NTb[nxt][:, g0:g0 + 4, :], in_=pt_[:].rearrange("p (j t) -> p j t", t=128),
                                func=AF.Copy), reads=[ptr_], writes=["NT%d" % nxt])
                        pp, ppr = S.bank(RP)
                        for j in range(4):
                            P.pe(lambda e, pp=pp, j=j, g0=g0, cur=cur, nxt=nxt: e.matmul(
                                pp[:, j * 128:(j + 1) * 128], lhsT=Nb[nxt][:, g0 + j, :], rhs=MTb[cur][:, g0 + j, :],
                                start=True, stop=False), reads=["N%d" % nxt, "MT%d" % cur], writes=[ppr])
                            P.pe(lambda e, pp=pp, j=j, g0=g0, cur=cur: e.matmul(
                                pp[:, j * 128:(j + 1) * 128], lhsT=identb, rhs=MTb[cur][:, g0 + j, :],
                                start=False, stop=True), reads=["MT%d" % cur], writes=[ppr])
                        P.act(lambda e, pp=pp, g0=g0, nxt=nxt: e.activation(
                            out=MTb[nxt][:, g0:g0 + 4, :], in_=pp[:].rearrange("p (j t) -> p j t", t=128),
                            func=AF.Copy), reads=[ppr], writes=["MT%d" % nxt])
                    cur = nxt
                MT = MTb[cur]
                mtn = "MT%d" % cur
                if RS_CUT is not None and RS_CUT < 3:
                    continue
                if smp:
                    segcol = self.c32x
                    seg16 = self.cst("seg16", True)
                    for q in range(NP):
                        P.dve(lambda e, q=q: e.tensor_tensor(
                            out=Am[:, q, :, :], in0=AR[:, q, 0, :].unsqueeze(1).to_broadcast([128, 16, 128]),
                            in1=segcol[:], op=ALU.mult), reads=["AR"], writes=["Am"])
                        P.pool(lambda e, q=q: e.tensor_tensor(
                            out=Bm[:, q, :, :], in0=tmB[:, q, :].unsqueeze(1).to_broadcast([128, 16, 128]),
                            in1=seg16[:, 0:16].unsqueeze(2).to_broadcast([128, 16, 128]), op=ALU.mult),
                            reads=["tmB"], writes=["Bm"])
                        P.pool(lambda e, q=q: e.tensor_tensor(
                            out=Km[:, q, :, :], in0=tmK[:, q, :].unsqueeze(1).to_broadcast([128, 16, 128]),
                            in1=seg16[:, 0:16].unsqueeze(2).to_broadcast([128, 16, 128]), op=ALU.mult),
                            reads=["tmK"], writes=["Km"])
                if RS_CUT is not None and RS_CUT < 4:
                    continue
                if need_out:
                    for q in range(NP):
                        po, por = S.banks[OB[q // 4]], "pb%d" % OB[q // 4]
                        osl = slice((q % 4) * 128, (q % 4) * 128 + 128)
                        if q % 4 == 0:
                            P.pe(lambda e, po=po: e.matmul(po[:, :], lhsT=zer[:, 0:128], rhs=zer[:, :], start=True,
                                                           stop=False), reads=["zer"], writes=[por])
                        for h in range(2):
                            P.pe(lambda e, po=po, osl=osl, q=q, h=h: e.matmul(
                                po[:, osl], lhsT=Vpad[:, h, q, :], rhs=mats[:, 2 * q + h, 3, :], start=False,
                                stop=False), reads=["Vpad", "mats"], writes=[por])
                        for sg_ in range(nseg):
                            P.pe(lambda e, po=po, q=q, sg_=sg_: e.matmul(
                                po[:, (q % 4) * 128 + sg_ * L:(q % 4) * 128 + (sg_ + 1) * L], lhsT=Tbf[:, sg_, q, :],
                                rhs=AR[:, q, 1, sg_ * L:(sg_ + 1) * L], start=False, stop=False),
                                reads=["Tbf", "AR"], writes=[por])
                if RS_CUT is not None and RS_CUT < 5:
                    continue
                for g0 in range(0, NP, 4):
                    pv, pvr = S.bank(RP)
                    for j in range(4):
                        q = g0 + j
                        for h in range(2):
                            P.pe(lambda e, pv=pv, j=j, q=q, h=h: e.matmul(
                                pv[:, j * 128 + 64 * h:j * 128 + 64 * h + 64], lhsT=mats[:, 2 * q + h, 2, :],
                                rhs=tmV[:, q, 64 * h:64 * h + 64], start=True, stop=True),
                                reads=["mats", "tmV"], writes=[pvr])
                    P.act(lambda e, pv=pv, g0=g0: e.activation(out=LV[:, g0:g0 + 4, :],
                                                               in_=pv[:].rearrange("p (j t) -> p j t", t=128),
                                                               func=AF.Copy), reads=[pvr], writes=["LV"])
                if RS_CUT is not None and RS_CUT < 6:
                    continue
                for g0 in range(0, NP, 4):
                    px, pxr = S.bank(RP)
                    for j in range(4):
                        q = g0 + j
                        for sg_ in range(nseg):
                            lh = Am[:, q, sg_, :] if smp else AR[:, q, 0, :]
                            P.pe(lambda e, px=px, j=j, q=q, sg_=sg_, lh=lh: e.matmul(
                                px[:, j * 128:(j + 1) * 128], lhsT=lh, rhs=Tbf[:, sg_, q, :], start=(sg_ == 0),
                                stop=(sg_ == nseg - 1)), reads=["Am" if smp else "AR", "Tbf"], writes=[pxr])
                    P.dve(lambda e, px=px, g0=g0: e.tensor_tensor(
                        out=Xb[:, g0:g0 + 4, :], in0=px[:].rearrange("p (j t) -> p j t", t=128),
                        in1=LV[:, g0:g0 + 4, :], op=ALU.add), reads=[pxr, "LV"], writes=["Xb"])
                if RS_CUT is not None and RS_CUT < 7:
                    continue
                for g0 in range(0, NP, 4):
                    pu, pur = S.bank(RP)
                    for j in range(4):
                        q = g0 + j
                        for h in range(2):
                            P.pe(lambda e, pu=pu, j=j, q=q, h=h: e.matmul(
                                pu[:, j * 128 + 64 * h:j * 128 + 64 * h + 64], lhsT=MT[:, 2 * q + h, :],
                                rhs=Xb[:, q, 64 * h:64 * h + 64], start=True, stop=True), reads=[mtn, "Xb"],
                                writes=[pur])
                    puv = pu[:].rearrange("p (j t) -> p j t", t=128)
                    P.act(lambda e, puv=puv, g0=g0: e.activation(out=Ub[:, g0:g0 + 4, :], in_=puv, func=AF.Copy),
                          reads=[pur], writes=["Ub"])
                    if need_out:
                        P.pool(lambda e, g0=g0: e.tensor_copy(out=Upad[:, 0, g0:g0 + 4, 0:64],
                                                              in_=Ub[:, g0:g0 + 4, 0:64]), reads=["Ub"],
                               writes=["Upad"])
                        P.pool(lambda e, g0=g0: e.tensor_copy(out=Upad[:, 1, g0:g0 + 4, 64:128],
                                                              in_=Ub[:, g0:g0 + 4, 64:128]), reads=["Ub"],
                               writes=["Upad"])
                if RS_CUT is not None and RS_CUT < 8:
                    continue
                if need_out:
                    for q in range(NP):
                        po, por = S.banks[OB[q // 4]], "pb%d" % OB[q // 4]
                        osl = slice((q % 4) * 128, (q % 4) * 128 + 128)
                        for h in range(2):
                            P.pe(lambda e, po=po, osl=osl, q=q, h=h: e.matmul(
                                po[:, osl], lhsT=Upad[:, h, q, :], rhs=mats[:, 2 * q + h, 1, :], start=False,
                                stop=(h == 1 and q % 4 == 3)), reads=["Upad", "mats"], writes=[por])
                if RS_CUT is not None and RS_CUT < 9:
                    continue
                for sg_ in range(nseg):
                    for g0 in range(0, NP, 4):
                        pt2, pt2r = S.bank(RP)
                        for j in range(4):
                            q = g0 + j
                            lb = Bm[:, q, sg_, :] if smp else tmB[:, q, :]
                            lk = Km[:, q, sg_, :] if smp else tmK[:, q, :]
                            P.pe(lambda e, pt2=pt2, j=j, q=q, lb=lb: e.matmul(
                                pt2[:, j * 128:(j + 1) * 128], lhsT=lb, rhs=Ub[:, q, :], start=True, stop=False),
                                reads=["Bm" if smp else "tmB", "Ub"], writes=[pt2r])
                            P.pe(lambda e, pt2=pt2, j=j, q=q, lk=lk: e.matmul(
                                pt2[:, j * 128:(j + 1) * 128], lhsT=lk, rhs=tmV[:, q, :], start=False, stop=True),
                                reads=["Km" if smp else "tmK", "tmV"], writes=[pt2r])
                        P.dve(lambda e, pt2=pt2: e.tensor_tensor(
                            out=tmpT[:], in0=pt2[:].rearrange("p (j t) -> p j t", t=128),
                            in1=blk64.unsqueeze(1).to_broadcast([128, 4, 128]), op=ALU.mult), reads=[pt2r],
                            writes=["tmpT"])
                        P.dve(lambda e, sg_=sg_, g0=g0: e.tensor_tensor(out=tmpT[:], in0=tmpT[:],
                                                                       in1=T32[:, sg_, g0:g0 + 4, :], op=ALU.add),
                              reads=["tmpT", "T32"], writes=["tmpT"])
                        P.dve(lambda e, sg_=sg_, g0=g0, gq0=gq0: e.tensor_tensor(
                            out=T32[:, sg_, g0:g0 + 4, :], in0=tmpT[:],
                            in1=gcs[:, g0:g0 + 4, gq0 + sg_:gq0 + sg_ + 1].to_broadcast([128, 4, 128]), op=ALU.mult),
                            reads=["tmpT", "gcs"], writes=["T32"])
                if not smp:
                    P.act(lambda e: e.activation(out=Tbf[:], in_=T32[:], func=AF.Copy), reads=["T32"], writes=["Tbf"])
                    if tile == c.NCT - 1:
                        P.dma(lambda e, p0=p0: e.dma_start(out=dr["rwkvp"][:, p0:p0 + NP, :], in_=T32[:, 0, :, :]),
                              S.key("so"), reads=["T32"])
                else:
                    for sg_ in range(16):
                        P.dma(lambda e, sg_=sg_, p0=p0: e.dma_start(out=dr["rwkvs"][sg_][:, p0:p0 + NP, :],
                                                                   in_=T32[:, sg_, :, :]), S.key("so"),
                              reads=["T32"])
                if not need_out:
                    continue
                if RS_CUT is not None and RS_CUT < 11:
                    continue
                P.dma(lambda e, t0=t0, p0=p0: e.dma_start(
                    out=bon[:], in_=dr["bonT"][p0 * 128:(p0 + NP) * 128, t0:t0 + 128].rearrange("(q p) t -> p q t",
                                                                                             p=128)),
                    S.key("l5"), writes=["bon"])
                P.dma(lambda e, t0=t0, p0=p0: e.dma_start(
                    out=gg[:], in_=dr["gT"][p0 * 128:(p0 + NP) * 128, t0:t0 + 128].rearrange("(q p) t -> p q t",
                                                                                           p=128)),
                    S.key("l6"), writes=["gg"])
                for g0 in range(0, NP, 4):
                    po, por = S.banks[OB[g0 // 4]], "pb%d" % OB[g0 // 4]
                    P.act(lambda e, po=po, g0=g0: e.activation(out=o32[:, g0:g0 + 4, :],
                                                               in_=po[:].rearrange("p (j t) -> p j t", t=128),
                                                               func=AF.Copy), reads=[por], writes=["o32"])
                    pm1, pm1r = S.bank(RP)
                    P.pe(lambda e, pm1=pm1, g0=g0: e.matmul(pm1[:, :], lhsT=blk64,
                                                            rhs=o32[:, g0:g0 + 4, :].rearrange("p j t -> p (j t)"),
                                                            start=True, stop=True), reads=["o32"], writes=[pm1r])
                    P.dve(lambda e, pm1=pm1, g0=g0: e.scalar_tensor_tensor(
                        out=cen[:, g0:g0 + 4, :], in0=pm1[:].rearrange("p (j t) -> p j t", t=128), scalar=-1.0 / 64,
                        in1=o32[:, g0:g0 + 4, :], op0=ALU.mult, op1=ALU.add), reads=[pm1r, "o32"], writes=["cen"])
                    P.act(lambda e, g0=g0: e.activation(out=sqv[:, g0:g0 + 4, :], in_=cen[:, g0:g0 + 4, :],
                                                        func=AF.Square), reads=["cen"], writes=["sqv"])
                    pm2, pm2r = S.bank(RP)
                    P.pe(lambda e, pm2=pm2, g0=g0: e.matmul(pm2[:, :], lhsT=blk64,
                                                            rhs=sqv[:, g0:g0 + 4, :].rearrange("p j t -> p (j t)"),
                                                            start=True, stop=True), reads=["sqv"], writes=[pm2r])
                    P.act(lambda e, pm2=pm2, g0=g0: e.activation(
                        out=rsd[:, g0:g0 + 4, :], in_=pm2[:].rearrange("p (j t) -> p j t", t=128), func=AF.Sqrt,
                        scale=1.0 / 64, bias=GN_EPS), reads=[pm2r], writes=["rsd"])
                    P.dve(lambda e, g0=g0: e.reciprocal(out=rsd[:, g0:g0 + 4, :], in_=rsd[:, g0:g0 + 4, :]),
                          reads=["rsd"], writes=["rsd"])
                    P.dve(lambda e, g0=g0: e.tensor_tensor(out=cen[:, g0:g0 + 4, :], in0=cen[:, g0:g0 + 4, :],
                                                           in1=rsd[:, g0:g0 + 4, :], op=ALU.mult),
                          reads=["cen", "rsd"], writes=["cen"])
                    for j in range(4):
                        q = g0 + j
                        P.pool(lambda e, q=q, p0=p0: e.tensor_scalar(
                            out=cen[:, q, :], in0=cen[:, q, :], scalar1=self.pcol("ln_x_w", p0 + q),
                            scalar2=self.pcol("ln_x_b", p0 + q), op0=ALU.mult, op1=ALU.add), reads=["cen"],
                            writes=["cen"])
                    P.dve(lambda e, g0=g0: e.tensor_tensor(out=cen[:, g0:g0 + 4, :], in0=cen[:, g0:g0 + 4, :],
                                                           in1=bon[:, g0:g0 + 4, :], op=ALU.add),
                          reads=["cen", "bon"], writes=["cen"])
                    P.dve(lambda e, g0=g0: e.tensor_tensor(out=obb[:, g0:g0 + 4, :], in0=cen[:, g0:g0 + 4, :],
                                                           in1=gg[:, g0:g0 + 4, :], op=ALU.mult),
                          reads=["cen", "gg"], writes=["obb"])
                P.dma(lambda e, t0=t0, p0=p0: e.dma_start(
                    out=dr["obT"][p0 * 128:(p0 + NP) * 128, t0:t0 + 128].rearrange("(q p) t -> p q t", p=128),
                    in_=obb[:]), S.key("so2"), reads=["obb"])

    def lin_stage(self, name, W, KC, kparts, a_src, t0, nt, chunks, epi, a_cast=False, extra=None, wblk=512):
        with Stage(self, name) as S:
            S.psum_banks(8)
            if a_cast:
                aT = S.sb("aT", [128, KC, nt], BF16)
                S.P.dma(lambda e: e.dma_start(out=aT[:], in_=a_src[:, t0:t0 + nt].rearrange("(k p) t -> p k t", p=128)),
                        S.key("aT"), writes=["aT.0"], q="pool")
            else:
                aT = self.load_aT(S, "aT", a_src, KC, t0, nt, kparts)
            S.stg = [S.sb("stg%d" % i, [128, nt], F32) for i in range(3)]
            S.side = {}
            S.cnt = 0
            if extra:
                extra(S)

            def evac(ci, c0, cw, pss):
                si = S.cnt % 3
                S.cnt += 1
                epi(S, ci, c0, cw, pss, S.stg[si], "stg%d" % si)

            self.fm_linear(S, W, KC, kparts, chunks, aT, "aT", nt, evac, wblk=wblk)

    def side_load(self, S, tag, src, r0, cw, t0, nt, dt=F32):
        if tag not in S.side:
            S.side[tag] = [[S.sb("sd%s%d" % (tag, i), [128, nt], dt) for i in range(2)], 0]
        bufs, k = S.side[tag]
        S.side[tag][1] = k + 1
        t = bufs[k % 2]
        rn = "sd%s%d" % (tag, k % 2)
        S.P.dma(lambda e: e.dma_start(out=t[0:cw, :], in_=src[r0:r0 + cw, t0:t0 + nt]), S.key(rn), writes=[rn])
        return t, rn

    def post_stages(self):
        c = self.cfg
        dr = self.dr
        T0, NT = c.HALO0, c.TPM
        ch32 = [(j * 128, 128) for j in range(32)]

        def epi_a(S, ci, c0, cw, pss, stg, sn):
            ga, gn = self.side_load(S, "ga", dr["gaT"], c0, cw, T0, NT)
            for (ps, pr, a0, n) in pss:
                S.P.dve(lambda e, ps=ps, a0=a0, n=n: e.tensor_tensor(out=stg[0:cw, a0:a0 + n], in0=ps,
                                                                     in1=ga[0:cw, a0:a0 + n], op=ALU.mult),
                        reads=[pr, gn], writes=[sn])
            S.P.dma(lambda e: e.dma_start(out=dr["yoT"][c0:c0 + cw, T0:T0 + NT], in_=stg[0:cw, :]), S.key("st"),
                    reads=[sn])

        if self.want("bra"):
            self.lin_stage("bra", dr["w_branch_a"], 16, 128, dr["oaT"], T0, NT, ch32, epi_a)

        def epi_b(S, ci, c0, cw, pss, stg, sn):
            gb, gn = self.side_load(S, "gb", dr["gbT"], c0, cw, T0, NT)
            ma, mn = self.side_load(S, "ma", dr["yoT"], c0, cw, T0, NT)
            mb, mbn = S.mixb[S.cnt % 2], "mixb%d" % (S.cnt % 2)
            for (ps, pr, a0, n) in pss:
                S.P.dve(lambda e, ps=ps, a0=a0, n=n: e.tensor_tensor(out=stg[0:cw, a0:a0 + n], in0=ps,
                                                                     in1=gb[0:cw, a0:a0 + n], op=ALU.mult),
                        reads=[pr, gn], writes=[sn])
            S.P.pool(lambda e: e.tensor_tensor(out=mb[0:cw, :], in0=stg[0:cw, :], in1=ma[0:cw, :], op=ALU.add),
                     reads=[sn, mn], writes=[mbn])
            S.P.dma(lambda e: e.dma_start(out=dr["mixT"][c0:c0 + cw, T0:T0 + NT], in_=mb[0:cw, :]), S.key("st"),
                    reads=[mbn])

        def extra_b(S):
            S.mixb = [S.sb("mixb%d" % i, [128, NT], BF16) for i in range(2)]

        if self.want("brb"):
            self.lin_stage("brb", dr["w_branch_b"], 16, 128, dr["obT"], T0, NT, ch32, epi_b, extra=extra_b)

        def epi_store(dst, tofs, ntk):
            def epi(S, ci, c0, cw, pss, stg, sn):
                for (ps, pr, a0, n) in pss:
                    self.copy_ps(S, stg[0:cw, a0:a0 + n], ps, pr, sn)
                S.P.dma(lambda e: e.dma_start(out=dst[c0:c0 + cw, tofs:tofs + ntk], in_=stg[0:cw, :]),
                        S.key("st"), reads=[sn])
            return epi

        if self.want("wout"):
            self.lin_stage("wo", dr["w_out"], 32, 128, dr["mixT"], T0, NT, ch32, epi_store(dr["yoT"], T0, NT))
        if self.want("norm2"):
            self.norm_stage("n2", dr["yoT"], T0, NT, "g_post_mix", dr["hfT"], res=dr["xT"], out_x=dr["x1T"],
                            gain2="g_pre_ffn")

        nfc = DFF // 128

        def epi_up(S, ci, c0, cw, pss, stg, sn):
            isg = c0 < DFF
            dst = dr["ugT"] if isg else dr["uvT"]
            r0 = c0 if isg else c0 - DFF
            for (ps, pr, a0, n) in pss:
                self.copy_ps(S, stg[0:cw, a0:a0 + n], ps, pr, sn)
            S.P.dma(lambda e: e.dma_start(out=dst[r0:r0 + cw, T0:T0 + NT], in_=stg[0:cw, :]), S.key("st"), reads=[sn])
            if isg:
                nctx = c.CTX - T0
                S.P.dma(lambda e: e.dma_start(out=dr["convTp"][r0:r0 + cw, :], in_=stg[0:cw, nctx - 2:nctx]),
                        S.key("st"), reads=[sn])
                lc, ln = S.lc[S.cnt % 2], "lc%d" % (S.cnt % 2)
                S.P.pool(lambda e: e.tensor_copy(
                    out=lc[0:cw, :, :], in_=stg[0:cw, NT - 128:NT].rearrange("p (s j) -> p s j", j=8)[:, :, 6:8]),
                    reads=[sn], writes=[ln])
                S.P.dma(lambda e: e.dma_start(out=dr["convTs"][r0:r0 + cw, :, :], in_=lc[0:cw, :, :]), S.key("st"),
                        reads=[ln])

        def extra_up(S):
            S.lc = [S.sb("lc%d" % i, [128, 16, 2], F32) for i in range(2)]

        if self.want("up"):
            chunks = [(j * 128, 128) for j in range(2 * nfc)]
            self.lin_stage("up", dr["w_up"], 32, 128, dr["hfT"], T0, NT, chunks, epi_up, extra=extra_up)
        if self.want("ffact"):
            self.ffact_stage()

        O0, NO = c.OWN0, c.TOW
        KH = nfc // 2

        def epi_d2(S, ci, c0, cw, pss, stg, sn):
            pa, pn = self.side_load(S, "pa", dr["fT"], c0, cw, O0, NO)
            for (ps, pr, a0, n) in pss:
                S.P.dve(lambda e, ps=ps, a0=a0, n=n: e.tensor_tensor(out=stg[0:cw, a0:a0 + n], in0=ps,
                                                                     in1=pa[0:cw, a0:a0 + n], op=ALU.add),
                        reads=[pr, pn], writes=[sn])
            S.P.dma(lambda e: e.dma_start(out=dr["x2T"][c0:c0 + cw, O0:O0 + NO], in_=stg[0:cw, :]), S.key("st"),
                    reads=[sn])

        if self.want("down"):
            self.lin_stage("d1", dr["w_down"][0:KH * 128, :], KH, 128, dr["actT"][0:KH * 128, :], O0, NO, ch32,
                           epi_store(dr["fT"], O0, NO), wblk=256)
            self.lin_stage("d2", dr["w_down"][KH * 128:, :], nfc - KH, 128, dr["actT"][KH * 128:, :], O0, NO, ch32,
                           epi_d2, wblk=256)
        if self.want("norm3"):
            self.norm_stage("n3", dr["x2T"], O0, NO, "g_post_ffn", dr["hpT"], res=dr["x1T"], out_x=dr["fT"],
                            gain2="g_pe")

        if self.want("pe"):
            self.lin_stage("pp", dr["w_pe"], 2, 128, dr["pT"], 0, NO, ch32, epi_store(dr["ppT"], O0, NO), a_cast=True)

            def epi_g(S, ci, c0, cw, pss, stg, sn):
                pp, ppn = self.side_load(S, "pp", dr["ppT"], c0, cw, O0, NO)
                x2, x2n = self.side_load(S, "x2", dr["fT"], c0, cw, O0, NO)
                for (ps, pr, a0, n) in pss:
                    S.P.act(lambda e, ps=ps, a0=a0, n=n: e.activation(out=stg[0:cw, a0:a0 + n], in_=ps,
                                                                      func=AF.Sigmoid), reads=[pr], writes=[sn])
                S.P.dve(lambda e: e.tensor_tensor(out=stg[0:cw, :], in0=stg[0:cw, :], in1=pp[0:cw, :], op=ALU.mult),
                        reads=[sn, ppn], writes=[sn])
                S.P.pool(lambda e: e.tensor_tensor(out=stg[0:cw, :], in0=stg[0:cw, :], in1=x2[0:cw, :], op=ALU.add),
                         reads=[sn, x2n], writes=[sn])
                S.P.dma(lambda e: e.dma_start(out=dr["yT"][c0:c0 + cw, :], in_=stg[0:cw, :]), S.key("st"), reads=[sn])

            self.lin_stage("pg", dr["w_pe_gate"], 32, 128, dr["hpT"], O0, NO, ch32, epi_g)

    def ffact_stage(self):
        c = self.cfg
        dr = self.dr
        NOP = c.CTX - c.OWN0
        with Stage(self, "fa") as S:
            P = S.P
            gp = [S.sb("gp%d" % i, [128, NOP + 2], F32) for i in range(2)]
            gs = [S.sb("gs%d" % i, [128, 16, 10], F32) for i in range(2)]
            st = [S.sb("st%d" % i, [128, 16, 2], F32) for i in range(2)]
            vv = [S.sb("vv%d" % i, [128, c.TOW], F32) for i in range(2)]
            cvs = [S.sb("cv%d" % i, [128, c.TOW], F32) for i in range(2)]
            uus = [S.sb("uu%d" % i, [128, c.TOW], F32) for i in range(2)]
            ab = [S.sb("ab%d" % i, [128, c.TOW], BF16) for i in range(2)]
            def chunk(j):
                b = j % 2
                cv, uu = cvs[b], uus[b]
                cvn, uun = "cv%d" % b, "uu%d" % b
                r0 = j * 128
                g_, gn = gp[b], "gp%d" % b
                s_, sn_ = gs[b], "gs%d" % b
                t_, tn = st[b], "st%d" % b
                v_, vn = vv[b], "vv%d" % b
                a_, an = ab[b], "ab%d" % b
                P.dma(lambda e, g_=g_, r0=r0: e.dma_start(out=g_[:, :], in_=dr["ugT"][r0:r0 + 128, c.OWN0 - 2:c.CTX]),
                      S.key("l"), writes=[gn])
                P.dma(lambda e, s_=s_, r0=r0: e.dma_start(
                    out=s_[:, :, 2:10], in_=dr["ugT"][r0:r0 + 128, c.CTX:c.TT].rearrange("p (s j) -> p s j", j=8)),
                    S.key("l"), writes=[sn_])
                P.dma(lambda e, t_=t_, r0=r0: e.dma_start(out=t_[:], in_=dr["sconvT"][r0:r0 + 128, :, :]),
                      S.key("l"), writes=[tn])
                P.dma(lambda e, v_=v_, r0=r0: e.dma_start(out=v_[:, :], in_=dr["uvT"][r0:r0 + 128, c.OWN0:c.TT]),
                      S.key("l"), writes=[vn])
                P.pool(lambda e, s_=s_, t_=t_: e.tensor_copy(out=s_[:, :, 0:2], in_=t_[:]), reads=[tn, sn_],
                       writes=[sn_])
                P.pool(lambda e, g_=g_: e.tensor_scalar(out=g_[:, 0:2], in0=g_[:, 0:2], scalar1=self.pcol("flag"),
                                                        scalar2=None, op0=ALU.mult), reads=[gn], writes=[gn])
                w = [self.pcol("conv_w%d" % k, j) for k in range(3)]
                bcol = self.pcol("conv_b", j)
                for (cvv, src, rn, nn) in ((cv[:, 0:NOP], lambda k, g_=g_: g_[:, k:k + NOP], gn, None),
                                           (cv[:, NOP:].rearrange("p (s j) -> p s j", j=8),
                                            lambda k, s_=s_: s_[:, :, k:k + 8], sn_, None)):
                    P.dve(lambda e, cvv=cvv, src=src, w=w, bcol=bcol: e.tensor_scalar(out=cvv, in0=src(2), scalar1=w[2], scalar2=bcol,
                                                                      op0=ALU.mult, op1=ALU.add), reads=[rn],
                          writes=[cvn])
                    P.dve(lambda e, cvv=cvv, src=src, w=w: e.scalar_tensor_tensor(out=cvv, in0=src(1), scalar=w[1], in1=cvv,
                                                                             op0=ALU.mult, op1=ALU.add),
                          reads=[rn, cvn], writes=[cvn])
                    P.dve(lambda e, cvv=cvv, src=src, w=w: e.scalar_tensor_tensor(out=cvv, in0=src(0), scalar=w[0], in1=cvv,
                                                                             op0=ALU.mult, op1=ALU.add),
                          reads=[rn, cvn], writes=[cvn])
                P.pool(lambda e: e.tensor_tensor(out=uu[:], in0=cv[:], in1=cv[:], op=ALU.mult), reads=[cvn],
                       writes=[uun])
                P.pool(lambda e: e.tensor_scalar(out=uu[:], in0=uu[:], scalar1=0.044715, scalar2=1.0, op0=ALU.mult,
                                                 op1=ALU.add), reads=[uun], writes=[uun])
                P.pool(lambda e: e.tensor_tensor(out=uu[:], in0=uu[:], in1=cv[:], op=ALU.mult), reads=[uun, cvn],
                       writes=[uun])
                P.act(lambda e: e.activation(out=uu[:], in_=uu[:], func=AF.Sigmoid, scale=1.5957691216057308),
                      reads=[uun], writes=[uun])
                P.dve(lambda e: e.tensor_tensor(out=uu[:], in0=uu[:], in1=cv[:], op=ALU.mult), reads=[uun, cvn],
                      writes=[uun])
                P.dve(lambda e, a_=a_, v_=v_: e.tensor_tensor(out=a_[:], in0=uu[:], in1=v_[:], op=ALU.mult),
                      reads=[uun, vn], writes=[an])
                P.dma(lambda e, a_=a_, r0=r0: e.dma_start(out=dr["actT"][r0:r0 + 128, c.OWN0:c.TT], in_=a_[:]),
                      S.key("s"), reads=[an])

            for j in range(DFF // 128):
                chunk(j)

    def kneg05(self):
        return self.pcd[:, 60:61]

    def build_rest(self):
        if self.want("gla"):
            self.gla_stage()
        if self.want("rfront"):
            self.rfront_stage()
        if self.want("rscan"):
            self.rscan_stage(False)
            self.rscan_stage(True)
        self.post_stages()

WNAMES = ["w_in", "w_alpha2", "w_branch_a", "w_decay2", "w_iclr2", "w_gate2", "w_branch_b", "w_out", "w_up",
          "w_down", "w_pe_gate", "w_pe"]


def host_core_inputs(cfg, inp, x_ctx, p_ctx, xs, ps, sgla, srwkv, sshift, sconv, flag, shared=None):
    d = {}
    d["xT"] = np.ascontiguousarray(np.concatenate([x_ctx, xs.reshape(128, D)], axis=0).T)
    d["pT"] = np.ascontiguousarray(np.concatenate([p_ctx[cfg.OWN0:], ps.reshape(128, PD)], axis=0).T)
    d["pcols"] = host_pcols(inp, flag)
    if shared is None:
        shared = {}
    if "consts" not in shared:
        shared["consts"] = host_consts()[0]
        t_ = np.arange(128)
        shared["segcol"] = np.ascontiguousarray(np.broadcast_to(
            (t_[None, :] // 8 == np.arange(16)[:, None]).astype(np.float32).reshape(1, 16 * 128), (128, 16 * 128)))
        for n in WNAMES:
            shared[n] = np.ascontiguousarray(np.asarray(inp[n][0], np.float32))
    d.update(shared)
    d["sgla"] = np.ascontiguousarray(sgla, np.float32)
    S = np.asarray(srwkv, np.float32).reshape(16, 16, 2, 64, 64)
    T = np.zeros((16, 2, 64, 16, 2, 64), np.float32)
    for hl in range(2):
        T[:, hl, :, :, hl, :] = S[:, :, hl].transpose(0, 3, 1, 2)
    d["srwkv"] = T.reshape(16, 128, 16, 128)
    d["sshiftT"] = np.ascontiguousarray(np.asarray(sshift, np.float32).T)
    d["sconvT"] = np.ascontiguousarray(np.asarray(sconv, np.float32).transpose(2, 0, 1))
    return d


def _unpad_rwkv(T):
    lead = T.shape[:-3]
    T = T.reshape(lead + (2, 64, 16, 2, 64))
    out = np.zeros(lead + (16, 2, 64, 64), np.float32)
    for hl in range(2):
        blk = T[..., hl, :, :, hl, :]
        out[..., :, hl, :, :] = np.moveaxis(blk, -3, -1)
    return out.reshape(lead + (32, 64, 64))


def host_outputs_core(cfg, r):
    o = {}
    yT = np.asarray(r["yT"])
    nown = cfg.CTX - cfg.OWN0
    o["y_own"] = yT[:, :nown].T
    o["y_smp"] = yT[:, nown:].T.reshape(16, 8, D)
    o["gla_p"] = np.asarray(r["glap"])
    o["gla_s"] = np.asarray(r["glas"])
    o["rwkv_p"] = _unpad_rwkv(np.asarray(r["rwkvp"]))
    o["rwkv_s"] = _unpad_rwkv(np.asarray(r["rwkvs"]))
    o["shift_p"] = np.asarray(r["shiftTp"])[:, 0]
    o["shift_s"] = np.asarray(r["shiftTs"]).T
    o["conv_p"] = np.asarray(r["convTp"]).T
    o["conv_s"] = np.asarray(r["convTs"]).transpose(1, 2, 0)
    return o


_NC_CACHE = {}


def kernel(**inputs):
    cfg = Cfg(16, 8)
    inp = {k: np.asarray(v) for k, v in inputs.items()}
    xp, xs = inp["x_prompt"], inp["x_sample"]
    pp, psm = inp["p_prompt"][0], inp["p_sample"][0]
    shared = {}
    in_maps = []
    for c in range(8):
        seq, half = c // 2, c % 2
        if half == 1:
            x_ctx, p_ctx = xp[seq], pp[seq]
        else:
            x_ctx = np.concatenate([np.zeros((1024, D), np.float32), xp[seq, :1024]], axis=0)
            p_ctx = np.concatenate([np.zeros((1024, PD), np.float32), pp[seq, :1024]], axis=0)
        sl = slice(16 * c, 16 * c + 16)
        in_maps.append(host_core_inputs(cfg, inp, x_ctx, p_ctx, xs[sl], psm[sl], inp["state_gla"][0, sl],
                                        inp["state_rwkv"][0, sl], inp["state_shift"][0, sl],
                                        inp["state_ffn_conv"][0, sl], flag=float(half), shared=shared))
    if "nc" not in _NC_CACHE:
        _NC_CACHE["nc"] = KB(cfg).build()
    res = run_bass_kernel_spmd(_NC_CACHE["nc"], in_maps, core_ids=list(range(8)))
    y_p = np.zeros((4, 2048, D), np.float32)
    y_s = np.zeros((128, 8, D), np.float32)
    gla_p = np.zeros((1, 4, GH, GDK, GDV), np.float32)
    rwkv_p = np.zeros((1, 4, RH, RN, RN), np.float32)
    shift_p = np.zeros((1, 4, 6592), np.float32)
    conv_p = np.zeros((1, 4, 2, DFF), np.float32)
    gla_s = np.zeros((1, 128, GH, GDK, GDV), np.float32)
    rwkv_s = np.zeros((1, 128, RH, RN, RN), np.float32)
    shift_s = np.zeros((1, 128, 6592), np.float32)
    conv_s = np.zeros((1, 128, 2, DFF), np.float32)
    for c in range(8):
        seq, half = c // 2, c % 2
        o = host_outputs_core(cfg, res.results[c])
        sl = slice(16 * c, 16 * c + 16)
        y_p[seq, half * 1024:(half + 1) * 1024] = o["y_own"]
        y_s[sl] = o["y_smp"]
        gla_s[0, sl] = o["gla_s"]
        rwkv_s[0, sl] = o["rwkv_s"]
        shift_s[0, sl] = o["shift_s"]
        conv_s[0, sl] = o["conv_s"]
        if half == 1:
            gla_p[0, seq] = o["gla_p"]
            rwkv_p[0, seq] = o["rwkv_p"]
            shift_p[0, seq] = o["shift_p"]
            conv_p[0, seq] = o["conv_p"]
    return (y_p, y_s, gla_p, rwkv_p, shift_p, conv_p, gla_s, rwkv_s, shift_s, conv_s)
```

```python
import contextlib
import numpy as np
import concourse.bass as bass
import concourse.mybir as mybir
from concourse.bass_utils import run_bass_kernel_spmd

F32 = mybir.dt.float32
BF16 = mybir.dt.bfloat16
ALU = mybir.AluOpType
AF = mybir.ActivationFunctionType
AX = mybir.AxisListType

D = 4096
DFF = 11008
PD = 256
GH, GDK, GDV = 4, 256, 512
RH, RN = 32, 64
EPS = 1e-6
GN_EPS = 64e-5
IN_TOTAL = 20944
PIECES = dict(q=(0, 1024), k=(1024, 1024), v=(2048, 2048), za=(4096, 16), zg=(4112, 2048),
              r=(6160, 2048), xw=(8208, 96), kr=(8304, 2048), vr=(10352, 2048), xa=(12400, 96),
              xg=(12496, 256), ga=(12752, 4096), gb=(16848, 4096))
ZR0 = 6160

COMPUTE = ("pe", "act", "dve", "pool")
import os as _os
RS_CUT = int(_os.environ["RS_CUT"]) if "RS_CUT" in _os.environ else None
RS_SUB = _os.environ.get("RS_SUB", "")


class Op:
    __slots__ = ("eng", "fn", "deps", "key", "sig", "cnt")

    def __init__(self, eng, fn, deps, key):
        self.eng, self.fn, self.deps, self.key = eng, fn, deps, key
        self.sig = False
        self.cnt = 0


class Prog:
    def __init__(self):
        self.ops = []
        self.lastw = {}
        self.rd_eng = {}
        self.rd_dma = {}

    def add(self, eng, fn, reads=(), writes=(), key=None):
        i = len(self.ops)
        deps = set()
        for r in reads:
            w = self.lastw.get(r)
            if w is not None:
                deps.add(w)
        for r in writes:
            w = self.lastw.get(r)
            if w is not None:
                deps.add(w)
            d = self.rd_eng.get(r)
            if d:
                deps.update(d.values())
            l = self.rd_dma.get(r)
            if l:
                deps.update(l)
        for r in reads:
            if key is not None:
                self.rd_dma.setdefault(r, []).append(i)
            else:
                self.rd_eng.setdefault(r, {})[eng] = i
        for r in writes:
            self.lastw[r] = i
            self.rd_eng[r] = {}
            self.rd_dma[r] = []
        deps.discard(i)
        self.ops.append(Op(eng, fn, deps, key))
        return i

    def pe(self, fn, reads=(), writes=()):
        return self.add("pe", fn, reads, writes)

    def act(self, fn, reads=(), writes=()):
        return self.add("act", fn, reads, writes)

    def dve(self, fn, reads=(), writes=()):
        return self.add("dve", fn, reads, writes)

    def pool(self, fn, reads=(), writes=()):
        return self.add("pool", fn, reads, writes)

    def dma(self, fn, key, reads=(), writes=(), q="sp"):
        base = (list(writes) + list(reads))[0]
        return self.add(q, fn, reads, writes, key=str(key).split("_")[0] + "_" + base)

    def emit(self, nc, stack, kb=None):
        ops = self.ops
        for o in ops:
            for d in o.deps:
                y = ops[d]
                if y.key is None and (y.eng != o.eng or o.key is not None or o.eng != "pe"):
                    y.sig = True
        ecnt = {e: 0 for e in COMPUTE}
        kcnt = {}
        kq = {}
        for o in ops:
            if o.key is not None:
                kcnt[o.key] = kcnt.get(o.key, 0) + 16
                o.cnt = kcnt[o.key]
                assert kq.setdefault(o.key, o.eng) == o.eng, ("dma key used from two queues", o.key)
            elif o.sig:
                ecnt[o.eng] += 1
                o.cnt = ecnt[o.eng]
        if not kb.esem:
            for e_ in COMPUTE:
                kb.esem[e_] = kb.gst.enter_context(nc.semaphore("s_" + e_))
                kb.ebase[e_] = 0
        esem = kb.esem
        eb = dict(kb.ebase)
        for o in ops:
            if o.key is None and o.sig:
                o.cnt += eb[o.eng]
        for e_ in COMPUTE:
            kb.ebase[e_] += ecnt[e_]
        ksem = {}
        kbase = {}
        nidx = {"sw": 0, "hw": 0}
        for k in kcnt:
            kind = "sw" if kq[k] == "pool" else "hw"
            sems, bases = kb.ksems[kind], kb.kbases[kind]
            i_ = nidx[kind]
            nidx[kind] += 1
            while len(sems) <= i_:
                sems.append(kb.gst.enter_context(nc.semaphore("%s%d" % (kind, len(sems)))))
                bases.append(0)
            ksem[k] = sems[i_]
            kbase[k] = bases[i_]
            bases[i_] += kcnt[k]
        for o in ops:
            if o.key is not None:
                o.cnt += kbase[o.key]
        for k in kcnt:
            kcnt[k] += kbase[k]
        engs = {}
        for o in ops:
            engs.setdefault(o.eng, []).append(o)
        block = stack.enter_context(nc.Block())

        def run(eng_name, e):
            waited = {}
            for o in engs.get(eng_name, ()):
                need = {}
                for d in o.deps:
                    y = ops[d]
                    if y.key is not None:
                        s = ("k", y.key)
                    elif y.eng != o.eng or o.key is not None or o.eng != "pe":
                        s = ("e", y.eng)
                    else:
                        continue
                    if y.cnt > need.get(s, 0):
                        need[s] = y.cnt
                for s, v in need.items():
                    if v > waited.get(s, 0):
                        waited[s] = v
                        e.wait_ge(ksem[s[1]] if s[0] == "k" else esem[s[1]], v)
                ins = o.fn(e)
                if o.key is not None:
                    ins.then_inc(ksem[o.key], 16)
                elif o.sig:
                    ins.then_inc(esem[o.eng], 1)
            if eng_name == "sp":
                for k, v in kcnt.items():
                    if v > waited.get(("k", k), 0):
                        e.wait_ge(ksem[k], v)

        @block.sync
        def _(e):
            run("sp", e)

        @block.tensor
        def _(e):
            run("pe", e)

        @block.scalar
        def _(e):
            run("act", e)

        @block.vector
        def _(e):
            run("dve", e)

        @block.gpsimd
        def _(e):
            run("pool", e)


class Rec:
    def __init__(self):
        self.buf = []

    def dve(self, fn, reads=(), writes=()):
        self.buf.append(("dve", fn, list(reads), list(writes), None))

    def act(self, fn, reads=(), writes=()):
        self.buf.append(("act", fn, list(reads), list(writes), None))

    def pool(self, fn, reads=(), writes=()):
        self.buf.append(("pool", fn, list(reads), list(writes), None))

    def pe(self, fn, reads=(), writes=()):
        self.buf.append(("pe", fn, list(reads), list(writes), None))

    def dma(self, fn, key, reads=(), writes=(), q="sp"):
        self.buf.append((q, fn, list(reads), list(writes), key))


def interleave_recs(P, recs):
    i = 0
    while any(i < len(r.buf) for r in recs):
        for r in recs:
            if i < len(r.buf):
                eng, fn, rd, wr, key = r.buf[i]
                if key is None:
                    P.add(eng, fn, rd, wr)
                else:
                    P.dma(fn, key, rd, wr, q=eng)
        i += 1


class Stage:
    def __init__(self, kb, name):
        self.kb, self.nc, self.name = kb, kb.nc, name
        self.st = contextlib.ExitStack()
        self.P = Prog()
        self.nps = 0
        self.rr = 0
        self.uid = 0

    def __enter__(self):
        self.st.__enter__()
        return self

    def __exit__(self, *a):
        if a[0] is None:
            self.P.emit(self.nc, self.st, self.kb)
        return self.st.__exit__(*a)

    def sb(self, name, shape, dt):
        return self.st.enter_context(self.nc.sbuf_tensor(self.name + "_" + name, list(shape), dt))

    def psum_banks(self, n=8):
        self.banks = [self.st.enter_context(self.nc.psum_tensor("%s_pb%d" % (self.name, i), [128, 512], F32))
                      for i in range(n)]
        self.nbanks = n

    def bank(self, pool=None):
        if pool is None:
            i = self.rr % self.nbanks
        else:
            i = pool[self.rr % len(pool)]
        self.rr += 1
        return self.banks[i], "pb%d" % i

    def key(self, base):
        return self.name + "_" + base

    def evac_eng(self):
        self.uid += 1
        return "act" if self.uid % 2 else "dve"


def tok_blocks(n, maxb=512):
    out = []
    a = 0
    while a < n:
        b = min(maxb, n - a)
        out.append((a, b))
        a += b
    return out


class Cfg:
    def __init__(self, nct=16, nown=8):
        self.NCT = nct
        self.NOWN = nown
        self.TT = (nct + 1) * 128
        self.CTX = nct * 128
        self.OWN0 = (nct - nown) * 128
        self.HALO0 = self.OWN0 - 128
        self.TOW = self.TT - self.OWN0
        self.TPM = self.TT - self.HALO0
        self.ZW = 1 + self.CTX + 16 * 9


def pcol_layout():
    lay = {}
    o = 0
    for name, n in [("g_pre_mix", 32), ("g_post_mix", 32), ("g_pre_ffn", 32), ("g_post_ffn", 32), ("g_pe", 32),
                    ("b_alpha", 8), ("gla_norm", 4), ("mu_r", 16), ("mu_xw", 1), ("mu_kr", 16), ("mu_vr", 16),
                    ("mu_xa", 1), ("mu_xg", 2), ("w0", 16), ("a0", 16), ("k_k", 16), ("k_a", 16), ("r_k", 16),
                    ("ln_x_w", 16), ("ln_x_b", 16), ("conv_w0", 86), ("conv_w1", 86), ("conv_w2", 86),
                    ("conv_b", 86), ("flag", 1)]:
        lay[name] = (o, n)
        o += n
    return lay, o


PCL, NPC = pcol_layout()


def host_pcols(inp, flag):
    t = np.zeros((128, NPC), np.float32)

    def put(name, vec):
        o, n = PCL[name]
        v = np.zeros(n * 128, np.float32)
        v[:vec.size] = vec.reshape(-1)
        t[:, o:o + n] = v.reshape(n, 128).T

    for nm in ["g_pre_mix", "g_post_mix", "g_pre_ffn", "g_post_ffn", "g_pe", "b_alpha", "gla_norm", "w0", "a0",
               "k_k", "k_a", "r_k", "ln_x_w", "ln_x_b", "conv_b"]:
        put(nm, np.asarray(inp[nm][0]))
    mu = np.asarray(inp["mu_shift"][0])
    put("mu_r", mu[0:2048])
    put("mu_xw", mu[2048:2144])
    put("mu_kr", mu[2144:4192])
    put("mu_vr", mu[4192:6240])
    put("mu_xa", mu[6240:6336])
    put("mu_xg", mu[6336:6592])
    cw = np.asarray(inp["conv_w"][0])
    for j in range(3):
        put("conv_w%d" % j, cw[j])
    o, n = PCL["flag"]
    t[:, o] = flag
    return t


def host_consts():
    c = {}
    p = np.arange(128)
    c["ident"] = np.eye(128, dtype=np.float32)
    c["ones"] = np.ones((128, 128), np.float32)
    c["blk64"] = (p[:, None] // 64 == p[None, :] // 64).astype(np.float32)
    for nm, L in (("p", 128), ("s", 8)):
        same = (p[:, None] // L == p[None, :] // L)
        c["incT_" + nm] = (same & (p[None, :] >= p[:, None])).astype(np.float32)
        c["strT_" + nm] = (same & (p[None, :] > p[:, None])).astype(np.float32)
        c["str_" + nm] = (same & (p[:, None] > p[None, :])).astype(np.float32)
    c["seg16"] = (p[:, None] // 8 == np.arange(16)[None, :]).astype(np.float32)
    names = ["ident", "ones", "blk64", "incT_p", "strT_p", "str_p", "incT_s", "strT_s", "str_s"]
    tab = np.concatenate([c[n] for n in names] + [np.pad(c["seg16"], ((0, 0), (0, 112)))], axis=1)
    return tab.astype(np.float32), names + ["seg16"]


CONST_NAMES = ["ident", "ones", "blk64", "incT_p", "strT_p", "str_p", "incT_s", "strT_s", "str_s", "seg16"]


class KB:
    def __init__(self, cfg, debug=False, stages=None):
        self.cfg = cfg
        self.debug = debug
        self.stages = stages
        self.nc = bass.Bass("TRN2", target_bir_lowering=False)
        self.gst = contextlib.ExitStack()
        self.dr = {}
        self.esem = {}
        self.ebase = {}
        self.ksems = {"sw": [], "hw": []}
        self.kbases = {"sw": [], "hw": []}

    def inp(self, name, shape, dt=F32):
        self.dr[name] = self.nc.dram_tensor(name, list(shape), dt, kind="ExternalInput").ap()
        return self.dr[name]

    def out(self, name, shape, dt=F32):
        self.dr[name] = self.nc.dram_tensor(name, list(shape), dt, kind="ExternalOutput").ap()
        return self.dr[name]

    def scr(self, name, shape, dt=F32):
        kind = "ExternalOutput" if self.debug else "Internal"
        self.dr[name] = self.nc.dram_tensor(name, list(shape), dt, kind=kind).ap()
        return self.dr[name]

    def want(self, s):
        return self.stages is None or s in self.stages

    def declare(self):
        c = self.cfg
        TT = c.TT
        i = self.inp
        i("xT", [D, TT])
        i("pT", [PD, c.TOW])
        i("pcols", [128, NPC])
        i("consts", [128, 128 * 10])
        i("segcol", [128, 16 * 128])
        i("sgla", [16, GH, GDK, GDV])
        i("srwkv", [16, 128, 16, 128])
        i("sshiftT", [6592, 16])
        i("sconvT", [DFF, 16, 2])
        need = {"w_in": "win", "w_branch_a": "bra", "w_branch_b": "brb", "w_out": "wout", "w_up": "up",
                "w_down": "down", "w_pe_gate": "pe"}
        for nm, shp in (("w_in", [D, IN_TOTAL]), ("w_alpha2", [16, 1024]), ("w_branch_a", [2048, D]),
                        ("w_decay2", [96, 2048]), ("w_iclr2", [96, 2048]), ("w_gate2", [256, 2048]),
                        ("w_branch_b", [2048, D]), ("w_out", [D, D]), ("w_up", [D, 2 * DFF]), ("w_down", [DFF, D]),
                        ("w_pe_gate", [D, D]), ("w_pe", [PD, D])):
            if nm in need and not self.want(need[nm]):
                continue
            i(nm, shp)
        o = self.out
        o("yT", [D, c.TOW])
        o("glap", [GH, GDK, GDV])
        o("glas", [16, GH, GDK, GDV])
        o("rwkvp", [128, 16, 128])
        o("rwkvs", [16, 128, 16, 128])
        o("shiftTp", [6592, 1])
        o("shiftTs", [6592, 16])
        o("convTp", [DFF, 2])
        o("convTs", [DFF, 16, 2])
        s = self.scr
        s("hT", [D, TT], BF16)
        for nm in ("q", "k", "zg", "ga", "gb"):
            s(nm + "T", [PIECES[nm][1], TT])
        s("zaT", [16, TT])
        for nm in ("r", "xw", "kr", "vr", "xa", "xg"):
            s(nm + "T", [PIECES[nm][1], c.ZW])
        s("vtm", [TT, 2048], BF16)
        s("oaT", [2048, TT], BF16)
        s("obT", [2048, TT], BF16)
        s("arbk", [128, 16, 4, TT], BF16)
        s("vTb", [128, 16, TT], BF16)
        s("gcT", [128, 16, c.NCT + 16])
        s("bonT", [2048, TT])
        s("gT", [2048, TT])
        s("mixT", [D, TT], BF16)
        s("yoT", [D, TT])
        s("x1T", [D, TT])
        s("hfT", [D, TT], BF16)
        s("ugT", [DFF, TT])
        s("uvT", [DFF, TT])
        s("actT", [DFF, TT], BF16)
        s("fT", [D, TT])
        s("x2T", [D, TT])
        s("hpT", [D, TT], BF16)
        s("ppT", [D, TT])
        s("pTb", [PD, TT], BF16)

    def consts(self):
        nc = self.nc
        g = self.gst
        self.pc = g.enter_context(nc.sbuf_tensor("pc", [128, NPC], F32))
        self.pcd = g.enter_context(nc.sbuf_tensor("pcd", [128, 64], F32))
        self.c32 = g.enter_context(nc.sbuf_tensor("c32", [128, 10, 128], F32))
        self.cbf = g.enter_context(nc.sbuf_tensor("cbf", [128, 10, 128], BF16))
        self.c32x = g.enter_context(nc.sbuf_tensor("c32x", [128, 16, 128], BF16))
        with Stage(self, "c") as S:
            P = S.P
            P.dma(lambda e: e.dma_start(out=self.pc[:], in_=self.dr["pcols"]), S.key("a"), writes=["pc"])
            P.dma(lambda e: e.dma_start(out=self.c32[:], in_=self.dr["consts"].rearrange("p (n c) -> p n c", c=128)),
                  S.key("b"), writes=["c32"])
            P.dve(lambda e: e.tensor_copy(out=self.cbf[:], in_=self.c32[:]), reads=["c32"], writes=["cbf"])
            P.dma(lambda e: e.dma_start(out=self.c32x[:], in_=self.dr["segcol"].rearrange("p (n c) -> p n c", c=128)),
                  S.key("d"), writes=["c32x"], q="pool")
            pcd, pc = self.pcd, self.pc
            o, n = PCL["b_alpha"]
            P.dve(lambda e: e.tensor_scalar(out=pcd[:, 0:8], in0=pc[:, o:o + 8], scalar1=-1.0, scalar2=None,
                                            op0=ALU.mult), reads=["pc"], writes=["pcd"])
            P.pool(lambda e: e.memset(pcd[:, 60:61], -0.5), writes=["pcd"])
            om, _ = PCL["mu_r"]
            P.dve(lambda e: e.tensor_scalar(out=pcd[:, 8:60], in0=pc[:, om:om + 52], scalar1=-1.0, scalar2=1.0,
                                            op0=ALU.mult, op1=ALU.add), reads=["pc"], writes=["pcd"])

    def cst(self, name, bf=False):
        i = CONST_NAMES.index(name)
        return (self.cbf if bf else self.c32)[:, i, :]

    def pcol(self, name, j=0, n=1):
        o, _ = PCL[name]
        return self.pc[:, o + j:o + j + n]

    def fm_linear(self, S, W, KC, kparts, chunks, aT, aname, nt, evac, wblk=512, wq="pool", krange=None):
        P, nc = S.P, self.nc
        tb = tok_blocks(nt)
        astep = max(1, KC // 4) if kparts == 128 else KC
        k0, k1 = (0, KC) if krange is None else krange
        nk = k1 - k0
        blocks = []
        cur = []
        for (c0, cw) in chunks:
            if cur and (cur[-1][0] + cur[-1][1] != c0 or (c0 + cw - cur[0][0]) > wblk):
                blocks.append(cur)
                cur = []
            cur.append((c0, cw))
        if cur:
            blocks.append(cur)
        ck = ("wb", nk, wblk)
        if not hasattr(S, "cache"):
            S.cache = {}
        if ck not in S.cache:
            S.cache[ck] = [S.sb("w%d_%d_%d" % (i, nk, wblk), [128, nk, wblk], BF16) for i in range(2)]
            S.wbi = getattr(S, "wbi", 0)
        wb = S.cache[ck]
        wstep = max(1, nk // 4) if kparts == 128 else nk

        def load(bi):
            blk = blocks[bi]
            b0 = blk[0][0]
            bw = blk[-1][0] + blk[-1][1] - b0
            t = wb[bi % 2]
            kp = kparts
            if kp == 128:
                src = W[k0 * 128:k1 * 128, b0:b0 + bw].rearrange("(k p) c -> p k c", p=128)
                step = max(1, nk // 4)
                for ka in range(0, nk, step):
                    kb_ = min(nk, ka + step)
                    P.dma(lambda e, t=t, src=src, ka=ka, kb_=kb_, bw=bw: e.dma_start(
                        out=t[:, ka:kb_, 0:bw], in_=src[:, ka:kb_, :]),
                        S.key("w%d" % (bi % 2)), writes=["w%d.%d" % (bi % 2, ka // step)], q=wq)
            else:
                P.dma(lambda e, t=t, b0=b0, bw=bw, kp=kp: e.dma_start(out=t[0:kp, 0, 0:bw], in_=W[0:kp, b0:b0 + bw]),
                      S.key("w%d" % (bi % 2)), writes=["w%d.0" % (bi % 2)], q=wq)

        load(0)
        ci = 0
        for bi, blk in enumerate(blocks):
            if bi + 1 < len(blocks):
                load(bi + 1)
            t = wb[bi % 2]
            b0 = blk[0][0]
            for (c0, cw) in blk:
                pss = []
                for (a0, n) in tb:
                    pt, pr = S.bank()
                    pss.append((pt, pr, a0, n))
                for k in range(nk):
                    for (pt, pr, a0, n) in pss:
                        P.pe(lambda e, pt=pt, t=t, k=k, c0=c0, cw=cw, b0=b0, a0=a0, n=n: e.matmul(
                            pt[0:cw, 0:n], lhsT=t[0:kparts, k, c0 - b0:c0 - b0 + cw],
                            rhs=aT[0:kparts, k0 + k, a0:a0 + n], start=(k == 0), stop=(k == nk - 1)),
                            reads=["w%d.%d" % (bi % 2, k // wstep), aname + ".%d" % ((k0 + k) // astep)], writes=[pr])
                evac(ci, c0, cw, [(pt[0:cw, 0:n], pr, a0, n) for (pt, pr, a0, n) in pss])
                ci += 1

    def load_aT(self, S, name, src, KC, t0, nt, kparts=128):
        t = S.sb(name, [128, KC, nt], BF16)
        if kparts == 128:
            v = src[:, t0:t0 + nt].rearrange("(k p) t -> p k t", p=128)
            step = max(1, KC // 4)
            for ka in range(0, KC, step):
                kb_ = min(KC, ka + step)
                S.P.dma(lambda e, ka=ka, kb_=kb_: e.dma_start(out=t[:, ka:kb_, :], in_=v[:, ka:kb_, :]),
                        S.key(name), writes=[name + ".%d" % (ka // step)])
        else:
            S.P.dma(lambda e: e.dma_start(out=t[0:kparts, 0, :], in_=src[0:kparts, t0:t0 + nt]),
                    S.key(name), writes=[name + ".0"])
        return t

    def copy_ps(self, S, out, ps, pr, wres, func=None, scale=1.0):
        eng = S.evac_eng()
        if func is not None or eng == "act":
            S.P.act(lambda e: e.activation(out=out, in_=ps, func=(func or AF.Copy), scale=scale),
                    reads=[pr], writes=[wres])
        else:
            S.P.dve(lambda e: e.tensor_copy(out=out, in_=ps), reads=[pr], writes=[wres])

    def stt_mm(self, P, on_pool, out, in0, scal, in1, reads, writes, tmp=None):
        if not on_pool:
            P.dve(lambda e: e.scalar_tensor_tensor(out=out, in0=in0, scalar=scal, in1=in1, op0=ALU.mult,
                                                   op1=ALU.mult), reads=reads, writes=writes)
        else:
            P.pool(lambda e: e.tensor_scalar(out=tmp, in0=in0, scalar1=scal, scalar2=None, op0=ALU.mult),
                   reads=reads, writes=["_pooltmp"])
            P.pool(lambda e: e.tensor_tensor(out=out, in0=tmp, in1=in1, op=ALU.mult),
                   reads=list(reads) + ["_pooltmp"], writes=writes)

    def norm_stage(self, name, src, t0, nt, gain, out_bf, res=None, out_x=None, gain2=None, NB=256):
        with Stage(self, name) as S:
            P = S.P
            S.psum_banks(2)
            s32 = S.sb("s32", [128, 32, NB], F32)
            sq = S.sb("sq", [128, 32, NB], F32)
            hb = S.sb("hb", [128, 32, NB], BF16)
            rs = S.sb("rs", [128, NB], F32)
            tmpc = S.sb("tmpc", [128, NB], F32)
            r32 = S.sb("r32", [128, 32, NB], F32) if res is not None else None
            ones = self.cst("ones")

            def stats(x, xname, nb):
                P.act(lambda e, x=x, nb=nb: e.activation(out=sq[:, :, 0:nb], in_=x, func=AF.Square),
                      reads=[xname], writes=["sq"])
                pt, pr = S.bank()
                for c in range(32):
                    P.pe(lambda e, c=c, pt=pt, nb=nb: e.matmul(pt[:, 0:nb], lhsT=ones, rhs=sq[:, c, 0:nb],
                                                                 start=(c == 0), stop=(c == 31)),
                         reads=["sq"], writes=[pr])
                P.act(lambda e, pt=pt, nb=nb: e.activation(out=rs[:, 0:nb], in_=pt[:, 0:nb], func=AF.Sqrt,
                                                           scale=1.0 / D, bias=EPS), reads=[pr], writes=["rs"])
                P.dve(lambda e, nb=nb: e.reciprocal(out=rs[:, 0:nb], in_=rs[:, 0:nb]), reads=["rs"], writes=["rs"])

            for (a0, nb) in tok_blocks(nt, NB):
                sv = src[:, t0 + a0:t0 + a0 + nb].rearrange("(c p) t -> p c t", p=128)
                for h in range(2):
                    P.dma(lambda e, h=h, sv=sv, nb=nb: e.dma_start(out=s32[:, 16 * h:16 * h + 16, 0:nb],
                                                                   in_=sv[:, 16 * h:16 * h + 16, :]),
                          S.key("ls"), writes=["s32"])
                if res is not None:
                    rv = res[:, t0 + a0:t0 + a0 + nb].rearrange("(c p) t -> p c t", p=128)
                    for h in range(2):
                        P.dma(lambda e, h=h, rv=rv, nb=nb: e.dma_start(out=r32[:, 16 * h:16 * h + 16, 0:nb],
                                                                       in_=rv[:, 16 * h:16 * h + 16, :]),
                              S.key("lr"), writes=["r32"])
                stats(s32[:, :, 0:nb], "s32", nb)
                if res is None:
                    for c in range(32):
                        self.stt_mm(P, c % 3 == 0, hb[:, c, 0:nb], s32[:, c, 0:nb], self.pcol(gain, c), rs[:, 0:nb],
                                    ["s32", "rs"], ["hb"], tmp=tmpc[:, 0:nb])
                else:
                    for c in range(32):
                        self.stt_mm(P, c % 3 == 0, s32[:, c, 0:nb], s32[:, c, 0:nb], self.pcol(gain, c), rs[:, 0:nb],
                                    ["s32", "rs"], ["s32"], tmp=tmpc[:, 0:nb])
                    P.dve(lambda e, nb=nb: e.tensor_tensor(out=r32[:, :, 0:nb], in0=r32[:, :, 0:nb],
                                                           in1=s32[:, :, 0:nb], op=ALU.add),
                          reads=["r32", "s32"], writes=["r32"])
                    ov = out_x[:, t0 + a0:t0 + a0 + nb].rearrange("(c p) t -> p c t", p=128)
                    P.dma(lambda e, ov=ov, nb=nb: e.dma_start(out=ov, in_=r32[:, :, 0:nb]), S.key("sx"),
                          reads=["r32"])
                    stats(r32[:, :, 0:nb], "r32", nb)
                    for c in range(32):
                        self.stt_mm(P, c % 3 == 0, hb[:, c, 0:nb], r32[:, c, 0:nb], self.pcol(gain2, c), rs[:, 0:nb],
                                    ["r32", "rs"], ["hb"], tmp=tmpc[:, 0:nb])
                hv = out_bf[:, t0 + a0:t0 + a0 + nb].rearrange("(c p) t -> p c t", p=128)
                P.dma(lambda e, hv=hv, nb=nb: e.dma_start(out=hv, in_=hb[:, :, 0:nb]), S.key("sh"), reads=["hb"])

    def win_stage(self, name, t0, nt, pieces):
        c = self.cfg
        with Stage(self, name) as S:
            P = S.P
            S.psum_banks(8)
            hT = self.load_aT(S, "hT", self.dr["hT"], 32, t0, nt)
            stg = [S.sb("stg%d" % i, [128, nt], F32) for i in range(3)]
            lastc = [S.sb("lastc%d" % i, [128, 16], F32) for i in range(3)]
            zt = S.sb("zt", [128, 16], F32)
            P.pool(lambda e: e.memset(zt[:], 0.0), writes=["zt"])
            cnt = [0]
            Win = self.dr["w_in"]
            n_ctx = max(0, min(c.CTX, t0 + nt) - t0)
            has_smp = (t0 + nt) > c.CTX
            for pn in pieces:
                if pn == "v":
                    continue
                p0, pw = PIECES[pn]
                chunks = [(p0 + a, min(128, pw - a)) for a in range(0, pw, 128)]
                dst = self.dr[pn + "T"]
                padded = pn in ("r", "xw", "kr", "vr", "xa", "xg")
                func = AF.Sigmoid if pn in ("ga", "gb") else None

                def evac(ci, c0, cw, pss, p0=p0, dst=dst, padded=padded, func=func, pn=pn):
                    si = cnt[0] % 3
                    cnt[0] += 1
                    sg = stg[si]
                    sn = "stg%d" % si
                    for (ps, pr, a0, n) in pss:
                        self.copy_ps(S, sg[0:cw, a0:a0 + n], ps, pr, sn, func=func)
                    r0 = c0 - p0
                    if not padded:
                        P.dma(lambda e: e.dma_start(out=dst[r0:r0 + cw, t0:t0 + nt], in_=sg[0:cw, :]),
                              S.key("st"), reads=[sn])
                    else:
                        if t0 == 0:
                            P.dma(lambda e: e.dma_start(out=dst[r0:r0 + cw, 0:1], in_=zt[0:cw, 0:1],
                                                        allow_slow_non_contiguous=True), S.key("st"),
                                  reads=["zt"])
                        if n_ctx > 0:
                            P.dma(lambda e: e.dma_start(out=dst[r0:r0 + cw, 1 + t0:1 + t0 + n_ctx],
                                                        in_=sg[0:cw, 0:n_ctx]), S.key("st"), reads=[sn])
                            if t0 + n_ctx == c.CTX:
                                zr = c0 - ZR0
                                P.dma(lambda e: e.dma_start(out=self.dr["shiftTp"][zr:zr + cw, 0:1],
                                                            in_=sg[0:cw, n_ctx - 1:n_ctx]), S.key("st"), reads=[sn])
                        if has_smp:
                            dv = dst[r0:r0 + cw, 1 + c.CTX:1 + c.CTX + 144].rearrange("p (s j) -> p s j", j=9)
                            sv = sg[0:cw, nt - 128:nt].rearrange("p (s j) -> p s j", j=8)
                            P.dma(lambda e: e.dma_start(out=dv[:, :, 1:9], in_=sv), S.key("st"), reads=[sn])
                            P.dma(lambda e: e.dma_start(out=dv[:, :, 0], in_=zt[0:cw, :],
                                                        allow_slow_non_contiguous=True), S.key("st"), reads=["zt"])
                            zr = c0 - ZR0
                            lt = lastc[si]
                            P.pool(lambda e: e.tensor_copy(out=lt[0:cw, :], in_=sv[:, :, 7]), reads=[sn],
                                   writes=["lastc%d" % si])
                            P.dma(lambda e: e.dma_start(out=self.dr["shiftTs"][zr:zr + cw, :], in_=lt[0:cw, :]),
                                  S.key("st"), reads=["lastc%d" % si])

                self.fm_linear(S, Win, 32, 128, chunks, hT, "hT", nt, evac)
            if "v" in pieces:
                p0, pw = PIECES["v"]
                wv = S.cache[("wb", 32, 512)]
                vst = [S.sb("vst%d" % i, [128, 512], BF16) for i in range(2)]
                for bi in range(4):
                    t = wv[bi % 2]
                    src = Win[:, p0 + bi * 512:p0 + (bi + 1) * 512].rearrange("(k p) c -> p k c", p=128)
                    for ka in range(0, 32, 8):
                        P.dma(lambda e, t=t, src=src, ka=ka: e.dma_start(out=t[:, ka:ka + 8, :],
                                                                       in_=src[:, ka:ka + 8, :]),
                              S.key("w%d" % (bi % 2)), writes=["w%d.%d" % (bi % 2, ka // 8)], q="pool")
                    for ti in range(nt // 128):
                        pt, pr = S.bank()
                        for k in range(32):
                            P.pe(lambda e, pt=pt, t=t, k=k, ti=ti: e.matmul(
                                pt[:, :], lhsT=hT[:, k, ti * 128:(ti + 1) * 128], rhs=t[:, k, :],
                                start=(k == 0), stop=(k == 31)), reads=["w%d.%d" % (bi % 2, k // 8), "hT.%d" % (k // 8)],
                                writes=[pr])
                        vi = (bi * (nt // 128) + ti) % 2
                        self.copy_ps(S, vst[vi][:, :], pt[:, :], pr, "vst%d" % vi)
                        P.dma(lambda e, vi=vi, ti=ti, bi=bi: e.dma_start(
                            out=self.dr["vtm"][t0 + ti * 128:t0 + (ti + 1) * 128, bi * 512:(bi + 1) * 512],
                            in_=vst[vi][:, :]), S.key("sv"), reads=["vst%d" % vi])

    def build(self):
        c = self.cfg
        self.declare()
        with self.gst:
            self.consts()
            if self.want("norm1"):
                self.norm_stage("n1", self.dr["xT"], 0, c.TT, "g_pre_mix", self.dr["hT"])
            if self.want("win"):
                allp = list(PIECES.keys())
                g0 = c.HALO0
                if g0 > 0:
                    self.win_stage("wa", 0, g0, [p_ for p_ in allp if p_ not in ("zg", "ga", "gb")])
                self.win_stage("wb", g0, c.TT - g0, allp)
            self.build_rest()
        return self.nc

    def gla_stage(self):
        c = self.cfg
        dr = self.dr
        with Stage(self, "gla") as S:
            P = S.P
            S.psum_banks(7)
            qk = S.sb("qk", [128, 16, 128], F32)
            zab = S.sb("zab", [16, 128], BF16)
            wal = S.sb("wal", [16, 1024], BF16)
            vb = S.sb("vb", [128, 2048], BF16)
            sp = S.sb("sp", [128, 8, 128], F32)
            cs = S.sb("cs", [128, 8, 128], F32)
            cs2 = S.sb("cs2", [128, 8, 128], F32)
            ex = S.sb("ex", [128, 8, 128], F32)
            onesf = S.sb("onesf", [128, 128], F32)
            zer = S.sb("zer", [128, 512], BF16)
            P.pool(lambda e: e.memset(zer[:], 0.0), writes=["zer"])
            qd = S.sb("qd", [128, 8, 128], BF16)
            kd = S.sb("kd", [128, 8, 128], BF16)
            kdp = S.sb("kdp", [128, 8, 128], BF16)
            kdt = S.sb("kdt", [128, 8, 128], BF16)
            att = S.sb("att", [128, 4, 128], BF16)
            S32 = S.sb("S32", [128, 8, 512], F32)
            Sbf = S.sb("Sbf", [128, 8, 512], BF16)
            sdec = S.sb("sdec", [128, 8, 16], F32)
            cend = S.sb("cend", [128, 8, 16], F32)
            o32 = S.sb("o32", [128, 16, 128], F32)
            sq = S.sb("sq", [128, 16, 128], F32)
            rstd = S.sb("rstd", [128, 4, 128], F32)
            zg = S.sb("zg", [128, 16, 128], F32)
            sg = S.sb("sg", [128, 16, 128], F32)
            oab = S.sb("oab", [128, 16, 128], BF16)
            Ss32 = [S.sb("Ss32_%d" % i, [128, 8, 512], F32) for i in range(2)]
            Ssbf = [S.sb("Ssbf_%d" % i, [128, 8, 512], BF16) for i in range(2)]
            kdm = S.sb("kdm", [128, 8, 16, 128], BF16)
            ptr = S.st.enter_context(self.nc.psum_tensor("gla_ptr", [128, 8, 128], BF16))
            rn = lambda t: {id(sp): "sp", id(cs): "cs", id(cs2): "cs2"}[id(t)]
            identb = self.cst("ident", True)
            ones32 = self.cst("ones")
            P.dma(lambda e: e.dma_start(out=wal[:], in_=dr["w_alpha2"]), S.key("c"), writes=["wal"], q="pool")
            P.pool(lambda e: e.memset(onesf[:], 1.0), writes=["onesf"])
            P.pool(lambda e: e.memset(S32[:], 0.0), writes=["S32"])
            P.pool(lambda e: e.memset(Sbf[:], 0.0), writes=["Sbf"])
            for tile in range(c.NCT + 1):
                smp = tile == c.NCT
                t0 = tile * 128
                need_out = t0 >= c.HALO0
                P.dma(lambda e, t0=t0: e.dma_start(out=qk[:, 0:8, :], in_=dr["qT"][:, t0:t0 + 128].rearrange(
                    "(c p) t -> p c t", p=128)), S.key("l1"), writes=["qk"])
                P.dma(lambda e, t0=t0: e.dma_start(out=qk[:, 8:16, :], in_=dr["kT"][:, t0:t0 + 128].rearrange(
                    "(c p) t -> p c t", p=128)), S.key("l1"), writes=["qk"])
                P.dma(lambda e, t0=t0: e.dma_start(out=zab[:], in_=dr["zaT"][:, t0:t0 + 128]), S.key("l2"),
                      writes=["zab"], q="pool")
                P.dma(lambda e, t0=t0: e.dma_start(out=vb[:], in_=dr["vtm"][t0:t0 + 128, :]), S.key("l3"),
                      writes=["vb"])
                for half in range(2):
                    pt, pr = S.bank([0, 1])
                    for cc in range(4):
                        ch = half * 4 + cc
                        P.pe(lambda e, pt=pt, cc=cc, ch=ch: e.matmul(pt[:, cc * 128:(cc + 1) * 128],
                                                                   lhsT=wal[:, ch * 128:(ch + 1) * 128], rhs=zab[:, :],
                                                                   start=True, stop=True),
                             reads=["wal", "zab"], writes=[pr])
                    for cc in range(4):
                        ch = half * 4 + cc
                        P.act(lambda e, pt=pt, cc=cc, ch=ch: e.activation(
                            out=sp[:, ch, :], in_=pt[:, cc * 128:(cc + 1) * 128], func=AF.Exp, scale=-1.0,
                            bias=self.pcd[:, ch:ch + 1]), reads=[pr, "pcd"], writes=["sp"])
                P.act(lambda e: e.activation(out=sp[:], in_=sp[:], func=AF.Ln, bias=1.0), reads=["sp"], writes=["sp"])
                if not smp:
                    for ch in range(8):
                        P.dve(lambda e, ch=ch: e.tensor_tensor_scan(out=cs[:, ch, :], data0=onesf[:, :],
                                                                    data1=sp[:, ch, :], initial=0.0,
                                                                    op0=ALU.mult, op1=ALU.add),
                              reads=["sp", "onesf"], writes=["cs"])
                    csf = cs
                    P.dve(lambda e: e.tensor_copy(out=cend[:, :, 0:1], in_=cs[:, :, 127:128]), reads=["cs"],
                          writes=["cend"])
                    nseg = 1
                else:
                    v = lambda t: t[:].rearrange("p c (s j) -> p (c s) j", j=8)
                    src, dst = sp, cs
                    for d in (1, 2, 4):
                        P.dve(lambda e, src=src, dst=dst, d=d: e.tensor_tensor(
                            out=v(dst)[:, :, d:8], in0=v(src)[:, :, d:8], in1=v(src)[:, :, 0:8 - d], op=ALU.add),
                            reads=[rn(src)], writes=[rn(dst)])
                        P.pool(lambda e, src=src, dst=dst, d=d: e.tensor_copy(out=v(dst)[:, :, 0:d],
                                                                              in_=v(src)[:, :, 0:d]),
                               reads=[rn(src)], writes=[rn(dst)])
                        src, dst = dst, (cs2 if dst is cs else cs)
                    csf = src
                    P.dve(lambda e, csf=csf: e.tensor_copy(
                        out=cend[:, :, :], in_=csf[:].rearrange("p c (s j) -> p c s j", j=8)[:, :, :, 7]),
                        reads=[rn(csf)], writes=["cend"])
                    nseg = 16
                cn = rn(csf)
                P.act(lambda e, csf=csf: e.activation(out=ex[:], in_=csf[:], func=AF.Exp, scale=-1.0 / 16),
                      reads=[cn], writes=["ex"])
                P.dve(lambda e: e.scalar_tensor_tensor(out=qd[:], in0=qk[:, 0:8, :], scalar=float(GDK) ** -0.5,
                                                       in1=ex[:], op0=ALU.mult, op1=ALU.mult),
                      reads=["qk", "ex"], writes=["qd"])
                P.act(lambda e, csf=csf: e.activation(out=ex[:], in_=csf[:], func=AF.Exp, scale=1.0 / 16),
                      reads=[cn], writes=["ex"])
                P.dve(lambda e: e.tensor_tensor(out=kd[:], in0=qk[:, 8:16, :], in1=ex[:], op=ALU.mult),
                      reads=["qk", "ex"], writes=["kd"])
                L = 128 // nseg
                P.dve(lambda e, csf=csf, nseg=nseg, L=L: e.tensor_tensor(
                    out=ex[:].rearrange("p c (s j) -> p c s j", j=L),
                    in0=csf[:].rearrange("p c (s j) -> p c s j", j=L),
                    in1=cend[:, :, 0:nseg].unsqueeze(3).to_broadcast([128, 8, nseg, L]), op=ALU.subtract),
                    reads=[cn, "cend"], writes=["ex"])
                P.act(lambda e: e.activation(out=ex[:], in_=ex[:], func=AF.Exp, scale=1.0 / 16), reads=["ex"],
                      writes=["ex"])
                P.dve(lambda e: e.tensor_tensor(out=kdp[:], in0=qk[:, 8:16, :], in1=ex[:], op=ALU.mult),
                      reads=["qk", "ex"], writes=["kdp"])
                P.act(lambda e, nseg=nseg: e.activation(out=sdec[:, :, 0:nseg], in_=cend[:, :, 0:nseg], func=AF.Exp,
                                                        scale=-1.0 / 16), reads=["cend"], writes=["sdec"])
                for ch in range(8):
                    P.pe(lambda e, ch=ch: e.transpose(ptr[:, ch, :], kdp[:, ch, :], identb), reads=["kdp"],
                         writes=["ptr"])
                P.act(lambda e: e.activation(out=kdt[:], in_=ptr[:], func=AF.Copy), reads=["ptr"], writes=["kdt"])
                pa, par = S.bank([2])
                for h in range(4):
                    for cc in range(2):
                        P.pe(lambda e, h=h, cc=cc, pa=pa: e.matmul(pa[:, h * 128:(h + 1) * 128],
                                                                 lhsT=kd[:, 2 * h + cc, :], rhs=qd[:, 2 * h + cc, :],
                                                                 start=(cc == 0), stop=(cc == 1)),
                             reads=["kd", "qd"], writes=[par])
                mk = self.cst("incT_s" if smp else "incT_p")
                P.dve(lambda e, pa=pa, mk=mk: e.tensor_tensor(
                    out=att[:], in0=pa[:].rearrange("p (h t) -> p h t", t=128),
                    in1=mk.unsqueeze(1).to_broadcast([128, 4, 128]), op=ALU.mult),
                    reads=[par], writes=["att"])
                pos = [S.bank([3 + h_]) for h_ in range(4)]
                for h in range(4):
                    po, por = pos[h]
                    P.pe(lambda e, po=po: e.matmul(po[:, :], lhsT=zer[:, 0:128], rhs=zer[:, :], start=True,
                                                   stop=False), reads=["zer"], writes=[por])
                    for j in range(4):
                        P.pe(lambda e, h=h, j=j, po=po: e.matmul(
                            po[:, j * 128:(j + 1) * 128], lhsT=vb[:, h * 512 + j * 128:h * 512 + (j + 1) * 128],
                            rhs=att[:, h, :], start=False, stop=False),
                            reads=["vb", "att"], writes=[por])
                        if not smp:
                            for cc in range(2):
                                P.pe(lambda e, h=h, j=j, cc=cc, po=po: e.matmul(
                                    po[:, j * 128:(j + 1) * 128], lhsT=Sbf[:, 2 * h + cc, j * 128:(j + 1) * 128],
                                    rhs=qd[:, 2 * h + cc, :], start=False, stop=(cc == 1 and j == 3)),
                                    reads=["Sbf", "qd"], writes=[por])
                if not smp:
                    for h in range(4):
                        for cc in range(2):
                            i = 2 * h + cc
                            pu, pur = S.bank([0, 1])
                            P.pe(lambda e, i=i, h=h, pu=pu: e.matmul(pu[:, :], lhsT=kdt[:, i, :],
                                                                     rhs=vb[:, h * 512:(h + 1) * 512], start=True,
                                                                     stop=True), reads=["kdt", "vb"], writes=[pur])
                            P.dve(lambda e, i=i, pu=pu: e.scalar_tensor_tensor(
                                out=S32[:, i, :], in0=S32[:, i, :], scalar=sdec[:, i, 0:1], in1=pu[:, :],
                                op0=ALU.mult, op1=ALU.add), reads=[pur, "S32", "sdec"], writes=["S32"])
                    P.act(lambda e: e.activation(out=Sbf[:], in_=S32[:], func=AF.Copy), reads=["S32"], writes=["Sbf"])
                    if tile == c.NCT - 1:
                        P.dma(lambda e: e.dma_start(out=dr["glap"].rearrange("h (c p) v -> p h c v", p=128),
                                                    in_=S32[:].rearrange("p (h c) v -> p h c v", c=2)),
                              S.key("so"), reads=["S32"])
                else:
                    seg16 = self.cst("seg16", True)
                    for i8 in range(8):
                        P.dve(lambda e, i8=i8: e.tensor_tensor(
                            out=kdm[:, i8, :, :], in0=kdt[:, i8, :].unsqueeze(1).to_broadcast([128, 16, 128]),
                            in1=seg16[:, 0:16].unsqueeze(2).to_broadcast([128, 16, 128]), op=ALU.mult),
                            reads=["kdt"], writes=["kdm"])
                    for si in range(16):
                        b = si % 2
                        s32, sbf = Ss32[b], Ssbf[b]
                        n32, nbf = "Ss32_%d" % b, "Ssbf_%d" % b
                        P.dma(lambda e, si=si, s32=s32: e.dma_start(
                            out=s32[:].rearrange("p (h c) v -> p h c v", c=2),
                            in_=dr["sgla"][si].rearrange("h (c p) v -> p h c v", p=128)), S.key("ls%d" % b),
                            writes=[n32])
                        P.act(lambda e, s32=s32, sbf=sbf: e.activation(out=sbf[:], in_=s32[:], func=AF.Copy),
                              reads=[n32], writes=[nbf])
                        for h in range(4):
                            po, por = pos[h]
                            for j in range(4):
                                for cc in range(2):
                                    P.pe(lambda e, h=h, j=j, cc=cc, po=po, sbf=sbf, si=si: e.matmul(
                                        po[:, j * 128 + si * 8:j * 128 + si * 8 + 8],
                                        lhsT=sbf[:, 2 * h + cc, j * 128:(j + 1) * 128],
                                        rhs=qd[:, 2 * h + cc, si * 8:si * 8 + 8], start=False,
                                        stop=(cc == 1 and j == 3 and si == 15)),
                                        reads=[nbf, "qd"], writes=[por])
                        for h in range(4):
                            for cc in range(2):
                                i = 2 * h + cc
                                pu, pur = S.bank([0, 1])
                                P.pe(lambda e, i=i, h=h, pu=pu, si=si: e.matmul(
                                    pu[:, :], lhsT=kdm[:, i, si, :], rhs=vb[:, h * 512:(h + 1) * 512], start=True,
                                    stop=True), reads=["kdm", "vb"], writes=[pur])
                                P.dve(lambda e, i=i, pu=pu, s32=s32, si=si: e.scalar_tensor_tensor(
                                    out=s32[:, i, :], in0=s32[:, i, :], scalar=sdec[:, i, si:si + 1], in1=pu[:, :],
                                    op0=ALU.mult, op1=ALU.add), reads=[pur, n32, "sdec"], writes=[n32])
                        P.dma(lambda e, si=si, s32=s32: e.dma_start(
                            out=dr["glas"][si].rearrange("h (c p) v -> p h c v", p=128),
                            in_=s32[:].rearrange("p (h c) v -> p h c v", c=2)), S.key("ss%d" % b), reads=[n32])
                if not need_out:
                    continue
                for h in range(4):
                    po, por = pos[h]
                    P.act(lambda e, h=h, po=po: e.activation(out=o32[:, 4 * h:4 * h + 4, :],
                                                             in_=po[:].rearrange("p (j t) -> p j t", t=128),
                                                             func=AF.Copy), reads=[por], writes=["o32"])
                    P.act(lambda e, h=h, po=po: e.activation(out=sq[:, 4 * h:4 * h + 4, :],
                                                             in_=po[:].rearrange("p (j t) -> p j t", t=128),
                                                             func=AF.Square), reads=[por], writes=["sq"])
                pst, pstr = S.bank([2])
                for h in range(4):
                    for j in range(4):
                        P.pe(lambda e, h=h, j=j, pst=pst: e.matmul(pst[:, h * 128:(h + 1) * 128], lhsT=ones32,
                                                                   rhs=sq[:, 4 * h + j, :], start=(j == 0),
                                                                   stop=(j == 3)), reads=["sq"], writes=[pstr])
                P.act(lambda e, pst=pst: e.activation(out=rstd[:], in_=pst[:].rearrange("p (h t) -> p h t", t=128),
                                                      func=AF.Sqrt, scale=1.0 / GDV, bias=EPS), reads=[pstr],
                      writes=["rstd"])
                P.dve(lambda e: e.reciprocal(out=rstd[:], in_=rstd[:]), reads=["rstd"], writes=["rstd"])
                P.dma(lambda e, t0=t0: e.dma_start(out=zg[:], in_=dr["zgT"][:, t0:t0 + 128].rearrange(
                    "(c p) t -> p c t", p=128)), S.key("l4"), writes=["zg"])
                P.act(lambda e: e.activation(out=sg[:], in_=zg[:], func=AF.Sigmoid), reads=["zg"], writes=["sg"])
                P.pool(lambda e: e.tensor_tensor(out=sg[:], in0=sg[:], in1=zg[:], op=ALU.mult), reads=["sg", "zg"],
                       writes=["sg"])
                P.pool(lambda e: e.tensor_tensor(
                    out=sg[:].rearrange("p (h j) t -> p h j t", j=4),
                    in0=sg[:].rearrange("p (h j) t -> p h j t", j=4),
                    in1=rstd[:].unsqueeze(2).to_broadcast([128, 4, 4, 128]), op=ALU.mult),
                    reads=["sg", "rstd"], writes=["sg"])
                for j in range(4):
                    P.dve(lambda e, j=j: e.scalar_tensor_tensor(
                        out=oab[:].rearrange("p (h j) t -> p h j t", j=4)[:, :, j, :],
                        in0=o32[:].rearrange("p (h j) t -> p h j t", j=4)[:, :, j, :],
                        scalar=self.pcol("gla_norm", j),
                        in1=sg[:].rearrange("p (h j) t -> p h j t", j=4)[:, :, j, :],
                        op0=ALU.mult, op1=ALU.mult), reads=["o32", "sg"], writes=["oab"])
                P.dma(lambda e, t0=t0: e.dma_start(out=dr["oaT"][:, t0:t0 + 128].rearrange("(c p) t -> p c t", p=128),
                                                   in_=oab[:]), S.key("so2"), reads=["oab"])

    def rfront_stage(self):
        c = self.cfg
        dr = self.dr
        with Stage(self, "rf") as S:
            P = S.P
            S.psum_banks(8)
            NB = 512
            W1 = NB + 16
            wdec = S.sb("wdec", [96, 2048], BF16)
            wicl = S.sb("wicl", [96, 2048], BF16)
            wgat = S.sb("wgat", [128, 2, 2048], BF16)
            P.dma(lambda e: e.dma_start(out=wdec[:], in_=dr["w_decay2"]), S.key("c1"), writes=["wdec"], q="pool")
            P.dma(lambda e: e.dma_start(out=wicl[:], in_=dr["w_iclr2"]), S.key("c1"), writes=["wicl"], q="pool")
            P.dma(lambda e: e.dma_start(out=wgat[:], in_=dr["w_gate2"].rearrange("(k p) c -> p k c", p=128)),
                  S.key("c1"), writes=["wgat"], q="pool")
            raw = {n: S.sb("raw_" + n, [128, W1], F32) for n in ("r", "k", "v")}
            rawx = {n: S.sb("rawx_" + n, [128, W1], F32) for n in ("xw", "xa", "xg0", "xg1")}
            sst = S.sb("sst", [128, 16], F32)
            tmp = S.sb("tmp", [128, NB], F32)
            xsh = S.sb("xsh", [128, NB], F32)
            thb = S.sb("thb", [96, NB], BF16)
            xab = S.sb("xab", [96, NB], BF16)
            sgx = S.sb("sgx", [128, 2, NB], BF16)
            rs_ = S.sb("rs_", [128, NB], F32)
            ks_ = S.sb("ks_", [128, NB], F32)
            vs_ = S.sb("vs_", [128, NB], F32)
            pl = S.sb("pl", [128, NB], F32)
            a_ = S.sb("a_", [128, NB], F32)
            g_ = S.sb("g_", [128, NB], F32)
            kk = S.sb("kk", [128, NB], F32)
            t1 = S.sb("t1", [128, NB], F32)
            t2 = S.sb("t2", [128, NB], F32)
            k2 = S.sb("k2", [128, NB], F32)
            cp = S.sb("cp", [128, NB], F32)
            ex1 = S.sb("ex1", [128, NB], F32)
            ex2 = S.sb("ex2", [128, NB], F32)
            bon = S.sb("bon", [128, NB], F32)
            arbk = S.sb("arbk", [128, 4, NB], BF16)
            vbf = S.sb("vbf", [128, NB], BF16)
            gc = S.sb("gc", [128, 16], F32)
            rmask = S.sb("rmask", [128, 2, NB], F32)
            P.pool(lambda e: e.memset(rmask[:], 1.0), writes=["rmask"])
            P.pool(lambda e: e.memset(rmask[:, 0, :].rearrange("p (s j) -> p s j", j=128)[:, :, 0:1], 0.0),
                   writes=["rmask"])
            P.pool(lambda e: e.memset(rmask[:, 1, :].rearrange("p (s j) -> p s j", j=8)[:, :, 0:1], 0.0),
                   writes=["rmask"])
            blk64 = self.cst("blk64")
            pcd = self.pcd
            set1 = dict(raw={n_: S.sb("rawB_" + n_, [128, W1], F32) for n_ in ("r", "k", "v")},
                        t=[S.sb("sstB", [128, 16], F32)] + [S.sb(nm_ + "B", [128, NB], F32) for nm_ in
                           ("tmp", "rs_", "ks_", "vs_", "pl", "a_", "g_", "kk", "t1", "t2", "k2", "cp", "ex1", "ex2",
                            "bon")] + [S.sb("arbkB", [128, 4, NB], BF16), S.sb("vbfB", [128, NB], BF16),
                                       S.sb("gcB", [128, 16], F32)])
            set0 = dict(raw=raw, t=[sst, tmp, rs_, ks_, vs_, pl, a_, g_, kk, t1, t2, k2, cp, ex1, ex2, bon, arbk, vbf,
                                    gc])
            BUFS = [set0, set1]

            def blocks():
                for a0 in range(0, c.CTX, NB):
                    yield (a0, min(NB, c.CTX - a0), False)
                yield (c.CTX, 128, True)

            def load_raw(PP, sst, t, tn, src, rows, r0, a0, n, smp, state_rows):
                if not smp:
                    PP.dma(lambda e: e.dma_start(out=t[0:rows, 0:n + 1], in_=src[r0:r0 + rows, a0:a0 + n + 1]),
                          S.key("lr"), writes=[tn])
                else:
                    PP.dma(lambda e: e.dma_start(out=t[0:rows, 0:144],
                                                in_=src[r0:r0 + rows, 1 + c.CTX:1 + c.CTX + 144]),
                          S.key("lr"), writes=[tn])
                    PP.dma(lambda e: e.dma_start(out=sst[0:rows, :], in_=dr["sshiftT"][state_rows:state_rows + rows, :]),
                          S.key("lr2"), writes=["sst"])
                    PP.pool(lambda e: e.tensor_copy(
                        out=t[0:rows, 0:144].rearrange("p (s j) -> p s j", j=9)[:, :, 0], in_=sst[0:rows, :]),
                        reads=["sst", tn], writes=[tn])

            def shift(PP, tmp, out, t, tn, rows, n, smp, mu, omm, wname):
                if not smp:
                    cur, prev = t[0:rows, 1:n + 1], t[0:rows, 0:n]
                    o, tm = out, tmp[0:rows, 0:n]
                else:
                    v9 = t[0:rows, 0:144].rearrange("p (s j) -> p s j", j=9)
                    cur, prev = v9[:, :, 1:9], v9[:, :, 0:8]
                    o = out.rearrange("p (s j) -> p s j", j=8)
                    tm = tmp[0:rows, 0:128].rearrange("p (s j) -> p s j", j=8)
                PP.pool(lambda e: e.tensor_scalar(out=tm, in0=prev, scalar1=mu, scalar2=None, op0=ALU.mult),
                       reads=[tn], writes=["tmp"])
                PP.dve(lambda e: e.scalar_tensor_tensor(out=o, in0=cur, scalar=omm, in1=tm, op0=ALU.mult, op1=ALU.add),
                      reads=[tn, "tmp"], writes=[wname])

            DBL = ['raw_r', 'raw_k', 'raw_v', 'sst', 'tmp', 'rs_', 'ks_', 'vs_', 'pl', 'a_', 'g_', 'kk', 't1', 't2', 'k2', 'cp', 'ex1', 'ex2', 'bon', 'arbk', 'vbf', 'gc']

            class SP:
                def __init__(self, P0, sfx, defer=False):
                    self.P0, self.sfx, self.defer, self.buf = P0, sfx, defer, []

                def ren(self, names):
                    return [(x + self.sfx) if x in DBL else x for x in names]

                def _op(self, eng, fn, reads, writes, key=None):
                    if self.defer:
                        self.buf.append((eng, fn, self.ren(reads), self.ren(writes), key))
                    elif key is None:
                        self.P0.add(eng, fn, self.ren(reads), self.ren(writes))
                    else:
                        self.P0.dma(fn, key, self.ren(reads), self.ren(writes), q=eng)

                def dve(self, fn, reads=(), writes=()):
                    self._op("dve", fn, reads, writes)

                def act(self, fn, reads=(), writes=()):
                    self._op("act", fn, reads, writes)

                def pool(self, fn, reads=(), writes=()):
                    self._op("pool", fn, reads, writes)

                def pe(self, fn, reads=(), writes=()):
                    self._op("pe", fn, reads, writes)

                def dma(self, fn, key, reads=(), writes=(), q="sp"):
                    self._op(q, fn, reads, writes, key=key)

            def interleave(recs):
                i = 0
                while any(i < len(r.buf) for r in recs):
                    for r in recs:
                        if i < len(r.buf):
                            eng, fn, rd, wr, key = r.buf[i]
                            if key is None:
                                P0.add(eng, fn, rd, wr)
                            else:
                                P0.dma(fn, key, rd, wr, q=eng)
                    i += 1

            P0 = P
            PS0 = SP(P0, "_0")

            def pair(p, a0, n, smp, mi):
                P = SP(P0, "_%d" % (p % 2), defer=True)
                X = BUFS[p % 2]
                raw = X["raw"]
                sst, tmp, rs_, ks_, vs_, pl, a_, g_, kk, t1, t2, k2, cp, ex1, ex2, bon, arbk, vbf, gc = X["t"]
                pc0 = p * 128
                load_raw(P, sst, raw["r"], "raw_r", dr["rT"], 128, pc0, a0, n, smp, ZR["r"] + pc0)
                load_raw(P, sst, raw["k"], "raw_k", dr["krT"], 128, pc0, a0, n, smp, ZR["kr"] + pc0)
                load_raw(P, sst, raw["v"], "raw_v", dr["vrT"], 128, pc0, a0, n, smp, ZR["vr"] + pc0)
                shift(P, tmp, rs_[:, 0:n], raw["r"], "raw_r", 128, n, smp, self.pcol("mu_r", p), pcd[:, 8 + p:9 + p], "rs_")
                shift(P, tmp, ks_[:, 0:n], raw["k"], "raw_k", 128, n, smp, self.pcol("mu_kr", p), pcd[:, 25 + p:26 + p], "ks_")
                shift(P, tmp, vs_[:, 0:n], raw["v"], "raw_v", 128, n, smp, self.pcol("mu_vr", p), pcd[:, 41 + p:42 + p], "vs_")
                pw, pwr = S.bank()
                P.pe(lambda e, pw=pw, p=p, n=n: e.matmul(pw[:, 0:n], lhsT=wdec[:, p * 128:(p + 1) * 128],
                                                         rhs=thb[:, 0:n], start=True, stop=True),
                     reads=["wdec", "thb"], writes=[pwr])
                P.dve(lambda e, pw=pw, p=p, n=n: e.tensor_scalar(out=t1[:, 0:n], in0=pw[:, 0:n],
                                                                 scalar1=self.pcol("w0", p), scalar2=-1.0,
                                                                 op0=ALU.add, op1=ALU.mult),
                      reads=[pwr], writes=["t1"])
                P.act(lambda e, n=n: e.activation(out=t1[:, 0:n], in_=t1[:, 0:n], func=AF.Exp), reads=["t1"],
                      writes=["t1"])
                P.act(lambda e, n=n: e.activation(out=t1[:, 0:n], in_=t1[:, 0:n], func=AF.Ln, bias=1.0),
                      reads=["t1"], writes=["t1"])
                P.act(lambda e, n=n: e.activation(out=pl[:, 0:n], in_=t1[:, 0:n], func=AF.Exp, scale=-1.0,
                                                  bias=self.kneg05()), reads=["t1"], writes=["pl"])
                pa, par = S.bank()
                P.pe(lambda e, pa=pa, p=p, n=n: e.matmul(pa[:, 0:n], lhsT=wicl[:, p * 128:(p + 1) * 128],
                                                         rhs=xab[:, 0:n], start=True, stop=True),
                     reads=["wicl", "xab"], writes=[par])
                P.act(lambda e, pa=pa, p=p, n=n: e.activation(out=a_[:, 0:n], in_=pa[:, 0:n], func=AF.Sigmoid,
                                                              bias=self.pcol("a0", p)), reads=[par], writes=["a_"])
                pg, pgr = S.bank()
                for kc in range(2):
                    P.pe(lambda e, pg=pg, p=p, n=n, kc=kc: e.matmul(
                        pg[:, 0:n], lhsT=wgat[:, kc, p * 128:(p + 1) * 128], rhs=sgx[:, kc, 0:n],
                        start=(kc == 0), stop=(kc == 1)), reads=["wgat", "sgx"], writes=[pgr])
                P.act(lambda e, pg=pg, n=n: e.activation(out=g_[:, 0:n], in_=pg[:, 0:n], func=AF.Copy),
                      reads=[pgr], writes=["g_"])
                P.dma(lambda e, pc0=pc0, a0=a0, n=n: e.dma_start(out=dr["gT"][pc0:pc0 + 128, a0:a0 + n],
                                                               in_=g_[:, 0:n]), S.key("sg"), reads=["g_"])
                P.pool(lambda e, p=p, n=n: e.tensor_scalar(out=kk[:, 0:n], in0=ks_[:, 0:n],
                                                           scalar1=self.pcol("k_k", p), scalar2=None,
                                                           op0=ALU.mult), reads=["ks_"], writes=["kk"])
                P.act(lambda e, n=n: e.activation(out=t2[:, 0:n], in_=kk[:, 0:n], func=AF.Square), reads=["kk"],
                      writes=["t2"])
                pq, pqr = S.bank()
                P.pe(lambda e, pq=pq, n=n: e.matmul(pq[:, 0:n], lhsT=blk64, rhs=t2[:, 0:n], start=True,
                                                     stop=True), reads=["t2"], writes=[pqr])
                P.dve(lambda e, pq=pq, n=n: e.tensor_scalar(out=t2[:, 0:n], in0=pq[:, 0:n], scalar1=1e-24,
                                                            scalar2=None, op0=ALU.max), reads=[pqr],
                      writes=["t2"])
                P.act(lambda e, n=n: e.activation(out=t2[:, 0:n], in_=t2[:, 0:n], func=AF.Sqrt), reads=["t2"],
                      writes=["t2"])
                P.dve(lambda e, n=n: e.reciprocal(out=t2[:, 0:n], in_=t2[:, 0:n]), reads=["t2"], writes=["t2"])
                P.dve(lambda e, n=n: e.tensor_tensor(out=kk[:, 0:n], in0=kk[:, 0:n], in1=t2[:, 0:n],
                                                     op=ALU.mult), reads=["kk", "t2"], writes=["kk"])
                P.dve(lambda e, p=p, n=n: e.tensor_scalar(out=t2[:, 0:n], in0=a_[:, 0:n],
                                                          scalar1=self.pcol("k_a", p), scalar2=self.pcol("k_a", p),
                                                          op0=ALU.mult, op1=ALU.subtract), reads=["a_"],
                      writes=["t2"])
                P.dve(lambda e, n=n: e.scalar_tensor_tensor(out=k2[:, 0:n], in0=t2[:, 0:n], scalar=1.0,
                                                            in1=ks_[:, 0:n], op0=ALU.add, op1=ALU.mult),
                      reads=["t2", "ks_"], writes=["k2"])
                P.dve(lambda e, p=p, n=n: e.scalar_tensor_tensor(out=t2[:, 0:n], in0=rs_[:, 0:n],
                                                                 scalar=self.pcol("r_k", p), in1=k2[:, 0:n],
                                                                 op0=ALU.mult, op1=ALU.mult),
                      reads=["rs_", "k2"], writes=["t2"])
                pb_, pbr = S.bank()
                P.pe(lambda e, pb_=pb_, n=n: e.matmul(pb_[:, 0:n], lhsT=blk64, rhs=t2[:, 0:n], start=True,
                                                       stop=True), reads=["t2"], writes=[pbr])
                P.dve(lambda e, pb_=pb_, n=n: e.tensor_tensor(out=bon[:, 0:n], in0=pb_[:, 0:n], in1=vs_[:, 0:n],
                                                              op=ALU.mult), reads=[pbr, "vs_"], writes=["bon"])
                P.dma(lambda e, pc0=pc0, a0=a0, n=n: e.dma_start(out=dr["bonT"][pc0:pc0 + 128, a0:a0 + n],
                                                               in_=bon[:, 0:n]), S.key("sb"), reads=["bon"])
                P.dve(lambda e, n=n, mi=mi: e.tensor_tensor_scan(out=cp[:, 0:n], data0=rmask[:, mi, 0:n],
                                                                 data1=pl[:, 0:n], initial=0.0, op0=ALU.mult,
                                                                 op1=ALU.add), reads=["pl", "rmask"], writes=["cp"])
                P.act(lambda e, n=n: e.activation(out=ex1[:, 0:n], in_=cp[:, 0:n], func=AF.Exp, scale=-1.0),
                      reads=["cp"], writes=["ex1"])
                P.dve(lambda e, n=n: e.tensor_tensor(out=arbk[:, 1, 0:n], in0=rs_[:, 0:n], in1=ex1[:, 0:n],
                                                     op=ALU.mult), reads=["rs_", "ex1"], writes=["arbk"])
                L = 8 if smp else 128
                ns = n // L
                P.pool(lambda e, n=n, L=L, ns=ns: e.tensor_copy(
                    out=gc[:, 0:ns], in_=ex1[:, 0:n].rearrange("p (s j) -> p s j", j=L)[:, :, L - 1]),
                    reads=["ex1"], writes=["gc"])
                gc0 = (c.NCT if smp else a0 // 128)
                P.dma(lambda e, p=p, gc0=gc0, ns=ns: e.dma_start(out=dr["gcT"][:, p, gc0:gc0 + ns],
                                                               in_=gc[:, 0:ns]), S.key("sc"), reads=["gc"])
                P.pool(lambda e, n=n: e.tensor_tensor(out=t2[:, 0:n], in0=pl[:, 0:n], in1=cp[:, 0:n],
                                                      op=ALU.subtract), reads=["pl", "cp", ], writes=["t2"])
                P.act(lambda e, n=n: e.activation(out=ex2[:, 0:n], in_=t2[:, 0:n], func=AF.Exp), reads=["t2"],
                      writes=["ex2"])
                P.dve(lambda e, n=n: e.scalar_tensor_tensor(out=arbk[:, 0, 0:n], in0=kk[:, 0:n], scalar=-1.0,
                                                            in1=ex2[:, 0:n], op0=ALU.mult, op1=ALU.mult),
                      reads=["kk", "ex2"], writes=["arbk"])
                P.act(lambda e, n=n: e.activation(out=ex2[:, 0:n], in_=cp[:, 0:n], func=AF.Exp), reads=["cp"],
                      writes=["ex2"])
                P.pool(lambda e, n=n: e.tensor_tensor(out=t2[:, 0:n], in0=kk[:, 0:n], in1=a_[:, 0:n],
                                                      op=ALU.mult), reads=["kk", "a_"], writes=["t2"])
                P.dve(lambda e, n=n: e.tensor_tensor(out=arbk[:, 2, 0:n], in0=t2[:, 0:n], in1=ex2[:, 0:n],
                                                     op=ALU.mult), reads=["t2", "ex2"], writes=["arbk"])
                P.dve(lambda e, n=n: e.tensor_tensor(out=arbk[:, 3, 0:n], in0=k2[:, 0:n], in1=ex2[:, 0:n],
                                                     op=ALU.mult), reads=["k2", "ex2"], writes=["arbk"])
                P.act(lambda e, n=n: e.activation(out=vbf[:, 0:n], in_=vs_[:, 0:n], func=AF.Copy), reads=["vs_"],
                      writes=["vbf"])
                P.dma(lambda e, p=p, a0=a0, n=n: e.dma_start(out=dr["arbk"][:, p, :, a0:a0 + n],
                                                           in_=arbk[:, :, 0:n]), S.key("sa"), reads=["arbk"])
                P.dma(lambda e, p=p, a0=a0, n=n: e.dma_start(out=dr["vTb"][:, p, a0:a0 + n], in_=vbf[:, 0:n]),
                      S.key("sv"), reads=["vbf"])
                return P

            ZR = {"r": 0, "xw": 2048, "kr": 2144, "vr": 4192, "xa": 6240, "xg": 6336}
            chunk_i = 0
            for (a0, n, smp) in blocks():
                mi = 1 if smp else 0
                load_raw(PS0, sst, rawx["xw"], "rawx_xw", dr["xwT"], 96, 0, a0, n, smp, ZR["xw"])
                shift(PS0, tmp, xsh[0:96, 0:n], rawx["xw"], "rawx_xw", 96, n, smp, self.pcol("mu_xw")[0:96], pcd[0:96, 24:25], "xsh")
                P.act(lambda e, n=n: e.activation(out=thb[:, 0:n], in_=xsh[0:96, 0:n], func=AF.Tanh), reads=["xsh"],
                      writes=["thb"])
                load_raw(PS0, sst, rawx["xa"], "rawx_xa", dr["xaT"], 96, 0, a0, n, smp, ZR["xa"])
                shift(PS0, tmp, xsh[0:96, 0:n], rawx["xa"], "rawx_xa", 96, n, smp, self.pcol("mu_xa")[0:96], pcd[0:96, 57:58], "xsh")
                P.act(lambda e, n=n: e.activation(out=xab[:, 0:n], in_=xsh[0:96, 0:n], func=AF.Copy), reads=["xsh"],
                      writes=["xab"])
                for j in range(2):
                    nm = "xg%d" % j
                    load_raw(PS0, sst, rawx[nm], "rawx_" + nm, dr["xgT"], 128, j * 128, a0, n, smp, ZR["xg"] + j * 128)
                    shift(PS0, tmp, xsh[:, 0:n], rawx[nm], "rawx_" + nm, 128, n, smp, self.pcol("mu_xg", j), pcd[:, 58 + j:59 + j],
                          "xsh")
                    P.act(lambda e, n=n, j=j: e.activation(out=sgx[:, j, 0:n], in_=xsh[:, 0:n], func=AF.Sigmoid),
                          reads=["xsh"], writes=["sgx"])
                for p in range(0, 16, 2):
                    interleave([pair(p, a0, n, smp, mi), pair(p + 1, a0, n, smp, mi)])

    def rscan_stage(self, smp):
        c = self.cfg
        dr = self.dr
        NP = 4 if smp else 8
        NH = 2 * NP
        nseg = 16 if smp else 1
        L = 128 // nseg
        sfx = "s" if smp else "p"
        with Stage(self, "rs" + sfx) as S:
            P = S.P
            S.psum_banks(7)
            ptr = S.st.enter_context(self.nc.psum_tensor("rs%s_ptr" % sfx, [128, 8, 128], BF16))
            ARBK = S.sb("ARBK", [128, NP, 4, 128], BF16)
            AR = ARBK[:, :, 0:2, :]
            BK = ARBK[:, :, 2:4, :]
            vT = S.sb("vT", [128, NP, 128], BF16)
            tmB = S.sb("tmB", [128, NP, 128], BF16)
            tmK = S.sb("tmK", [128, NP, 128], BF16)
            tmV = S.sb("tmV", [128, NP, 128], BF16)
            Vpad = S.sb("Vpad", [128, 2, NP, 128], BF16)
            Upad = S.sb("Upad", [128, 2, NP, 128], BF16)
            mats = S.sb("mats", [128, NH, 4, 128], BF16)
            Nb = [S.sb("N%d" % i, [128, NH, 128], BF16) for i in range(2)]
            NTb = [S.sb("NT%d" % i, [128, NH, 128], BF16) for i in range(2)]
            MTb = [S.sb("MT%d" % i, [128, NH, 128], BF16) for i in range(2)]
            maskM = S.sb("maskM", [128, 4, 128], F32)
            T32 = S.sb("T32", [128, nseg, NP, 128], F32)
            Tbf = S.sb("Tbf", [128, nseg, NP, 128], BF16)
            LV = S.sb("LV", [128, NP, 128], F32)
            Xb = S.sb("Xb", [128, NP, 128], BF16)
            Ub = S.sb("Ub", [128, NP, 128], BF16)
            gcs = S.sb("gcs", [128, NP, c.NCT + 16], F32)
            tmpT = S.sb("tmpT", [128, 4, 128], F32)
            o32 = S.sb("o32", [128, NP, 128], F32)
            cen = S.sb("cen", [128, NP, 128], F32)
            sqv = S.sb("sqv", [128, NP, 128], F32)
            rsd = S.sb("rsd", [128, NP, 128], F32)
            bon = S.sb("bon", [128, NP, 128], F32)
            gg = S.sb("gg", [128, NP, 128], F32)
            obb = S.sb("obb", [128, NP, 128], BF16)
            if smp:
                Am = S.sb("Am", [128, NP, 16, 128], BF16)
                Bm = S.sb("Bm", [128, NP, 16, 128], BF16)
                Km = S.sb("Km", [128, NP, 16, 128], BF16)
            identb = self.cst("ident", True)
            blk64 = self.cst("blk64")
            blk64b = self.cst("blk64", True)
            strT = self.cst("strT_" + sfx)
            incT = self.cst("incT_" + sfx)
            strN = self.cst("str_" + sfx)
            for i_, m_ in enumerate((strT, incT, strT, incT)):
                P.pool(lambda e, i_=i_, m_=m_: e.tensor_copy(out=maskM[:, i_, :], in_=m_), writes=["maskM"])
            zer = S.sb("zer", [128, 512], BF16)
            P.pool(lambda e: e.memset(zer[:], 0.0), writes=["zer"])
            P.pool(lambda e: e.memset(Vpad[:], 0.0), writes=["Vpad"])
            P.pool(lambda e: e.memset(Upad[:], 0.0), writes=["Upad"])
            RP = [0, 1, 2]
            OB = [3, 4, 5, 6]
            levels = []
            pw_ = 2
            while pw_ < L:
                levels.append(pw_)
                pw_ *= 2
            work = ([(t, p0) for p0 in range(0, 16, NP) for t in range(c.NCT)] if not smp
                    else [(c.NCT, p0) for p0 in range(0, 16, NP)])
            for (tile, p0) in work:
                t0 = tile * 128
                need_out = t0 >= c.HALO0
                if not smp and tile == 0:
                    P.pool(lambda e: e.memset(T32[:], 0.0), writes=["T32"])
                    P.pool(lambda e: e.memset(Tbf[:], 0.0), writes=["Tbf"])
                P.dma(lambda e, t0=t0, p0=p0: e.dma_start(
                    out=ARBK[:].rearrange("p q f t -> p (q f) t"),
                    in_=dr["arbk"][:, p0:p0 + NP, :, t0:t0 + 128].rearrange("p q f t -> p (q f) t")),
                    S.key("l1"), writes=["AR", "BK"])
                P.dma(lambda e, t0=t0, p0=p0: e.dma_start(out=vT[:], in_=dr["vTb"][:, p0:p0 + NP, t0:t0 + 128]),
                      S.key("l2"), writes=["vT"])
                gq0 = c.NCT if smp else tile
                if smp or tile == 0:
                    P.dma(lambda e, p0=p0: e.dma_start(out=gcs[:], in_=dr["gcT"][:, p0:p0 + NP, :]),
                          S.key("l3"), writes=["gcs"])
                if smp:
                    for sg_ in range(16):
                        P.dma(lambda e, sg_=sg_, p0=p0: e.dma_start(out=T32[:, sg_, :, :],
                                                                   in_=dr["srwkv"][sg_][:, p0:p0 + NP, :]),
                              S.key("l4"), writes=["T32"])
                    P.act(lambda e: e.activation(out=Tbf[:], in_=T32[:], func=AF.Copy), reads=["T32"], writes=["Tbf"])
                for (src, fi, dst, dn) in ((BK, 0, tmB, "tmB"), (BK, 1, tmK, "tmK"), (vT, None, tmV, "tmV")):
                    if RS_CUT is not None and RS_CUT <= -2:
                        continue
                    for g0 in range(0, NP, 8):
                        gn = min(8, NP - g0)
                        for q in range(gn):
                            in_ = src[:, g0 + q, fi, :] if fi is not None else src[:, g0 + q, :]
                            P.pe(lambda e, q=q, in_=in_: e.transpose(ptr[:, q, :], in_, identb),
                                 reads=["BK" if fi is not None else "vT"], writes=["ptr"])
                        P.act(lambda e, dst=dst, g0=g0, gn=gn: e.activation(out=dst[:, g0:g0 + gn, :],
                                                                            in_=ptr[:, 0:gn, :], func=AF.Copy),
                              reads=["ptr"], writes=[dn])
                        if dst is tmV:
                            P.pool(lambda e, g0=g0, gn=gn: e.tensor_copy(out=Vpad[:, 0, g0:g0 + gn, 0:64],
                                                                        in_=tmV[:, g0:g0 + gn, 0:64]), reads=["tmV"],
                                   writes=["Vpad"])
                            P.pool(lambda e, g0=g0, gn=gn: e.tensor_copy(out=Vpad[:, 1, g0:g0 + gn, 64:128],
                                                                        in_=tmV[:, g0:g0 + gn, 64:128]), reads=["tmV"],
                                   writes=["Vpad"])
                if RS_CUT is not None and RS_CUT < 1:
                    continue
                for hh in range(NH):
                    if "m" in RS_SUB:
                        break
                    q, h = hh // 2, hh % 2
                    pm, pmr = S.bank(RP)
                    rr_ = slice(64 * h, 64 * h + 64)
                    P.pe(lambda e, pm=pm, q=q, rr_=rr_: e.matmul(
                        pm[:, 0:256], lhsT=BK[rr_, q, 0, :], rhs=AR[rr_, q, :, :].rearrange("p f t -> p (f t)"),
                        start=True, stop=True), reads=["BK", "AR"], writes=[pmr])
                    P.pe(lambda e, pm=pm, q=q, rr_=rr_: e.matmul(
                        pm[:, 256:512], lhsT=BK[rr_, q, 1, :], rhs=AR[rr_, q, :, :].rearrange("p f t -> p (f t)"),
                        start=True, stop=True), reads=["BK", "AR"], writes=[pmr])
                    if "e" in RS_SUB:
                        continue
                    P.dve(lambda e, pm=pm, hh=hh: e.tensor_tensor(
                        out=mats[:, hh, :, :], in0=pm[:].rearrange("p (f t) -> p f t", t=128), in1=maskM[:],
                        op=ALU.mult), reads=[pmr, "maskM"], writes=["mats"])
                for q0 in range(0, NP, 4):
                    for h in range(2):
                        pl_, plr = S.bank(RP)
                        rr_ = slice(64 * h, 64 * h + 64)
                        for j in range(4):
                            q = q0 + j
                            P.pe(lambda e, pl_=pl_, j=j, q=q, rr_=rr_: e.matmul(
                                pl_[:, j * 128:(j + 1) * 128], lhsT=AR[rr_, q, 0, :], rhs=BK[rr_, q, 0, :], start=True,
                                stop=True), reads=["AR", "BK"], writes=[plr])
                        P.dve(lambda e, pl_=pl_, q0=q0, h=h: e.tensor_tensor(
                            out=Nb[0][:].rearrange("p (q h) t -> p q h t", h=2)[:, q0:q0 + 4, h, :],
                            in0=pl_[:].rearrange("p (j t) -> p j t", t=128),
                            in1=strN.unsqueeze(1).to_broadcast([128, 4, 128]), op=ALU.mult), reads=[plr],
                            writes=["N0"])
                if RS_CUT is not None and RS_CUT < 2:
                    continue
                P.pool(lambda e: e.tensor_copy(out=NTb[0][:], in_=mats[:, :, 0, :]), reads=["mats"], writes=["NT0"])
                P.pool(lambda e: e.tensor_tensor(out=MTb[0][:], in0=mats[:, :, 0, :],
                                                 in1=identb.unsqueeze(1).to_broadcast([128, NH, 128]), op=ALU.add),
                       reads=["mats"], writes=["MT0"])
                cur = 0
                for li, lv in enumerate(levels):
                    last = li == len(levels) - 1
                    nxt = 1 - cur
                    for g0 in range(0, NH, 4):
                        pn, pnr = S.bank(RP)
                        for j in range(4):
                            P.pe(lambda e, pn=pn, j=j, g0=g0, cur=cur: e.matmul(
                                pn[:, j * 128:(j + 1) * 128], lhsT=NTb[cur][:, g0 + j, :], rhs=Nb[cur][:, g0 + j, :],
                                start=True, stop=True), reads=["N%d" % cur, "NT%d" % cur], writes=[pnr])
                        P.act(lambda e, pn=pn, g0=g0, nxt=nxt: e.activation(
                            out=Nb[nxt][:, g0:g0 + 4, :], in_=pn[:].rearrange("p (j t) -> p j t", t=128),
                            func=AF.Copy), reads=[pnr], writes=["N%d" % nxt])
                        if not last:
                            pt_, ptr_ = S.bank(RP)
                            for j in range(4):
                                P.pe(lambda e, pt_=pt_, j=j, g0=g0, cur=cur: e.matmul(
                                    pt_[:, j * 128:(j + 1) * 128], lhsT=Nb[cur][:, g0 + j, :],
                                    rhs=NTb[cur][:, g0 + j, :], start=True, stop=True),
                                    reads=["N%d" % cur, "NT%d" % cur], writes=[ptr_])
                            P.act(lambda e, pt_=pt_, g0=g0, nxt=nxt: e.activation(
                                out=NTb[nxt][:, g0:g0 + 4, :], in_=pt_[:].rearrange("p (j t) -> p j t", t=128),
                                func=AF.Copy), reads=[ptr_], writes=["NT%d" % nxt])
                        pp, ppr = S.bank(RP)
                        for j in range(4):
                            P.pe(lambda e, pp=pp, j=j, g0=g0, cur=cur, nxt=nxt: e.matmul(
                                pp[:, j * 128:(j + 1) * 128], lhsT=Nb[nxt][:, g0 + j, :], rhs=MTb[cur][:, g0 + j, :],
                                start=True, stop=False), reads=["N%d" % nxt, "MT%d" % cur], writes=[ppr])
                            P.pe(lambda e, pp=pp, j=j, g0=g0, cur=cur: e.matmul(
                                pp[:, j * 128:(j + 1) * 128], lhsT=identb, rhs=MTb[cur][:, g0 + j, :],
                                start=False, stop=True), reads=["MT%d" % cur], writes=[ppr])
                        P.act(lambda e, pp=pp, g0=g0, nxt=nxt: e.activation(
                            out=MTb[nxt][:, g0:g0 + 4, :], in_=pp[:].rearrange("p (j t) -> p j t", t=128),
                            func=AF.Copy), reads=[ppr], writes=["MT%d" % nxt])
                    cur = nxt
                MT = MTb[cur]
                mtn = "MT%d" % cur
                if RS_CUT is not None and RS_CUT < 3:
                    continue
                if smp:
                    segcol = self.c32x
                    seg16 = self.cst("seg16", True)
                    for q in range(NP):
                        P.dve(lambda e, q=q: e.tensor_tensor(
                            out=Am[:, q, :, :], in0=AR[:, q, 0, :].unsqueeze(1).to_broadcast([128, 16, 128]),
                            in1=segcol[:], op=ALU.mult), reads=["AR"], writes=["Am"])
                        P.pool(lambda e, q=q: e.tensor_tensor(
                            out=Bm[:, q, :, :], in0=tmB[:, q, :].unsqueeze(1).to_broadcast([128, 16, 128]),
                            in1=seg16[:, 0:16].unsqueeze(2).to_broadcast([128, 16, 128]), op=ALU.mult),
                            reads=["tmB"], writes=["Bm"])
                        P.pool(lambda e, q=q: e.tensor_tensor(
                            out=Km[:, q, :, :], in0=tmK[:, q, :].unsqueeze(1).to_broadcast([128, 16, 128]),
                            in1=seg16[:, 0:16].unsqueeze(2).to_broadcast([128, 16, 128]), op=ALU.mult),
                            reads=["tmK"], writes=["Km"])
                if RS_CUT is not None and RS_CUT < 4:
                    continue
                if need_out:
                    for q in range(NP):
                        po, por = S.banks[OB[q // 4]], "pb%d" % OB[q // 4]
                        osl = slice((q % 4) * 128, (q % 4) * 128 + 128)
                        if q % 4 == 0:
                            P.pe(lambda e, po=po: e.matmul(po[:, :], lhsT=zer[:, 0:128], rhs=zer[:, :], start=True,
                                                           stop=False), reads=["zer"], writes=[por])
                        for h in range(2):
                            P.pe(lambda e, po=po, osl=osl, q=q, h=h: e.matmul(
                                po[:, osl], lhsT=Vpad[:, h, q, :], rhs=mats[:, 2 * q + h, 3, :], start=False,
                                stop=False), reads=["Vpad", "mats"], writes=[por])
                        for sg_ in range(nseg):
                            P.pe(lambda e, po=po, q=q, sg_=sg_: e.matmul(
                                po[:, (q % 4) * 128 + sg_ * L:(q % 4) * 128 + (sg_ + 1) * L], lhsT=Tbf[:, sg_, q, :],
                                rhs=AR[:, q, 1, sg_ * L:(sg_ + 1) * L], start=False, stop=False),
                                reads=["Tbf", "AR"], writes=[por])
                if RS_CUT is not None and RS_CUT < 5:
                    continue
                for g0 in range(0, NP, 4):
                    pv, pvr = S.bank(RP)
                    for j in range(4):
                        q = g0 + j
                        for h in range(2):
                            P.pe(lambda e, pv=pv, j=j, q=q, h=h: e.matmul(
                                pv[:, j * 128 + 64 * h:j * 128 + 64 * h + 64], lhsT=mats[:, 2 * q + h, 2, :],
                                rhs=tmV[:, q, 64 * h:64 * h + 64], start=True, stop=True),
                                reads=["mats", "tmV"], writes=[pvr])
                    P.act(lambda e, pv=pv, g0=g0: e.activation(out=LV[:, g0:g0 + 4, :],
                                                               in_=pv[:].rearrange("p (j t) -> p j t", t=128),
                                                               func=AF.Copy), reads=[pvr], writes=["LV"])
                if RS_CUT is not None and RS_CUT < 6:
                    continue
                for g0 in range(0, NP, 4):
                    px, pxr = S.bank(RP)
                    for j in range(4):
                        q = g0 + j
                        for sg_ in range(nseg):
                            lh = Am[:, q, sg_, :] if smp else AR[:, q, 0, :]
                            P.pe(lambda e, px=px, j=j, q=q, sg_=sg_, lh=lh: e.matmul(
                                px[:, j * 128:(j + 1) * 128], lhsT=lh, rhs=Tbf[:, sg_, q, :], start=(sg_ == 0),
                                stop=(sg_ == nseg - 1)), reads=["Am" if smp else "AR", "Tbf"], writes=[pxr])
                    P.dve(lambda e, px=px, g0=g0: e.tensor_tensor(
                        out=Xb[:, g0:g0 + 4, :], in0=px[:].rearrange("p (j t) -> p j t", t=128),
                        in1=LV[:, g0:g0 + 4, :], op=ALU.add), reads=[pxr, "LV"], writes=["Xb"])
                if RS_CUT is not None and RS_CUT < 7:
                    continue
                for g0 in range(0, NP, 4):
                    pu, pur = S.bank(RP)
                    for j in range(4):
                        q = g0 + j
                        for h in range(2):
                            P.pe(lambda e, pu=pu, j=j, q=q, h=h: e.matmul(
                                pu[:, j * 128 + 64 * h:j * 128 + 64 * h + 64], lhsT=MT[:, 2 * q + h, :],
                                rhs=Xb[:, q, 64 * h:64 * h + 64], start=True, stop=True), reads=[mtn, "Xb"],
                                writes=[pur])
                    puv = pu[:].rearrange("p (j t) -> p j t", t=128)
                    P.act(lambda e, puv=puv, g0=g0: e.activation(out=Ub[:, g0:g0 + 4, :], in_=puv, func=AF.Copy),
                          reads=[pur], writes=["Ub"])
                    if need_out:
                        P.pool(lambda e, g0=g0: e.tensor_copy(out=Upad[:, 0, g0:g0 + 4, 0:64],
                                                              in_=Ub[:, g0:g0 + 4, 0:64]), reads=["Ub"],
                               writes=["Upad"])
                        P.pool(lambda e, g0=g0: e.tensor_copy(out=Upad[:, 1, g0:g0 + 4, 64:128],
                                                              in_=Ub[:, g0:g0 + 4, 64:128]), reads=["Ub"],
                               writes=["Upad"])
                if RS_CUT is not None and RS_CUT < 8:
                    continue
                if need_out:
                    for q in range(NP):
                        po, por = S.banks[OB[q // 4]], "pb%d" % OB[q // 4]
                        osl = slice((q % 4) * 128, (q % 4) * 128 + 128)
                        for h in range(2):
                            P.pe(lambda e, po=po, osl=osl, q=q, h=h: e.matmul(
                                po[:, osl], lhsT=Upad[:, h, q, :], rhs=mats[:, 2 * q + h, 1, :], start=False,
                                stop=(h == 1 and q % 4 == 3)), reads=["Upad", "mats"], writes=[por])
                if RS_CUT is not None and RS_CUT < 9:
                    continue
                for sg_ in range(nseg):
                    for g0 in range(0, NP, 4):
                        pt2, pt2r = S.bank(RP)
                        for j in range(4):
                            q = g0 + j
                            lb = Bm[:, q, sg_, :] if smp else tmB[:, q, :]
                            lk = Km[:, q, sg_, :] if smp else tmK[:, q, :]
                            P.pe(lambda e, pt2=pt2, j=j, q=q, lb=lb: e.matmul(
                                pt2[:, j * 128:(j + 1) * 128], lhsT=lb, rhs=Ub[:, q, :], start=True, stop=False),
                                reads=["Bm" if smp else "tmB", "Ub"], writes=[pt2r])
                            P.pe(lambda e, pt2=pt2, j=j, q=q, lk=lk: e.matmul(
                                pt2[:, j * 128:(j + 1) * 128], lhsT=lk, rhs=tmV[:, q, :], start=False, stop=True),
                                reads=["Km" if smp else "tmK", "tmV"], writes=[pt2r])
                        P.dve(lambda e, pt2=pt2: e.tensor_tensor(
                            out=tmpT[:], in0=pt2[:].rearrange("p (j t) -> p j t", t=128),
                            in1=blk64.unsqueeze(1).to_broadcast([128, 4, 128]), op=ALU.mult), reads=[pt2r],
                            writes=["tmpT"])
                        P.dve(lambda e, sg_=sg_, g0=g0: e.tensor_tensor(out=tmpT[:], in0=tmpT[:],
                                                                       in1=T32[:, sg_, g0:g0 + 4, :], op=ALU.add),
                              reads=["tmpT", "T32"], writes=["tmpT"])
                        P.dve(lambda e, sg_=sg_, g0=g0, gq0=gq0: e.tensor_tensor(
                            out=T32[:, sg_, g0:g0 + 4, :], in0=tmpT[:],
                            in1=gcs[:, g0:g0 + 4, gq0 + sg_:gq0 + sg_ + 1].to_broadcast([128, 4, 128]), op=ALU.mult),
                            reads=["tmpT", "gcs"], writes=["T32"])
                if not smp:
                    P.act(lambda e: e.activation(out=Tbf[:], in_=T32[:], func=AF.Copy), reads=["T32"], writes=["Tbf"])
                    if tile == c.NCT - 1:
                        P.dma(lambda e, p0=p0: e.dma_start(out=dr["rwkvp"][:, p0:p0 + NP, :], in_=T32[:, 0, :, :]),
                              S.key("so"), reads=["T32"])
                else:
                    for sg_ in range(16):
                        P.dma(lambda e, sg_=sg_, p0=p0: e.dma_start(out=dr["rwkvs"][sg_][:, p0:p0 + NP, :],
                                                                   in_=T32[:, sg_, :, :]), S.key("so"),
                              reads=["T32"])
                if not need_out:
                    continue
                if RS_CUT is not None and RS_CUT < 11:
                    continue
                P.dma(lambda e, t0=t0, p0=p0: e.dma_start(
                    out=bon[:], in_=dr["bonT"][p0 * 128:(p0 + NP) * 128, t0:t0 + 128].rearrange("(q p) t -> p q t",
                                                                                             p=128)),
                    S.key("l5"), writes=["bon"])
                P.dma(lambda e, t0=t0, p0=p0: e.dma_start(
                    out=gg[:], in_=dr["gT"][p0 * 128:(p0 + NP) * 128, t0:t0 + 128].rearrange("(q p) t -> p q t",
                                                                                           p=128)),
                    S.key("l6"), writes=["gg"])
                for g0 in range(0, NP, 4):
                    po, por = S.banks[OB[g0 // 4]], "pb%d" % OB[g0 // 4]
                    P.act(lambda e, po=po, g0=g0: e.activation(out=o32[:, g0:g0 + 4, :],
                                                               in_=po[:].rearrange("p (j t) -> p j t", t=128),
                                                               func=AF.Copy), reads=[por], writes=["o32"])
                    pm1, pm1r = S.bank(RP)
                    P.pe(lambda e, pm1=pm1, g0=g0: e.matmul(pm1[:, :], lhsT=blk64,
                                                            rhs=o32[:, g0:g0 + 4, :].rearrange("p j t -> p (j t)"),
                                                            start=True, stop=True), reads=["o32"], writes=[pm1r])
                    P.dve(lambda e, pm1=pm1, g0=g0: e.scalar_tensor_tensor(
                        out=cen[:, g0:g0 + 4, :], in0=pm1[:].rearrange("p (j t) -> p j t", t=128), scalar=-1.0 / 64,
                        in1=o32[:, g0:g0 + 4, :], op0=ALU.mult, op1=ALU.add), reads=[pm1r, "o32"], writes=["cen"])
                    P.act(lambda e, g0=g0: e.activation(out=sqv[:, g0:g0 + 4, :], in_=cen[:, g0:g0 + 4, :],
                                                        func=AF.Square), reads=["cen"], writes=["sqv"])
                    pm2, pm2r = S.bank(RP)
                    P.pe(lambda e, pm2=pm2, g0=g0: e.matmul(pm2[:, :], lhsT=blk64,
                                                            rhs=sqv[:, g0:g0 + 4, :].rearrange("p j t -> p (j t)"),
                                                            start=True, stop=True), reads=["sqv"], writes=[pm2r])
                    P.act(lambda e, pm2=pm2, g0=g0: e.activation(
                        out=rsd[:, g0:g0 + 4, :], in_=pm2[:].rearrange("p (j t) -> p j t", t=128), func=AF.Sqrt,
                        scale=1.0 / 64, bias=GN_EPS), reads=[pm2r], writes=["rsd"])
                    P.dve(lambda e, g0=g0: e.reciprocal(out=rsd[:, g0:g0 + 4, :], in_=rsd[:, g0:g0 + 4, :]),
                          reads=["rsd"], writes=["rsd"])
                    P.dve(lambda e, g0=g0: e.tensor_tensor(out=cen[:, g0:g0 + 4, :], in0=cen[:, g0:g0 + 4, :],
                                                           in1=rsd[:, g0:g0 + 4, :], op=ALU.mult),
                          reads=["cen", "rsd"], writes=["cen"])
                    for j in range(4):
                        q = g0 + j
                        P.pool(lambda e, q=q, p0=p0: e.tensor_scalar(
                            out=cen[:, q, :], in0=cen[:, q, :], scalar1=self.pcol("ln_x_w", p0 + q),
                            scalar2=self.pcol("ln_x_b", p0 + q), op0=ALU.mult, op1=ALU.add), reads=["cen"],
                            writes=["cen"])
                    P.dve(lambda e, g0=g0: e.tensor_tensor(out=cen[:, g0:g0 + 4, :], in0=cen[:, g0:g0 + 4, :],
                                                           in1=bon[:, g0:g0 + 4, :], op=ALU.add),
                          reads=["cen", "bon"], writes=["cen"])
                    P.dve(lambda e, g0=g0: e.tensor_tensor(out=obb[:, g0:g0 + 4, :], in0=cen[:, g0:g0 + 4, :],
                                                           in1=gg[:, g0:g0 + 4, :], op=ALU.mult),
                          reads=["cen", "gg"], writes=["obb"])
                P.dma(lambda e, t0=t0, p0=p0: e.dma_start(
                    out=dr["obT"][p0 * 128:(p0 + NP) * 128, t0:t0 + 128].rearrange("(q p) t -> p q t", p=128),
                    in_=obb[:]), S.key("so2"), reads=["obb"])

    def lin_stage(self, name, W, KC, kparts, a_src, t0, nt, chunks, epi, a_cast=False, extra=None, wblk=512):
        with Stage(self, name) as S:
            S.psum_banks(8)
            if a_cast:
                aT = S.sb("aT", [128, KC, nt], BF16)
                S.P.dma(lambda e: e.dma_start(out=aT[:], in_=a_src[:, t0:t0 + nt].rearrange("(k p) t -> p k t", p=128)),
                        S.key("aT"), writes=["aT.0"], q="pool")
            else:
                aT = self.load_aT(S, "aT", a_src, KC, t0, nt, kparts)
            S.stg = [S.sb("stg%d" % i, [128, nt], F32) for i in range(3)]
            S.side = {}
            S.cnt = 0
            if extra:
                extra(S)

            def evac(ci, c0, cw, pss):
                si = S.cnt % 3
                S.cnt += 1
                epi(S, ci, c0, cw, pss, S.stg[si], "stg%d" % si)

            self.fm_linear(S, W, KC, kparts, chunks, aT, "aT", nt, evac, wblk=wblk)

    def side_load(self, S, tag, src, r0, cw, t0, nt, dt=F32):
        if tag not in S.side:
            S.side[tag] = [[S.sb("sd%s%d" % (tag, i), [128, nt], dt) for i in range(2)], 0]
        bufs, k = S.side[tag]
        S.side[tag][1] = k + 1
        t = bufs[k % 2]
        rn = "sd%s%d" % (tag, k % 2)
        S.P.dma(lambda e: e.dma_start(out=t[0:cw, :], in_=src[r0:r0 + cw, t0:t0 + nt]), S.key(rn), writes=[rn])
        return t, rn

    def post_stages(self):
        c = self.cfg
        dr = self.dr
        T0, NT = c.HALO0, c.TPM
        ch32 = [(j * 128, 128) for j in range(32)]

        def epi_a(S, ci, c0, cw, pss, stg, sn):
            ga, gn = self.side_load(S, "ga", dr["gaT"], c0, cw, T0, NT)
            for (ps, pr, a0, n) in pss:
                S.P.dve(lambda e, ps=ps, a0=a0, n=n: e.tensor_tensor(out=stg[0:cw, a0:a0 + n], in0=ps,
                                                                     in1=ga[0:cw, a0:a0 + n], op=ALU.mult),
                        reads=[pr, gn], writes=[sn])
            S.P.dma(lambda e: e.dma_start(out=dr["yoT"][c0:c0 + cw, T0:T0 + NT], in_=stg[0:cw, :]), S.key("st"),
                    reads=[sn])

        if self.want("bra"):
            self.lin_stage("bra", dr["w_branch_a"], 16, 128, dr["oaT"], T0, NT, ch32, epi_a)

        def epi_b(S, ci, c0, cw, pss, stg, sn):
            gb, gn = self.side_load(S, "gb", dr["gbT"], c0, cw, T0, NT)
            ma, mn = self.side_load(S, "ma", dr["yoT"], c0, cw, T0, NT)
            mb, mbn = S.mixb[S.cnt % 2], "mixb%d" % (S.cnt % 2)
            for (ps, pr, a0, n) in pss:
                S.P.dve(lambda e, ps=ps, a0=a0, n=n: e.tensor_tensor(out=stg[0:cw, a0:a0 + n], in0=ps,
                                                                     in1=gb[0:cw, a0:a0 + n], op=ALU.mult),
                        reads=[pr, gn], writes=[sn])
            S.P.pool(lambda e: e.tensor_tensor(out=mb[0:cw, :], in0=stg[0:cw, :], in1=ma[0:cw, :], op=ALU.add),
                     reads=[sn, mn], writes=[mbn])
            S.P.dma(lambda e: e.dma_start(out=dr["mixT"][c0:c0 + cw, T0:T0 + NT], in_=mb[0:cw, :]), S.key("st"),
                    reads=[mbn])

        def extra_b(S):
            S.mixb = [S.sb("mixb%d" % i, [128, NT], BF16) for i in range(2)]

        if self.want("brb"):
            self.lin_stage("brb", dr["w_branch_b"], 16, 128, dr["obT"], T0, NT, ch32, epi_b, extra=extra_b)

        def epi_store(dst, tofs, ntk):
            def epi(S, ci, c0, cw, pss, stg, sn):
                for (ps, pr, a0, n) in pss:
                    self.copy_ps(S, stg[0:cw, a0:a0 + n], ps, pr, sn)
                S.P.dma(lambda e: e.dma_start(out=dst[c0:c0 + cw, tofs:tofs + ntk], in_=stg[0:cw, :]),
                        S.key("st"), reads=[sn])
            return epi

        if self.want("wout"):
            self.lin_stage("wo", dr["w_out"], 32, 128, dr["mixT"], T0, NT, ch32, epi_store(dr["yoT"], T0, NT))
        if self.want("norm2"):
            self.norm_stage("n2", dr["yoT"], T0, NT, "g_post_mix", dr["hfT"], res=dr["xT"], out_x=dr["x1T"],
                            gain2="g_pre_ffn")

        nfc = DFF // 128

        def epi_up(S, ci, c0, cw, pss, stg, sn):
            isg = c0 < DFF
            dst = dr["ugT"] if isg else dr["uvT"]
            r0 = c0 if isg else c0 - DFF
            for (ps, pr, a0, n) in pss:
                self.copy_ps(S, stg[0:cw, a0:a0 + n], ps, pr, sn)
            S.P.dma(lambda e: e.dma_start(out=dst[r0:r0 + cw, T0:T0 + NT], in_=stg[0:cw, :]), S.key("st"), reads=[sn])
            if isg:
                nctx = c.CTX - T0
                S.P.dma(lambda e: e.dma_start(out=dr["convTp"][r0:r0 + cw, :], in_=stg[0:cw, nctx - 2:nctx]),
                        S.key("st"), reads=[sn])
                lc, ln = S.lc[S.cnt % 2], "lc%d" % (S.cnt % 2)
                S.P.pool(lambda e: e.tensor_copy(
                    out=lc[0:cw, :, :], in_=stg[0:cw, NT - 128:NT].rearrange("p (s j) -> p s j", j=8)[:, :, 6:8]),
                    reads=[sn], writes=[ln])
                S.P.dma(lambda e: e.dma_start(out=dr["convTs"][r0:r0 + cw, :, :], in_=lc[0:cw, :, :]), S.key("st"),
                        reads=[ln])

        def extra_up(S):
            S.lc = [S.sb("lc%d" % i, [128, 16, 2], F32) for i in range(2)]

        if self.want("up"):
            chunks = [(j * 128, 128) for j in range(2 * nfc)]
            self.lin_stage("up", dr["w_up"], 32, 128, dr["hfT"], T0, NT, chunks, epi_up, extra=extra_up)
        if self.want("ffact"):
            self.ffact_stage()

        O0, NO = c.OWN0, c.TOW
        KH = nfc // 2

        def epi_d2(S, ci, c0, cw, pss, stg, sn):
            pa, pn = self.side_load(S, "pa", dr["fT"], c0, cw, O0, NO)
            for (ps, pr, a0, n) in pss:
                S.P.dve(lambda e, ps=ps, a0=a0, n=n: e.tensor_tensor(out=stg[0:cw, a0:a0 + n], in0=ps,
                                                                     in1=pa[0:cw, a0:a0 + n], op=ALU.add),
                        reads=[pr, pn], writes=[sn])
            S.P.dma(lambda e: e.dma_start(out=dr["x2T"][c0:c0 + cw, O0:O0 + NO], in_=stg[0:cw, :]), S.key("st"),
                    reads=[sn])

        if self.want("down"):
            self.lin_stage("d1", dr["w_down"][0:KH * 128, :], KH, 128, dr["actT"][0:KH * 128, :], O0, NO, ch32,
                           epi_store(dr["fT"], O0, NO), wblk=256)
            self.lin_stage("d2", dr["w_down"][KH * 128:, :], nfc - KH, 128, dr["actT"][KH * 128:, :], O0, NO, ch32,
                           epi_d2, wblk=256)
        if self.want("norm3"):
            self.norm_stage("n3", dr["x2T"], O0, NO, "g_post_ffn", dr["hpT"], res=dr["x1T"], out_x=dr["fT"],
                            gain2="g_pe")

        if self.want("pe"):
            self.lin_stage("pp", dr["w_pe"], 2, 128, dr["pT"], 0, NO, ch32, epi_store(dr["ppT"], O0, NO), a_cast=True)

            def epi_g(S, ci, c0, cw, pss, stg, sn):
                pp, ppn = self.side_load(S, "pp", dr["ppT"], c0, cw, O0, NO)
                x2, x2n = self.side_load(S, "x2", dr["fT"], c0, cw, O0, NO)
                for (ps, pr, a0, n) in pss:
                    S.P.act(lambda e, ps=ps, a0=a0, n=n: e.activation(out=stg[0:cw, a0:a0 + n], in_=ps,
                                                                      func=AF.Sigmoid), reads=[pr], writes=[sn])
                S.P.dve(lambda e: e.tensor_tensor(out=stg[0:cw, :], in0=stg[0:cw, :], in1=pp[0:cw, :], op=ALU.mult),
                        reads=[sn, ppn], writes=[sn])
                S.P.pool(lambda e: e.tensor_tensor(out=stg[0:cw, :], in0=stg[0:cw, :], in1=x2[0:cw, :], op=ALU.add),
                         reads=[sn, x2n], writes=[sn])
                S.P.dma(lambda e: e.dma_start(out=dr["yT"][c0:c0 + cw, :], in_=stg[0:cw, :]), S.key("st"), reads=[sn])

            self.lin_stage("pg", dr["w_pe_gate"], 32, 128, dr["hpT"], O0, NO, ch32, epi_g)

    def ffact_stage(self):
        c = self.cfg
        dr = self.dr
        NOP = c.CTX - c.OWN0
        with Stage(self, "fa") as S:
            P = S.P
            gp = [S.sb("gp%d" % i, [128, NOP + 2], F32) for i in range(2)]
            gs = [S.sb("gs%d" % i, [128, 16, 10], F32) for i in range(2)]
            st = [S.sb("st%d" % i, [128, 16, 2], F32) for i in range(2)]
            vv = [S.sb("vv%d" % i, [128, c.TOW], F32) for i in range(2)]
            cvs = [S.sb("cv%d" % i, [128, c.TOW], F32) for i in range(2)]
            uus = [S.sb("uu%d" % i, [128, c.TOW], F32) for i in range(2)]
            ab = [S.sb("ab%d" % i, [128, c.TOW], BF16) for i in range(2)]
            P0 = S.P

            def chunk(j):
                P = Rec()
                b = j % 2
                cv, uu = cvs[b], uus[b]
                cvn, uun = "cv%d" % b, "uu%d" % b
                r0 = j * 128
                g_, gn = gp[b], "gp%d" % b
                s_, sn_ = gs[b], "gs%d" % b
                t_, tn = st[b], "st%d" % b
                v_, vn = vv[b], "vv%d" % b
                a_, an = ab[b], "ab%d" % b
                P.dma(lambda e, g_=g_, r0=r0: e.dma_start(out=g_[:, :], in_=dr["ugT"][r0:r0 + 128, c.OWN0 - 2:c.CTX]),
                      S.key("l"), writes=[gn])
                P.dma(lambda e, s_=s_, r0=r0: e.dma_start(
                    out=s_[:, :, 2:10], in_=dr["ugT"][r0:r0 + 128, c.CTX:c.TT].rearrange("p (s j) -> p s j", j=8)),
                    S.key("l"), writes=[sn_])
                P.dma(lambda e, t_=t_, r0=r0: e.dma_start(out=t_[:], in_=dr["sconvT"][r0:r0 + 128, :, :]),
                      S.key("l"), writes=[tn])
                P.dma(lambda e, v_=v_, r0=r0: e.dma_start(out=v_[:, :], in_=dr["uvT"][r0:r0 + 128, c.OWN0:c.TT]),
                      S.key("l"), writes=[vn])
                P.pool(lambda e, s_=s_, t_=t_: e.tensor_copy(out=s_[:, :, 0:2], in_=t_[:]), reads=[tn, sn_],
                       writes=[sn_])
                P.pool(lambda e, g_=g_: e.tensor_scalar(out=g_[:, 0:2], in0=g_[:, 0:2], scalar1=self.pcol("flag"),
                                                        scalar2=None, op0=ALU.mult), reads=[gn], writes=[gn])
                w = [self.pcol("conv_w%d" % k, j) for k in range(3)]
                bcol = self.pcol("conv_b", j)
                for (cvv, src, rn, nn) in ((cv[:, 0:NOP], lambda k, g_=g_: g_[:, k:k + NOP], gn, None),
                                           (cv[:, NOP:].rearrange("p (s j) -> p s j", j=8),
                                            lambda k, s_=s_: s_[:, :, k:k + 8], sn_, None)):
                    P.dve(lambda e, cvv=cvv, src=src, w=w, bcol=bcol: e.tensor_scalar(out=cvv, in0=src(2), scalar1=w[2], scalar2=bcol,
                                                                      op0=ALU.mult, op1=ALU.add), reads=[rn],
                          writes=[cvn])
                    P.dve(lambda e, cvv=cvv, src=src, w=w: e.scalar_tensor_tensor(out=cvv, in0=src(1), scalar=w[1], in1=cvv,
                                                                             op0=ALU.mult, op1=ALU.add),
                          reads=[rn, cvn], writes=[cvn])
                    P.dve(lambda e, cvv=cvv, src=src, w=w: e.scalar_tensor_tensor(out=cvv, in0=src(0), scalar=w[0], in1=cvv,
                                                                             op0=ALU.mult, op1=ALU.add),
                          reads=[rn, cvn], writes=[cvn])
                P.pool(lambda e: e.tensor_tensor(out=uu[:], in0=cv[:], in1=cv[:], op=ALU.mult), reads=[cvn],
                       writes=[uun])
                P.pool(lambda e: e.tensor_scalar(out=uu[:], in0=uu[:], scalar1=0.044715, scalar2=1.0, op0=ALU.mult,
                                                 op1=ALU.add), reads=[uun], writes=[uun])
                P.pool(lambda e: e.tensor_tensor(out=uu[:], in0=uu[:], in1=cv[:], op=ALU.mult), reads=[uun, cvn],
                       writes=[uun])
                P.act(lambda e: e.activation(out=uu[:], in_=uu[:], func=AF.Sigmoid, scale=1.5957691216057308),
                      reads=[uun], writes=[uun])
                P.dve(lambda e: e.tensor_tensor(out=uu[:], in0=uu[:], in1=cv[:], op=ALU.mult), reads=[uun, cvn],
                      writes=[uun])
                P.dve(lambda e, a_=a_, v_=v_: e.tensor_tensor(out=a_[:], in0=uu[:], in1=v_[:], op=ALU.mult),
                      reads=[uun, vn], writes=[an])
                P.dma(lambda e, a_=a_, r0=r0: e.dma_start(out=dr["actT"][r0:r0 + 128, c.OWN0:c.TT], in_=a_[:]),
                      S.key("s"), reads=[an])
                return P

            for j in range(0, DFF // 128, 2):
                interleave_recs(P0, [chunk(j), chunk(j + 1)])

    def kneg05(self):
        return self.pcd[:, 60:61]

    def build_rest(self):
        if self.want("gla"):
            self.gla_stage()
        if self.want("rfront"):
            self.rfront_stage()
        if self.want("rscan"):
            self.rscan_stage(False)
            self.rscan_stage(True)
        self.post_stages()

WNAMES = ["w_in", "w_alpha2", "w_branch_a", "w_decay2", "w_iclr2", "w_gate2", "w_branch_b", "w_out", "w_up",
          "w_down", "w_pe_gate", "w_pe"]


def host_core_inputs(cfg, inp, x_ctx, p_ctx, xs, ps, sgla, srwkv, sshift, sconv, flag, shared=None):
    d = {}
    d["xT"] = np.ascontiguousarray(np.concatenate([x_ctx, xs.reshape(128, D)], axis=0).T)
    d["pT"] = np.ascontiguousarray(np.concatenate([p_ctx[cfg.OWN0:], ps.reshape(128, PD)], axis=0).T)
    d["pcols"] = host_pcols(inp, flag)
    if shared is None:
        shared = {}
    if "consts" not in shared:
        shared["consts"] = host_consts()[0]
        t_ = np.arange(128)
        shared["segcol"] = np.ascontiguousarray(np.broadcast_to(
            (t_[None, :] // 8 == np.arange(16)[:, None]).astype(np.float32).reshape(1, 16 * 128), (128, 16 * 128)))
        for n in WNAMES:
            shared[n] = np.ascontiguousarray(np.asarray(inp[n][0], np.float32))
    d.update(shared)
    d["sgla"] = np.ascontiguousarray(sgla, np.float32)
    S = np.asarray(srwkv, np.float32).reshape(16, 16, 2, 64, 64)
    T = np.zeros((16, 2, 64, 16, 2, 64), np.float32)
    for hl in range(2):
        T[:, hl, :, :, hl, :] = S[:, :, hl].transpose(0, 3, 1, 2)
    d["srwkv"] = T.reshape(16, 128, 16, 128)
    d["sshiftT"] = np.ascontiguousarray(np.asarray(sshift, np.float32).T)
    d["sconvT"] = np.ascontiguousarray(np.asarray(sconv, np.float32).transpose(2, 0, 1))
    return d


def _unpad_rwkv(T):
    lead = T.shape[:-3]
    T = T.reshape(lead + (2, 64, 16, 2, 64))
    out = np.zeros(lead + (16, 2, 64, 64), np.float32)
    for hl in range(2):
        blk = T[..., hl, :, :, hl, :]
        out[..., :, hl, :, :] = np.moveaxis(blk, -3, -1)
    return out.reshape(lead + (32, 64, 64))


def host_outputs_core(cfg, r):
    o = {}
    yT = np.asarray(r["yT"])
    nown = cfg.CTX - cfg.OWN0
    o["y_own"] = yT[:, :nown].T
    o["y_smp"] = yT[:, nown:].T.reshape(16, 8, D)
    o["gla_p"] = np.asarray(r["glap"])
    o["gla_s"] = np.asarray(r["glas"])
    o["rwkv_p"] = _unpad_rwkv(np.asarray(r["rwkvp"]))
    o["rwkv_s"] = _unpad_rwkv(np.asarray(r["rwkvs"]))
    o["shift_p"] = np.asarray(r["shiftTp"])[:, 0]
    o["shift_s"] = np.asarray(r["shiftTs"]).T
    o["conv_p"] = np.asarray(r["convTp"]).T
    o["conv_s"] = np.asarray(r["convTs"]).transpose(1, 2, 0)
    return o


_NC_CACHE = {}


def kernel(**inputs):
    cfg = Cfg(16, 8)
    inp = {k: np.asarray(v) for k, v in inputs.items()}
    xp, xs = inp["x_prompt"], inp["x_sample"]
    pp, psm = inp["p_prompt"][0], inp["p_sample"][0]
    shared = {}
    in_maps = []
    for c in range(8):
        seq, half = c // 2, c % 2
        if half == 1:
            x_ctx, p_ctx = xp[seq], pp[seq]
        else:
            x_ctx = np.concatenate([np.zeros((1024, D), np.float32), xp[seq, :1024]], axis=0)
            p_ctx = np.concatenate([np.zeros((1024, PD), np.float32), pp[seq, :1024]], axis=0)
        sl = slice(16 * c, 16 * c + 16)
        in_maps.append(host_core_inputs(cfg, inp, x_ctx, p_ctx, xs[sl], psm[sl], inp["state_gla"][0, sl],
                                        inp["state_rwkv"][0, sl], inp["state_shift"][0, sl],
                                        inp["state_ffn_conv"][0, sl], flag=float(half), shared=shared))
    if "nc" not in _NC_CACHE:
        _NC_CACHE["nc"] = KB(cfg).build()
    res = run_bass_kernel_spmd(_NC_CACHE["nc"], in_maps, core_ids=list(range(8)))
    y_p = np.zeros((4, 2048, D), np.float32)
    y_s = np.zeros((128, 8, D), np.float32)
    gla_p = np.zeros((1, 4, GH, GDK, GDV), np.float32)
    rwkv_p = np.zeros((1, 4, RH, RN, RN), np.float32)
    shift_p = np.zeros((1, 4, 6592), np.float32)
    conv_p = np.zeros((1, 4, 2, DFF), np.float32)
    gla_s = np.zeros((1, 128, GH, GDK, GDV), np.float32)
    rwkv_s = np.zeros((1, 128, RH, RN, RN), np.float32)
    shift_s = np.zeros((1, 128, 6592), np.float32)
    conv_s = np.zeros((1, 128, 2, DFF), np.float32)
    for c in range(8):
        seq, half = c // 2, c % 2
        o = host_outputs_core(cfg, res.results[c])
        sl = slice(16 * c, 16 * c + 16)
        y_p[seq, half * 1024:(half + 1) * 1024] = o["y_own"]
        y_s[sl] = o["y_smp"]
        gla_s[0, sl] = o["gla_s"]
        rwkv_s[0, sl] = o["rwkv_s"]
        shift_s[0, sl] = o["shift_s"]
        conv_s[0, sl] = o["conv_s"]
        if half == 1:
            gla_p[0, seq] = o["gla_p"]
            rwkv_p[0, seq] = o["rwkv_p"]
            shift_p[0, seq] = o["shift_p"]
            conv_p[0, seq] = o["conv_p"]
    return (y_p, y_s, gla_p, rwkv_p, shift_p, conv_p, gla_s, rwkv_s, shift_s, conv_s)
```

```python
import contextlib
import numpy as np
import concourse.bass as bass
import concourse.mybir as mybir
from concourse.bass_utils import run_bass_kernel_spmd

F32 = mybir.dt.float32
BF16 = mybir.dt.bfloat16
ALU = mybir.AluOpType
AF = mybir.ActivationFunctionType
AX = mybir.AxisListType

D = 4096
DFF = 11008
PD = 256
GH, GDK, GDV = 4, 256, 512
RH, RN = 32, 64
EPS = 1e-6
GN_EPS = 64e-5
IN_TOTAL = 20944
PIECES = dict(q=(0, 1024), k=(1024, 1024), v=(2048, 2048), za=(4096, 16), zg=(4112, 2048),
              r=(6160, 2048), xw=(8208, 96), kr=(8304, 2048), vr=(10352, 2048), xa=(12400, 96),
              xg=(12496, 256), ga=(12752, 4096), gb=(16848, 4096))
ZR0 = 6160

COMPUTE = ("pe", "act", "dve", "pool")
import os as _os
RS_CUT = int(_os.environ["RS_CUT"]) if "RS_CUT" in _os.environ else None
RS_SUB = _os.environ.get("RS_SUB", "")


class Op:
    __slots__ = ("eng", "fn", "deps", "key", "sig", "cnt")

    def __init__(self, eng, fn, deps, key):
        self.eng, self.fn, self.deps, self.key = eng, fn, deps, key
        self.sig = False
        self.cnt = 0


class Prog:
    def __init__(self):
        self.ops = []
        self.lastw = {}
        self.rd_eng = {}
        self.rd_dma = {}

    def add(self, eng, fn, reads=(), writes=(), key=None):
        i = len(self.ops)
        deps = set()
        for r in reads:
            w = self.lastw.get(r)
            if w is not None:
                deps.add(w)
        for r in writes:
            w = self.lastw.get(r)
            if w is not None:
                deps.add(w)
            d = self.rd_eng.get(r)
            if d:
                deps.update(d.values())
            l = self.rd_dma.get(r)
            if l:
                deps.update(l)
        for r in reads:
            if key is not None:
                self.rd_dma.setdefault(r, []).append(i)
            else:
                self.rd_eng.setdefault(r, {})[eng] = i
        for r in writes:
            self.lastw[r] = i
            self.rd_eng[r] = {}
            self.rd_dma[r] = []
        deps.discard(i)
        self.ops.append(Op(eng, fn, deps, key))
        return i

    def pe(self, fn, reads=(), writes=()):
        return self.add("pe", fn, reads, writes)

    def act(self, fn, reads=(), writes=()):
        return self.add("act", fn, reads, writes)

    def dve(self, fn, reads=(), writes=()):
        return self.add("dve", fn, reads, writes)

    def pool(self, fn, reads=(), writes=()):
        return self.add("pool", fn, reads, writes)

    def dma(self, fn, key, reads=(), writes=(), q="sp"):
        base = (list(writes) + list(reads))[0]
        return self.add(q, fn, reads, writes, key=str(key).split("_")[0] + "_" + base)

    def emit(self, nc, stack, kb=None):
        ops = self.ops
        for o in ops:
            for d in o.deps:
                y = ops[d]
                if y.key is None and (y.eng != o.eng or o.key is not None or o.eng != "pe"):
                    y.sig = True
        ecnt = {e: 0 for e in COMPUTE}
        kcnt = {}
        kq = {}
        for o in ops:
            if o.key is not None:
                kcnt[o.key] = kcnt.get(o.key, 0) + 16
                o.cnt = kcnt[o.key]
                assert kq.setdefault(o.key, o.eng) == o.eng, ("dma key used from two queues", o.key)
            elif o.sig:
                ecnt[o.eng] += 1
                o.cnt = ecnt[o.eng]
        if not kb.esem:
            for e_ in COMPUTE:
                kb.esem[e_] = kb.gst.enter_context(nc.semaphore("s_" + e_))
                kb.ebase[e_] = 0
        esem = kb.esem
        eb = dict(kb.ebase)
        for o in ops:
            if o.key is None and o.sig:
                o.cnt += eb[o.eng]
        for e_ in COMPUTE:
            kb.ebase[e_] += ecnt[e_]
        ksem = {}
        kbase = {}
        nidx = {"sw": 0, "hw": 0}
        for k in kcnt:
            kind = "sw" if kq[k] == "pool" else "hw"
            sems, bases = kb.ksems[kind], kb.kbases[kind]
            i_ = nidx[kind]
            nidx[kind] += 1
            while len(sems) <= i_:
                sems.append(kb.gst.enter_context(nc.semaphore("%s%d" % (kind, len(sems)))))
                bases.append(0)
            ksem[k] = sems[i_]
            kbase[k] = bases[i_]
            bases[i_] += kcnt[k]
        for o in ops:
            if o.key is not None:
                o.cnt += kbase[o.key]
        for k in kcnt:
            kcnt[k] += kbase[k]
        engs = {}
        for o in ops:
            engs.setdefault(o.eng, []).append(o)
        block = stack.enter_context(nc.Block())

        def run(eng_name, e):
            waited = {}
            for o in engs.get(eng_name, ()):
                need = {}
                for d in o.deps:
                    y = ops[d]
                    if y.key is not None:
                        s = ("k", y.key)
                    elif y.eng != o.eng or o.key is not None or o.eng != "pe":
                        s = ("e", y.eng)
                    else:
                        continue
                    if y.cnt > need.get(s, 0):
                        need[s] = y.cnt
                for s, v in need.items():
                    if v > waited.get(s, 0):
                        waited[s] = v
                        e.wait_ge(ksem[s[1]] if s[0] == "k" else esem[s[1]], v)
                ins = o.fn(e)
                if o.key is not None:
                    ins.then_inc(ksem[o.key], 16)
                elif o.sig:
                    ins.then_inc(esem[o.eng], 1)
            if eng_name == "sp":
                for k, v in kcnt.items():
                    if v > waited.get(("k", k), 0):
                        e.wait_ge(ksem[k], v)

        @block.sync
        def _(e):
            run("sp", e)

        @block.tensor
        def _(e):
            run("pe", e)

        @block.scalar
        def _(e):
            run("act", e)

        @block.vector
        def _(e):
            run("dve", e)

        @block.gpsimd
        def _(e):
            run("pool", e)


class Rec:
    def __init__(self):
        self.buf = []

    def dve(self, fn, reads=(), writes=()):
        self.buf.append(("dve", fn, list(reads), list(writes), None))

    def act(self, fn, reads=(), writes=()):
        self.buf.append(("act", fn, list(reads), list(writes), None))

    def pool(self, fn, reads=(), writes=()):
        self.buf.append(("pool", fn, list(reads), list(writes), None))

    def pe(self, fn, reads=(), writes=()):
        self.buf.append(("pe", fn, list(reads), list(writes), None))

    def dma(self, fn, key, reads=(), writes=(), q="sp"):
        self.buf.append((q, fn, list(reads), list(writes), key))


def interleave_recs(P, recs):
    i = 0
    while any(i < len(r.buf) for r in recs):
        for r in recs:
            if i < len(r.buf):
                eng, fn, rd, wr, key = r.buf[i]
                if key is None:
                    P.add(eng, fn, rd, wr)
                else:
                    P.dma(fn, key, rd, wr, q=eng)
        i += 1


class Stage:
    def __init__(self, kb, name):
        self.kb, self.nc, self.name = kb, kb.nc, name
        self.st = contextlib.ExitStack()
        self.P = Prog()
        self.nps = 0
        self.rr = 0
        self.uid = 0

    def __enter__(self):
        self.st.__enter__()
        return self

    def __exit__(self, *a):
        if a[0] is None:
            self.P.emit(self.nc, self.st, self.kb)
        return self.st.__exit__(*a)

    def sb(self, name, shape, dt):
        return self.st.enter_context(self.nc.sbuf_tensor(self.name + "_" + name, list(shape), dt))

    def psum_banks(self, n=8):
        self.banks = [self.st.enter_context(self.nc.psum_tensor("%s_pb%d" % (self.name, i), [128, 512], F32))
                      for i in range(n)]
        self.nbanks = n

    def bank(self, pool=None):
        if pool is None:
            i = self.rr % self.nbanks
        else:
            i = pool[self.rr % len(pool)]
        self.rr += 1
        return self.banks[i], "pb%d" % i

    def key(self, base):
        return self.name + "_" + base

    def evac_eng(self):
        self.uid += 1
        return "act" if self.uid % 2 else "dve"


def tok_blocks(n, maxb=512):
    out = []
    a = 0
    while a < n:
        b = min(maxb, n - a)
        out.append((a, b))
        a += b
    return out


class Cfg:
    def __init__(self, nct=16, nown=8):
        self.NCT = nct
        self.NOWN = nown
        self.TT = (nct + 1) * 128
        self.CTX = nct * 128
        self.OWN0 = (nct - nown) * 128
        self.HALO0 = self.OWN0 - 128
        self.TOW = self.TT - self.OWN0
        self.TPM = self.TT - self.HALO0
        self.ZW = 1 + self.CTX + 16 * 9


def pcol_layout():
    lay = {}
    o = 0
    for name, n in [("g_pre_mix", 32), ("g_post_mix", 32), ("g_pre_ffn", 32), ("g_post_ffn", 32), ("g_pe", 32),
                    ("b_alpha", 8), ("gla_norm", 4), ("mu_r", 16), ("mu_xw", 1), ("mu_kr", 16), ("mu_vr", 16),
                    ("mu_xa", 1), ("mu_xg", 2), ("w0", 16), ("a0", 16), ("k_k", 16), ("k_a", 16), ("r_k", 16),
                    ("ln_x_w", 16), ("ln_x_b", 16), ("conv_w0", 86), ("conv_w1", 86), ("conv_w2", 86),
                    ("conv_b", 86), ("flag", 1)]:
        lay[name] = (o, n)
        o += n
    return lay, o


PCL, NPC = pcol_layout()


def host_pcols(inp, flag):
    t = np.zeros((128, NPC), np.float32)

    def put(name, vec):
        o, n = PCL[name]
        v = np.zeros(n * 128, np.float32)
        v[:vec.size] = vec.reshape(-1)
        t[:, o:o + n] = v.reshape(n, 128).T

    for nm in ["g_pre_mix", "g_post_mix", "g_pre_ffn", "g_post_ffn", "g_pe", "b_alpha", "gla_norm", "w0", "a0",
               "k_k", "k_a", "r_k", "ln_x_w", "ln_x_b", "conv_b"]:
        put(nm, np.asarray(inp[nm][0]))
    mu = np.asarray(inp["mu_shift"][0])
    put("mu_r", mu[0:2048])
    put("mu_xw", mu[2048:2144])
    put("mu_kr", mu[2144:4192])
    put("mu_vr", mu[4192:6240])
    put("mu_xa", mu[6240:6336])
    put("mu_xg", mu[6336:6592])
    cw = np.asarray(inp["conv_w"][0])
    for j in range(3):
        put("conv_w%d" % j, cw[j])
    o, n = PCL["flag"]
    t[:, o] = flag
    return t


def host_consts():
    c = {}
    p = np.arange(128)
    c["ident"] = np.eye(128, dtype=np.float32)
    c["ones"] = np.ones((128, 128), np.float32)
    c["blk64"] = (p[:, None] // 64 == p[None, :] // 64).astype(np.float32)
    for nm, L in (("p", 128), ("s", 8)):
        same = (p[:, None] // L == p[None, :] // L)
        c["incT_" + nm] = (same & (p[None, :] >= p[:, None])).astype(np.float32)
        c["strT_" + nm] = (same & (p[None, :] > p[:, None])).astype(np.float32)
        c["str_" + nm] = (same & (p[:, None] > p[None, :])).astype(np.float32)
    c["seg16"] = (p[:, None] // 8 == np.arange(16)[None, :]).astype(np.float32)
    names = ["ident", "ones", "blk64", "incT_p", "strT_p", "str_p", "incT_s", "strT_s", "str_s"]
    tab = np.concatenate([c[n] for n in names] + [np.pad(c["seg16"], ((0, 0), (0, 112)))], axis=1)
    return tab.astype(np.float32), names + ["seg16"]


CONST_NAMES = ["ident", "ones", "blk64", "incT_p", "strT_p", "str_p", "incT_s", "strT_s", "str_s", "seg16"]


class KB:
    def __init__(self, cfg, debug=False, stages=None):
        self.cfg = cfg
        self.debug = debug
        self.stages = stages
        self.nc = bass.Bass("TRN2", target_bir_lowering=False)
        self.gst = contextlib.ExitStack()
        self.dr = {}
        self.esem = {}
        self.ebase = {}
        self.ksems = {"sw": [], "hw": []}
        self.kbases = {"sw": [], "hw": []}

    def inp(self, name, shape, dt=F32):
        self.dr[name] = self.nc.dram_tensor(name, list(shape), dt, kind="ExternalInput").ap()
        return self.dr[name]

    def out(self, name, shape, dt=F32):
        self.dr[name] = self.nc.dram_tensor(name, list(shape), dt, kind="ExternalOutput").ap()
        return self.dr[name]

    def scr(self, name, shape, dt=F32):
        kind = "ExternalOutput" if self.debug else "Internal"
        self.dr[name] = self.nc.dram_tensor(name, list(shape), dt, kind=kind).ap()
        return self.dr[name]

    def want(self, s):
        return self.stages is None or s in self.stages

    def declare(self):
        c = self.cfg
        TT = c.TT
        i = self.inp
        i("xT", [D, TT])
        i("pT", [PD, c.TOW])
        i("pcols", [128, NPC])
        i("consts", [128, 128 * 10])
        i("segcol", [128, 16 * 128])
        i("sgla", [16, GH, GDK, GDV])
        i("srwkv", [16, 128, 16, 128])
        i("sshiftT", [6592, 16])
        i("sconvT", [DFF, 16, 2])
        need = {"w_in": "win", "w_branch_a": "bra", "w_branch_b": "brb", "w_out": "wout", "w_up": "up",
                "w_down": "down", "w_pe_gate": "pe"}
        for nm, shp in (("w_in", [D, IN_TOTAL]), ("w_alpha2", [16, 1024]), ("w_branch_a", [2048, D]),
                        ("w_decay2", [96, 2048]), ("w_iclr2", [96, 2048]), ("w_gate2", [256, 2048]),
                        ("w_branch_b", [2048, D]), ("w_out", [D, D]), ("w_up", [D, 2 * DFF]), ("w_down", [DFF, D]),
                        ("w_pe_gate", [D, D]), ("w_pe", [PD, D])):
            if nm in need and not self.want(need[nm]):
                continue
            i(nm, shp)
        o = self.out
        o("yT", [D, c.TOW])
        o("glap", [GH, GDK, GDV])
        o("glas", [16, GH, GDK, GDV])
        o("rwkvp", [128, 16, 128])
        o("rwkvs", [16, 128, 16, 128])
        o("shiftTp", [6592, 1])
        o("shiftTs", [6592, 16])
        o("convTp", [DFF, 2])
        o("convTs", [DFF, 16, 2])
        s = self.scr
        s("hT", [D, TT], BF16)
        for nm in ("q", "k", "zg", "ga", "gb"):
            s(nm + "T", [PIECES[nm][1], TT])
        s("zaT", [16, TT])
        for nm in ("r", "xw", "kr", "vr", "xa", "xg"):
            s(nm + "T", [PIECES[nm][1], c.ZW])
        s("vtm", [TT, 2048], BF16)
        s("oaT", [2048, TT], BF16)
        s("obT", [2048, TT], BF16)
        s("arbk", [128, 16, 4, TT], BF16)
        s("vTb", [128, 16, TT], BF16)
        s("gcT", [128, 16, c.NCT + 16])
        s("bonT", [2048, TT])
        s("gT", [2048, TT])
        s("mixT", [D, TT], BF16)
        s("yoT", [D, TT])
        s("x1T", [D, TT])
        s("hfT", [D, TT], BF16)
        s("ugT", [DFF, TT])
        s("uvT", [DFF, TT])
        s("actT", [DFF, TT], BF16)
        s("fT", [D, TT])
        s("x2T", [D, TT])
        s("hpT", [D, TT], BF16)
        s("ppT", [D, TT])
        s("pTb", [PD, TT], BF16)

    def consts(self):
        nc = self.nc
        g = self.gst
        self.pc = g.enter_context(nc.sbuf_tensor("pc", [128, NPC], F32))
        self.pcd = g.enter_context(nc.sbuf_tensor("pcd", [128, 64], F32))
        self.c32 = g.enter_context(nc.sbuf_tensor("c32", [128, 10, 128], F32))
        self.cbf = g.enter_context(nc.sbuf_tensor("cbf", [128, 10, 128], BF16))
        self.c32x = g.enter_context(nc.sbuf_tensor("c32x", [128, 16, 128], BF16))
        with Stage(self, "c") as S:
            P = S.P
            P.dma(lambda e: e.dma_start(out=self.pc[:], in_=self.dr["pcols"]), S.key("a"), writes=["pc"])
            P.dma(lambda e: e.dma_start(out=self.c32[:], in_=self.dr["consts"].rearrange("p (n c) -> p n c", c=128)),
                  S.key("b"), writes=["c32"])
            P.dve(lambda e: e.tensor_copy(out=self.cbf[:], in_=self.c32[:]), reads=["c32"], writes=["cbf"])
            P.dma(lambda e: e.dma_start(out=self.c32x[:], in_=self.dr["segcol"].rearrange("p (n c) -> p n c", c=128)),
                  S.key("d"), writes=["c32x"], q="pool")
            pcd, pc = self.pcd, self.pc
            o, n = PCL["b_alpha"]
            P.dve(lambda e: e.tensor_scalar(out=pcd[:, 0:8], in0=pc[:, o:o + 8], scalar1=-1.0, scalar2=None,
                                            op0=ALU.mult), reads=["pc"], writes=["pcd"])
            P.pool(lambda e: e.memset(pcd[:, 60:61], -0.5), writes=["pcd"])
            om, _ = PCL["mu_r"]
            P.dve(lambda e: e.tensor_scalar(out=pcd[:, 8:60], in0=pc[:, om:om + 52], scalar1=-1.0, scalar2=1.0,
                                            op0=ALU.mult, op1=ALU.add), reads=["pc"], writes=["pcd"])

    def cst(self, name, bf=False):
        i = CONST_NAMES.index(name)
        return (self.cbf if bf else self.c32)[:, i, :]

    def pcol(self, name, j=0, n=1):
        o, _ = PCL[name]
        return self.pc[:, o + j:o + j + n]

    def fm_linear(self, S, W, KC, kparts, chunks, aT, aname, nt, evac, wblk=512, wq="pool", krange=None):
        P, nc = S.P, self.nc
        tb = tok_blocks(nt)
        astep = max(1, KC // 4) if kparts == 128 else KC
        k0, k1 = (0, KC) if krange is None else krange
        nk = k1 - k0
        blocks = []
        cur = []
        for (c0, cw) in chunks:
            if cur and (cur[-1][0] + cur[-1][1] != c0 or (c0 + cw - cur[0][0]) > wblk):
                blocks.append(cur)
                cur = []
            cur.append((c0, cw))
        if cur:
            blocks.append(cur)
        ck = ("wb", nk, wblk)
        if not hasattr(S, "cache"):
            S.cache = {}
        if ck not in S.cache:
            S.cache[ck] = [S.sb("w%d_%d_%d" % (i, nk, wblk), [128, nk, wblk], BF16) for i in range(2)]
            S.wbi = getattr(S, "wbi", 0)
        wb = S.cache[ck]
        wstep = max(1, nk // 4) if kparts == 128 else nk

        def load(bi):
            blk = blocks[bi]
            b0 = blk[0][0]
            bw = blk[-1][0] + blk[-1][1] - b0
            t = wb[bi % 2]
            kp = kparts
            if kp == 128:
                src = W[k0 * 128:k1 * 128, b0:b0 + bw].rearrange("(k p) c -> p k c", p=128)
                step = max(1, nk // 4)
                for ka in range(0, nk, step):
                    kb_ = min(nk, ka + step)
                    P.dma(lambda e, t=t, src=src, ka=ka, kb_=kb_, bw=bw: e.dma_start(
                        out=t[:, ka:kb_, 0:bw], in_=src[:, ka:kb_, :]),
                        S.key("w%d" % (bi % 2)), writes=["w%d.%d" % (bi % 2, ka // step)], q=wq)
            else:
                P.dma(lambda e, t=t, b0=b0, bw=bw, kp=kp: e.dma_start(out=t[0:kp, 0, 0:bw], in_=W[0:kp, b0:b0 + bw]),
                      S.key("w%d" % (bi % 2)), writes=["w%d.0" % (bi % 2)], q=wq)

        load(0)
        ci = 0
        for bi, blk in enumerate(blocks):
            if bi + 1 < len(blocks):
                load(bi + 1)
            t = wb[bi % 2]
            b0 = blk[0][0]
            for (c0, cw) in blk:
                pss = []
                for (a0, n) in tb:
                    pt, pr = S.bank()
                    pss.append((pt, pr, a0, n))
                for k in range(nk):
                    for (pt, pr, a0, n) in pss:
                        P.pe(lambda e, pt=pt, t=t, k=k, c0=c0, cw=cw, b0=b0, a0=a0, n=n: e.matmul(
                            pt[0:cw, 0:n], lhsT=t[0:kparts, k, c0 - b0:c0 - b0 + cw],
                            rhs=aT[0:kparts, k0 + k, a0:a0 + n], start=(k == 0), stop=(k == nk - 1)),
                            reads=["w%d.%d" % (bi % 2, k // wstep), aname + ".%d" % ((k0 + k) // astep)], writes=[pr])
                evac(ci, c0, cw, [(pt[0:cw, 0:n], pr, a0, n) for (pt, pr, a0, n) in pss])
                ci += 1

    def load_aT(self, S, name, src, KC, t0, nt, kparts=128):
        t = S.sb(name, [128, KC, nt], BF16)
        if kparts == 128:
            v = src[:, t0:t0 + nt].rearrange("(k p) t -> p k t", p=128)
            step = max(1, KC // 4)
            for ka in range(0, KC, step):
                kb_ = min(KC, ka + step)
                S.P.dma(lambda e, ka=ka, kb_=kb_: e.dma_start(out=t[:, ka:kb_, :], in_=v[:, ka:kb_, :]),
                        S.key(name), writes=[name + ".%d" % (ka // step)])
        else:
            S.P.dma(lambda e: e.dma_start(out=t[0:kparts, 0, :], in_=src[0:kparts, t0:t0 + nt]),
                    S.key(name), writes=[name + ".0"])
        return t

    def copy_ps(self, S, out, ps, pr, wres, func=None, scale=1.0):
        eng = S.evac_eng()
        if func is not None or eng == "act":
            S.P.act(lambda e: e.activation(out=out, in_=ps, func=(func or AF.Copy), scale=scale),
                    reads=[pr], writes=[wres])
        else:
            S.P.dve(lambda e: e.tensor_copy(out=out, in_=ps), reads=[pr], writes=[wres])

    def stt_mm(self, P, on_pool, out, in0, scal, in1, reads, writes, tmp=None):
        if not on_pool:
            P.dve(lambda e: e.scalar_tensor_tensor(out=out, in0=in0, scalar=scal, in1=in1, op0=ALU.mult,
                                                   op1=ALU.mult), reads=reads, writes=writes)
        else:
            P.pool(lambda e: e.tensor_scalar(out=tmp, in0=in0, scalar1=scal, scalar2=None, op0=ALU.mult),
                   reads=reads, writes=["_pooltmp"])
            P.pool(lambda e: e.tensor_tensor(out=out, in0=tmp, in1=in1, op=ALU.mult),
                   reads=list(reads) + ["_pooltmp"], writes=writes)

    def norm_stage(self, name, src, t0, nt, gain, out_bf, res=None, out_x=None, gain2=None, NB=256):
        with Stage(self, name) as S:
            P = S.P
            S.psum_banks(2)
            s32 = S.sb("s32", [128, 32, NB], F32)
            sq = S.sb("sq", [128, 32, NB], F32)
            hb = S.sb("hb", [128, 32, NB], BF16)
            rs = S.sb("rs", [128, NB], F32)
            tmpc = S.sb("tmpc", [128, NB], F32)
            r32 = S.sb("r32", [128, 32, NB], F32) if res is not None else None
            ones = self.cst("ones")

            def stats(x, xname, nb):
                P.act(lambda e, x=x, nb=nb: e.activation(out=sq[:, :, 0:nb], in_=x, func=AF.Square),
                      reads=[xname], writes=["sq"])
                pt, pr = S.bank()
                for c in range(32):
                    P.pe(lambda e, c=c, pt=pt, nb=nb: e.matmul(pt[:, 0:nb], lhsT=ones, rhs=sq[:, c, 0:nb],
                                                                 start=(c == 0), stop=(c == 31)),
                         reads=["sq"], writes=[pr])
                P.act(lambda e, pt=pt, nb=nb: e.activation(out=rs[:, 0:nb], in_=pt[:, 0:nb], func=AF.Sqrt,
                                                           scale=1.0 / D, bias=EPS), reads=[pr], writes=["rs"])
                P.dve(lambda e, nb=nb: e.reciprocal(out=rs[:, 0:nb], in_=rs[:, 0:nb]), reads=["rs"], writes=["rs"])

            for (a0, nb) in tok_blocks(nt, NB):
                sv = src[:, t0 + a0:t0 + a0 + nb].rearrange("(c p) t -> p c t", p=128)
                for h in range(2):
                    P.dma(lambda e, h=h, sv=sv, nb=nb: e.dma_start(out=s32[:, 16 * h:16 * h + 16, 0:nb],
                                                                   in_=sv[:, 16 * h:16 * h + 16, :]),
                          S.key("ls"), writes=["s32"])
                if res is not None:
                    rv = res[:, t0 + a0:t0 + a0 + nb].rearrange("(c p) t -> p c t", p=128)
                    for h in range(2):
                        P.dma(lambda e, h=h, rv=rv, nb=nb: e.dma_start(out=r32[:, 16 * h:16 * h + 16, 0:nb],
                                                                       in_=rv[:, 16 * h:16 * h + 16, :]),
                              S.key("lr"), writes=["r32"])
                stats(s32[:, :, 0:nb], "s32", nb)
                if res is None:
                    for c in range(32):
                        self.stt_mm(P, c % 3 == 0, hb[:, c, 0:nb], s32[:, c, 0:nb], self.pcol(gain, c), rs[:, 0:nb],
                                    ["s32", "rs"], ["hb"], tmp=tmpc[:, 0:nb])
                else:
                    for c in range(32):
                        self.stt_mm(P, c % 3 == 0, s32[:, c, 0:nb], s32[:, c, 0:nb], self.pcol(gain, c), rs[:, 0:nb],
                                    ["s32", "rs"], ["s32"], tmp=tmpc[:, 0:nb])
                    P.dve(lambda e, nb=nb: e.tensor_tensor(out=r32[:, :, 0:nb], in0=r32[:, :, 0:nb],
                                                           in1=s32[:, :, 0:nb], op=ALU.add),
                          reads=["r32", "s32"], writes=["r32"])
                    ov = out_x[:, t0 + a0:t0 + a0 + nb].rearrange("(c p) t -> p c t", p=128)
                    P.dma(lambda e, ov=ov, nb=nb: e.dma_start(out=ov, in_=r32[:, :, 0:nb]), S.key("sx"),
                          reads=["r32"])
                    stats(r32[:, :, 0:nb], "r32", nb)
                    for c in range(32):
                        self.stt_mm(P, c % 3 == 0, hb[:, c, 0:nb], r32[:, c, 0:nb], self.pcol(gain2, c), rs[:, 0:nb],
                                    ["r32", "rs"], ["hb"], tmp=tmpc[:, 0:nb])
                hv = out_bf[:, t0 + a0:t0 + a0 + nb].rearrange("(c p) t -> p c t", p=128)
                P.dma(lambda e, hv=hv, nb=nb: e.dma_start(out=hv, in_=hb[:, :, 0:nb]), S.key("sh"), reads=["hb"])

    def win_stage(self, name, t0, nt, pieces):
        c = self.cfg
        with Stage(self, name) as S:
            P = S.P
            S.psum_banks(8)
            hT = self.load_aT(S, "hT", self.dr["hT"], 32, t0, nt)
            stg = [S.sb("stg%d" % i, [128, nt], F32) for i in range(3)]
            lastc = [S.sb("lastc%d" % i, [128, 16], F32) for i in range(3)]
            zt = S.sb("zt", [128, 16], F32)
            P.pool(lambda e: e.memset(zt[:], 0.0), writes=["zt"])
            cnt = [0]
            Win = self.dr["w_in"]
            n_ctx = max(0, min(c.CTX, t0 + nt) - t0)
            has_smp = (t0 + nt) > c.CTX
            for pn in pieces:
                if pn == "v":
                    continue
                p0, pw = PIECES[pn]
                chunks = [(p0 + a, min(128, pw - a)) for a in range(0, pw, 128)]
                dst = self.dr[pn + "T"]
                padded = pn in ("r", "xw", "kr", "vr", "xa", "xg")
                func = AF.Sigmoid if pn in ("ga", "gb") else None

                def evac(ci, c0, cw, pss, p0=p0, dst=dst, padded=padded, func=func, pn=pn):
                    si = cnt[0] % 3
                    cnt[0] += 1
                    sg = stg[si]
                    sn = "stg%d" % si
                    for (ps, pr, a0, n) in pss:
                        self.copy_ps(S, sg[0:cw, a0:a0 + n], ps, pr, sn, func=func)
                    r0 = c0 - p0
                    if not padded:
                        P.dma(lambda e: e.dma_start(out=dst[r0:r0 + cw, t0:t0 + nt], in_=sg[0:cw, :]),
                              S.key("st"), reads=[sn])
                    else:
                        if t0 == 0:
                            P.dma(lambda e: e.dma_start(out=dst[r0:r0 + cw, 0:1], in_=zt[0:cw, 0:1],
                                                        allow_slow_non_contiguous=True), S.key("st"),
                                  reads=["zt"])
                        if n_ctx > 0:
                            P.dma(lambda e: e.dma_start(out=dst[r0:r0 + cw, 1 + t0:1 + t0 + n_ctx],
                                                        in_=sg[0:cw, 0:n_ctx]), S.key("st"), reads=[sn])
                            if t0 + n_ctx == c.CTX:
                                zr = c0 - ZR0
                                P.dma(lambda e: e.dma_start(out=self.dr["shiftTp"][zr:zr + cw, 0:1],
                                                            in_=sg[0:cw, n_ctx - 1:n_ctx]), S.key("st"), reads=[sn])
                        if has_smp:
                            dv = dst[r0:r0 + cw, 1 + c.CTX:1 + c.CTX + 144].rearrange("p (s j) -> p s j", j=9)
                            sv = sg[0:cw, nt - 128:nt].rearrange("p (s j) -> p s j", j=8)
                            P.dma(lambda e: e.dma_start(out=dv[:, :, 1:9], in_=sv), S.key("st"), reads=[sn])
                            P.dma(lambda e: e.dma_start(out=dv[:, :, 0], in_=zt[0:cw, :],
                                                        allow_slow_non_contiguous=True), S.key("st"), reads=["zt"])
                            zr = c0 - ZR0
                            lt = lastc[si]
                            P.pool(lambda e: e.tensor_copy(out=lt[0:cw, :], in_=sv[:, :, 7]), reads=[sn],
                                   writes=["lastc%d" % si])
                            P.dma(lambda e: e.dma_start(out=self.dr["shiftTs"][zr:zr + cw, :], in_=lt[0:cw, :]),
                                  S.key("st"), reads=["lastc%d" % si])

                self.fm_linear(S, Win, 32, 128, chunks, hT, "hT", nt, evac)
            if "v" in pieces:
                p0, pw = PIECES["v"]
                wv = S.cache[("wb", 32, 512)]
                vst = [S.sb("vst%d" % i, [128, 512], BF16) for i in range(2)]
                for bi in range(4):
                    t = wv[bi % 2]
                    src = Win[:, p0 + bi * 512:p0 + (bi + 1) * 512].rearrange("(k p) c -> p k c", p=128)
                    for ka in range(0, 32, 8):
                        P.dma(lambda e, t=t, src=src, ka=ka: e.dma_start(out=t[:, ka:ka + 8, :],
                                                                       in_=src[:, ka:ka + 8, :]),
                              S.key("w%d" % (bi % 2)), writes=["w%d.%d" % (bi % 2, ka // 8)], q="pool")
                    for ti in range(nt // 128):
                        pt, pr = S.bank()
                        for k in range(32):
                            P.pe(lambda e, pt=pt, t=t, k=k, ti=ti: e.matmul(
                                pt[:, :], lhsT=hT[:, k, ti * 128:(ti + 1) * 128], rhs=t[:, k, :],
                                start=(k == 0), stop=(k == 31)), reads=["w%d.%d" % (bi % 2, k // 8), "hT.%d" % (k // 8)],
                                writes=[pr])
                        vi = (bi * (nt // 128) + ti) % 2
                        self.copy_ps(S, vst[vi][:, :], pt[:, :], pr, "vst%d" % vi)
                        P.dma(lambda e, vi=vi, ti=ti, bi=bi: e.dma_start(
                            out=self.dr["vtm"][t0 + ti * 128:t0 + (ti + 1) * 128, bi * 512:(bi + 1) * 512],
                            in_=vst[vi][:, :]), S.key("sv"), reads=["vst%d" % vi])

    def build(self):
        c = self.cfg
        self.declare()
        with self.gst:
            self.consts()
            if self.want("norm1"):
                self.norm_stage("n1", self.dr["xT"], 0, c.TT, "g_pre_mix", self.dr["hT"])
            if self.want("win"):
                allp = list(PIECES.keys())
                g0 = c.HALO0
                if g0 > 0:
                    self.win_stage("wa", 0, g0, [p_ for p_ in allp if p_ not in ("zg", "ga", "gb")])
                self.win_stage("wb", g0, c.TT - g0, allp)
            self.build_rest()
        return self.nc

    def gla_stage(self):
        c = self.cfg
        dr = self.dr
        with Stage(self, "gla") as S:
            P = S.P
            S.psum_banks(7)
            qk = S.sb("qk", [128, 16, 128], F32)
            zab = S.sb("zab", [16, 128], BF16)
            wal = S.sb("wal", [16, 1024], BF16)
            vb = S.sb("vb", [128, 2048], BF16)
            sp = S.sb("sp", [128, 8, 128], F32)
            cs = S.sb("cs", [128, 8, 128], F32)
            cs2 = S.sb("cs2", [128, 8, 128], F32)
            ex = S.sb("ex", [128, 8, 128], F32)
            onesf = S.sb("onesf", [128, 128], F32)
            zer = S.sb("zer", [128, 512], BF16)
            P.pool(lambda e: e.memset(zer[:], 0.0), writes=["zer"])
            qd = S.sb("qd", [128, 8, 128], BF16)
            kd = S.sb("kd", [128, 8, 128], BF16)
            kdp = S.sb("kdp", [128, 8, 128], BF16)
            kdt = S.sb("kdt", [128, 8, 128], BF16)
            att = S.sb("att", [128, 4, 128], BF16)
            S32 = S.sb("S32", [128, 8, 512], F32)
            Sbf = S.sb("Sbf", [128, 8, 512], BF16)
            sdec = S.sb("sdec", [128, 8, 16], F32)
            cend = S.sb("cend", [128, 8, 16], F32)
            o32 = S.sb("o32", [128, 16, 128], F32)
            sq = S.sb("sq", [128, 16, 128], F32)
            rstd = S.sb("rstd", [128, 4, 128], F32)
            zg = S.sb("zg", [128, 16, 128], F32)
            sg = S.sb("sg", [128, 16, 128], F32)
            oab = S.sb("oab", [128, 16, 128], BF16)
            Ss32 = [S.sb("Ss32_%d" % i, [128, 8, 512], F32) for i in range(2)]
            Ssbf = [S.sb("Ssbf_%d" % i, [128, 8, 512], BF16) for i in range(2)]
            kdm = S.sb("kdm", [128, 8, 16, 128], BF16)
            ptr = S.st.enter_context(self.nc.psum_tensor("gla_ptr", [128, 8, 128], BF16))
            rn = lambda t: {id(sp): "sp", id(cs): "cs", id(cs2): "cs2"}[id(t)]
            identb = self.cst("ident", True)
            ones32 = self.cst("ones")
            P.dma(lambda e: e.dma_start(out=wal[:], in_=dr["w_alpha2"]), S.key("c"), writes=["wal"], q="pool")
            P.pool(lambda e: e.memset(onesf[:], 1.0), writes=["onesf"])
            P.pool(lambda e: e.memset(S32[:], 0.0), writes=["S32"])
            P.pool(lambda e: e.memset(Sbf[:], 0.0), writes=["Sbf"])
            for tile in range(c.NCT + 1):
                smp = tile == c.NCT
                t0 = tile * 128
                need_out = t0 >= c.HALO0
                P.dma(lambda e, t0=t0: e.dma_start(out=qk[:, 0:8, :], in_=dr["qT"][:, t0:t0 + 128].rearrange(
                    "(c p) t -> p c t", p=128)), S.key("l1"), writes=["qk"])
                P.dma(lambda e, t0=t0: e.dma_start(out=qk[:, 8:16, :], in_=dr["kT"][:, t0:t0 + 128].rearrange(
                    "(c p) t -> p c t", p=128)), S.key("l1"), writes=["qk"])
                P.dma(lambda e, t0=t0: e.dma_start(out=zab[:], in_=dr["zaT"][:, t0:t0 + 128]), S.key("l2"),
                      writes=["zab"], q="pool")
                P.dma(lambda e, t0=t0: e.dma_start(out=vb[:], in_=dr["vtm"][t0:t0 + 128, :]), S.key("l3"),
                      writes=["vb"])
                for half in range(2):
                    pt, pr = S.bank([0, 1])
                    for cc in range(4):
                        ch = half * 4 + cc
                        P.pe(lambda e, pt=pt, cc=cc, ch=ch: e.matmul(pt[:, cc * 128:(cc + 1) * 128],
                                                                   lhsT=wal[:, ch * 128:(ch + 1) * 128], rhs=zab[:, :],
                                                                   start=True, stop=True),
                             reads=["wal", "zab"], writes=[pr])
                    for cc in range(4):
                        ch = half * 4 + cc
                        P.act(lambda e, pt=pt, cc=cc, ch=ch: e.activation(
                            out=sp[:, ch, :], in_=pt[:, cc * 128:(cc + 1) * 128], func=AF.Exp, scale=-1.0,
                            bias=self.pcd[:, ch:ch + 1]), reads=[pr, "pcd"], writes=["sp"])
                P.act(lambda e: e.activation(out=sp[:], in_=sp[:], func=AF.Ln, bias=1.0), reads=["sp"], writes=["sp"])
                if not smp:
                    for ch in range(8):
                        P.dve(lambda e, ch=ch: e.tensor_tensor_scan(out=cs[:, ch, :], data0=onesf[:, :],
                                                                    data1=sp[:, ch, :], initial=0.0,
                                                                    op0=ALU.mult, op1=ALU.add),
                              reads=["sp", "onesf"], writes=["cs"])
                    csf = cs
                    P.dve(lambda e: e.tensor_copy(out=cend[:, :, 0:1], in_=cs[:, :, 127:128]), reads=["cs"],
                          writes=["cend"])
                    nseg = 1
                else:
                    v = lambda t: t[:].rearrange("p c (s j) -> p (c s) j", j=8)
                    src, dst = sp, cs
                    for d in (1, 2, 4):
                        P.dve(lambda e, src=src, dst=dst, d=d: e.tensor_tensor(
                            out=v(dst)[:, :, d:8], in0=v(src)[:, :, d:8], in1=v(src)[:, :, 0:8 - d], op=ALU.add),
                            reads=[rn(src)], writes=[rn(dst)])
                        P.pool(lambda e, src=src, dst=dst, d=d: e.tensor_copy(out=v(dst)[:, :, 0:d],
                                                                              in_=v(src)[:, :, 0:d]),
                               reads=[rn(src)], writes=[rn(dst)])
                        src, dst = dst, (cs2 if dst is cs else cs)
                    csf = src
                    P.dve(lambda e, csf=csf: e.tensor_copy(
                        out=cend[:, :, :], in_=csf[:].rearrange("p c (s j) -> p c s j", j=8)[:, :, :, 7]),
                        reads=[rn(csf)], writes=["cend"])
                    nseg = 16
                cn = rn(csf)
                P.act(lambda e, csf=csf: e.activation(out=ex[:], in_=csf[:], func=AF.Exp, scale=-1.0 / 16),
                      reads=[cn], writes=["ex"])
                P.dve(lambda e: e.scalar_tensor_tensor(out=qd[:], in0=qk[:, 0:8, :], scalar=float(GDK) ** -0.5,
                                                       in1=ex[:], op0=ALU.mult, op1=ALU.mult),
                      reads=["qk", "ex"], writes=["qd"])
                P.act(lambda e, csf=csf: e.activation(out=ex[:], in_=csf[:], func=AF.Exp, scale=1.0 / 16),
                      reads=[cn], writes=["ex"])
                P.dve(lambda e: e.tensor_tensor(out=kd[:], in0=qk[:, 8:16, :], in1=ex[:], op=ALU.mult),
                      reads=["qk", "ex"], writes=["kd"])
                L = 128 // nseg
                P.dve(lambda e, csf=csf, nseg=nseg, L=L: e.tensor_tensor(
                    out=ex[:].rearrange("p c (s j) -> p c s j", j=L),
                    in0=csf[:].rearrange("p c (s j) -> p c s j", j=L),
                    in1=cend[:, :, 0:nseg].unsqueeze(3).to_broadcast([128, 8, nseg, L]), op=ALU.subtract),
                    reads=[cn, "cend"], writes=["ex"])
                P.act(lambda e: e.activation(out=ex[:], in_=ex[:], func=AF.Exp, scale=1.0 / 16), reads=["ex"],
                      writes=["ex"])
                P.dve(lambda e: e.tensor_tensor(out=kdp[:], in0=qk[:, 8:16, :], in1=ex[:], op=ALU.mult),
                      reads=["qk", "ex"], writes=["kdp"])
                P.act(lambda e, nseg=nseg: e.activation(out=sdec[:, :, 0:nseg], in_=cend[:, :, 0:nseg], func=AF.Exp,
                                                        scale=-1.0 / 16), reads=["cend"], writes=["sdec"])
                for ch in range(8):
                    P.pe(lambda e, ch=ch: e.transpose(ptr[:, ch, :], kdp[:, ch, :], identb), reads=["kdp"],
                         writes=["ptr"])
                P.act(lambda e: e.activation(out=kdt[:], in_=ptr[:], func=AF.Copy), reads=["ptr"], writes=["kdt"])
                pa, par = S.bank([2])
                for h in range(4):
                    for cc in range(2):
                        P.pe(lambda e, h=h, cc=cc, pa=pa: e.matmul(pa[:, h * 128:(h + 1) * 128],
                                                                 lhsT=kd[:, 2 * h + cc, :], rhs=qd[:, 2 * h + cc, :],
                                                                 start=(cc == 0), stop=(cc == 1)),
                             reads=["kd", "qd"], writes=[par])
                mk = self.cst("incT_s" if smp else "incT_p")
                P.dve(lambda e, pa=pa, mk=mk: e.tensor_tensor(
                    out=att[:], in0=pa[:].rearrange("p (h t) -> p h t", t=128),
                    in1=mk.unsqueeze(1).to_broadcast([128, 4, 128]), op=ALU.mult),
                    reads=[par], writes=["att"])
                pos = [S.bank([3 + h_]) for h_ in range(4)]
                for h in range(4):
                    po, por = pos[h]
                    P.pe(lambda e, po=po: e.matmul(po[:, :], lhsT=zer[:, 0:128], rhs=zer[:, :], start=True,
                                                   stop=False), reads=["zer"], writes=[por])
                    for j in range(4):
                        P.pe(lambda e, h=h, j=j, po=po: e.matmul(
                            po[:, j * 128:(j + 1) * 128], lhsT=vb[:, h * 512 + j * 128:h * 512 + (j + 1) * 128],
                            rhs=att[:, h, :], start=False, stop=False),
                            reads=["vb", "att"], writes=[por])
                        if not smp:
                            for cc in range(2):
                                P.pe(lambda e, h=h, j=j, cc=cc, po=po: e.matmul(
                                    po[:, j * 128:(j + 1) * 128], lhsT=Sbf[:, 2 * h + cc, j * 128:(j + 1) * 128],
                                    rhs=qd[:, 2 * h + cc, :], start=False, stop=(cc == 1 and j == 3)),
                                    reads=["Sbf", "qd"], writes=[por])
                if not smp:
                    for h in range(4):
                        for cc in range(2):
                            i = 2 * h + cc
                            pu, pur = S.bank([0, 1])
                            P.pe(lambda e, i=i, h=h, pu=pu: e.matmul(pu[:, :], lhsT=kdt[:, i, :],
                                                                     rhs=vb[:, h * 512:(h + 1) * 512], start=True,
                                                                     stop=True), reads=["kdt", "vb"], writes=[pur])
                            P.dve(lambda e, i=i, pu=pu: e.scalar_tensor_tensor(
                                out=S32[:, i, :], in0=S32[:, i, :], scalar=sdec[:, i, 0:1], in1=pu[:, :],
                                op0=ALU.mult, op1=ALU.add), reads=[pur, "S32", "sdec"], writes=["S32"])
                    P.act(lambda e: e.activation(out=Sbf[:], in_=S32[:], func=AF.Copy), reads=["S32"], writes=["Sbf"])
                    if tile == c.NCT - 1:
                        P.dma(lambda e: e.dma_start(out=dr["glap"].rearrange("h (c p) v -> p h c v", p=128),
                                                    in_=S32[:].rearrange("p (h c) v -> p h c v", c=2)),
                              S.key("so"), reads=["S32"])
                else:
                    seg16 = self.cst("seg16", True)
                    for i8 in range(8):
                        P.dve(lambda e, i8=i8: e.tensor_tensor(
                            out=kdm[:, i8, :, :], in0=kdt[:, i8, :].unsqueeze(1).to_broadcast([128, 16, 128]),
                            in1=seg16[:, 0:16].unsqueeze(2).to_broadcast([128, 16, 128]), op=ALU.mult),
                            reads=["kdt"], writes=["kdm"])
                    for si in range(16):
                        b = si % 2
                        s32, sbf = Ss32[b], Ssbf[b]
                        n32, nbf = "Ss32_%d" % b, "Ssbf_%d" % b
                        P.dma(lambda e, si=si, s32=s32: e.dma_start(
                            out=s32[:].rearrange("p (h c) v -> p h c v", c=2),
                            in_=dr["sgla"][si].rearrange("h (c p) v -> p h c v", p=128)), S.key("ls%d" % b),
                            writes=[n32])
                        P.act(lambda e, s32=s32, sbf=sbf: e.activation(out=sbf[:], in_=s32[:], func=AF.Copy),
                              reads=[n32], writes=[nbf])
                        for h in range(4):
                            po, por = pos[h]
                            for j in range(4):
                                for cc in range(2):
                                    P.pe(lambda e, h=h, j=j, cc=cc, po=po, sbf=sbf, si=si: e.matmul(
                                        po[:, j * 128 + si * 8:j * 128 + si * 8 + 8],
                                        lhsT=sbf[:, 2 * h + cc, j * 128:(j + 1) * 128],
                                        rhs=qd[:, 2 * h + cc, si * 8:si * 8 + 8], start=False,
                                        stop=(cc == 1 and j == 3 and si == 15)),
                                        reads=[nbf, "qd"], writes=[por])
                        for h in range(4):
                            for cc in range(2):
                                i = 2 * h + cc
                                pu, pur = S.bank([0, 1])
                                P.pe(lambda e, i=i, h=h, pu=pu, si=si: e.matmul(
                                    pu[:, :], lhsT=kdm[:, i, si, :], rhs=vb[:, h * 512:(h + 1) * 512], start=True,
                                    stop=True), reads=["kdm", "vb"], writes=[pur])
                                P.dve(lambda e, i=i, pu=pu, s32=s32, si=si: e.scalar_tensor_tensor(
                                    out=s32[:, i, :], in0=s32[:, i, :], scalar=sdec[:, i, si:si + 1], in1=pu[:, :],
                                    op0=ALU.mult, op1=ALU.add), reads=[pur, n32, "sdec"], writes=[n32])
                        P.dma(lambda e, si=si, s32=s32: e.dma_start(
                            out=dr["glas"][si].rearrange("h (c p) v -> p h c v", p=128),
                            in_=s32[:].rearrange("p (h c) v -> p h c v", c=2)), S.key("ss%d" % b), reads=[n32])
                if not need_out:
                    continue
                for h in range(4):
                    po, por = pos[h]
                    P.act(lambda e, h=h, po=po: e.activation(out=o32[:, 4 * h:4 * h + 4, :],
                                                             in_=po[:].rearrange("p (j t) -> p j t", t=128),
                                                             func=AF.Copy), reads=[por], writes=["o32"])
                    P.act(lambda e, h=h, po=po: e.activation(out=sq[:, 4 * h:4 * h + 4, :],
                                                             in_=po[:].rearrange("p (j t) -> p j t", t=128),
                                                             func=AF.Square), reads=[por], writes=["sq"])
                pst, pstr = S.bank([2])
                for h in range(4):
                    for j in range(4):
                        P.pe(lambda e, h=h, j=j, pst=pst: e.matmul(pst[:, h * 128:(h + 1) * 128], lhsT=ones32,
                                                                   rhs=sq[:, 4 * h + j, :], start=(j == 0),
                                                                   stop=(j == 3)), reads=["sq"], writes=[pstr])
                P.act(lambda e, pst=pst: e.activation(out=rstd[:], in_=pst[:].rearrange("p (h t) -> p h t", t=128),
                                                      func=AF.Sqrt, scale=1.0 / GDV, bias=EPS), reads=[pstr],
                      writes=["rstd"])
                P.dve(lambda e: e.reciprocal(out=rstd[:], in_=rstd[:]), reads=["rstd"], writes=["rstd"])
                P.dma(lambda e, t0=t0: e.dma_start(out=zg[:], in_=dr["zgT"][:, t0:t0 + 128].rearrange(
                    "(c p) t -> p c t", p=128)), S.key("l4"), writes=["zg"])
                P.act(lambda e: e.activation(out=sg[:], in_=zg[:], func=AF.Sigmoid), reads=["zg"], writes=["sg"])
                P.pool(lambda e: e.tensor_tensor(out=sg[:], in0=sg[:], in1=zg[:], op=ALU.mult), reads=["sg", "zg"],
                       writes=["sg"])
                P.pool(lambda e: e.tensor_tensor(
                    out=sg[:].rearrange("p (h j) t -> p h j t", j=4),
                    in0=sg[:].rearrange("p (h j) t -> p h j t", j=4),
                    in1=rstd[:].unsqueeze(2).to_broadcast([128, 4, 4, 128]), op=ALU.mult),
                    reads=["sg", "rstd"], writes=["sg"])
                for j in range(4):
                    P.dve(lambda e, j=j: e.scalar_tensor_tensor(
                        out=oab[:].rearrange("p (h j) t -> p h j t", j=4)[:, :, j, :],
                        in0=o32[:].rearrange("p (h j) t -> p h j t", j=4)[:, :, j, :],
                        scalar=self.pcol("gla_norm", j),
                        in1=sg[:].rearrange("p (h j) t -> p h j t", j=4)[:, :, j, :],
                        op0=ALU.mult, op1=ALU.mult), reads=["o32", "sg"], writes=["oab"])
                P.dma(lambda e, t0=t0: e.dma_start(out=dr["oaT"][:, t0:t0 + 128].rearrange("(c p) t -> p c t", p=128),
                                                   in_=oab[:]), S.key("so2"), reads=["oab"])

    def rfront_stage(self):
        c = self.cfg
        dr = self.dr
        with Stage(self, "rf") as S:
            P = S.P
            S.psum_banks(8)
            NB = 512
            W1 = NB + 16
            wdec = S.sb("wdec", [96, 2048], BF16)
            wicl = S.sb("wicl", [96, 2048], BF16)
            wgat = S.sb("wgat", [128, 2, 2048], BF16)
            P.dma(lambda e: e.dma_start(out=wdec[:], in_=dr["w_decay2"]), S.key("c1"), writes=["wdec"], q="pool")
            P.dma(lambda e: e.dma_start(out=wicl[:], in_=dr["w_iclr2"]), S.key("c1"), writes=["wicl"], q="pool")
            P.dma(lambda e: e.dma_start(out=wgat[:], in_=dr["w_gate2"].rearrange("(k p) c -> p k c", p=128)),
                  S.key("c1"), writes=["wgat"], q="pool")
            raw = {n: S.sb("raw_" + n, [128, W1], F32) for n in ("r", "k", "v")}
            rawx = {n: S.sb("rawx_" + n, [128, W1], F32) for n in ("xw", "xa", "xg0", "xg1")}
            sst = S.sb("sst", [128, 16], F32)
            tmp = S.sb("tmp", [128, NB], F32)
            xsh = S.sb("xsh", [128, NB], F32)
            thb = S.sb("thb", [96, NB], BF16)
            xab = S.sb("xab", [96, NB], BF16)
            sgx = S.sb("sgx", [128, 2, NB], BF16)
            rs_ = S.sb("rs_", [128, NB], F32)
            ks_ = S.sb("ks_", [128, NB], F32)
            vs_ = S.sb("vs_", [128, NB], F32)
            pl = S.sb("pl", [128, NB], F32)
            a_ = S.sb("a_", [128, NB], F32)
            g_ = S.sb("g_", [128, NB], F32)
            kk = S.sb("kk", [128, NB], F32)
            t1 = S.sb("t1", [128, NB], F32)
            t2 = S.sb("t2", [128, NB], F32)
            k2 = S.sb("k2", [128, NB], F32)
            cp = S.sb("cp", [128, NB], F32)
            ex1 = S.sb("ex1", [128, NB], F32)
            ex2 = S.sb("ex2", [128, NB], F32)
            bon = S.sb("bon", [128, NB], F32)
            arbk = S.sb("arbk", [128, 4, NB], BF16)
            vbf = S.sb("vbf", [128, NB], BF16)
            gc = S.sb("gc", [128, 16], F32)
            rmask = S.sb("rmask", [128, 2, NB], F32)
            P.pool(lambda e: e.memset(rmask[:], 1.0), writes=["rmask"])
            P.pool(lambda e: e.memset(rmask[:, 0, :].rearrange("p (s j) -> p s j", j=128)[:, :, 0:1], 0.0),
                   writes=["rmask"])
            P.pool(lambda e: e.memset(rmask[:, 1, :].rearrange("p (s j) -> p s j", j=8)[:, :, 0:1], 0.0),
                   writes=["rmask"])
            blk64 = self.cst("blk64")
            pcd = self.pcd
            set1 = dict(raw={n_: S.sb("rawB_" + n_, [128, W1], F32) for n_ in ("r", "k", "v")},
                        t=[S.sb("sstB", [128, 16], F32)] + [S.sb(nm_ + "B", [128, NB], F32) for nm_ in
                           ("tmp", "rs_", "ks_", "vs_", "pl", "a_", "g_", "kk", "t1", "t2", "k2", "cp", "ex1", "ex2",
                            "bon")] + [S.sb("arbkB", [128, 4, NB], BF16), S.sb("vbfB", [128, NB], BF16),
                                       S.sb("gcB", [128, 16], F32)])
            set0 = dict(raw=raw, t=[sst, tmp, rs_, ks_, vs_, pl, a_, g_, kk, t1, t2, k2, cp, ex1, ex2, bon, arbk, vbf,
                                    gc])
            BUFS = [set0, set1]

            def blocks():
                for a0 in range(0, c.CTX, NB):
                    yield (a0, min(NB, c.CTX - a0), False)
                yield (c.CTX, 128, True)

            def load_raw(PP, sst, t, tn, src, rows, r0, a0, n, smp, state_rows):
                if not smp:
                    PP.dma(lambda e: e.dma_start(out=t[0:rows, 0:n + 1], in_=src[r0:r0 + rows, a0:a0 + n + 1]),
                          S.key("lr"), writes=[tn])
                else:
                    PP.dma(lambda e: e.dma_start(out=t[0:rows, 0:144],
                                                in_=src[r0:r0 + rows, 1 + c.CTX:1 + c.CTX + 144]),
                          S.key("lr"), writes=[tn])
                    PP.dma(lambda e: e.dma_start(out=sst[0:rows, :], in_=dr["sshiftT"][state_rows:state_rows + rows, :]),
                          S.key("lr2"), writes=["sst"])
                    PP.pool(lambda e: e.tensor_copy(
                        out=t[0:rows, 0:144].rearrange("p (s j) -> p s j", j=9)[:, :, 0], in_=sst[0:rows, :]),
                        reads=["sst", tn], writes=[tn])

            def shift(PP, tmp, out, t, tn, rows, n, smp, mu, omm, wname):
                if not smp:
                    cur, prev = t[0:rows, 1:n + 1], t[0:rows, 0:n]
                    o, tm = out, tmp[0:rows, 0:n]
                else:
                    v9 = t[0:rows, 0:144].rearrange("p (s j) -> p s j", j=9)
                    cur, prev = v9[:, :, 1:9], v9[:, :, 0:8]
                    o = out.rearrange("p (s j) -> p s j", j=8)
                    tm = tmp[0:rows, 0:128].rearrange("p (s j) -> p s j", j=8)
                PP.act(lambda e: e.activation(out=tm, in_=prev, func=AF.Copy, scale=mu),
                       reads=[tn], writes=["tmp"])
                PP.dve(lambda e: e.scalar_tensor_tensor(out=o, in0=cur, scalar=omm, in1=tm, op0=ALU.mult, op1=ALU.add),
                      reads=[tn, "tmp"], writes=[wname])

            DBL = ['raw_r', 'raw_k', 'raw_v', 'sst', 'tmp', 'rs_', 'ks_', 'vs_', 'pl', 'a_', 'g_', 'kk', 't1', 't2', 'k2', 'cp', 'ex1', 'ex2', 'bon', 'arbk', 'vbf', 'gc']

            class SP:
                def __init__(self, P0, sfx, defer=False):
                    self.P0, self.sfx, self.defer, self.buf = P0, sfx, defer, []

                def ren(self, names):
                    return [(x + self.sfx) if x in DBL else x for x in names]

                def _op(self, eng, fn, reads, writes, key=None):
                    if self.defer:
                        self.buf.append((eng, fn, self.ren(reads), self.ren(writes), key))
                    elif key is None:
                        self.P0.add(eng, fn, self.ren(reads), self.ren(writes))
                    else:
                        self.P0.dma(fn, key, self.ren(reads), self.ren(writes), q=eng)

                def dve(self, fn, reads=(), writes=()):
                    self._op("dve", fn, reads, writes)

                def act(self, fn, reads=(), writes=()):
                    self._op("act", fn, reads, writes)

                def pool(self, fn, reads=(), writes=()):
                    self._op("pool", fn, reads, writes)

                def pe(self, fn, reads=(), writes=()):
                    self._op("pe", fn, reads, writes)

                def dma(self, fn, key, reads=(), writes=(), q="sp"):
                    self._op(q, fn, reads, writes, key=key)

            def interleave(recs):
                i = 0
                while any(i < len(r.buf) for r in recs):
                    for r in recs:
                        if i < len(r.buf):
                            eng, fn, rd, wr, key = r.buf[i]
                            if key is None:
                                P0.add(eng, fn, rd, wr)
                            else:
                                P0.dma(fn, key, rd, wr, q=eng)
                    i += 1

            P0 = P
            PS0 = SP(P0, "_0")

            def pair(p, a0, n, smp, mi):
                P = SP(P0, "_%d" % (p % 2), defer=True)
                X = BUFS[p % 2]
                raw = X["raw"]
                sst, tmp, rs_, ks_, vs_, pl, a_, g_, kk, t1, t2, k2, cp, ex1, ex2, bon, arbk, vbf, gc = X["t"]
                pc0 = p * 128
                load_raw(P, sst, raw["r"], "raw_r", dr["rT"], 128, pc0, a0, n, smp, ZR["r"] + pc0)
                load_raw(P, sst, raw["k"], "raw_k", dr["krT"], 128, pc0, a0, n, smp, ZR["kr"] + pc0)
                load_raw(P, sst, raw["v"], "raw_v", dr["vrT"], 128, pc0, a0, n, smp, ZR["vr"] + pc0)
                shift(P, tmp, rs_[:, 0:n], raw["r"], "raw_r", 128, n, smp, self.pcol("mu_r", p), pcd[:, 8 + p:9 + p], "rs_")
                shift(P, tmp, ks_[:, 0:n], raw["k"], "raw_k", 128, n, smp, self.pcol("mu_kr", p), pcd[:, 25 + p:26 + p], "ks_")
                shift(P, tmp, vs_[:, 0:n], raw["v"], "raw_v", 128, n, smp, self.pcol("mu_vr", p), pcd[:, 41 + p:42 + p], "vs_")
                pw, pwr = S.bank()
                P.pe(lambda e, pw=pw, p=p, n=n: e.matmul(pw[:, 0:n], lhsT=wdec[:, p * 128:(p + 1) * 128],
                                                         rhs=thb[:, 0:n], start=True, stop=True),
                     reads=["wdec", "thb"], writes=[pwr])
                P.dve(lambda e, pw=pw, p=p, n=n: e.tensor_scalar(out=t1[:, 0:n], in0=pw[:, 0:n],
                                                                 scalar1=self.pcol("w0", p), scalar2=-1.0,
                                                                 op0=ALU.add, op1=ALU.mult),
                      reads=[pwr], writes=["t1"])
                P.act(lambda e, n=n: e.activation(out=t1[:, 0:n], in_=t1[:, 0:n], func=AF.Exp), reads=["t1"],
                      writes=["t1"])
                P.act(lambda e, n=n: e.activation(out=t1[:, 0:n], in_=t1[:, 0:n], func=AF.Ln, bias=1.0),
                      reads=["t1"], writes=["t1"])
                P.act(lambda e, n=n: e.activation(out=pl[:, 0:n], in_=t1[:, 0:n], func=AF.Exp, scale=-1.0,
                                                  bias=self.kneg05()), reads=["t1"], writes=["pl"])
                pa, par = S.bank()
                P.pe(lambda e, pa=pa, p=p, n=n: e.matmul(pa[:, 0:n], lhsT=wicl[:, p * 128:(p + 1) * 128],
                                                         rhs=xab[:, 0:n], start=True, stop=True),
                     reads=["wicl", "xab"], writes=[par])
                P.act(lambda e, pa=pa, p=p, n=n: e.activation(out=a_[:, 0:n], in_=pa[:, 0:n], func=AF.Sigmoid,
                                                              bias=self.pcol("a0", p)), reads=[par], writes=["a_"])
                pg, pgr = S.bank()
                for kc in range(2):
                    P.pe(lambda e, pg=pg, p=p, n=n, kc=kc: e.matmul(
                        pg[:, 0:n], lhsT=wgat[:, kc, p * 128:(p + 1) * 128], rhs=sgx[:, kc, 0:n],
                        start=(kc == 0), stop=(kc == 1)), reads=["wgat", "sgx"], writes=[pgr])
                P.act(lambda e, pg=pg, n=n: e.activation(out=g_[:, 0:n], in_=pg[:, 0:n], func=AF.Copy),
                      reads=[pgr], writes=["g_"])
                P.dma(lambda e, pc0=pc0, a0=a0, n=n: e.dma_start(out=dr["gT"][pc0:pc0 + 128, a0:a0 + n],
                                                               in_=g_[:, 0:n]), S.key("sg"), reads=["g_"])
                P.pool(lambda e, p=p, n=n: e.tensor_scalar(out=kk[:, 0:n], in0=ks_[:, 0:n],
                                                           scalar1=self.pcol("k_k", p), scalar2=None,
                                                           op0=ALU.mult), reads=["ks_"], writes=["kk"])
                P.act(lambda e, n=n: e.activation(out=t2[:, 0:n], in_=kk[:, 0:n], func=AF.Square), reads=["kk"],
                      writes=["t2"])
                pq, pqr = S.bank()
                P.pe(lambda e, pq=pq, n=n: e.matmul(pq[:, 0:n], lhsT=blk64, rhs=t2[:, 0:n], start=True,
                                                     stop=True), reads=["t2"], writes=[pqr])
                P.dve(lambda e, pq=pq, n=n: e.tensor_scalar(out=t2[:, 0:n], in0=pq[:, 0:n], scalar1=1e-24,
                                                            scalar2=None, op0=ALU.max), reads=[pqr],
                      writes=["t2"])
                P.act(lambda e, n=n: e.activation(out=t2[:, 0:n], in_=t2[:, 0:n], func=AF.Sqrt), reads=["t2"],
                      writes=["t2"])
                P.dve(lambda e, n=n: e.reciprocal(out=t2[:, 0:n], in_=t2[:, 0:n]), reads=["t2"], writes=["t2"])
                P.dve(lambda e, n=n: e.tensor_tensor(out=kk[:, 0:n], in0=kk[:, 0:n], in1=t2[:, 0:n],
                                                     op=ALU.mult), reads=["kk", "t2"], writes=["kk"])
                P.dve(lambda e, p=p, n=n: e.tensor_scalar(out=t2[:, 0:n], in0=a_[:, 0:n],
                                                          scalar1=self.pcol("k_a", p), scalar2=self.pcol("k_a", p),
                                                          op0=ALU.mult, op1=ALU.subtract), reads=["a_"],
                      writes=["t2"])
                P.dve(lambda e, n=n: e.scalar_tensor_tensor(out=k2[:, 0:n], in0=t2[:, 0:n], scalar=1.0,
                                                            in1=ks_[:, 0:n], op0=ALU.add, op1=ALU.mult),
                      reads=["t2", "ks_"], writes=["k2"])
                P.dve(lambda e, p=p, n=n: e.scalar_tensor_tensor(out=t2[:, 0:n], in0=rs_[:, 0:n],
                                                                 scalar=self.pcol("r_k", p), in1=k2[:, 0:n],
                                                                 op0=ALU.mult, op1=ALU.mult),
                      reads=["rs_", "k2"], writes=["t2"])
                pb_, pbr = S.bank()
                P.pe(lambda e, pb_=pb_, n=n: e.matmul(pb_[:, 0:n], lhsT=blk64, rhs=t2[:, 0:n], start=True,
                                                       stop=True), reads=["t2"], writes=[pbr])
                P.dve(lambda e, pb_=pb_, n=n: e.tensor_tensor(out=bon[:, 0:n], in0=pb_[:, 0:n], in1=vs_[:, 0:n],
                                                              op=ALU.mult), reads=[pbr, "vs_"], writes=["bon"])
                P.dma(lambda e, pc0=pc0, a0=a0, n=n: e.dma_start(out=dr["bonT"][pc0:pc0 + 128, a0:a0 + n],
                                                               in_=bon[:, 0:n]), S.key("sb"), reads=["bon"])
                P.dve(lambda e, n=n, mi=mi: e.tensor_tensor_scan(out=cp[:, 0:n], data0=rmask[:, mi, 0:n],
                                                                 data1=pl[:, 0:n], initial=0.0, op0=ALU.mult,
                                                                 op1=ALU.add), reads=["pl", "rmask"], writes=["cp"])
                P.act(lambda e, n=n: e.activation(out=ex1[:, 0:n], in_=cp[:, 0:n], func=AF.Exp, scale=-1.0),
                      reads=["cp"], writes=["ex1"])
                P.dve(lambda e, n=n: e.tensor_tensor(out=arbk[:, 1, 0:n], in0=rs_[:, 0:n], in1=ex1[:, 0:n],
                                                     op=ALU.mult), reads=["rs_", "ex1"], writes=["arbk"])
                L = 8 if smp else 128
                ns = n // L
                P.pool(lambda e, n=n, L=L, ns=ns: e.tensor_copy(
                    out=gc[:, 0:ns], in_=ex1[:, 0:n].rearrange("p (s j) -> p s j", j=L)[:, :, L - 1]),
                    reads=["ex1"], writes=["gc"])
                gc0 = (c.NCT if smp else a0 // 128)
                P.dma(lambda e, p=p, gc0=gc0, ns=ns: e.dma_start(out=dr["gcT"][:, p, gc0:gc0 + ns],
                                                               in_=gc[:, 0:ns]), S.key("sc"), reads=["gc"])
                P.pool(lambda e, n=n: e.tensor_tensor(out=t2[:, 0:n], in0=pl[:, 0:n], in1=cp[:, 0:n],
                                                      op=ALU.subtract), reads=["pl", "cp", ], writes=["t2"])
                P.act(lambda e, n=n: e.activation(out=ex2[:, 0:n], in_=t2[:, 0:n], func=AF.Exp), reads=["t2"],
                      writes=["ex2"])
                P.dve(lambda e, n=n: e.scalar_tensor_tensor(out=arbk[:, 0, 0:n], in0=kk[:, 0:n], scalar=-1.0,
                                                            in1=ex2[:, 0:n], op0=ALU.mult, op1=ALU.mult),
                      reads=["kk", "ex2"], writes=["arbk"])
                P.act(lambda e, n=n: e.activation(out=ex2[:, 0:n], in_=cp[:, 0:n], func=AF.Exp), reads=["cp"],
                      writes=["ex2"])
                P.pool(lambda e, n=n: e.tensor_tensor(out=t2[:, 0:n], in0=kk[:, 0:n], in1=a_[:, 0:n],
                                                      op=ALU.mult), reads=["kk", "a_"], writes=["t2"])
                P.dve(lambda e, n=n: e.tensor_tensor(out=arbk[:, 2, 0:n], in0=t2[:, 0:n], in1=ex2[:, 0:n],
                                                     op=ALU.mult), reads=["t2", "ex2"], writes=["arbk"])
                P.dve(lambda e, n=n: e.tensor_tensor(out=arbk[:, 3, 0:n], in0=k2[:, 0:n], in1=ex2[:, 0:n],
                                                     op=ALU.mult), reads=["k2", "ex2"], writes=["arbk"])
                P.act(lambda e, n=n: e.activation(out=vbf[:, 0:n], in_=vs_[:, 0:n], func=AF.Copy), reads=["vs_"],
                      writes=["vbf"])
                P.dma(lambda e, p=p, a0=a0, n=n: e.dma_start(out=dr["arbk"][:, p, :, a0:a0 + n],
                                                           in_=arbk[:, :, 0:n]), S.key("sa"), reads=["arbk"])
                P.dma(lambda e, p=p, a0=a0, n=n: e.dma_start(out=dr["vTb"][:, p, a0:a0 + n], in_=vbf[:, 0:n]),
                      S.key("sv"), reads=["vbf"])
                return P

            ZR = {"r": 0, "xw": 2048, "kr": 2144, "vr": 4192, "xa": 6240, "xg": 6336}
            chunk_i = 0
            for (a0, n, smp) in blocks():
                mi = 1 if smp else 0
                load_raw(PS0, sst, rawx["xw"], "rawx_xw", dr["xwT"], 96, 0, a0, n, smp, ZR["xw"])
                shift(PS0, tmp, xsh[0:96, 0:n], rawx["xw"], "rawx_xw", 96, n, smp, self.pcol("mu_xw")[0:96], pcd[0:96, 24:25], "xsh")
                P.act(lambda e, n=n: e.activation(out=thb[:, 0:n], in_=xsh[0:96, 0:n], func=AF.Tanh), reads=["xsh"],
                      writes=["thb"])
                load_raw(PS0, sst, rawx["xa"], "rawx_xa", dr["xaT"], 96, 0, a0, n, smp, ZR["xa"])
                shift(PS0, tmp, xsh[0:96, 0:n], rawx["xa"], "rawx_xa", 96, n, smp, self.pcol("mu_xa")[0:96], pcd[0:96, 57:58], "xsh")
                P.act(lambda e, n=n: e.activation(out=xab[:, 0:n], in_=xsh[0:96, 0:n], func=AF.Copy), reads=["xsh"],
                      writes=["xab"])
                for j in range(2):
                    nm = "xg%d" % j
                    load_raw(PS0, sst, rawx[nm], "rawx_" + nm, dr["xgT"], 128, j * 128, a0, n, smp, ZR["xg"] + j * 128)
                    shift(PS0, tmp, xsh[:, 0:n], rawx[nm], "rawx_" + nm, 128, n, smp, self.pcol("mu_xg", j), pcd[:, 58 + j:59 + j],
                          "xsh")
                    P.act(lambda e, n=n, j=j: e.activation(out=sgx[:, j, 0:n], in_=xsh[:, 0:n], func=AF.Sigmoid),
                          reads=["xsh"], writes=["sgx"])
                for p in range(0, 16, 2):
                    interleave([pair(p, a0, n, smp, mi), pair(p + 1, a0, n, smp, mi)])

    def rscan_stage(self, smp):
        c = self.cfg
        dr = self.dr
        NP = 4 if smp else 8
        NH = 2 * NP
        nseg = 16 if smp else 1
        L = 128 // nseg
        sfx = "s" if smp else "p"
        with Stage(self, "rs" + sfx) as S:
            P = S.P
            S.psum_banks(7)
            ptr = S.st.enter_context(self.nc.psum_tensor("rs%s_ptr" % sfx, [128, 8, 128], BF16))
            ARBK = S.sb("ARBK", [128, NP, 4, 128], BF16)
            AR = ARBK[:, :, 0:2, :]
            BK = ARBK[:, :, 2:4, :]
            vT = S.sb("vT", [128, NP, 128], BF16)
            tmB = S.sb("tmB", [128, NP, 128], BF16)
            tmK = S.sb("tmK", [128, NP, 128], BF16)
            tmV = S.sb("tmV", [128, NP, 128], BF16)
            Vpad = S.sb("Vpad", [128, 2, NP, 128], BF16)
            Upad = S.sb("Upad", [128, 2, NP, 128], BF16)
            mats = S.sb("mats", [128, NH, 4, 128], BF16)
            Nb = [S.sb("N%d" % i, [128, NH, 128], BF16) for i in range(2)]
            NTb = [S.sb("NT%d" % i, [128, NH, 128], BF16) for i in range(2)]
            MTb = [S.sb("MT%d" % i, [128, NH, 128], BF16) for i in range(2)]
            maskM = S.sb("maskM", [128, 4, 128], F32)
            T32 = S.sb("T32", [128, nseg, NP, 128], F32)
            Tbf = S.sb("Tbf", [128, nseg, NP, 128], BF16)
            LV = S.sb("LV", [128, NP, 128], F32)
            Xb = S.sb("Xb", [128, NP, 128], BF16)
            Ub = S.sb("Ub", [128, NP, 128], BF16)
            gcs = S.sb("gcs", [128, NP, c.NCT + 16], F32)
            tmpT = S.sb("tmpT", [128, 4, 128], F32)
            o32 = S.sb("o32", [128, NP, 128], F32)
            cen = S.sb("cen", [128, NP, 128], F32)
            sqv = S.sb("sqv", [128, NP, 128], F32)
            rsd = S.sb("rsd", [128, NP, 128], F32)
            bon = S.sb("bon", [128, NP, 128], F32)
            gg = S.sb("gg", [128, NP, 128], F32)
            obb = S.sb("obb", [128, NP, 128], BF16)
            if smp:
                Am = S.sb("Am", [128, NP, 16, 128], BF16)
                Bm = S.sb("Bm", [128, NP, 16, 128], BF16)
                Km = S.sb("Km", [128, NP, 16, 128], BF16)
            identb = self.cst("ident", True)
            blk64 = self.cst("blk64")
            blk64b = self.cst("blk64", True)
            strT = self.cst("strT_" + sfx)
            incT = self.cst("incT_" + sfx)
            strN = self.cst("str_" + sfx)
            for i_, m_ in enumerate((strT, incT, strT, incT)):
                P.pool(lambda e, i_=i_, m_=m_: e.tensor_copy(out=maskM[:, i_, :], in_=m_), writes=["maskM"])
            zer = S.sb("zer", [128, 512], BF16)
            P.pool(lambda e: e.memset(zer[:], 0.0), writes=["zer"])
            P.pool(lambda e: e.memset(Vpad[:], 0.0), writes=["Vpad"])
            P.pool(lambda e: e.memset(Upad[:], 0.0), writes=["Upad"])
            RP = [0, 1, 2]
            OB = [3, 4, 5, 6]
            levels = []
            pw_ = 2
            while pw_ < L:
                levels.append(pw_)
                pw_ *= 2
            work = ([(t, p0) for p0 in range(0, 16, NP) for t in range(c.NCT)] if not smp
                    else [(c.NCT, p0) for p0 in range(0, 16, NP)])
            for (tile, p0) in work:
                t0 = tile * 128
                need_out = t0 >= c.HALO0
                if not smp and tile == 0:
                    P.pool(lambda e: e.memset(T32[:], 0.0), writes=["T32"])
                    P.pool(lambda e: e.memset(Tbf[:], 0.0), writes=["Tbf"])
                P.dma(lambda e, t0=t0, p0=p0: e.dma_start(
                    out=ARBK[:].rearrange("p q f t -> p (q f) t"),
                    in_=dr["arbk"][:, p0:p0 + NP, :, t0:t0 + 128].rearrange("p q f t -> p (q f) t")),
                    S.key("l1"), writes=["AR", "BK"])
                P.dma(lambda e, t0=t0, p0=p0: e.dma_start(out=vT[:], in_=dr["vTb"][:, p0:p0 + NP, t0:t0 + 128]),
                      S.key("l2"), writes=["vT"])
                gq0 = c.NCT if smp else tile
                if smp or tile == 0:
                    P.dma(lambda e, p0=p0: e.dma_start(out=gcs[:], in_=dr["gcT"][:, p0:p0 + NP, :]),
                          S.key("l3"), writes=["gcs"])
                if smp:
                    for sg_ in range(16):
                        P.dma(lambda e, sg_=sg_, p0=p0: e.dma_start(out=T32[:, sg_, :, :],
                                                                   in_=dr["srwkv"][sg_][:, p0:p0 + NP, :]),
                              S.key("l4"), writes=["T32"])
                    P.act(lambda e: e.activation(out=Tbf[:], in_=T32[:], func=AF.Copy), reads=["T32"], writes=["Tbf"])
                for (src, fi, dst, dn) in ((BK, 0, tmB, "tmB"), (BK, 1, tmK, "tmK"), (vT, None, tmV, "tmV")):
                    if RS_CUT is not None and RS_CUT <= -2:
                        continue
                    for g0 in range(0, NP, 8):
                        gn = min(8, NP - g0)
                        for q in range(gn):
                            in_ = src[:, g0 + q, fi, :] if fi is not None else src[:, g0 + q, :]
                            P.pe(lambda e, q=q, in_=in_: e.transpose(ptr[:, q, :], in_, identb),
                                 reads=["BK" if fi is not None else "vT"], writes=["ptr"])
                        P.act(lambda e, dst=dst, g0=g0, gn=gn: e.activation(out=dst[:, g0:g0 + gn, :],
                                                                            in_=ptr[:, 0:gn, :], func=AF.Copy),
                              reads=["ptr"], writes=[dn])
                        if dst is tmV:
                            P.pool(lambda e, g0=g0, gn=gn: e.tensor_copy(out=Vpad[:, 0, g0:g0 + gn, 0:64],
                                                                        in_=tmV[:, g0:g0 + gn, 0:64]), reads=["tmV"],
                                   writes=["Vpad"])
                            P.pool(lambda e, g0=g0, gn=gn: e.tensor_copy(out=Vpad[:, 1, g0:g0 + gn, 64:128],
                                                                        in_=tmV[:, g0:g0 + gn, 64:128]), reads=["tmV"],
                                   writes=["Vpad"])
                if RS_CUT is not None and RS_CUT < 1:
                    continue
                for hh in range(NH):
                    if "m" in RS_SUB:
                        break
                    q, h = hh // 2, hh % 2
                    pm, pmr = S.bank(RP)
                    rr_ = slice(64 * h, 64 * h + 64)
                    P.pe(lambda e, pm=pm, q=q, rr_=rr_: e.matmul(
                        pm[:, 0:256], lhsT=BK[rr_, q, 0, :], rhs=AR[rr_, q, :, :].rearrange("p f t -> p (f t)"),
                        start=True, stop=True), reads=["BK", "AR"], writes=[pmr])
                    P.pe(lambda e, pm=pm, q=q, rr_=rr_: e.matmul(
                        pm[:, 256:512], lhsT=BK[rr_, q, 1, :], rhs=AR[rr_, q, :, :].rearrange("p f t -> p (f t)"),
                        start=True, stop=True), reads=["BK", "AR"], writes=[pmr])
                    if "e" in RS_SUB:
                        continue
                    P.dve(lambda e, pm=pm, hh=hh: e.tensor_tensor(
                        out=mats[:, hh, :, :], in0=pm[:].rearrange("p (f t) -> p f t", t=128), in1=maskM[:],
                        op=ALU.mult), reads=[pmr, "maskM"], writes=["mats"])
                for q0 in range(0, NP, 4):
                    for h in range(2):
                        pl_, plr = S.bank(RP)
                        rr_ = slice(64 * h, 64 * h + 64)
                        for j in range(4):
                            q = q0 + j
                            P.pe(lambda e, pl_=pl_, j=j, q=q, rr_=rr_: e.matmul(
                                pl_[:, j * 128:(j + 1) * 128], lhsT=AR[rr_, q, 0, :], rhs=BK[rr_, q, 0, :], start=True,
                                stop=True), reads=["AR", "BK"], writes=[plr])
                        P.dve(lambda e, pl_=pl_, q0=q0, h=h: e.tensor_tensor(
                            out=Nb[0][:].rearrange("p (q h) t -> p q h t", h=2)[:, q0:q0 + 4, h, :],
                            in0=pl_[:].rearrange("p (j t) -> p j t", t=128),
                            in1=strN.unsqueeze(1).to_broadcast([128, 4, 128]), op=ALU.mult), reads=[plr],
                            writes=["N0"])
                if RS_CUT is not None and RS_CUT < 2:
                    continue
                P.pool(lambda e: e.tensor_copy(out=NTb[0][:], in_=mats[:, :, 0, :]), reads=["mats"], writes=["NT0"])
                P.pool(lambda e: e.tensor_tensor(out=MTb[0][:], in0=mats[:, :, 0, :],
                                                 in1=identb.unsqueeze(1).to_broadcast([128, NH, 128]), op=ALU.add),
                       reads=["mats"], writes=["MT0"])
                cur = 0
                for li, lv in enumerate(levels):
                    last = li == len(levels) - 1
                    nxt = 1 - cur
                    for g0 in range(0, NH, 4):
                        pn, pnr = S.bank(RP)
                        for j in range(4):
                            P.pe(lambda e, pn=pn, j=j, g0=g0, cur=cur: e.matmul(
                                pn[:, j * 128:(j + 1) * 128], lhsT=NTb[cur][:, g0 + j, :], rhs=Nb[cur][:, g0 + j, :],
                                start=True, stop=True), reads=["N%d" % cur, "NT%d" % cur], writes=[pnr])
                        P.act(lambda e, pn=pn, g0=g0, nxt=nxt: e.activation(
                            out=Nb[nxt][:, g0:g0 + 4, :], in_=pn[:].rearrange("p (j t) -> p j t", t=128),
                            func=AF.Copy), reads=[pnr], writes=["N%d" % nxt])
                        if not last:
                            pt_, ptr_ = S.bank(RP)
                            for j in range(4):
                                P.pe(lambda e, pt_=pt_, j=j, g0=g0, cur=cur: e.matmul(
                                    pt_[:, j * 128:(j + 1) * 128], lhsT=Nb[cur][:, g0 + j, :],
                                    rhs=NTb[cur][:, g0 + j, :], start=True, stop=True),
                                    reads=["N%d" % cur, "NT%d" % cur], writes=[ptr_])
                            P.act(lambda e, pt_=pt_, g0=g0, nxt=nxt: e.activation(
                                out=NTb[nxt][:, g0:g0 + 4, :], in_=pt_[:].rearrange("p (j t) -> p j t", t=128),
                                func=AF.Copy), reads=[ptr_], writes=["NT%d" % nxt])
                        pp, ppr = S.bank(RP)
                        for j in range(4):
                            P.pe(lambda e, pp=pp, j=j, g0=g0, cur=cur, nxt=nxt: e.matmul(
                                pp[:, j * 128:(j + 1) * 128], lhsT=Nb[nxt][:, g0 + j, :], rhs=MTb[cur][:, g0 + j, :],
                                start=True, stop=False), reads=["N%d" % nxt, "MT%d" % cur], writes=[ppr])
                            P.pe(lambda e, pp=pp, j=j, g0=g0, cur=cur: e.matmul(
                                pp[:, j * 128:(j + 1) * 128], lhsT=identb, rhs=MTb[cur][:, g0 + j, :],
                                start=False, stop=True), reads=["MT%d" % cur], writes=[ppr])
                        P.act(lambda e, pp=pp, g0=g0, nxt=nxt: e.activation(
                            out=MTb[nxt][:, g0:g0 + 4, :], in_=pp[:].rearrange("p (j t) -> p j t", t=128),
                            func=AF.Copy), reads=[ppr], writes=["MT%d" % nxt])
                    cur = nxt
                MT = MTb[cur]
                mtn = "MT%d" % cur
                if RS_CUT is not None and RS_CUT < 3:
                    continue
                if smp:
                    segcol = self.c32x
                    seg16 = self.cst("seg16", True)
                    for q in range(NP):
                        P.dve(lambda e, q=q: e.tensor_tensor(
                            out=Am[:, q, :, :], in0=AR[:, q, 0, :].unsqueeze(1).to_broadcast([128, 16, 128]),
                            in1=segcol[:], op=ALU.mult), reads=["AR"], writes=["Am"])
                        P.pool(lambda e, q=q: e.tensor_tensor(
                            out=Bm[:, q, :, :], in0=tmB[:, q, :].unsqueeze(1).to_broadcast([128, 16, 128]),
                            in1=seg16[:, 0:16].unsqueeze(2).to_broadcast([128, 16, 128]), op=ALU.mult),
                            reads=["tmB"], writes=["Bm"])
                        P.pool(lambda e, q=q: e.tensor_tensor(
                            out=Km[:, q, :, :], in0=tmK[:, q, :].unsqueeze(1).to_broadcast([128, 16, 128]),
                            in1=seg16[:, 0:16].unsqueeze(2).to_broadcast([128, 16, 128]), op=ALU.mult),
                            reads=["tmK"], writes=["Km"])
                if RS_CUT is not None and RS_CUT < 4:
                    continue
                if need_out:
                    for q in range(NP):
                        po, por = S.banks[OB[q // 4]], "pb%d" % OB[q // 4]
                        osl = slice((q % 4) * 128, (q % 4) * 128 + 128)
                        if q % 4 == 0:
                            P.pe(lambda e, po=po: e.matmul(po[:, :], lhsT=zer[:, 0:128], rhs=zer[:, :], start=True,
                                                           stop=False), reads=["zer"], writes=[por])
                        for h in range(2):
                            P.pe(lambda e, po=po, osl=osl, q=q, h=h: e.matmul(
                                po[:, osl], lhsT=Vpad[:, h, q, :], rhs=mats[:, 2 * q + h, 3, :], start=False,
                                stop=False), reads=["Vpad", "mats"], writes=[por])
                        for sg_ in range(nseg):
                            P.pe(lambda e, po=po, q=q, sg_=sg_: e.matmul(
                                po[:, (q % 4) * 128 + sg_ * L:(q % 4) * 128 + (sg_ + 1) * L], lhsT=Tbf[:, sg_, q, :],
                                rhs=AR[:, q, 1, sg_ * L:(sg_ + 1) * L], start=False, stop=False),
                                reads=["Tbf", "AR"], writes=[por])
                if RS_CUT is not None and RS_CUT < 5:
                    continue
                for g0 in range(0, NP, 4):
                    pv, pvr = S.bank(RP)
                    for j in range(4):
                        q = g0 + j
                        for h in range(2):
                            P.pe(lambda e, pv=pv, j=j, q=q, h=h: e.matmul(
                                pv[:, j * 128 + 64 * h:j * 128 + 64 * h + 64], lhsT=mats[:, 2 * q + h, 2, :],
                                rhs=tmV[:, q, 64 * h:64 * h + 64], start=True, stop=True),
                                reads=["mats", "tmV"], writes=[pvr])
                    P.act(lambda e, pv=pv, g0=g0: e.activation(out=LV[:, g0:g0 + 4, :],
                                                               in_=pv[:].rearrange("p (j t) -> p j t", t=128),
                                                               func=AF.Copy), reads=[pvr], writes=["LV"])
                if RS_CUT is not None and RS_CUT < 6:
                    continue
                for g0 in range(0, NP, 4):
                    px, pxr = S.bank(RP)
                    for j in range(4):
                        q = g0 + j
                        for sg_ in range(nseg):
                            lh = Am[:, q, sg_, :] if smp else AR[:, q, 0, :]
                            P.pe(lambda e, px=px, j=j, q=q, sg_=sg_, lh=lh: e.matmul(
                                px[:, j * 128:(j + 1) * 128], lhsT=lh, rhs=Tbf[:, sg_, q, :], start=(sg_ == 0),
                                stop=(sg_ == nseg - 1)), reads=["Am" if smp else "AR", "Tbf"], writes=[pxr])
                    P.dve(lambda e, px=px, g0=g0: e.tensor_tensor(
                        out=Xb[:, g0:g0 + 4, :], in0=px[:].rearrange("p (j t) -> p j t", t=128),
                        in1=LV[:, g0:g0 + 4, :], op=ALU.add), reads=[pxr, "LV"], writes=["Xb"])
                if RS_CUT is not None and RS_CUT < 7:
                    continue
                for g0 in range(0, NP, 4):
                    pu, pur = S.bank(RP)
                    for j in range(4):
                        q = g0 + j
                        for h in range(2):
                            P.pe(lambda e, pu=pu, j=j, q=q, h=h: e.matmul(
                                pu[:, j * 128 + 64 * h:j * 128 + 64 * h + 64], lhsT=MT[:, 2 * q + h, :],
                                rhs=Xb[:, q, 64 * h:64 * h + 64], start=True, stop=True), reads=[mtn, "Xb"],
                                writes=[pur])
                    puv = pu[:].rearrange("p (j t) -> p j t", t=128)
                    P.act(lambda e, puv=puv, g0=g0: e.activation(out=Ub[:, g0:g0 + 4, :], in_=puv, func=AF.Copy),
                          reads=[pur], writes=["Ub"])
                    if need_out:
                        P.pool(lambda e, g0=g0: e.tensor_copy(out=Upad[:, 0, g0:g0 + 4, 0:64],
                                                              in_=Ub[:, g0:g0 + 4, 0:64]), reads=["Ub"],
                               writes=["Upad"])
                        P.pool(lambda e, g0=g0: e.tensor_copy(out=Upad[:, 1, g0:g0 + 4, 64:128],
                                                              in_=Ub[:, g0:g0 + 4, 64:128]), reads=["Ub"],
                               writes=["Upad"])
                if RS_CUT is not None and RS_CUT < 8:
                    continue
                if need_out:
                    for q in range(NP):
                        po, por = S.banks[OB[q // 4]], "pb%d" % OB[q // 4]
                        osl = slice((q % 4) * 128, (q % 4) * 128 + 128)
                        for h in range(2):
                            P.pe(lambda e, po=po, osl=osl, q=q, h=h: e.matmul(
                                po[:, osl], lhsT=Upad[:, h, q, :], rhs=mats[:, 2 * q + h, 1, :], start=False,
                                stop=(h == 1 and q % 4 == 3)), reads=["Upad", "mats"], writes=[por])
                if RS_CUT is not None and RS_CUT < 9:
                    continue
                for sg_ in range(nseg):
                    for g0 in range(0, NP, 4):
                        pt2, pt2r = S.bank(RP)
                        for j in range(4):
                            q = g0 + j
                            lb = Bm[:, q, sg_, :] if smp else tmB[:, q, :]
                            lk = Km[:, q, sg_, :] if smp else tmK[:, q, :]
                            P.pe(lambda e, pt2=pt2, j=j, q=q, lb=lb: e.matmul(
                                pt2[:, j * 128:(j + 1) * 128], lhsT=lb, rhs=Ub[:, q, :], start=True, stop=False),
                                reads=["Bm" if smp else "tmB", "Ub"], writes=[pt2r])
                            P.pe(lambda e, pt2=pt2, j=j, q=q, lk=lk: e.matmul(
                                pt2[:, j * 128:(j + 1) * 128], lhsT=lk, rhs=tmV[:, q, :], start=False, stop=True),
                                reads=["Km" if smp else "tmK", "tmV"], writes=[pt2r])
                        P.dve(lambda e, pt2=pt2: e.tensor_tensor(
                            out=tmpT[:], in0=pt2[:].rearrange("p (j t) -> p j t", t=128),
                            in1=blk64.unsqueeze(1).to_broadcast([128, 4, 128]), op=ALU.mult), reads=[pt2r],
                            writes=["tmpT"])
                        P.dve(lambda e, sg_=sg_, g0=g0: e.tensor_tensor(out=tmpT[:], in0=tmpT[:],
                                                                       in1=T32[:, sg_, g0:g0 + 4, :], op=ALU.add),
                              reads=["tmpT", "T32"], writes=["tmpT"])
                        P.dve(lambda e, sg_=sg_, g0=g0, gq0=gq0: e.tensor_tensor(
                            out=T32[:, sg_, g0:g0 + 4, :], in0=tmpT[:],
                            in1=gcs[:, g0:g0 + 4, gq0 + sg_:gq0 + sg_ + 1].to_broadcast([128, 4, 128]), op=ALU.mult),
                            reads=["tmpT", "gcs"], writes=["T32"])
                if not smp:
                    P.act(lambda e: e.activation(out=Tbf[:], in_=T32[:], func=AF.Copy), reads=["T32"], writes=["Tbf"])
                    if tile == c.NCT - 1:
                        P.dma(lambda e, p0=p0: e.dma_start(out=dr["rwkvp"][:, p0:p0 + NP, :], in_=T32[:, 0, :, :]),
                              S.key("so"), reads=["T32"])
                else:
                    for sg_ in range(16):
                        P.dma(lambda e, sg_=sg_, p0=p0: e.dma_start(out=dr["rwkvs"][sg_][:, p0:p0 + NP, :],
                                                                   in_=T32[:, sg_, :, :]), S.key("so"),
                              reads=["T32"])
                if not need_out:
                    continue
                if RS_CUT is not None and RS_CUT < 11:
                    continue
                P.dma(lambda e, t0=t0, p0=p0: e.dma_start(
                    out=bon[:], in_=dr["bonT"][p0 * 128:(p0 + NP) * 128, t0:t0 + 128].rearrange("(q p) t -> p q t",
                                                                                             p=128)),
                    S.key("l5"), writes=["bon"])
                P.dma(lambda e, t0=t0, p0=p0: e.dma_start(
                    out=gg[:], in_=dr["gT"][p0 * 128:(p0 + NP) * 128, t0:t0 + 128].rearrange("(q p) t -> p q t",
                                                                                           p=128)),
                    S.key("l6"), writes=["gg"])
                for g0 in range(0, NP, 4):
                    po, por = S.banks[OB[g0 // 4]], "pb%d" % OB[g0 // 4]
                    P.act(lambda e, po=po, g0=g0: e.activation(out=o32[:, g0:g0 + 4, :],
                                                               in_=po[:].rearrange("p (j t) -> p j t", t=128),
                                                               func=AF.Copy), reads=[por], writes=["o32"])
                    pm1, pm1r = S.bank(RP)
                    P.pe(lambda e, pm1=pm1, g0=g0: e.matmul(pm1[:, :], lhsT=blk64,
                                                            rhs=o32[:, g0:g0 + 4, :].rearrange("p j t -> p (j t)"),
                                                            start=True, stop=True), reads=["o32"], writes=[pm1r])
                    P.dve(lambda e, pm1=pm1, g0=g0: e.scalar_tensor_tensor(
                        out=cen[:, g0:g0 + 4, :], in0=pm1[:].rearrange("p (j t) -> p j t", t=128), scalar=-1.0 / 64,
                        in1=o32[:, g0:g0 + 4, :], op0=ALU.mult, op1=ALU.add), reads=[pm1r, "o32"], writes=["cen"])
                    P.act(lambda e, g0=g0: e.activation(out=sqv[:, g0:g0 + 4, :], in_=cen[:, g0:g0 + 4, :],
                                                        func=AF.Square), reads=["cen"], writes=["sqv"])
                    pm2, pm2r = S.bank(RP)
                    P.pe(lambda e, pm2=pm2, g0=g0: e.matmul(pm2[:, :], lhsT=blk64,
                                                            rhs=sqv[:, g0:g0 + 4, :].rearrange("p j t -> p (j t)"),
                                                            start=True, stop=True), reads=["sqv"], writes=[pm2r])
                    P.act(lambda e, pm2=pm2, g0=g0: e.activation(
                        out=rsd[:, g0:g0 + 4, :], in_=pm2[:].rearrange("p (j t) -> p j t", t=128), func=AF.Sqrt,
                        scale=1.0 / 64, bias=GN_EPS), reads=[pm2r], writes=["rsd"])
                    P.dve(lambda e, g0=g0: e.reciprocal(out=rsd[:, g0:g0 + 4, :], in_=rsd[:, g0:g0 + 4, :]),
                          reads=["rsd"], writes=["rsd"])
                    P.dve(lambda e, g0=g0: e.tensor_tensor(out=cen[:, g0:g0 + 4, :], in0=cen[:, g0:g0 + 4, :],
                                                           in1=rsd[:, g0:g0 + 4, :], op=ALU.mult),
                          reads=["cen", "rsd"], writes=["cen"])
                    for j in range(4):
                        q = g0 + j
                        P.pool(lambda e, q=q, p0=p0: e.tensor_scalar(
                            out=cen[:, q, :], in0=cen[:, q, :], scalar1=self.pcol("ln_x_w", p0 + q),
                            scalar2=self.pcol("ln_x_b", p0 + q), op0=ALU.mult, op1=ALU.add), reads=["cen"],
                            writes=["cen"])
                    P.dve(lambda e, g0=g0: e.tensor_tensor(out=cen[:, g0:g0 + 4, :], in0=cen[:, g0:g0 + 4, :],
                                                           in1=bon[:, g0:g0 + 4, :], op=ALU.add),
                          reads=["cen", "bon"], writes=["cen"])
                    P.dve(lambda e, g0=g0: e.tensor_tensor(out=obb[:, g0:g0 + 4, :], in0=cen[:, g0:g0 + 4, :],
                                                           in1=gg[:, g0:g0 + 4, :], op=ALU.mult),
                          reads=["cen", "gg"], writes=["obb"])
                P.dma(lambda e, t0=t0, p0=p0: e.dma_start(
                    out=dr["obT"][p0 * 128:(p0 + NP) * 128, t0:t0 + 128].rearrange("(q p) t -> p q t", p=128),
                    in_=obb[:]), S.key("so2"), reads=["obb"])

    def lin_stage(self, name, W, KC, kparts, a_src, t0, nt, chunks, epi, a_cast=False, extra=None, wblk=512):
        with Stage(self, name) as S:
            S.psum_banks(8)
            if a_cast:
                aT = S.sb("aT", [128, KC, nt], BF16)
                S.P.dma(lambda e: e.dma_start(out=aT[:], in_=a_src[:, t0:t0 + nt].rearrange("(k p) t -> p k t", p=128)),
                        S.key("aT"), writes=["aT.0"], q="pool")
            else:
                aT = self.load_aT(S, "aT", a_src, KC, t0, nt, kparts)
            S.stg = [S.sb("stg%d" % i, [128, nt], F32) for i in range(3)]
            S.side = {}
            S.cnt = 0
            if extra:
                extra(S)

            def evac(ci, c0, cw, pss):
                si = S.cnt % 3
                S.cnt += 1
                epi(S, ci, c0, cw, pss, S.stg[si], "stg%d" % si)

            self.fm_linear(S, W, KC, kparts, chunks, aT, "aT", nt, evac, wblk=wblk)

    def side_load(self, S, tag, src, r0, cw, t0, nt, dt=F32):
        if tag not in S.side:
            S.side[tag] = [[S.sb("sd%s%d" % (tag, i), [128, nt], dt) for i in range(2)], 0]
        bufs, k = S.side[tag]
        S.side[tag][1] = k + 1
        t = bufs[k % 2]
        rn = "sd%s%d" % (tag, k % 2)
        S.P.dma(lambda e: e.dma_start(out=t[0:cw, :], in_=src[r0:r0 + cw, t0:t0 + nt]), S.key(rn), writes=[rn])
        return t, rn

    def post_stages(self):
        c = self.cfg
        dr = self.dr
        T0, NT = c.HALO0, c.TPM
        ch32 = [(j * 128, 128) for j in range(32)]

        def epi_a(S, ci, c0, cw, pss, stg, sn):
            ga, gn = self.side_load(S, "ga", dr["gaT"], c0, cw, T0, NT)
            for (ps, pr, a0, n) in pss:
                S.P.dve(lambda e, ps=ps, a0=a0, n=n: e.tensor_tensor(out=stg[0:cw, a0:a0 + n], in0=ps,
                                                                     in1=ga[0:cw, a0:a0 + n], op=ALU.mult),
                        reads=[pr, gn], writes=[sn])
            S.P.dma(lambda e: e.dma_start(out=dr["yoT"][c0:c0 + cw, T0:T0 + NT], in_=stg[0:cw, :]), S.key("st"),
                    reads=[sn])

        if self.want("bra"):
            self.lin_stage("bra", dr["w_branch_a"], 16, 128, dr["oaT"], T0, NT, ch32, epi_a)

        def epi_b(S, ci, c0, cw, pss, stg, sn):
            gb, gn = self.side_load(S, "gb", dr["gbT"], c0, cw, T0, NT)
            ma, mn = self.side_load(S, "ma", dr["yoT"], c0, cw, T0, NT)
            mb, mbn = S.mixb[S.cnt % 2], "mixb%d" % (S.cnt % 2)
            for (ps, pr, a0, n) in pss:
                S.P.dve(lambda e, ps=ps, a0=a0, n=n: e.tensor_tensor(out=stg[0:cw, a0:a0 + n], in0=ps,
                                                                     in1=gb[0:cw, a0:a0 + n], op=ALU.mult),
                        reads=[pr, gn], writes=[sn])
            S.P.pool(lambda e: e.tensor_tensor(out=mb[0:cw, :], in0=stg[0:cw, :], in1=ma[0:cw, :], op=ALU.add),
                     reads=[sn, mn], writes=[mbn])
            S.P.dma(lambda e: e.dma_start(out=dr["mixT"][c0:c0 + cw, T0:T0 + NT], in_=mb[0:cw, :]), S.key("st"),
                    reads=[mbn])

        def extra_b(S):
            S.mixb = [S.sb("mixb%d" % i, [128, NT], BF16) for i in range(2)]

        if self.want("brb"):
            self.lin_stage("brb", dr["w_branch_b"], 16, 128, dr["obT"], T0, NT, ch32, epi_b, extra=extra_b)

        def epi_store(dst, tofs, ntk):
            def epi(S, ci, c0, cw, pss, stg, sn):
                for (ps, pr, a0, n) in pss:
                    self.copy_ps(S, stg[0:cw, a0:a0 + n], ps, pr, sn)
                S.P.dma(lambda e: e.dma_start(out=dst[c0:c0 + cw, tofs:tofs + ntk], in_=stg[0:cw, :]),
                        S.key("st"), reads=[sn])
            return epi

        if self.want("wout"):
            self.lin_stage("wo", dr["w_out"], 32, 128, dr["mixT"], T0, NT, ch32, epi_store(dr["yoT"], T0, NT))
        if self.want("norm2"):
            self.norm_stage("n2", dr["yoT"], T0, NT, "g_post_mix", dr["hfT"], res=dr["xT"], out_x=dr["x1T"],
                            gain2="g_pre_ffn")

        nfc = DFF // 128

        def epi_up(S, ci, c0, cw, pss, stg, sn):
            isg = c0 < DFF
            dst = dr["ugT"] if isg else dr["uvT"]
            r0 = c0 if isg else c0 - DFF
            for (ps, pr, a0, n) in pss:
                self.copy_ps(S, stg[0:cw, a0:a0 + n], ps, pr, sn)
            S.P.dma(lambda e: e.dma_start(out=dst[r0:r0 + cw, T0:T0 + NT], in_=stg[0:cw, :]), S.key("st"), reads=[sn])
            if isg:
                nctx = c.CTX - T0
                S.P.dma(lambda e: e.dma_start(out=dr["convTp"][r0:r0 + cw, :], in_=stg[0:cw, nctx - 2:nctx]),
                        S.key("st"), reads=[sn])
                lc, ln = S.lc[S.cnt % 2], "lc%d" % (S.cnt % 2)
                S.P.pool(lambda e: e.tensor_copy(
                    out=lc[0:cw, :, :], in_=stg[0:cw, NT - 128:NT].rearrange("p (s j) -> p s j", j=8)[:, :, 6:8]),
                    reads=[sn], writes=[ln])
                S.P.dma(lambda e: e.dma_start(out=dr["convTs"][r0:r0 + cw, :, :], in_=lc[0:cw, :, :]), S.key("st"),
                        reads=[ln])

        def extra_up(S):
            S.lc = [S.sb("lc%d" % i, [128, 16, 2], F32) for i in range(2)]

        if self.want("up"):
            chunks = [(j * 128, 128) for j in range(2 * nfc)]
            self.lin_stage("up", dr["w_up"], 32, 128, dr["hfT"], T0, NT, chunks, epi_up, extra=extra_up)
        if self.want("ffact"):
            self.ffact_stage()

        O0, NO = c.OWN0, c.TOW
        KH = nfc // 2

        def epi_d2(S, ci, c0, cw, pss, stg, sn):
            pa, pn = self.side_load(S, "pa", dr["fT"], c0, cw, O0, NO)
            for (ps, pr, a0, n) in pss:
                S.P.dve(lambda e, ps=ps, a0=a0, n=n: e.tensor_tensor(out=stg[0:cw, a0:a0 + n], in0=ps,
                                                                     in1=pa[0:cw, a0:a0 + n], op=ALU.add),
                        reads=[pr, pn], writes=[sn])
            S.P.dma(lambda e: e.dma_start(out=dr["x2T"][c0:c0 + cw, O0:O0 + NO], in_=stg[0:cw, :]), S.key("st"),
                    reads=[sn])

        if self.want("down"):
            self.lin_stage("d1", dr["w_down"][0:KH * 128, :], KH, 128, dr["actT"][0:KH * 128, :], O0, NO, ch32,
                           epi_store(dr["fT"], O0, NO), wblk=256)
            self.lin_stage("d2", dr["w_down"][KH * 128:, :], nfc - KH, 128, dr["actT"][KH * 128:, :], O0, NO, ch32,
                           epi_d2, wblk=256)
        if self.want("norm3"):
            self.norm_stage("n3", dr["x2T"], O0, NO, "g_post_ffn", dr["hpT"], res=dr["x1T"], out_x=dr["fT"],
                            gain2="g_pe")

        if self.want("pe"):
            self.lin_stage("pp", dr["w_pe"], 2, 128, dr["pT"], 0, NO, ch32, epi_store(dr["ppT"], O0, NO), a_cast=True)

            def epi_g(S, ci, c0, cw, pss, stg, sn):
                pp, ppn = self.side_load(S, "pp", dr["ppT"], c0, cw, O0, NO)
                x2, x2n = self.side_load(S, "x2", dr["fT"], c0, cw, O0, NO)
                for (ps, pr, a0, n) in pss:
                    S.P.act(lambda e, ps=ps, a0=a0, n=n: e.activation(out=stg[0:cw, a0:a0 + n], in_=ps,
                                                                      func=AF.Sigmoid), reads=[pr], writes=[sn])
                S.P.dve(lambda e: e.tensor_tensor(out=stg[0:cw, :], in0=stg[0:cw, :], in1=pp[0:cw, :], op=ALU.mult),
                        reads=[sn, ppn], writes=[sn])
                S.P.pool(lambda e: e.tensor_tensor(out=stg[0:cw, :], in0=stg[0:cw, :], in1=x2[0:cw, :], op=ALU.add),
                         reads=[sn, x2n], writes=[sn])
                S.P.dma(lambda e: e.dma_start(out=dr["yT"][c0:c0 + cw, :], in_=stg[0:cw, :]), S.key("st"), reads=[sn])

            self.lin_stage("pg", dr["w_pe_gate"], 32, 128, dr["hpT"], O0, NO, ch32, epi_g)

    def ffact_stage(self):
        c = self.cfg
        dr = self.dr
        NOP = c.CTX - c.OWN0
        with Stage(self, "fa") as S:
            P = S.P
            gp = [S.sb("gp%d" % i, [128, NOP + 2], F32) for i in range(2)]
            gs = [S.sb("gs%d" % i, [128, 16, 10], F32) for i in range(2)]
            st = [S.sb("st%d" % i, [128, 16, 2], F32) for i in range(2)]
            vv = [S.sb("vv%d" % i, [128, c.TOW], F32) for i in range(2)]
            cvs = [S.sb("cv%d" % i, [128, c.TOW], F32) for i in range(2)]
            uus = [S.sb("uu%d" % i, [128, c.TOW], F32) for i in range(2)]
            ab = [S.sb("ab%d" % i, [128, c.TOW], BF16) for i in range(2)]
            P0 = S.P

            def chunk(j):
                P = Rec()
                b = j % 2
                cv, uu = cvs[b], uus[b]
                cvn, uun = "cv%d" % b, "uu%d" % b
                r0 = j * 128
                g_, gn = gp[b], "gp%d" % b
                s_, sn_ = gs[b], "gs%d" % b
                t_, tn = st[b], "st%d" % b
                v_, vn = vv[b], "vv%d" % b
                a_, an = ab[b], "ab%d" % b
                P.dma(lambda e, g_=g_, r0=r0: e.dma_start(out=g_[:, :], in_=dr["ugT"][r0:r0 + 128, c.OWN0 - 2:c.CTX]),
                      S.key("l"), writes=[gn])
                P.dma(lambda e, s_=s_, r0=r0: e.dma_start(
                    out=s_[:, :, 2:10], in_=dr["ugT"][r0:r0 + 128, c.CTX:c.TT].rearrange("p (s j) -> p s j", j=8)),
                    S.key("l"), writes=[sn_])
                P.dma(lambda e, t_=t_, r0=r0: e.dma_start(out=t_[:], in_=dr["sconvT"][r0:r0 + 128, :, :]),
                      S.key("l"), writes=[tn])
                P.dma(lambda e, v_=v_, r0=r0: e.dma_start(out=v_[:, :], in_=dr["uvT"][r0:r0 + 128, c.OWN0:c.TT]),
                      S.key("l"), writes=[vn])
                P.pool(lambda e, s_=s_, t_=t_: e.tensor_copy(out=s_[:, :, 0:2], in_=t_[:]), reads=[tn, sn_],
                       writes=[sn_])
                P.pool(lambda e, g_=g_: e.tensor_scalar(out=g_[:, 0:2], in0=g_[:, 0:2], scalar1=self.pcol("flag"),
                                                        scalar2=None, op0=ALU.mult), reads=[gn], writes=[gn])
                w = [self.pcol("conv_w%d" % k, j) for k in range(3)]
                bcol = self.pcol("conv_b", j)
                for (cvv, src, rn, nn) in ((cv[:, 0:NOP], lambda k, g_=g_: g_[:, k:k + NOP], gn, None),
                                           (cv[:, NOP:].rearrange("p (s j) -> p s j", j=8),
                                            lambda k, s_=s_: s_[:, :, k:k + 8], sn_, None)):
                    P.dve(lambda e, cvv=cvv, src=src, w=w, bcol=bcol: e.tensor_scalar(out=cvv, in0=src(2), scalar1=w[2], scalar2=bcol,
                                                                      op0=ALU.mult, op1=ALU.add), reads=[rn],
                          writes=[cvn])
                    P.dve(lambda e, cvv=cvv, src=src, w=w: e.scalar_tensor_tensor(out=cvv, in0=src(1), scalar=w[1], in1=cvv,
                                                                             op0=ALU.mult, op1=ALU.add),
                          reads=[rn, cvn], writes=[cvn])
                    P.dve(lambda e, cvv=cvv, src=src, w=w: e.scalar_tensor_tensor(out=cvv, in0=src(0), scalar=w[0], in1=cvv,
                                                                             op0=ALU.mult, op1=ALU.add),
                          reads=[rn, cvn], writes=[cvn])
                P.pool(lambda e: e.tensor_tensor(out=uu[:], in0=cv[:], in1=cv[:], op=ALU.mult), reads=[cvn],
                       writes=[uun])
                P.pool(lambda e: e.tensor_scalar(out=uu[:], in0=uu[:], scalar1=0.044715, scalar2=1.0, op0=ALU.mult,
                                                 op1=ALU.add), reads=[uun], writes=[uun])
                P.pool(lambda e: e.tensor_tensor(out=uu[:], in0=uu[:], in1=cv[:], op=ALU.mult), reads=[uun, cvn],
                       writes=[uun])
                P.act(lambda e: e.activation(out=uu[:], in_=uu[:], func=AF.Sigmoid, scale=1.5957691216057308),
                      reads=[uun], writes=[uun])
                P.dve(lambda e: e.tensor_tensor(out=uu[:], in0=uu[:], in1=cv[:], op=ALU.mult), reads=[uun, cvn],
                      writes=[uun])
                P.dve(lambda e, a_=a_, v_=v_: e.tensor_tensor(out=a_[:], in0=uu[:], in1=v_[:], op=ALU.mult),
                      reads=[uun, vn], writes=[an])
                P.dma(lambda e, a_=a_, r0=r0: e.dma_start(out=dr["actT"][r0:r0 + 128, c.OWN0:c.TT], in_=a_[:]),
                      S.key("s"), reads=[an])
                return P

            for j in range(0, DFF // 128, 2):
                interleave_recs(P0, [chunk(j), chunk(j + 1)])

    def kneg05(self):
        return self.pcd[:, 60:61]

    def build_rest(self):
        if self.want("gla"):
            self.gla_stage()
        if self.want("rfront"):
            self.rfront_stage()
        if self.want("rscan"):
            self.rscan_stage(False)
            self.rscan_stage(True)
        self.post_stages()

WNAMES = ["w_in", "w_alpha2", "w_branch_a", "w_decay2", "w_iclr2", "w_gate2", "w_branch_b", "w_out", "w_up",
          "w_down", "w_pe_gate", "w_pe"]


def host_core_inputs(cfg, inp, x_ctx, p_ctx, xs, ps, sgla, srwkv, sshift, sconv, flag, shared=None):
    d = {}
    d["xT"] = np.ascontiguousarray(np.concatenate([x_ctx, xs.reshape(128, D)], axis=0).T)
    d["pT"] = np.ascontiguousarray(np.concatenate([p_ctx[cfg.OWN0:], ps.reshape(128, PD)], axis=0).T)
    d["pcols"] = host_pcols(inp, flag)
    if shared is None:
        shared = {}
    if "consts" not in shared:
        shared["consts"] = host_consts()[0]
        t_ = np.arange(128)
        shared["segcol"] = np.ascontiguousarray(np.broadcast_to(
            (t_[None, :] // 8 == np.arange(16)[:, None]).astype(np.float32).reshape(1, 16 * 128), (128, 16 * 128)))
        for n in WNAMES:
            shared[n] = np.ascontiguousarray(np.asarray(inp[n][0], np.float32))
    d.update(shared)
    d["sgla"] = np.ascontiguousarray(sgla, np.float32)
    S = np.asarray(srwkv, np.float32).reshape(16, 16, 2, 64, 64)
    T = np.zeros((16, 2, 64, 16, 2, 64), np.float32)
    for hl in range(2):
        T[:, hl, :, :, hl, :] = S[:, :, hl].transpose(0, 3, 1, 2)
    d["srwkv"] = T.reshape(16, 128, 16, 128)
    d["sshiftT"] = np.ascontiguousarray(np.asarray(sshift, np.float32).T)
    d["sconvT"] = np.ascontiguousarray(np.asarray(sconv, np.float32).transpose(2, 0, 1))
    return d


def _unpad_rwkv(T):
    lead = T.shape[:-3]
    T = T.reshape(lead + (2, 64, 16, 2, 64))
    out = np.zeros(lead + (16, 2, 64, 64), np.float32)
    for hl in range(2):
        blk = T[..., hl, :, :, hl, :]
        out[..., :, hl, :, :] = np.moveaxis(blk, -3, -1)
    return out.reshape(lead + (32, 64, 64))


def host_outputs_core(cfg, r):
    o = {}
    yT = np.asarray(r["yT"])
    nown = cfg.CTX - cfg.OWN0
    o["y_own"] = yT[:, :nown].T
    o["y_smp"] = yT[:, nown:].T.reshape(16, 8, D)
    o["gla_p"] = np.asarray(r["glap"])
    o["gla_s"] = np.asarray(r["glas"])
    o["rwkv_p"] = _unpad_rwkv(np.asarray(r["rwkvp"]))
    o["rwkv_s"] = _unpad_rwkv(np.asarray(r["rwkvs"]))
    o["shift_p"] = np.asarray(r["shiftTp"])[:, 0]
    o["shift_s"] = np.asarray(r["shiftTs"]).T
    o["conv_p"] = np.asarray(r["convTp"]).T
    o["conv_s"] = np.asarray(r["convTs"]).transpose(1, 2, 0)
    return o


_NC_CACHE = {}


def kernel(**inputs):
    cfg = Cfg(16, 8)
    inp = {k: np.asarray(v) for k, v in inputs.items()}
    xp, xs = inp["x_prompt"], inp["x_sample"]
    pp, psm = inp["p_prompt"][0], inp["p_sample"][0]
    shared = {}
    in_maps = []
    for c in range(8):
        seq, half = c // 2, c % 2
        if half == 1:
            x_ctx, p_ctx = xp[seq], pp[seq]
        else:
            x_ctx = np.concatenate([np.zeros((1024, D), np.float32), xp[seq, :1024]], axis=0)
            p_ctx = np.concatenate([np.zeros((1024, PD), np.float32), pp[seq, :1024]], axis=0)
        sl = slice(16 * c, 16 * c + 16)
        in_maps.append(host_core_inputs(cfg, inp, x_ctx, p_ctx, xs[sl], psm[sl], inp["state_gla"][0, sl],
                                        inp["state_rwkv"][0, sl], inp["state_shift"][0, sl],
                                        inp["state_ffn_conv"][0, sl], flag=float(half), shared=shared))
    if "nc" not in _NC_CACHE:
        _NC_CACHE["nc"] = KB(cfg).build()
    res = run_bass_kernel_spmd(_NC_CACHE["nc"], in_maps, core_ids=list(range(8)))
    y_p = np.zeros((4, 2048, D), np.float32)
    y_s = np.zeros((128, 8, D), np.float32)
    gla_p = np.zeros((1, 4, GH, GDK, GDV), np.float32)
    rwkv_p = np.zeros((1, 4, RH, RN, RN), np.float32)
    shift_p = np.zeros((1, 4, 6592), np.float32)
    conv_p = np.zeros((1, 4, 2, DFF), np.float32)
    gla_s = np.zeros((1, 128, GH, GDK, GDV), np.float32)
    rwkv_s = np.zeros((1, 128, RH, RN, RN), np.float32)
    shift_s = np.zeros((1, 128, 6592), np.float32)
    conv_s = np.zeros((1, 128, 2, DFF), np.float32)
    for c in range(8):
        seq, half = c // 2, c % 2
        o = host_outputs_core(cfg, res.results[c])
        sl = slice(16 * c, 16 * c + 16)
        y_p[seq, half * 1024:(half + 1) * 1024] = o["y_own"]
        y_s[sl] = o["y_smp"]
        gla_s[0, sl] = o["gla_s"]
        rwkv_s[0, sl] = o["rwkv_s"]
        shift_s[0, sl] = o["shift_s"]
        conv_s[0, sl] = o["conv_s"]
        if half == 1:
            gla_p[0, seq] = o["gla_p"]
            rwkv_p[0, seq] = o["rwkv_p"]
            shift_p[0, seq] = o["shift_p"]
            conv_p[0, seq] = o["conv_p"]
    return (y_p, y_s, gla_p, rwkv_p, shift_p, conv_p, gla_s, rwkv_s, shift_s, conv_s)
```
